# Optimizing a Trainium2 kernel written in Bass

```python
import math
import jax, jax.numpy as jnp
from jax import lax
import numpy as np

D_MODEL = 1024
BATCH = 8
SEQ = 4096
DEPTH = 4

N_MIXERS = 3
D_FF = 2816
EPS = 1e-6
GDN_HEADS = 8
GDN_DK = 128
GDN_DV = 128
GDN_CONV = 4
GDN_CHUNK = 64
SC_WIDTH = 3
NSA_HEADS = 16
NSA_KV_HEADS = 4
NSA_GROUP = NSA_HEADS // NSA_KV_HEADS
NSA_DH = 64
CMP_BLOCK = 32
CMP_STRIDE = 16
CMP_HIDDEN = 256
SLC_BLOCK = 64
SLC_TOPK = 16
N_LOCAL = 2
WINDOW = 512
NSA_Q_BLOCK = 32
ROPE_THETA = 10000.0
NEG = -1e30
FORCE = 1e9
N_GDN = (DEPTH + 2) // 3
N_SC = (DEPTH + 1) // 3
N_NSA = DEPTH // 3

kernel_name = "hybrid_gdn_shortconv_nsa_macaron"


def rmsnorm(x, w):
    xf = x.astype(jnp.float32)
    y = xf * lax.rsqrt(jnp.mean(xf * xf, -1, keepdims=True) + EPS)
    return (y * w.astype(jnp.float32)).astype(x.dtype)


def l2norm(x):
    xf = x.astype(jnp.float32)
    return xf * lax.rsqrt(jnp.sum(xf * xf, -1, keepdims=True) + EPS)


def causal_depthwise_conv(x, w):
    width = w.shape[0]
    S = x.shape[1]
    xp = jnp.pad(x, ((0, 0), (width - 1, 0), (0, 0)))
    y = xp[:, 0:S] * w[0]
    for j in range(1, width):
        y = y + xp[:, j:j + S] * w[j]
    return y


def swiglu(h, w_gate_up, w_down):
    g, u = jnp.split(h @ w_gate_up, 2, axis=-1)
    return (jax.nn.silu(g) * u) @ w_down


def rope_tables(positions, dim):
    inv = 1.0 / (ROPE_THETA ** (jnp.arange(0, dim, 2, dtype=jnp.float32) / dim))
    ang = positions.astype(jnp.float32)[..., None] * inv
    ang = jnp.concatenate([ang, ang], axis=-1)[:, :, None, :]
    return jnp.cos(ang), jnp.sin(ang)


def apply_rope(x, cos, sin):
    x1, x2 = jnp.split(x, 2, axis=-1)
    rot = jnp.concatenate([-x2, x1], axis=-1)
    return (x * cos + rot * sin).astype(x.dtype)


def gated_delta_rule_chunked(q, k, v, g, beta):
    B, S, H, dk = q.shape
    dv = v.shape[-1]
    C = GDN_CHUNK
    N = S // C

    def to_chunks(t):
        return jnp.moveaxis(t.reshape((B, N, C, H) + t.shape[3:]), 3, 1)

    q = to_chunks(q) * (dk ** -0.5)
    k = to_chunks(k)
    v = to_chunks(v)
    g = to_chunks(g)
    beta = to_chunks(beta)
    gc = jnp.cumsum(g, axis=-1)
    causal = jnp.tril(jnp.ones((C, C), bool))
    strict = jnp.tril(jnp.ones((C, C), bool), -1)
    decay = jnp.exp(jnp.where(causal, gc[..., :, None] - gc[..., None, :], -jnp.inf))
    k_beta = k * beta[..., None]
    A = jnp.where(strict, jnp.einsum('bhncd,bhned->bhnce', k_beta, k) * decay, 0.0)
    eye = jnp.eye(C, dtype=jnp.float32)
    u = lax.linalg.triangular_solve(eye + A, v * beta[..., None], left_side=True,
                                    lower=True, unit_diagonal=True)
    w = lax.linalg.triangular_solve(eye + A, k_beta * jnp.exp(gc)[..., None], left_side=True,
                                    lower=True, unit_diagonal=True)
    attn_intra = jnp.where(causal, jnp.einsum('bhncd,bhned->bhnce', q, k) * decay, 0.0)
    q_dec = q * jnp.exp(gc)[..., None]
    k_dec = k * jnp.exp(gc[..., -1:] - gc)[..., None]
    g_last = jnp.exp(gc[..., -1])
    xs = tuple(jnp.moveaxis(t, 2, 0) for t in (q_dec, k_dec, u, w, attn_intra, g_last))

    def step(state, inp):
        qd, kd, u_c, w_c, a_c, gl = inp
        v_new = u_c - jnp.einsum('bhcd,bhde->bhce', w_c, state)
        o = jnp.einsum('bhcd,bhde->bhce', qd, state) + jnp.einsum('bhce,bhef->bhcf', a_c, v_new)
        state = state * gl[..., None, None] + jnp.einsum('bhcd,bhce->bhde', kd, v_new)
        return state, o

    s0 = jnp.zeros((B, H, dk, dv), jnp.float32)
    _, o = lax.scan(step, s0, xs)
    return jnp.transpose(o, (1, 0, 3, 2, 4)).reshape(B, S, H, dv)


def gdn_mixer(h, w_in, conv_w, A_log, dt_bias, out_norm, w_out):
    B, S, _ = h.shape
    H, dk, dv = GDN_HEADS, GDN_DK, GDN_DV
    n_qkv = 2 * H * dk + H * dv
    proj = h @ w_in
    qkv, gate, a, b = jnp.split(proj, [n_qkv, n_qkv + H * dv, n_qkv + H * dv + H], axis=-1)
    qkv = jax.nn.silu(causal_depthwise_conv(qkv, conv_w))
    q, k, v = jnp.split(qkv, [H * dk, 2 * H * dk], axis=-1)
    q = l2norm(q.reshape(B, S, H, dk))
    k = l2norm(k.reshape(B, S, H, dk))
    v = v.reshape(B, S, H, dv).astype(jnp.float32)
    beta = jax.nn.sigmoid(b.astype(jnp.float32))
    g = -jnp.exp(A_log.astype(jnp.float32)) * jax.nn.softplus(
        a.astype(jnp.float32) + dt_bias.astype(jnp.float32))
    o = gated_delta_rule_chunked(q, k, v, g, beta)
    o = rmsnorm(o, out_norm) * jax.nn.silu(gate.reshape(B, S, H, dv).astype(jnp.float32))
    return o.reshape(B, S, H * dv).astype(h.dtype) @ w_out


def short_conv_mixer(h, w_in, conv_w, w_out):
    b_gate, c_gate, xin = jnp.split(h @ w_in, 3, axis=-1)
    y = causal_depthwise_conv(c_gate * xin, conv_w)
    return (b_gate * y) @ w_out


def nsa_mixer(h, cos, sin, w_in, q_norm, k_norm, cmp_pe, cmp_w1, cmp_b1, cmp_w2, cmp_b2, w_out):
    B, S, _ = h.shape
    H, Hk, G, dh = NSA_HEADS, NSA_KV_HEADS, NSA_GROUP, NSA_DH
    kvw = Hk * dh
    sizes = [H * dh] + [kvw] * 6
    splits = [int(s) for s in np.cumsum(sizes)]
    q, kc, vc, ks, vs, kw, vw, gates = jnp.split(h @ w_in, splits, axis=-1)
    q = rmsnorm(q.reshape(B, S, H, dh), q_norm)
    q_rot = apply_rope(q, cos, sin)
    ks = apply_rope(rmsnorm(ks.reshape(B, S, Hk, dh), k_norm[1]), cos, sin)
    kw = apply_rope(rmsnorm(kw.reshape(B, S, Hk, dh), k_norm[2]), cos, sin)
    vs = vs.reshape(B, S, Hk, dh)
    vw = vw.reshape(B, S, Hk, dh)
    gates = jax.nn.sigmoid(gates.reshape(B, S, H, 3).astype(jnp.float32))

    r = CMP_BLOCK // CMP_STRIDE
    n_chunks = S // CMP_STRIDE
    Nc = n_chunks - r + 1

    def compress(t, i):
        c = t.reshape(B, n_chunks, CMP_STRIDE, Hk, dh)
        blocks = jnp.concatenate([c[:, j:j + Nc] for j in range(r)], axis=2)
        blocks = blocks + cmp_pe[i][:, None, :]
        blocks = jnp.moveaxis(blocks, 3, 2).reshape(B, Nc, Hk, CMP_BLOCK * dh)
        hid = jax.nn.gelu(blocks @ cmp_w1[i] + cmp_b1[i])
        return hid @ cmp_w2[i] + cmp_b2[i]

    kc = rmsnorm(compress(kc.reshape(B, S, Hk, dh), 0), k_norm[0])
    vc = compress(vc.reshape(B, S, Hk, dh), 1)

    qn_t = q.reshape(B, S, Hk, G, dh).transpose(0, 2, 3, 1, 4)
    qr_t = q_rot.reshape(B, S, Hk, G, dh).transpose(0, 2, 3, 1, 4)
    kc_t = kc.transpose(0, 2, 1, 3)
    vc_t = vc.transpose(0, 2, 1, 3)
    Ns = S // SLC_BLOCK
    n_sel = min(SLC_TOPK, Ns)
    ks_blk = ks.transpose(0, 2, 1, 3).reshape(B, Hk, Ns, SLC_BLOCK, dh)
    vs_blk = vs.transpose(0, 2, 1, 3).reshape(B, Hk, Ns, SLC_BLOCK, dh)
    pad = ((0, 0), (0, 0), (WINDOW, 0), (0, 0))
    kw_pad = jnp.pad(kw.transpose(0, 2, 1, 3), pad)
    vw_pad = jnp.pad(vw.transpose(0, 2, 1, 3), pad)

    cmp_end = jnp.arange(Nc) * CMP_STRIDE + CMP_BLOCK - 1
    ci = jnp.arange(Nc)[:, None]
    sj = jnp.arange(Ns)[None, :]
    overlap = jnp.clip(jnp.minimum(ci * CMP_STRIDE + CMP_BLOCK, (sj + 1) * SLC_BLOCK)
                       - jnp.maximum(ci * CMP_STRIDE, sj * SLC_BLOCK), 0, None)
    overlap = overlap.astype(jnp.float32) / CMP_BLOCK
    blk_ids = jnp.arange(Ns)
    b_idx = jnp.arange(B)[:, None, None, None]
    h_idx = jnp.arange(Hk)[None, :, None, None]
    scale = dh ** -0.5
    QB = NSA_Q_BLOCK

    def block(i):
        s0 = i * QB
        tpos = s0 + jnp.arange(QB)
        qn = lax.dynamic_slice_in_dim(qn_t, s0, QB, axis=3).astype(jnp.float32)
        qr = lax.dynamic_slice_in_dim(qr_t, s0, QB, axis=3).astype(jnp.float32)
        sc = jnp.einsum('bhgqd,bhkd->bhgqk', qn, kc_t) * scale
        cmask = cmp_end[None, :] <= tpos[:, None]
        p_c = jax.nn.softmax(jnp.where(cmask, sc, NEG), axis=-1)
        p_c = jnp.where(cmask.any(-1)[:, None], p_c, 0.0)
        o_c = jnp.einsum('bhgqk,bhkd->bhgqd', p_c, vc_t)
        imp = jnp.einsum('bhgqc,cs->bhqs', p_c, overlap)
        svalid = (blk_ids * SLC_BLOCK)[None, :] <= tpos[:, None]
        dist = (tpos // SLC_BLOCK)[:, None] - blk_ids[None, :]
        forced = (blk_ids == 0)[None, :] | ((dist >= 0) & (dist < N_LOCAL))
        score = jnp.where(svalid & forced, FORCE, jnp.where(svalid, imp, -1.0))
        top_score, idx = lax.top_k(score, n_sel)
        sel_valid = top_score >= 0.0
        kb = ks_blk[b_idx, h_idx, idx]
        vb = vs_blk[b_idx, h_idx, idx]
        ss = jnp.einsum('bhgqd,bhqnkd->bhgqnk', qr, kb) * scale
        kpos = idx[..., None] * SLC_BLOCK + jnp.arange(SLC_BLOCK)
        smask = sel_valid[..., None] & (kpos <= tpos[:, None, None])
        ss = jnp.where(smask[:, :, None], ss, NEG)
        p_s = jax.nn.softmax(ss.reshape(B, Hk, G, QB, n_sel * SLC_BLOCK), axis=-1)
        p_s = p_s.reshape(B, Hk, G, QB, n_sel, SLC_BLOCK)
        o_s = jnp.einsum('bhgqnk,bhqnkd->bhgqd', p_s, vb)
        kwin = lax.dynamic_slice_in_dim(kw_pad, s0, WINDOW + QB, axis=2)
        vwin = lax.dynamic_slice_in_dim(vw_pad, s0, WINDOW + QB, axis=2)
        wpos = s0 - WINDOW + jnp.arange(WINDOW + QB)
        wmask = ((wpos[None, :] <= tpos[:, None]) & (wpos[None, :] > tpos[:, None] - WINDOW)
                 & (wpos[None, :] >= 0))
        sw = jnp.einsum('bhgqd,bhkd->bhgqk', qr, kwin) * scale
        p_w = jax.nn.softmax(jnp.where(wmask, sw, NEG), axis=-1)
        o_w = jnp.einsum('bhgqk,bhkd->bhgqd', p_w, vwin)
        return o_c, o_s, o_w

    o_c, o_s, o_w = lax.map(block, jnp.arange(S // QB))

    def to_bshd(o):
        return jnp.transpose(o, (1, 0, 4, 2, 3, 5)).reshape(B, S, H, dh)

    o = (gates[..., 0:1] * to_bshd(o_c) + gates[..., 1:2] * to_bshd(o_s)
         + gates[..., 2:3] * to_bshd(o_w))
    return o.reshape(B, S, H * dh).astype(h.dtype) @ w_out


def setup_inputs(seed: int = 0) -> dict:
    key = jax.random.key(seed)
    ks = jax.random.split(key, 26)
    f32 = jnp.float32
    D, F = D_MODEL, D_FF

    def nrm(k, shape, scale):
        return jax.random.normal(k, shape, f32) * scale

    def gain(k, shape):
        return 1.0 + 0.02 * jax.random.normal(k, shape, f32)

    gdn_in = 2 * GDN_HEADS * GDN_DK + 2 * GDN_HEADS * GDN_DV + 2 * GDN_HEADS
    nsa_in = NSA_HEADS * NSA_DH + 6 * NSA_KV_HEADS * NSA_DH + 3 * NSA_HEADS
    n_conv = 2 * GDN_HEADS * GDN_DK + GDN_HEADS * GDN_DV
    dt = jnp.exp(jax.random.uniform(ks[7], (N_GDN, GDN_HEADS), f32,
                                    minval=math.log(1e-3), maxval=math.log(1e-1)))
    return {
        "x": jax.random.normal(ks[0], (BATCH, SEQ, D), f32),
        "positions": jnp.broadcast_to(jnp.arange(SEQ, dtype=jnp.int32), (BATCH, SEQ)),
        "ffn_norm": gain(ks[1], (DEPTH, 2, D)),
        "ffn_w_gate_up": nrm(ks[2], (DEPTH, 2, D, 2 * F), D ** -0.5),
        "ffn_w_down": nrm(ks[3], (DEPTH, 2, F, D), F ** -0.5),
        "mixer_norm": gain(ks[4], (DEPTH, D)),
        "gdn_w_in": nrm(ks[5], (N_GDN, D, gdn_in), D ** -0.5),
        "gdn_conv_w": nrm(ks[6], (N_GDN, GDN_CONV, n_conv), GDN_CONV ** -0.5),
        "gdn_A_log": jnp.log(jax.random.uniform(ks[8], (N_GDN, GDN_HEADS), f32, minval=1.0, maxval=16.0)),
        "gdn_dt_bias": dt + jnp.log(-jnp.expm1(-dt)),
        "gdn_out_norm": gain(ks[9], (N_GDN, GDN_DV)),
        "gdn_w_out": nrm(ks[10], (N_GDN, GDN_HEADS * GDN_DV, D), (GDN_HEADS * GDN_DV) ** -0.5),
        "sc_w_in": nrm(ks[11], (N_SC, D, 3 * D), D ** -0.5),
        "sc_conv_w": nrm(ks[12], (N_SC, SC_WIDTH, D), SC_WIDTH ** -0.5),
        "sc_w_out": nrm(ks[13], (N_SC, D, D), D ** -0.5),
        "nsa_w_in": nrm(ks[14], (N_NSA, D, nsa_in), D ** -0.5),
        "nsa_q_norm": gain(ks[15], (N_NSA, NSA_DH)),
        "nsa_k_norm": gain(ks[16], (N_NSA, 3, NSA_DH)),
        "nsa_cmp_pe": nrm(ks[17], (N_NSA, 2, CMP_BLOCK, NSA_DH), 0.02),
        "nsa_cmp_w1": nrm(ks[18], (N_NSA, 2, CMP_BLOCK * NSA_DH, CMP_HIDDEN), (CMP_BLOCK * NSA_DH) ** -0.5),
        "nsa_cmp_b1": nrm(ks[19], (N_NSA, 2, CMP_HIDDEN), 0.01),
        "nsa_cmp_w2": nrm(ks[20], (N_NSA, 2, CMP_HIDDEN, NSA_DH), CMP_HIDDEN ** -0.5),
        "nsa_cmp_b2": nrm(ks[21], (N_NSA, 2, NSA_DH), 0.01),
        "nsa_w_out": nrm(ks[22], (N_NSA, NSA_HEADS * NSA_DH, D), (NSA_HEADS * NSA_DH) ** -0.5),
    }


def reference(x, positions, ffn_norm, ffn_w_gate_up, ffn_w_down, mixer_norm,
              gdn_w_in, gdn_conv_w, gdn_A_log, gdn_dt_bias, gdn_out_norm, gdn_w_out,
              sc_w_in, sc_conv_w, sc_w_out,
              nsa_w_in, nsa_q_norm, nsa_k_norm, nsa_cmp_pe, nsa_cmp_w1, nsa_cmp_b1,
              nsa_cmp_w2, nsa_cmp_b2, nsa_w_out):
    cos, sin = rope_tables(positions, NSA_DH)
    h = x
    for layer in range(DEPTH):
        h = h + 0.5 * swiglu(rmsnorm(h, ffn_norm[layer, 0]), ffn_w_gate_up[layer, 0], ffn_w_down[layer, 0])
        hn = rmsnorm(h, mixer_norm[layer])
        kind = layer % N_MIXERS
        j = layer // N_MIXERS
        if kind == 0:
            mix = gdn_mixer(hn, gdn_w_in[j], gdn_conv_w[j], gdn_A_log[j], gdn_dt_bias[j],
                            gdn_out_norm[j], gdn_w_out[j])
        elif kind == 1:
            mix = short_conv_mixer(hn, sc_w_in[j], sc_conv_w[j], sc_w_out[j])
        else:
            mix = nsa_mixer(hn, cos, sin, nsa_w_in[j], nsa_q_norm[j], nsa_k_norm[j], nsa_cmp_pe[j],
                            nsa_cmp_w1[j], nsa_cmp_b1[j], nsa_cmp_w2[j], nsa_cmp_b2[j], nsa_w_out[j])
        h = h + mix
        h = h + 0.5 * swiglu(rmsnorm(h, ffn_norm[layer, 1]), ffn_w_gate_up[layer, 1], ffn_w_down[layer, 1])
    return h
```

```python
import contextlib
import numpy as np
import concourse.bass as bass
import concourse.mybir as mybir
from concourse.bass_utils import run_bass_kernel_spmd

F32 = mybir.dt.float32
BF16 = mybir.dt.bfloat16
I32 = mybir.dt.int32
AF = mybir.ActivationFunctionType
ALU = mybir.AluOpType
AX = mybir.AxisListType

D = 1024
SEQ = 4096
DEPTH = 4
DFF = 2816
NCH = DFF // 128
EPS = 1e-6

ENGS = ("pe", "act", "dve", "pool", "sp")


class Op:
    __slots__ = ("eng", "fn", "deps", "sig", "cnt", "dkey", "dcnt", "reads", "writes", "bar")

    def __init__(self, eng, fn, reads, writes, dkey):
        self.eng = eng
        self.fn = fn
        self.reads = reads
        self.writes = writes
        self.dkey = dkey
        self.deps = []
        self.sig = False
        self.cnt = 0
        self.dcnt = 0
        self.bar = None


class Sched:
    def __init__(self, nc):
        self.nc = nc
        self.ops = {e: [] for e in ENGS}
        self.res = {}
        self.dma_cnt = {}
        self.nops = 0
        self.cut = False

    def add(self, eng, fn, reads=(), writes=(), dkey=None, ndma=1):
        if self.cut:
            return None
        reads = tuple(reads)
        writes = tuple(writes)
        op = Op(eng, fn, reads, writes, dkey)
        deps = {}
        for k in reads:
            r = self.res.get(k)
            if r is not None and r[0] is not None:
                deps[id(r[0])] = r[0]
        for k in writes:
            r = self.res.get(k)
            if r is not None:
                if r[0] is not None:
                    deps[id(r[0])] = r[0]
                for q in r[1]:
                    deps[id(q)] = q
        final = []
        rset = set(reads)
        wset = set(writes)
        for d in deps.values():
            if d is op:
                continue
            if d.dkey is None and dkey is None and d.eng == eng:
                if eng == "pe":
                    continue
                if eng != "pool" and not (rset.intersection(d.writes)) and not (wset.intersection(d.writes)):
                    continue
            final.append(d)
            if d.dkey is None:
                d.sig = True
        op.deps = final
        for k in reads:
            r = self.res.get(k)
            if r is None:
                self.res[k] = [None, [op]]
            else:
                r[1].append(op)
        for k in writes:
            self.res[k] = [op, []]
        if dkey is not None:
            self.dma_cnt[dkey] = self.dma_cnt.get(dkey, 0) + 16 * ndma
            op.dcnt = self.dma_cnt[dkey]
        self.ops[eng].append(op)
        self.nops += 1
        return op

    def stop_at(self, name):
        import os
        if os.environ.get("NSA_STOP") == name:
            self.cut = True

    def barrier(self):
        if self.cut:
            return
        snap_ops = {}
        for e in ENGS:
            last = None
            for o in reversed(self.ops[e]):
                if o.bar is None and o.dkey is None:
                    last = o
                    break
            if last is not None:
                last.sig = True
                snap_ops[e] = last
        dsnap = dict(self.dma_cnt)
        for e in ENGS:
            op = Op(e, None, (), (), None)
            op.bar = (dict(snap_ops), dsnap)
            self.ops[e].append(op)
        self.res = {}

    def emit(self):
        nc = self.nc
        with contextlib.ExitStack() as st:
            esem = {e: st.enter_context(nc.semaphore("s_" + e)) for e in ENGS}
            dsem = {k: st.enter_context(nc.semaphore("d_%d" % i)) for i, k in enumerate(self.dma_cnt)}
            for e in ENGS:
                c = 0
                for o in self.ops[e]:
                    if o.sig:
                        c += 1
                    o.cnt = c
            block = st.enter_context(nc.Block())
            final_d = dict(self.dma_cnt)

            def run(e, eng):
                seen = {}

                def wait(sem, key, val):
                    if val <= 0 or seen.get(key, 0) >= val:
                        return
                    seen[key] = val
                    eng.wait_ge(sem, val)

                for o in self.ops[e]:
                    if o.bar is not None:
                        so, ds = o.bar
                        for e2, lo in so.items():
                            if e2 != e:
                                wait(esem[e2], ("e", e2), lo.cnt)
                        for k, v in ds.items():
                            wait(dsem[k], ("d", k), v)
                        continue
                    need = {}
                    for d in o.deps:
                        if d.dkey is not None:
                            key, v, sem = ("d", d.dkey), d.dcnt, dsem[d.dkey]
                        else:
                            key, v, sem = ("e", d.eng), d.cnt, esem[d.eng]
                        if need.get(key, (None, 0))[1] < v:
                            need[key] = (sem, v)
                    for key, (sem, v) in need.items():
                        wait(sem, key, v)
                    r = o.fn(eng)
                    if o.dkey is not None:
                        if not isinstance(r, (list, tuple)):
                            r = [r]
                        for ins in r:
                            ins.then_inc(dsem[o.dkey], 16)
                    elif o.sig:
                        r.then_inc(esem[e], 1)
                if e == "sp":
                    for k, v in final_d.items():
                        wait(dsem[k], ("d", k), v)
                    for e2 in ENGS:
                        if e2 != e:
                            for o in reversed(self.ops[e2]):
                                if o.sig:
                                    wait(esem[e2], ("e", e2), o.cnt)
                                    break

            @block.tensor
            def _(eng):
                run("pe", eng)

            @block.scalar
            def _(eng):
                run("act", eng)

            @block.vector
            def _(eng):
                run("dve", eng)

            @block.gpsimd
            def _(eng):
                run("pool", eng)

            @block.sync
            def _(eng):
                run("sp", eng)


class SBAlloc:
    def __init__(self, big, ncols):
        self.big = big
        self.ncols = ncols
        self.top = 0
        self.peak = 0

    def mark(self):
        return self.top

    def release(self, m):
        self.top = m

    def f32(self, n):
        o = self.top
        self.top += n
        assert self.top <= self.ncols, "SBUF overflow %d > %d" % (self.top, self.ncols)
        self.peak = max(self.peak, self.top)
        return self.big[:, o:o + n]

    def bf16(self, n):
        w = (n + 1) // 2
        return self.f32(w).bitcast(BF16)[:, 0:n]


SB_COLS = 53000


class K:
    def __init__(self, ntok=SEQ, T=1024):
        self.ntok = ntok
        self.T = T
        self.nc = bass.Bass("TRN2", target_bir_lowering=False)
        self.st = contextlib.ExitStack()
        self.dram = {}

    def din(self, name, shape, dt=F32):
        ap = self.nc.dram_tensor(name, list(shape), dt, kind="ExternalInput").ap()
        self.dram[name] = ap
        return ap

    def dout(self, name, shape, dt=F32):
        ap = self.nc.dram_tensor(name, list(shape), dt, kind="ExternalOutput").ap()
        self.dram[name] = ap
        return ap

    def begin(self):
        nc = self.nc
        big = self.st.enter_context(nc.sbuf_tensor("SB", [128, SB_COLS], F32))
        self.ps = self.st.enter_context(nc.psum_tensor("PS", [128, 8 * 512], F32))
        self.sb = SBAlloc(big, SB_COLS)
        self.S = Sched(nc)
        S = self.S
        sb = self.sb
        self.ones32 = sb.f32(128)
        self.epsc = sb.f32(1)
        S.add("pool", lambda e: e.memset(self.ones32, 1.0), writes=["ones32"])
        S.add("pool", lambda e: e.memset(self.epsc, EPS), writes=["epsc"])

    def bank(self, b, n=512):
        return self.ps[:, b * 512:b * 512 + n]

    def finish(self):
        self.S.emit()
        self.st.close()
        return self.nc

    def rmsnorm_tile(self, x32, xkey, xn, xnkey, gamma, gkey, T, sq, rstd, psb, tag):
        S = self.S
        for s in range(T // 512):
            pt = self.bank(psb)
            for k in range(8):
                xs = x32[:, k * T + s * 512:k * T + (s + 1) * 512]
                q = sq[k % 2]
                S.add("act", lambda e, q=q, xs=xs: e.activation(out=q, in_=xs, func=AF.Square),
                      reads=[(xkey, k)], writes=[(tag + "sq", k % 2)])
                S.add("pe", lambda e, q=q, k=k, pt=pt: e.matmul(out=pt, lhsT=self.ones32, rhs=q,
                                                                start=(k == 0), stop=(k == 7)),
                      reads=[(tag + "sq", k % 2), "ones32"], writes=[("ps", psb)])
            S.add("act", lambda e, pt=pt: e.activation(out=rstd, in_=pt, func=AF.Sqrt, bias=self.epsc, scale=1.0 / D),
                  reads=[("ps", psb), "epsc"], writes=[tag + "rstd"])
            S.add("dve", lambda e: e.reciprocal(out=rstd, in_=rstd), reads=[tag + "rstd"], writes=[tag + "rstd"])
            for k in range(8):
                xs = x32[:, k * T + s * 512:k * T + (s + 1) * 512]
                xo = xn[:, k * T + s * 512:k * T + (s + 1) * 512]
                S.add("dve", lambda e, xs=xs, xo=xo, k=k: e.scalar_tensor_tensor(
                    out=xo, in0=xs, scalar=gamma[:, k:k + 1], in1=rstd, op0=ALU.mult, op1=ALU.mult),
                    reads=[(xkey, k), tag + "rstd", gkey], writes=[(xnkey, k, s)])

    def ffn_phase(self, src, dst, wgu, wd, gamma_col):
        S, sb = self.S, self.sb
        S.barrier()
        m = sb.mark()
        T = self.T
        NT = self.ntok // T
        NS = T // 512
        x32 = [sb.f32(8 * T) for _ in range(2)]
        xn = [sb.bf16(8 * T) for _ in range(2)]
        h = sb.bf16(NCH * T)
        NWG = 3
        wgb = [sb.bf16(2 * 8 * 128) for _ in range(NWG)]
        wdb = [sb.bf16(NCH * 128) for _ in range(2)]
        sq = [sb.f32(512) for _ in range(2)]
        rstd = sb.f32(512)
        sg = [sb.f32(512) for _ in range(2)]
        gam = sb.f32(8)
        S.add("sp", lambda e: e.dma_start(out=gam, in_=gamma_col), writes=["gam"], dkey="gam")

        def load_x(t):
            b = t % 2
            for k in range(8):
                S.add("sp", lambda e, k=k, b=b, t=t: e.dma_start(
                    out=x32[b][:, k * T:(k + 1) * T], in_=src[k * 128:(k + 1) * 128, t * T:(t + 1) * T]),
                    writes=[("x32_%d" % b, k)], dkey=("x32", b, k))

        def norm(t):
            b = t % 2
            self.rmsnorm_tile(x32[b], "x32_%d" % b, xn[b], "xn_%d" % b, gam, "gam", T, sq, rstd, 6, "f")

        wcount = [0, 0]

        def gateup(t):
            b = t % 2
            for c in range(NCH):
                wi = wcount[0] % NWG
                wcount[0] += 1
                wb = wgb[wi]
                S.add("pool", lambda e, c=c, wb=wb: [
                    e.dma_start(out=wb[:, j * 1024:(j + 1) * 1024], in_=wgu[c, :, j * 1024:(j + 1) * 1024])
                    for j in range(2)], writes=[("wg", wi)], dkey=("wg", wi), ndma=2)
                if c == 3 and t + 1 < NT and self.pipe and stage != 4:
                    load_x(t + 1)
                for s in range(NS):
                    gb = 0 + (s % 2) * 2
                    ub = 1 + (s % 2) * 2
                    pg, pu = self.bank(gb), self.bank(ub)
                    for j, pt, pb in ((0, pg, gb), (1, pu, ub)):
                        for k in range(8):
                            S.add("pe", lambda e, j=j, k=k, pt=pt, wb=wb, s=s: e.matmul(
                                out=pt, lhsT=wb[:, (j * 8 + k) * 128:(j * 8 + k + 1) * 128],
                                rhs=xn[b][:, k * T + s * 512:k * T + (s + 1) * 512],
                                start=(k == 0), stop=(k == 7)),
                                reads=[("wg", wi), ("xn_%d" % b, k, s)], writes=[("ps", pb)])
                    sgt = sg[s % 2]
                    S.add("act", lambda e, sgt=sgt, pg=pg: e.activation(out=sgt, in_=pg, func=AF.Silu),
                          reads=[("ps", gb)], writes=[("sg", s % 2)])
                    ho = h[:, c * T + s * 512:c * T + (s + 1) * 512]
                    S.add("dve", lambda e, ho=ho, sgt=sgt, pu=pu: e.tensor_tensor(out=ho, in0=pu, in1=sgt, op=ALU.mult),
                          reads=[("ps", ub), ("sg", s % 2)], writes=[("h", c, s)])

        def down(t):
            b = t % 2
            for o in range(8):
                wi = wcount[1] % 2
                wcount[1] += 1
                wb = wdb[wi]
                S.add("pool", lambda e, o=o, wb=wb: [
                    e.dma_start(out=wb[:, j * 1408:(j + 1) * 1408], in_=wd[o, :, j * 1408:(j + 1) * 1408])
                    for j in range(2)], writes=[("wd", wi)], dkey=("wd", wi), ndma=2)
                for s in range(NS):
                    pb = 4 + (o * NS + s) % 2
                    pt = self.bank(pb)
                    for c in range(NCH):
                        S.add("pe", lambda e, c=c, pt=pt, wb=wb, s=s: e.matmul(
                            out=pt, lhsT=wb[:, c * 128:(c + 1) * 128],
                            rhs=h[:, c * T + s * 512:c * T + (s + 1) * 512],
                            start=(c == 0), stop=(c == NCH - 1)),
                            reads=[("wd", wi), ("h", c, s)], writes=[("ps", pb)])
                    xs = x32[b][:, o * T + s * 512:o * T + (s + 1) * 512]
                    S.add("dve", lambda e, xs=xs, pt=pt: e.scalar_tensor_tensor(
                        out=xs, in0=pt, scalar=0.5, in1=xs, op0=ALU.mult, op1=ALU.add),
                        reads=[("ps", pb), ("x32_%d" % b, o)], writes=[("x32_%d" % b, o)])
                S.add("act", lambda e, o=o, b=b, t=t: e.dma_start(
                    out=dst[o * 128:(o + 1) * 128, t * T:(t + 1) * T], in_=x32[b][:, o * T:(o + 1) * T]),
                    reads=[("x32_%d" % b, o)], dkey=("st", b, o))

        import os
        stage = int(os.environ.get("FFN_STAGE", "9"))
        load_x(0)
        norm(0)
        if stage == 0:
            for k in range(8):
                S.add("dve", lambda e, k=k: e.tensor_copy(out=x32[0][:, k * T:(k + 1) * T], in_=xn[0][:, k * T:(k + 1) * T]),
                      reads=[("xn_0", k, 0), ("xn_0", k, 1)], writes=[("x32_0", k)])
                S.add("act", lambda e, k=k: e.dma_start(out=dst[k * 128:(k + 1) * 128, 0:T], in_=x32[0][:, k * T:(k + 1) * T]),
                      reads=[("x32_0", k)], dkey=("st", 0, k))
            sb.release(m)
            return
        self.pipe = stage != 2
        for t in range(NT):
            if not self.pipe and t > 0:
                load_x(t)
                norm(t)
            gateup(t)
            if stage == 1:
                for k in range(8):
                    S.add("dve", lambda e, k=k: e.tensor_copy(out=x32[0][:, k * T:(k + 1) * T], in_=h[:, k * T:(k + 1) * T]),
                          reads=[("h", k, 0), ("h", k, 1)], writes=[("x32_0", k)])
                    S.add("act", lambda e, k=k: e.dma_start(out=dst[k * 128:(k + 1) * 128, 0:T], in_=x32[0][:, k * T:(k + 1) * T]),
                          reads=[("x32_0", k)], dkey=("st", 0, k))
                sb.release(m)
                return
            if stage == 4 and t + 1 < NT:
                load_x(t + 1)
            if t + 1 < NT and self.pipe and stage != 3:
                norm(t + 1)
            down(t)
            if t + 1 < NT and stage == 3:
                norm(t + 1)
        sb.release(m)


    def wload(self, wb, src2d, n, key):
        S = self.S
        pieces = [(a, min(a + 2048, n)) for a in range(0, n, 2048)]
        S.add("pool", lambda e: [e.dma_start(out=wb[:, a:b], in_=src2d[:, a:b]) for a, b in pieces],
              writes=[key], dkey=key, ndma=len(pieces))

    def load_x_tile(self, src, x32, xkey, t, T, eng="sp"):
        for k in range(8):
            self.S.add(eng, lambda e, k=k: e.dma_start(
                out=x32[:, k * T:(k + 1) * T], in_=src[k * 128:(k + 1) * 128, t * T:(t + 1) * T]),
                writes=[(xkey, k)], dkey=(xkey, k))

    def sc_phase(self, src, dst, w_in, w_out, cw_d, gamma_col):
        S, sb = self.S, self.sb
        S.barrier()
        m = sb.mark()
        T = self.T
        NT = self.ntok // T
        NS = T // 512
        x32 = sb.f32(8 * T)
        xn = sb.bf16(8 * T)
        zb = sb.f32(8 * (T + 2))
        v = sb.bf16(8 * T)
        NW = 6
        wbs = [sb.bf16(1024) for _ in range(NW)]
        wob = [sb.bf16(1024) for _ in range(2)]
        sq = [sb.f32(512) for _ in range(2)]
        rstd = sb.f32(512)
        csb = [sb.f32(512) for _ in range(2)]
        ysb = [sb.f32(512) for _ in range(2)]
        gam = sb.f32(8)
        cw = sb.f32(24)
        S.add("sp", lambda e: e.dma_start(out=gam, in_=gamma_col), writes=["gam"], dkey="gam")
        S.add("sp", lambda e: e.dma_start(out=cw, in_=cw_d), writes=["cw"], dkey="cw")
        for j in range(8):
            S.add("pool", lambda e, j=j: e.memset(zb[:, j * (T + 2):j * (T + 2) + 2], 0.0), writes=[("zh", j)])
        wc = [0, 0]
        for t in range(NT):
            self.load_x_tile(src, x32, "x32", t, T)
            self.rmsnorm_tile(x32, "x32", xn, "xn", gam, "gam", T, sq, rstd, 6, "s")
            for j in range(8):
                wl = []
                for q in range(3):
                    oc = (1, 2, 0)[q] * 8 + j
                    wi = wc[0] % NW
                    wc[0] += 1
                    self.wload(wbs[wi], w_in[oc], 1024, ("win", wi))
                    wl.append(wi)
                z0 = j * (T + 2)
                for s_ in range(NS):
                    banks = (0 + 3 * (s_ % 2), 1 + 3 * (s_ % 2), 2 + 3 * (s_ % 2))
                    for q in range(3):
                        pt = self.bank(banks[q])
                        wb = wbs[wl[q]]
                        for k in range(8):
                            S.add("pe", lambda e, k=k, pt=pt, wb=wb, s_=s_: e.matmul(
                                out=pt, lhsT=wb[:, k * 128:(k + 1) * 128],
                                rhs=xn[:, k * T + s_ * 512:k * T + (s_ + 1) * 512],
                                start=(k == 0), stop=(k == 7)),
                                reads=[("win", wl[q]), ("xn", k, s_)], writes=[("ps", banks[q])])
                    pc, px, pbg = self.bank(banks[0]), self.bank(banks[1]), self.bank(banks[2])
                    cs, ys = csb[s_ % 2], ysb[s_ % 2]
                    zc = zb[:, z0 + 2 + s_ * 512:z0 + 2 + (s_ + 1) * 512]
                    zm1 = zb[:, z0 + 1 + s_ * 512:z0 + 1 + (s_ + 1) * 512]
                    zm2 = zb[:, z0 + s_ * 512:z0 + (s_ + 1) * 512]
                    S.add("act", lambda e, cs=cs, pc=pc: e.copy(out=cs, in_=pc),
                          reads=[("ps", banks[0])], writes=[("cs", s_ % 2)])
                    S.add("dve", lambda e, zc=zc, px=px, cs=cs: e.tensor_tensor(out=zc, in0=px, in1=cs, op=ALU.mult),
                          reads=[("ps", banks[1]), ("cs", s_ % 2)], writes=[("z", j, s_)])
                    S.add("act", lambda e, ys=ys, zc=zc, j=j: e.activation(out=ys, in_=zc, func=AF.Identity,
                                                                        scale=cw[:, j * 3 + 2:j * 3 + 3]),
                          reads=[("z", j, s_), "cw"], writes=[("ys", s_ % 2)])
                    hk = [("z", j, s_ - 1)] if s_ > 0 else [("zh", j)]
                    S.add("dve", lambda e, ys=ys, zm1=zm1, j=j: e.scalar_tensor_tensor(
                        out=ys, in0=zm1, scalar=cw[:, j * 3 + 1:j * 3 + 2], in1=ys, op0=ALU.mult, op1=ALU.add),
                        reads=[("z", j, s_), ("ys", s_ % 2), "cw"] + hk, writes=[("ys", s_ % 2)])
                    S.add("dve", lambda e, ys=ys, zm2=zm2, j=j: e.scalar_tensor_tensor(
                        out=ys, in0=zm2, scalar=cw[:, j * 3:j * 3 + 1], in1=ys, op0=ALU.mult, op1=ALU.add),
                        reads=[("z", j, s_), ("ys", s_ % 2), "cw"] + hk, writes=[("ys", s_ % 2)])
                    vo = v[:, j * T + s_ * 512:j * T + (s_ + 1) * 512]
                    S.add("dve", lambda e, vo=vo, pbg=pbg, ys=ys: e.tensor_tensor(out=vo, in0=pbg, in1=ys, op=ALU.mult),
                          reads=[("ps", banks[2]), ("ys", s_ % 2)], writes=[("v", j, s_)])
                S.add("pool", lambda e, z0=z0: e.tensor_copy(out=zb[:, z0:z0 + 2], in_=zb[:, z0 + T:z0 + T + 2]),
                      reads=[("z", j, s2) for s2 in range(NS)], writes=[("zh", j)])
            for o in range(8):
                wi = wc[1] % 2
                wc[1] += 1
                self.wload(wob[wi], w_out[o], 1024, ("wout", wi))
                for s_ in range(NS):
                    pb = 6 + (o * NS + s_) % 2
                    pt = self.bank(pb)
                    for k in range(8):
                        S.add("pe", lambda e, k=k, pt=pt, wi=wi, s_=s_: e.matmul(
                            out=pt, lhsT=wob[wi][:, k * 128:(k + 1) * 128],
                            rhs=v[:, k * T + s_ * 512:k * T + (s_ + 1) * 512],
                            start=(k == 0), stop=(k == 7)),
                            reads=[("wout", wi), ("v", k, s_)], writes=[("ps", pb)])
                    xs = x32[:, o * T + s_ * 512:o * T + (s_ + 1) * 512]
                    S.add("dve", lambda e, xs=xs, pt=pt: e.tensor_tensor(out=xs, in0=pt, in1=xs, op=ALU.add),
                          reads=[("ps", pb), ("x32", o)], writes=[("x32", o)])
                S.add("act", lambda e, o=o, t=t: e.dma_start(
                    out=dst[o * 128:(o + 1) * 128, t * T:(t + 1) * T], in_=x32[:, o * T:(o + 1) * T]),
                    reads=[("x32", o)], dkey=("st", o))
        sb.release(m)


def prep_proj(w, kch=8):
    Kd, N = w.shape
    assert Kd == kch * 128 and N % 128 == 0
    a = w.reshape(kch, 128, N // 128, 128)
    return np.ascontiguousarray(a.transpose(2, 1, 0, 3)).reshape(N // 128, 128, kch * 128)


def prep_ffn_weights(w_gate_up, w_down):
    w = w_gate_up.reshape(8, 128, 2, NCH, 128)
    wgu = np.ascontiguousarray(w.transpose(3, 1, 2, 0, 4)).reshape(NCH, 128, 2 * 8 * 128)
    w2 = w_down.reshape(NCH, 128, 8, 128)
    wd = np.ascontiguousarray(w2.transpose(2, 1, 0, 3)).reshape(8, 128, NCH * 128)
    return wgu, wd


def norm_cols(w):
    return np.ascontiguousarray(np.asarray(w, np.float32).reshape(8, 128).T)


N_NORM = DEPTH * 3
def nsa_shapes():
    return dict(wka=(8, 128, 1024), wv=(128, 4096), w1=(2, 128, 8192), peT=(2, 128, 32), w2k=(128, 256), w2v=(128, 128),
                b2v=(1, 64), ncs=(128, NCS_W), nbc=(128, NBC_W), selc=(8, 128, 512), wq=(8, 128, 1024),
                wg=(24, 128, 1024), wo=(8, 128, 1024))


def build_program(ntok=SEQ, layers=DEPTH, skip=()):
    k = K(ntok=ntok)
    xT = k.din("xT", [D, ntok])
    yT = k.dout("yT", [D, ntok])
    gam = k.din("gam", [128, 8 * N_NORM])
    wgu = [[k.din("wgu_%d_%d" % (l, f), [NCH, 128, 2048]) for f in range(2)] for l in range(layers)]
    wd = [[k.din("wd_%d_%d" % (l, f), [8, 128, NCH * 128]) for f in range(2)] for l in range(layers)]
    sc_in = k.din("sc_w_in", [24, 128, 1024])
    sc_out = k.din("sc_w_out", [8, 128, 1024])
    sc_cw = k.din("sc_cw", [128, 24])
    cst = k.din("cst", [128, CST_W])
    gd = []
    for j in range(2):
        p = "gdn%d_" % j
        gd.append(dict(w_in=k.din(p + "w_in", [32, 128, 1024]), wab=k.din(p + "wab", [128, 128]), cw=k.din(p + "cw", [128, 96]),
                       alog=k.din(p + "alog", [128, 8]), dtb=k.din(p + "dtb", [128, 8]), onw=k.din(p + "onw", [128, 1]),
                       w_out=k.din(p + "w_out", [8, 128, 1024])))
    pos = k.din("pos", [1, ntok], I32)
    nsaW = {n: k.din("nsa_" + n, list(shp)) for n, shp in nsa_shapes().items()}
    k.begin()

    def g(i):
        return gam[:, i * 8:(i + 1) * 8]

    for l in range(layers):
        k.ffn_phase(xT if l == 0 else yT, yT, wgu[l][0], wd[l][0], g(l * 3 + 0))
        kind = l % 3
        if kind == 1 and "sc" not in skip:
            k.sc_phase(yT, yT, sc_in, sc_out, sc_cw, g(l * 3 + 1))
        if kind == 2 and "nsa" not in skip:
            k.nsa_phase(yT, yT, nsaW, pos, g(l * 3 + 1))
        if kind == 0 and "gdn" not in skip:
            q = gd[l // 3]
            k.gdn_phase(yT, yT, q["w_in"], q["wab"], q["cw"], q["alog"], q["dtb"], q["onw"], q["w_out"], cst, g(l * 3 + 1))
        k.ffn_phase(yT, yT, wgu[l][1], wd[l][1], g(l * 3 + 2))
    nc = k.finish()
    return k, nc


def prep_inputs(inp, layers=DEPTH):
    f = lambda a: np.asarray(a, dtype=np.float32)
    com = {}
    gcols = []
    for l in range(DEPTH):
        gcols += [norm_cols(f(inp["ffn_norm"])[l, 0]), norm_cols(f(inp["mixer_norm"])[l]), norm_cols(f(inp["ffn_norm"])[l, 1])]
    com["gam"] = np.ascontiguousarray(np.concatenate(gcols, axis=1))
    for l in range(layers):
        for ff in range(2):
            a, b = prep_ffn_weights(f(inp["ffn_w_gate_up"])[l, ff], f(inp["ffn_w_down"])[l, ff])
            com["wgu_%d_%d" % (l, ff)] = a
            com["wd_%d_%d" % (l, ff)] = b
    com["sc_w_in"] = prep_proj(f(inp["sc_w_in"])[0])
    com["sc_w_out"] = prep_proj(f(inp["sc_w_out"])[0])
    com["cst"] = make_consts()
    for j in range(2):
        d = prep_gdn(f(inp["gdn_w_in"])[j], f(inp["gdn_conv_w"])[j], f(inp["gdn_A_log"])[j], f(inp["gdn_dt_bias"])[j],
                     f(inp["gdn_out_norm"])[j], f(inp["gdn_w_out"])[j])
        for kk_, vv_ in d.items():
            com["gdn%d_%s" % (j, kk_)] = vv_
    for n_, a_ in prep_nsa(inp).items():
        assert tuple(a_.shape) == tuple(nsa_shapes()[n_]), (n_, a_.shape)
        com["nsa_" + n_] = a_
    cwt = f(inp["sc_conv_w"])[0]
    com["sc_cw"] = np.ascontiguousarray(cwt.reshape(3, 8, 128).transpose(2, 1, 0)).reshape(128, 24)
    return com


def kernel(**inputs):
    x = np.asarray(inputs["x"], dtype=np.float32)
    B = x.shape[0]
    com = prep_inputs(inputs)
    k, nc = build_program()
    in_maps = []
    for b in range(B):
        m = dict(com)
        m["xT"] = np.ascontiguousarray(x[b].T)
        m["pos"] = np.ascontiguousarray(np.asarray(inputs["positions"])[b].astype(np.int32).reshape(1, -1))
        in_maps.append(m)
    res = run_bass_kernel_spmd(nc, in_maps, core_ids=list(range(B)))
    out = np.stack([np.ascontiguousarray(res.results[b]["yT"].T) for b in range(B)], axis=0)
    return out.astype(np.float32)


GC = 64
CST_OFF = {}
_o = 0
for _n, _w in (("id128", 128), ("LT", 64), ("mincl", 512), ("mstrict", 512), ("sel63", 128), ("ones64", 64), ("idrep", 512)):
    CST_OFF[_n] = (_o, _w)
    _o += _w
CST_W = _o


def make_consts():
    c = np.zeros((128, CST_W), np.float32)

    def put(name, arr):
        o, w = CST_OFF[name]
        c[:arr.shape[0], o:o + w] = arr

    put("id128", np.eye(128, dtype=np.float32))
    i = np.arange(64)
    put("LT", (i[:, None] <= i[None, :]).astype(np.float32))
    mincl = (i[None, :] <= i[:, None]).astype(np.float32)
    mstr = (i[None, :] < i[:, None]).astype(np.float32)
    put("mincl", np.tile(mincl, (1, 8)))
    put("mstrict", np.tile(mstr, (1, 8)))
    s = np.zeros((64, 128), np.float32)
    s[63, :] = 1.0
    put("sel63", s)
    put("ones64", np.ones((64, 64), np.float32))
    put("idrep", np.tile(np.eye(64, dtype=np.float32), (1, 8)))
    return c


def gdn_phase(self, src, dst, w_in, wab_d, cw_d, alog_d, dtb_d, onw_d, w_out, cst_d, gamma_col):
    S, sb = self.S, self.sb
    S.barrier()
    m = sb.mark()
    T = 512
    NT = self.ntok // T
    C = GC
    NCK = T // C
    H = 8
    A = S.add
    cst = sb.f32(CST_W)
    A("sp", lambda e: e.dma_start(out=cst, in_=cst_d), writes=["cst"], dkey="cst")

    def cs_(name, rows=64):
        o, w = CST_OFF[name]
        return cst[0:rows, o:o + w]

    id128 = cs_("id128", 128)
    id64 = cst[0:64, CST_OFF["id128"][0]:CST_OFF["id128"][0] + 64]
    LT, mincl, mstrict, sel63, ones64, idrep = (cs_("LT"), cs_("mincl"), cs_("mstrict"), cs_("sel63"),
                                                 cs_("ones64"), cs_("idrep"))
    x32 = sb.f32(8 * T)
    xn = sb.bf16(8 * T)
    qkv = sb.f32(24 * T)
    gs = sb.f32(8 * T)
    og = sb.bf16(8 * T)
    St = sb.f32(H * 128)
    halo = sb.f32(24 * 3)
    pre = [sb.f32(T + 3) for _ in range(2)]
    yb = [sb.f32(T) for _ in range(2)]
    sq = [sb.f32(512) for _ in range(2)]
    rstd = sb.f32(512)
    gam = sb.f32(8)
    cw = sb.f32(96)
    wab = sb.bf16(128)
    alog = sb.f32(8)
    dtb = sb.f32(8)
    nega = sb.f32(8)
    onw = sb.f32(1)
    NW = 4
    wbs = [sb.bf16(1024) for _ in range(NW)]
    wob = [sb.bf16(1024) for _ in range(2)]
    g_t = sb.f32(NCK * 8)
    be_t = sb.f32(NCK * 8)
    tmp_ab = sb.f32(NCK * 8)
    gcs = sb.f32(8)
    egl = sb.f32(8)
    ekd = sb.f32(8)
    egc = sb.f32(8)
    bege = sb.f32(8)
    rrhs = sb.f32(512)
    Em = sb.f32(512)
    ETm = sb.f32(512)
    Pm = [sb.f32(512) for _ in range(2)]
    PTm = [sb.f32(512) for _ in range(2)]
    attT = sb.f32(512)
    Bm = sb.f32(H * 256)
    kdec = sb.f32(H * 128)
    wT = sb.f32(512)
    vnew = sb.f32(H * 128)
    om = sb.f32(H * 128)
    osq = sb.f32(H * 128)
    ss8 = sb.f32(8)

    A("sp", lambda e: e.dma_start(out=gam, in_=gamma_col), writes=["gam"], dkey="gam")
    A("sp", lambda e: e.dma_start(out=cw, in_=cw_d), writes=["cw"], dkey="cw")
    A("sp", lambda e: e.dma_start(out=alog, in_=alog_d), writes=["alog"], dkey="alog")
    A("sp", lambda e: e.dma_start(out=dtb, in_=dtb_d), writes=["dtb"], dkey="dtb")
    A("sp", lambda e: e.dma_start(out=onw, in_=onw_d), writes=["onw"], dkey="onw")
    A("pool", lambda e: e.dma_start(out=wab, in_=wab_d), writes=["wab"], dkey="wab")
    A("pool", lambda e: e.memset(halo, 0.0), writes=["halo"])
    A("pool", lambda e: e.memset(St, 0.0), writes=["S"])
    A("act", lambda e: e.activation(out=nega[0:64, :], in_=alog[0:64, :], func=AF.Exp), reads=["alog"], writes=["nega"])
    A("dve", lambda e: e.tensor_scalar(out=nega[0:64, :], in0=nega[0:64, :], scalar1=-1.0, scalar2=None, op0=ALU.mult),
      reads=["nega"], writes=["nega"])

    def bc(ap2, n):
        return ap2.unsqueeze(2).broadcast_to([ap2.shape[0], ap2.shape[1], n])

    def v3(ap, a):
        return ap.rearrange("p (a b) -> p a b", a=a)

    wc = [0, 0]
    for t in range(NT):
        self.load_x_tile(src, x32, "x32", t, T)
        self.rmsnorm_tile(x32, "x32", xn, "xn", gam, "gam", T, sq, rstd, 7, "g")
        xnk = [("xn", k, 0) for k in range(8)]
        pab = self.ps[0:64, 6 * 512:6 * 512 + NCK * 16]
        for c in range(NCK):
            for k in range(8):
                A("pe", lambda e, c=c, k=k: e.matmul(
                    out=pab[:, c * 16:(c + 1) * 16], lhsT=xn[:, k * T + c * C:k * T + (c + 1) * C],
                    rhs=wab[:, k * 16:(k + 1) * 16], start=(k == 0), stop=(k == 7), skip_group_check=True),
                    reads=[("xn", k, 0), "wab"], writes=[("ps", 6)])
        pab3 = pab.rearrange("p (c n) -> p c n", n=16)
        g3, be3, tm3 = v3(g_t[0:64, :], NCK), v3(be_t[0:64, :], NCK), v3(tmp_ab[0:64, :], NCK)
        dtb3 = dtb[0:64, :].unsqueeze(1).broadcast_to([64, NCK, 8])
        nega3 = nega[0:64, :].unsqueeze(1).broadcast_to([64, NCK, 8])
        A("dve", lambda e: e.tensor_tensor(out=tm3, in0=pab3[:, :, 0:8], in1=dtb3, op=ALU.add),
          reads=[("ps", 6), "dtb"], writes=["tmp_ab"])
        A("act", lambda e: e.activation(out=be3, in_=pab3[:, :, 8:16], func=AF.Sigmoid), reads=[("ps", 6)], writes=["be_t"])
        A("act", lambda e: e.activation(out=tm3, in_=tm3, func=AF.Exp), reads=["tmp_ab"], writes=["tmp_ab"])
        A("act", lambda e: e.activation(out=tm3, in_=tm3, func=AF.Ln, bias=1.0), reads=["tmp_ab"], writes=["tmp_ab"])
        A("dve", lambda e: e.tensor_tensor(out=g3, in0=tm3, in1=nega3, op=ALU.mult), reads=["tmp_ab", "nega"], writes=["g_t"])
        for oc in range(32):
            wi = wc[0] % NW
            wc[0] += 1
            self.wload(wbs[wi], w_in[oc], 1024, ("win", wi))
            pb = oc % 2
            pt = self.bank(pb)
            for k in range(8):
                A("pe", lambda e, k=k, pt=pt, wi=wi: e.matmul(out=pt, lhsT=wbs[wi][:, k * 128:(k + 1) * 128],
                                                            rhs=xn[:, k * T:(k + 1) * T], start=(k == 0), stop=(k == 7)),
                  reads=[("win", wi), ("xn", k, 0)], writes=[("ps", pb)])
            if oc >= 24:
                h = oc - 24
                A("act", lambda e, h=h, pt=pt: e.activation(out=gs[:, h * T:(h + 1) * T], in_=pt, func=AF.Silu),
                  reads=[("ps", pb)], writes=[("gs", h)])
                continue
            pr, y = pre[oc % 2], yb[oc % 2]
            pk, yk = ("pre", oc % 2), ("yb", oc % 2)
            A("pool", lambda e, pr=pr, oc=oc: e.tensor_copy(out=pr[:, 0:3], in_=halo[:, oc * 3:oc * 3 + 3]),
              reads=["halo"], writes=[pk])
            A("act", lambda e, pr=pr, pt=pt: e.copy(out=pr[:, 3:3 + T], in_=pt), reads=[("ps", pb)], writes=[pk])
            A("act", lambda e, y=y, pr=pr, oc=oc: e.activation(out=y, in_=pr[:, 3:3 + T], func=AF.Identity,
                                                               scale=cw[:, oc * 4 + 3:oc * 4 + 4]),
              reads=[pk, "cw"], writes=[yk])
            for tap in (2, 1, 0):
                A("dve", lambda e, y=y, pr=pr, oc=oc, tap=tap: e.scalar_tensor_tensor(
                    out=y, in0=pr[:, tap:tap + T], scalar=cw[:, oc * 4 + tap:oc * 4 + tap + 1], in1=y,
                    op0=ALU.mult, op1=ALU.add), reads=[pk, yk, "cw"], writes=[yk])
            A("pool", lambda e, pr=pr, oc=oc: e.tensor_copy(out=halo[:, oc * 3:oc * 3 + 3], in_=pr[:, T:T + 3]),
              reads=[pk], writes=["halo"])
            qo = qkv[:, oc * T:(oc + 1) * T]
            A("act", lambda e, qo=qo, y=y: e.activation(out=qo, in_=y, func=AF.Silu), reads=[yk], writes=[("qkv", oc)])
            if oc < 16:
                q2 = sq[oc % 2]
                A("act", lambda e, q2=q2, qo=qo: e.activation(out=q2, in_=qo, func=AF.Square),
                  reads=[("qkv", oc)], writes=[("gsq", oc % 2)])
                A("pe", lambda e, q2=q2: e.matmul(out=self.bank(7), lhsT=self.ones32, rhs=q2, start=True, stop=True),
                  reads=[("gsq", oc % 2)], writes=[("ps", 7)])
                A("act", lambda e: e.activation(out=rstd, in_=self.bank(7), func=AF.Sqrt, bias=self.epsc, scale=1.0),
                  reads=[("ps", 7)], writes=["grstd"])
                A("dve", lambda e: e.reciprocal(out=rstd, in_=rstd), reads=["grstd"], writes=["grstd"])
                sc_ = (128.0 ** -0.5) if oc < 8 else 1.0
                A("dve", lambda e, qo=qo, sc_=sc_: e.scalar_tensor_tensor(out=qo, in0=qo, scalar=sc_, in1=rstd,
                                                                        op0=ALU.mult, op1=ALU.mult),
                  reads=[("qkv", oc), "grstd"], writes=[("qkv", oc)])
        import os
        if os.environ.get('GDN_MAXC'):
            A('pool', lambda e: e.memset(og, 0.0), writes=[('og', c_) for c_ in range(NCK)])
        def chunk(c):
            cs = slice(c * C, (c + 1) * C)

            def qT(h):
                return qkv[:, h * T + c * C:h * T + (c + 1) * C]

            def kT(h):
                return qkv[:, (8 + h) * T + c * C:(8 + h) * T + (c + 1) * C]

            def vT(h):
                return qkv[:, (16 + h) * T + c * C:(16 + h) * T + (c + 1) * C]

            qk_keys = [("qkv", o_) for o_ in range(24)]
            g_c = g_t[0:64, c * 8:(c + 1) * 8]
            be_c = be_t[0:64, c * 8:(c + 1) * 8]
            psm = self.ps[:, 7 * 512:8 * 512]
            A("pe", lambda e: e.matmul(out=psm[0:64, 0:8], lhsT=LT, rhs=g_c, start=True, stop=True, skip_group_check=True),
              reads=["g_t", "cst"], writes=[("ps", 7)])
            A("act", lambda e: e.copy(out=gcs[0:64, :], in_=psm[0:64, 0:8]), reads=[("ps", 7)], writes=["gcs"])
            A("act", lambda e: e.activation(out=egc[0:64, :], in_=psm[0:64, 0:8], func=AF.Exp), reads=[("ps", 7)], writes=["egc"])
            A("pe", lambda e: e.matmul(out=psm[:, 8:16], lhsT=sel63, rhs=gcs[0:64, :], start=True, stop=True, skip_group_check=True),
              reads=["gcs", "cst"], writes=[("ps", 7)])
            A("act", lambda e: e.activation(out=egl, in_=psm[:, 8:16], func=AF.Exp), reads=[("ps", 7)], writes=["egl"])
            A("dve", lambda e: e.tensor_tensor(out=ekd[0:64, :], in0=psm[0:64, 8:16], in1=gcs[0:64, :], op=ALU.subtract),
              reads=[("ps", 7), "gcs"], writes=["ekd"])
            A("act", lambda e: e.activation(out=ekd[0:64, :], in_=ekd[0:64, :], func=AF.Exp), reads=["ekd"], writes=["ekd"])
            A("dve", lambda e: e.tensor_tensor(out=bege[0:64, :], in0=egc[0:64, :], in1=be_c, op=ALU.mult),
              reads=["egc", "be_t"], writes=["bege"])
            A("dve", lambda e: e.tensor_tensor(out=v3(rrhs[0:64, :], 8), in0=v3(idrep, 8), in1=bc(gcs[0:64, :], 64), op=ALU.mult),
              reads=["gcs", "cst"], writes=["rrhs"])
            A("pe", lambda e: e.matmul(out=self.ps[0:64, 6 * 512:7 * 512], lhsT=ones64, rhs=rrhs[0:64, :], start=True, stop=True),
              reads=["rrhs", "cst"], writes=[("ps", 6)])
            E3 = v3(Em[0:64, :], 8)
            A("dve", lambda e: e.tensor_tensor(out=E3, in0=v3(self.ps[0:64, 6 * 512:7 * 512], 8), in1=bc(gcs[0:64, :], 64),
                                               op=ALU.subtract), reads=[("ps", 6), "gcs"], writes=["E"])
            A("dve", lambda e: e.tensor_scalar(out=Em[0:64, :], in0=Em[0:64, :], scalar1=0.0, scalar2=None, op0=ALU.max),
              reads=["E"], writes=["E"])
            A("act", lambda e: e.activation(out=Em[0:64, :], in_=Em[0:64, :], func=AF.Exp, scale=-1.0), reads=["E"], writes=["E"])
            A("dve", lambda e: e.tensor_tensor(out=Em[0:64, :], in0=Em[0:64, :], in1=mincl, op=ALU.mult),
              reads=["E", "cst"], writes=["E"])
            b4 = self.ps[0:64, 4 * 512:5 * 512]
            b5 = self.ps[0:64, 5 * 512:6 * 512]
            for h in range(H):
                A("pe", lambda e, h=h: e.matmul(out=b4[:, h * 64:(h + 1) * 64], lhsT=kT(h), rhs=kT(h), start=True, stop=True,
                                                skip_group_check=True), reads=qk_keys, writes=[("ps", 4)])
            P0, PT0 = Pm[0], PTm[0]
            A("dve", lambda e: e.tensor_tensor(out=P0[0:64, :], in0=b4, in1=Em[0:64, :], op=ALU.mult),
              reads=[("ps", 4), "E"], writes=[("P", 0)])
            A("dve", lambda e: e.tensor_tensor(out=P0[0:64, :], in0=P0[0:64, :], in1=mstrict, op=ALU.mult),
              reads=[("P", 0), "cst"], writes=[("P", 0)])
            A("dve", lambda e: e.tensor_tensor(out=v3(P0[0:64, :], 8), in0=v3(P0[0:64, :], 8), in1=bc(be_c, 64), op=ALU.mult),
              reads=[("P", 0), "be_t"], writes=[("P", 0)])
            for h in range(H):
                A("pe", lambda e, h=h: e.matmul(out=b5[:, h * 64:(h + 1) * 64], lhsT=P0[0:64, h * 64:(h + 1) * 64], rhs=id64,
                                                start=True, stop=True, skip_group_check=True),
                  reads=[("P", 0), "cst"], writes=[("ps", 5)])
            A("act", lambda e: e.copy(out=PT0[0:64, :], in_=b5), reads=[("ps", 5)], writes=[("PT", 0)])
            for h in range(H):
                A("pe", lambda e, h=h: e.matmul(out=b4[:, h * 64:(h + 1) * 64], lhsT=Em[0:64, h * 64:(h + 1) * 64], rhs=id64,
                                                start=True, stop=True, skip_group_check=True),
                  reads=["E", "cst"], writes=[("ps", 4)])
            A("act", lambda e: e.copy(out=ETm[0:64, :], in_=b4), reads=[("ps", 4)], writes=["ET"])
            for h in range(H):
                A("pe", lambda e, h=h: e.matmul(out=b5[:, h * 64:(h + 1) * 64], lhsT=kT(h), rhs=qT(h), start=True, stop=True,
                                                skip_group_check=True), reads=qk_keys, writes=[("ps", 5)])
            A("dve", lambda e: e.tensor_tensor(out=attT[0:64, :], in0=b5, in1=ETm[0:64, :], op=ALU.mult),
              reads=[("ps", 5), "ET"], writes=["attT"])
            B4 = v3(Bm[0:64, :], 8)
            for nm, fT, bb in (("k", kT, 0), ("v", vT, 2)):
                for h in range(H):
                    bk = bb + h // 4
                    A("pe", lambda e, h=h, fT=fT, bk=bk: e.matmul(
                        out=self.ps[0:64, bk * 512 + (h % 4) * 128:bk * 512 + (h % 4 + 1) * 128], lhsT=fT(h), rhs=id128,
                        start=True, stop=True, skip_group_check=True), reads=qk_keys + ["cst"], writes=[("ps", bk)])
            for half in range(2):
                hs = slice(half * 4, half * 4 + 4)
                pk3 = v3(self.ps[0:64, half * 512:(half + 1) * 512], 4)
                pv3 = v3(self.ps[0:64, (2 + half) * 512:(3 + half) * 512], 4)
                A("dve", lambda e, hs=hs, pk3=pk3: e.tensor_tensor(out=B4[:, hs, 128:256], in0=pk3, in1=bc(bege[0:64, hs], 128),
                                                                  op=ALU.mult), reads=[("ps", half), "bege"], writes=[("B", half)])
                A("dve", lambda e, hs=hs, pk3=pk3: e.tensor_tensor(out=v3(kdec[0:64, :], 8)[:, hs, :], in0=pk3,
                                                                  in1=bc(ekd[0:64, hs], 128), op=ALU.mult),
                  reads=[("ps", half), "ekd"], writes=[("kdec", half)])
                A("dve", lambda e, hs=hs, pv3=pv3: e.tensor_tensor(out=B4[:, hs, 0:128], in0=pv3, in1=bc(be_c[:, hs], 128),
                                                                  op=ALU.mult), reads=[("ps", 2 + half), "be_t"], writes=[("B", half)])
            cur = 0
            for lvl in range(6):
                P, PT = Pm[cur], PTm[cur]
                for h in range(H):
                    bk = h // 2
                    A("pe", lambda e, h=h, bk=bk, PT=PT: e.matmul(
                        out=self.ps[0:64, bk * 512 + (h % 2) * 256:bk * 512 + (h % 2 + 1) * 256],
                        lhsT=PT[0:64, h * 64:(h + 1) * 64], rhs=Bm[0:64, h * 256:(h + 1) * 256], start=True, stop=True,
                        skip_group_check=True), reads=[("PT", cur), ("B", h // 4)], writes=[("ps", bk)])
                for bk in range(4):
                    bs = Bm[0:64, bk * 512:(bk + 1) * 512]
                    A("dve", lambda e, bk=bk, bs=bs, lvl=lvl: e.tensor_tensor(
                        out=bs, in0=bs, in1=self.ps[0:64, bk * 512:(bk + 1) * 512],
                        op=(ALU.subtract if lvl == 0 else ALU.add)), reads=[("ps", bk), ("B", bk // 2)], writes=[("B", bk // 2)])
                if lvl < 5:
                    nxt = 1 - cur
                    for h in range(H):
                        A("pe", lambda e, h=h, P=P, PT=PT: e.matmul(out=b4[:, h * 64:(h + 1) * 64], lhsT=PT[0:64, h * 64:(h + 1) * 64],
                                                                    rhs=P[0:64, h * 64:(h + 1) * 64], start=True, stop=True,
                                                                    skip_group_check=True),
                          reads=[("P", cur), ("PT", cur)], writes=[("ps", 4)])
                    for h in range(H):
                        A("pe", lambda e, h=h, P=P, PT=PT: e.matmul(out=b5[:, h * 64:(h + 1) * 64], lhsT=P[0:64, h * 64:(h + 1) * 64],
                                                                    rhs=PT[0:64, h * 64:(h + 1) * 64], start=True, stop=True,
                                                                    skip_group_check=True),
                          reads=[("P", cur), ("PT", cur)], writes=[("ps", 5)])
                    A("act", lambda e, nxt=nxt: e.copy(out=Pm[nxt][0:64, :], in_=b4), reads=[("ps", 4)], writes=[("P", nxt)])
                    A("act", lambda e, nxt=nxt: e.copy(out=PTm[nxt][0:64, :], in_=b5), reads=[("ps", 5)], writes=[("PT", nxt)])
                    cur = nxt
            b6 = self.ps[:, 6 * 512:7 * 512]
            for h in range(H):
                A("pe", lambda e, h=h: e.matmul(out=b6[:, h * 64:(h + 1) * 64], lhsT=Bm[0:64, h * 256 + 128:h * 256 + 256], rhs=id64,
                                                start=True, stop=True, skip_group_check=True),
                  reads=[("B", h // 4), "cst"], writes=[("ps", 6)])
            A("act", lambda e: e.copy(out=wT, in_=b6), reads=[("ps", 6)], writes=["wT"])
            for h in range(H):
                bk = h // 4
                A("pe", lambda e, h=h, bk=bk: e.matmul(out=self.ps[0:64, bk * 512 + (h % 4) * 128:bk * 512 + (h % 4 + 1) * 128],
                                                       lhsT=wT[:, h * 64:(h + 1) * 64], rhs=St[:, h * 128:(h + 1) * 128],
                                                       start=True, stop=True, skip_group_check=True),
                  reads=["wT", "S"], writes=[("ps", bk)])
            vn3 = v3(vnew[0:64, :], 8)
            for half in range(2):
                hs = slice(half * 4, half * 4 + 4)
                A("dve", lambda e, hs=hs, half=half: e.tensor_tensor(
                    out=vn3[:, hs, :], in0=B4[:, hs, 0:128], in1=v3(self.ps[0:64, half * 512:(half + 1) * 512], 4), op=ALU.subtract),
                    reads=[("ps", half), ("B", half)], writes=[("vnew", half)])
            for h in range(H):
                bk = 2 + h // 4
                A("pe", lambda e, h=h, bk=bk: e.matmul(out=self.ps[0:64, bk * 512 + (h % 4) * 128:bk * 512 + (h % 4 + 1) * 128],
                                                       lhsT=qT(h), rhs=St[:, h * 128:(h + 1) * 128], start=True, stop=True,
                                                       skip_group_check=True), reads=qk_keys + ["S"], writes=[("ps", bk)])
            o3 = v3(om[0:64, :], 8)
            for half in range(2):
                hs = slice(half * 4, half * 4 + 4)
                A("dve", lambda e, hs=hs, half=half: e.tensor_tensor(
                    out=o3[:, hs, :], in0=v3(self.ps[0:64, (2 + half) * 512:(3 + half) * 512], 4), in1=bc(egc[0:64, hs], 128),
                    op=ALU.mult), reads=[("ps", 2 + half), "egc"], writes=[("o", half)])
            for h in range(H):
                bk = h // 4
                A("pe", lambda e, h=h, bk=bk: e.matmul(out=self.ps[0:64, bk * 512 + (h % 4) * 128:bk * 512 + (h % 4 + 1) * 128],
                                                       lhsT=attT[0:64, h * 64:(h + 1) * 64], rhs=vnew[0:64, h * 128:(h + 1) * 128],
                                                       start=True, stop=True, skip_group_check=True),
                  reads=["attT", ("vnew", h // 4)], writes=[("ps", bk)])
            for half in range(2):
                hs = slice(half * 4, half * 4 + 4)
                A("dve", lambda e, hs=hs, half=half: e.tensor_tensor(
                    out=o3[:, hs, :], in0=o3[:, hs, :], in1=v3(self.ps[0:64, half * 512:(half + 1) * 512], 4), op=ALU.add),
                    reads=[("ps", half), ("o", half)], writes=[("o", half)])
            for h in range(H):
                bk = 2 + h // 4
                A("pe", lambda e, h=h, bk=bk: e.matmul(out=self.ps[:, bk * 512 + (h % 4) * 128:bk * 512 + (h % 4 + 1) * 128],
                                                       lhsT=kdec[0:64, h * 128:(h + 1) * 128], rhs=vnew[0:64, h * 128:(h + 1) * 128],
                                                       start=True, stop=True, skip_group_check=True),
                  reads=[("kdec", h // 4), ("vnew", h // 4)], writes=[("ps", bk)])
            A("dve", lambda e: e.tensor_tensor(out=v3(St, 8), in0=v3(St, 8), in1=bc(egl, 128), op=ALU.mult),
              reads=["S", "egl"], writes=["S"])
            for half in range(2):
                A("dve", lambda e, half=half: e.tensor_tensor(out=St[:, half * 512:(half + 1) * 512], in0=St[:, half * 512:(half + 1) * 512],
                                                             in1=self.ps[:, (2 + half) * 512:(3 + half) * 512], op=ALU.add),
                  reads=[("ps", 2 + half), "S"], writes=["S"])
            A("pool", lambda e: e.tensor_tensor(out=osq[0:64, :], in0=om[0:64, :], in1=om[0:64, :], op=ALU.mult),
              reads=[("o", 0), ("o", 1)], writes=["osq"])
            A("dve", lambda e: e.tensor_reduce(out=ss8[0:64, :], in_=v3(osq[0:64, :], 8), axis=AX.X, op=ALU.add),
              reads=["osq"], writes=["ss8"])
            A("act", lambda e: e.activation(out=ss8[0:64, :], in_=ss8[0:64, :], func=AF.Sqrt, bias=self.epsc[0:64, :], scale=1.0 / 128),
              reads=["ss8"], writes=["ss8"])
            A("dve", lambda e: e.reciprocal(out=ss8[0:64, :], in_=ss8[0:64, :]), reads=["ss8"], writes=["ss8"])
            A("dve", lambda e: e.tensor_tensor(out=o3, in0=o3, in1=bc(ss8[0:64, :], 128), op=ALU.mult),
              reads=["ss8", ("o", 0), ("o", 1)], writes=[("o", 0), ("o", 1)])
            for h in range(H):
                A("pe", lambda e, h=h: e.matmul(out=b6[:, h * 64:(h + 1) * 64], lhsT=om[0:64, h * 128:(h + 1) * 128], rhs=id64,
                                                start=True, stop=True, skip_group_check=True),
                  reads=[("o", h // 4), "cst"], writes=[("ps", 6)])
            og3 = v3(og, 8)[:, :, c * C:(c + 1) * C]
            gs3 = v3(gs, 8)[:, :, c * C:(c + 1) * C]
            A("dve", lambda e, og3=og3, gs3=gs3: e.scalar_tensor_tensor(out=og3, in0=v3(b6, 8), scalar=onw[:, 0:1], in1=gs3,
                                                                       op0=ALU.mult, op1=ALU.mult),
              reads=[("ps", 6), "onw"] + [("gs", h) for h in range(H)], writes=[("og", c)])
        for c in range(min(NCK, int(os.environ.get('GDN_MAXC', '99')))):
            chunk(c)
        for o in range(8):
            wi = wc[1] % 2
            wc[1] += 1
            self.wload(wob[wi], w_out[o], 1024, ("wout", wi))
            pb = o % 2
            pt = self.bank(pb)
            for k in range(8):
                A("pe", lambda e, k=k, pt=pt, wi=wi: e.matmul(out=pt, lhsT=wob[wi][:, k * 128:(k + 1) * 128],
                                                            rhs=og[:, k * T:(k + 1) * T], start=(k == 0), stop=(k == 7)),
                  reads=[("wout", wi)] + [("og", c) for c in range(NCK)], writes=[("ps", pb)])
            xs = x32[:, o * T:(o + 1) * T]
            A("dve", lambda e, xs=xs, pt=pt: e.tensor_tensor(out=xs, in0=pt, in1=xs, op=ALU.add),
              reads=[("ps", pb), ("x32", o)], writes=[("x32", o)])
            A("act", lambda e, o=o, t=t: e.dma_start(out=dst[o * 128:(o + 1) * 128, t * T:(t + 1) * T], in_=x32[:, o * T:(o + 1) * T]),
              reads=[("x32", o)], dkey=("st", o))
    self.dbg = dict(wT=wT, qkv=qkv, gs=gs, g_t=g_t, be_t=be_t, gcs=gcs, egl=egl, ekd=ekd, egc=egc, Em=Em, ETm=ETm, P0=Pm[0], P1=Pm[1], attT=attT, Bm=Bm, kdec=kdec, vnew=vnew, om=om, St=St, og=og, xn=xn, x32=x32)
    sb.release(m)


K.gdn_phase = gdn_phase


def prep_gdn(w_in, conv_w, A_log, dt_bias, out_norm, w_out):
    d = {}
    d["w_in"] = prep_proj(np.ascontiguousarray(w_in[:, :4096]))
    wab = w_in[:, 4096:4112].reshape(8, 128, 16)
    d["wab"] = np.ascontiguousarray(wab.transpose(1, 0, 2)).reshape(128, 128)
    d["cw"] = np.ascontiguousarray(conv_w.reshape(4, 24, 128).transpose(2, 1, 0)).reshape(128, 96)
    d["alog"] = np.ascontiguousarray(np.broadcast_to(A_log[None, :], (128, 8)))
    d["dtb"] = np.ascontiguousarray(np.broadcast_to(dt_bias[None, :], (128, 8)))
    d["onw"] = np.ascontiguousarray(out_norm.reshape(128, 1))
    d["w_out"] = prep_proj(w_out)
    return d


NEGB = -30000.0
VW = 386
VOFF = (0, 65, 193, 258)
NCS = {}
_o = 0
for _n, _w in (("ones_bd", 128), ("rperm", 128), ("id128", 128), ("inv", 1), ("qw", 1), ("kw3", 3), ("b2k", 1),
               ("hb1", 4), ("sel0", 128), ("sel64", 128)):
    NCS[_n] = (_o, _w)
    _o += _w
NCS_W = _o
NBC = {}
_o = 0
for _n, _w in (("efull", 4096), ("id128", 128), ("causb", 4 * 512), ("bandb", 4 * 512), ("cmpb", 512), ("cmpb0", 512),
               ("ovl", 8 * 64), ("cmpr0", 512)):
    NBC[_n] = (_o, _w)
    _o += _w
NBC_W = _o


def prep_nsa(inp):
    f = lambda a: np.asarray(a, dtype=np.float32)
    w = f(inp["nsa_w_in"])[0]
    d = {}
    colsA = np.concatenate([np.arange(1024, 1536), np.arange(1536, 1792), np.arange(2048, 2304)])
    d["wka"] = prep_proj(np.ascontiguousarray(w[:, colsA]))
    colsV = np.concatenate([np.arange(1792, 2048), np.arange(2304, 2560)])
    wv = w[:, colsV].reshape(8, 128, 512)
    d["wv"] = np.ascontiguousarray(wv.transpose(1, 0, 2)).reshape(128, 8 * 512)
    W1 = f(inp["nsa_cmp_w1"])[0]
    w1 = W1.reshape(2, 32, 64, 256).transpose(0, 2, 1, 3).reshape(2, 64, 32 * 256)
    d["w1"] = np.ascontiguousarray(np.concatenate([w1, w1], axis=1))
    pe = f(inp["nsa_cmp_pe"])[0]
    peT = pe.transpose(0, 2, 1)
    d["peT"] = np.ascontiguousarray(np.concatenate([peT, peT], axis=1))
    W2 = f(inp["nsa_cmp_w2"])[0]
    w2k = W2[0].reshape(2, 128, 64).transpose(1, 0, 2)
    d["w2k"] = np.ascontiguousarray(np.concatenate([w2k, w2k], axis=2)).reshape(128, 256)
    d["w2v"] = np.ascontiguousarray(W2[1].reshape(2, 128, 64).transpose(1, 0, 2)).reshape(128, 128)
    b1 = f(inp["nsa_cmp_b1"])[0]
    b2 = f(inp["nsa_cmp_b2"])[0]
    d["b2v"] = np.ascontiguousarray(b2[1].reshape(1, 64))
    c = np.zeros((128, NCS_W), np.float32)

    def put(name, arr):
        o, wd_ = NCS[name]
        c[:arr.shape[0], o:o + wd_] = arr

    ob = np.zeros((128, 128), np.float32)
    ob[:64, :64] = 1
    ob[64:, 64:] = 1
    put("ones_bd", ob)
    rp = np.zeros((128, 128), np.float32)
    for blk in (0, 64):
        for m_ in range(32):
            rp[blk + m_ + 32, blk + m_] = -1.0
            rp[blk + m_, blk + m_ + 32] = 1.0
    put("rperm", rp)
    put("id128", np.eye(128, dtype=np.float32))
    inv = (1.0 / (10000.0 ** (np.arange(0, 64, 2, dtype=np.float32) / 64))).astype(np.float32)
    put("inv", np.tile(inv, 4).reshape(128, 1))
    put("qw", np.tile(f(inp["nsa_q_norm"])[0], 2).reshape(128, 1))
    kn = f(inp["nsa_k_norm"])[0]
    put("kw3", np.tile(kn.T, (2, 1)))
    put("b2k", np.tile(b2[0], 2).reshape(128, 1))
    put("hb1", b1.reshape(2, 2, 128).transpose(2, 0, 1).reshape(128, 4))
    s0 = np.zeros((128, 128), np.float32)
    s0[0, :] = 1.0
    put("sel0", s0)
    s64 = np.zeros((128, 128), np.float32)
    s64[64, :] = 1.0
    put("sel64", s64)
    d["ncs"] = c
    bc_ = np.zeros((128, NBC_W), np.float32)

    def putb(name, arr):
        o, wd_ = NBC[name]
        bc_[:arr.shape[0], o:o + wd_] = arr

    keys = np.arange(4096)
    putb("efull", (keys[None, :] // 64 == np.arange(64)[:, None]).astype(np.float32))
    putb("id128", np.eye(128, dtype=np.float32))
    kk = np.arange(128)[:, None]
    qq = np.arange(512)[None, :]
    putb("causb", np.concatenate([np.where(dd * 128 + kk > qq, NEGB, 0.0) for dd in range(4)], axis=1))
    putb("bandb", np.concatenate([np.where(kk + e_ * 128 <= qq, NEGB, 0.0) for e_ in range(4)], axis=1))
    jj = np.arange(32)[:, None]
    cm = np.where(16 * jj + 15 > qq, NEGB, 0.0)
    putb("cmpb", cm)
    cm0 = cm.copy()
    cm0[0, :] = NEGB
    putb("cmpb0", cm0)
    r0 = np.zeros((32, 512), np.float32)
    r0[0, :] = NEGB
    ov = np.zeros((32, 8, 64), np.float32)
    for tp in range(8):
        for j in range(32):
            n = 32 * tp + j - 1
            if n < 0:
                continue
            for s_ in range(64):
                lo = max(16 * n, 64 * s_)
                hi = min(16 * n + 32, 64 * s_ + 64)
                ov[j, tp, s_] = max(hi - lo, 0) / 32.0
    putb("ovl", ov.reshape(32, 512))
    putb("cmpr0", r0)
    d["nbc"] = bc_
    sm = np.zeros((8, 128, 2, 4, 64), np.float32)
    for t in range(8):
        for blk in range(4):
            tq = t * 512 + blk * 128 + np.arange(128)[:, None]
            s_ = np.arange(64)[None, :]
            valid = (s_ * 64 <= tq)
            dist = tq // 64 - s_
            forced = (s_ == 0) | ((dist >= 0) & (dist < 2))
            sm[t, :, 0, blk, :] = valid
            sm[t, :, 1, blk, :] = np.where(valid & forced, 1e9, 0.0) + np.where(valid, 0.0, -1.0)
    d["selc"] = sm.reshape(8, 128, 512)
    qcols = []
    for pp in range(2):
        for i in range(4):
            for g in (2 * pp, 2 * pp + 1):
                h = g * 4 + i
                qcols.append(np.arange(h * 64, (h + 1) * 64))
    gcols = []
    for pp in range(2):
        for r in range(3):
            for i in range(4):
                for g in (2 * pp, 2 * pp + 1):
                    h = g * 4 + i
                    gcols.append(np.full(64, 2560 + h * 3 + r))
    d["wq"] = prep_proj(np.ascontiguousarray(w[:, np.concatenate(qcols)]))
    d["wg"] = prep_proj(np.ascontiguousarray(w[:, np.concatenate(gcols)]))
    wo = f(inp["nsa_w_out"])[0]
    d["wo"] = prep_proj(np.ascontiguousarray(wo[np.concatenate(qcols), :]))
    return d


import math
TWO_PI = 2.0 * math.pi
CW1 = 6.28125
CW2 = TWO_PI - CW1


def rope_tables(self, pos_d, t, T, cosb, sinb, wk, inv_col, tag, wkeys=None):
    A = self.S.add
    ti = wk[0].bitcast(I32)
    ang, kf = wk[1], wk[2]
    if wkeys is None:
        wkeys = [tag + "w0", tag + "w1", tag + "w2"]
    K0, K1, K2 = wkeys
    A("sp", lambda e: e.dma_start(out=ti, in_=pos_d[0:1, t * T:(t + 1) * T].broadcast_to([128, T])),
      writes=[K0], dkey=tag + "pos")
    A("dve", lambda e: e.tensor_copy(out=ang, in_=ti), reads=[K0], writes=[K1])
    A("dve", lambda e: e.tensor_scalar(out=ang, in0=ang, scalar1=inv_col, scalar2=None, op0=ALU.mult),
      reads=[K1, "ncs"], writes=[K1])
    A("dve", lambda e: e.tensor_scalar(out=ti, in0=ang, scalar1=1.0 / TWO_PI, scalar2=None, op0=ALU.mult),
      reads=[K1], writes=[K0])
    A("dve", lambda e: e.tensor_copy(out=kf, in_=ti), reads=[K0], writes=[K2])
    A("dve", lambda e: e.scalar_tensor_tensor(out=ang, in0=kf, scalar=-CW1, in1=ang, op0=ALU.mult, op1=ALU.add),
      reads=[K1, K2], writes=[K1])
    A("dve", lambda e: e.scalar_tensor_tensor(out=ang, in0=kf, scalar=-CW2, in1=ang, op0=ALU.mult, op1=ALU.add),
      reads=[K1, K2], writes=[K1])

    def wrap(x, key):
        A("dve", lambda e: e.tensor_scalar(out=kf, in0=x, scalar1=math.pi, scalar2=-TWO_PI, op0=ALU.is_gt, op1=ALU.mult),
          reads=[key], writes=[K2])
        A("dve", lambda e: e.tensor_tensor(out=x, in0=x, in1=kf, op=ALU.add), reads=[key, K2], writes=[key])
        A("dve", lambda e: e.tensor_scalar(out=kf, in0=x, scalar1=-math.pi, scalar2=TWO_PI, op0=ALU.is_lt, op1=ALU.mult),
          reads=[key], writes=[K2])
        A("dve", lambda e: e.tensor_tensor(out=x, in0=x, in1=kf, op=ALU.add), reads=[key, K2], writes=[key])

    wrap(ang, K1)
    A("act", lambda e: e.activation(out=sinb, in_=ang, func=AF.Sin), reads=[K1], writes=[tag + "sin"])
    A("dve", lambda e: e.tensor_scalar(out=ang, in0=ang, scalar1=math.pi / 2, scalar2=None, op0=ALU.add),
      reads=[K1], writes=[K1])
    wrap(ang, K1)
    A("act", lambda e: e.activation(out=cosb, in_=ang, func=AF.Sin), reads=[K1], writes=[tag + "cos"])


K.rope_tables = rope_tables


def headnorm_rope(self, pt, pkey, wcol, cosb, sinb, tag, outs, scale, wk, ncs, T):
    A = self.S.add
    sqv, rs, xnr, t1 = wk
    ones_bd, rperm = ncs["ones_bd"], ncs["rperm"]
    A("act", lambda e: e.activation(out=sqv, in_=pt, func=AF.Square), reads=[pkey], writes=[tag + "sq"])
    A("pe", lambda e: e.matmul(out=self.bank(2, T), lhsT=ones_bd, rhs=sqv, start=True, stop=True),
      reads=[tag + "sq", "ncs"], writes=[("ps", 2)])
    A("act", lambda e: e.activation(out=rs, in_=self.bank(2, T), func=AF.Sqrt, bias=self.epsc, scale=1.0 / 64),
      reads=[("ps", 2)], writes=[tag + "rs"])
    A("dve", lambda e: e.reciprocal(out=rs, in_=rs), reads=[tag + "rs"], writes=[tag + "rs"])
    A("dve", lambda e: e.scalar_tensor_tensor(out=xnr, in0=pt, scalar=wcol, in1=rs, op0=ALU.mult, op1=ALU.mult),
      reads=[pkey, tag + "rs", "ncs"], writes=[tag + "xn"])
    outs = [o_ if len(o_) == 4 else (o_[0], o_[1], o_[2], slice(0, 128)) for o_ in outs]
    need_rope = any(o_[1] for o_ in outs)
    if need_rope:
        A("pe", lambda e: e.matmul(out=self.bank(3, T), lhsT=rperm, rhs=xnr, start=True, stop=True),
          reads=[tag + "xn", "ncs"], writes=[("ps", 3)])
    roped = False
    for o_ap, rope, okey, rows in outs:
        if not rope:
            A("act", lambda e, o_ap=o_ap, rows=rows: e.activation(out=o_ap[rows, :], in_=xnr[rows, :], func=AF.Copy, scale=scale),
              reads=[tag + "xn"], writes=[okey])
        else:
            if not roped:
                A("pool", lambda e: e.tensor_tensor(out=t1, in0=xnr, in1=cosb, op=ALU.mult),
                  reads=[tag + "xn", "ropecos"], writes=[tag + "t1"])
                A("dve", lambda e: e.tensor_tensor(out=rs, in0=self.bank(3, T), in1=sinb, op=ALU.mult),
                  reads=[("ps", 3), "ropesin", tag + "rs"], writes=[tag + "rs"])
                A("dve", lambda e: e.tensor_tensor(out=t1, in0=t1, in1=rs, op=ALU.add),
                  reads=[tag + "t1", tag + "rs"], writes=[tag + "t1"])
                roped = True
            A("act", lambda e, o_ap=o_ap, rows=rows: e.activation(out=o_ap[rows, :], in_=t1[rows, :], func=AF.Copy, scale=scale),
              reads=[tag + "t1"], writes=[okey])


K.headnorm_rope = headnorm_rope


def nsa_phase(self, src, dst, W, pos_d, gamma_col):
    S, sb = self.S, self.sb
    S.barrier()
    m = sb.mark()
    A = S.add
    T = 512
    NT = self.ntok // T
    SQ = self.ntok
    NKT = SQ // 128

    def v3(ap, a):
        return ap.rearrange("p (a b) -> p a b", a=a)

    ncs_t = sb.f32(NCS_W)
    A("sp", lambda e: e.dma_start(out=ncs_t, in_=W["ncs"]), writes=["ncs"], dkey="ncs")
    ncs = {n: ncs_t[:, o:o + w] for n, (o, w) in NCS.items()}
    ksT = sb.bf16(2 * SQ)
    kwT = sb.bf16(2 * 1024)
    vsS = sb.bf16(NKT * VW)
    vwS = sb.bf16(8 * VW)
    kcT = sb.bf16(4 * 32 * NT)
    vcS = sb.bf16(NT * VW)
    gam = sb.f32(8)
    A("sp", lambda e: e.dma_start(out=gam, in_=gamma_col), writes=["gam"], dkey="gam")
    for st_, nm in ((vsS, "vsS"), (vwS, "vwS"), (vcS, "vcS")):
        A("pool", lambda e, st_=st_: e.memset(st_, 0.0), writes=[nm])
        n_t = st_.shape[1] // VW
        s3 = st_.rearrange("p (t w) -> p t w", w=VW)
        for col in (64, 65, 257, 258):
            A("pool", lambda e, s3=s3, col=col: e.memset(s3[:, :, col:col + 1], 1.0), writes=[nm])
    S.barrier()
    mB = sb.mark()

    x32 = sb.f32(8 * T)
    xn = sb.bf16(8 * T)
    sq = [sb.f32(512) for _ in range(2)]
    rstd = sb.f32(512)
    cosb, sinb = sb.f32(T), sb.f32(T)
    rwk = [sb.f32(T) for _ in range(3)]
    hwk = [sb.f32(T) for _ in range(4)]
    kraw = [sb.bf16(16 + T) for _ in range(8)]
    w1 = [sb.bf16(8192) for _ in range(2)]
    peT = [sb.bf16(32) for _ in range(2)]
    w2k = sb.bf16(256)
    w2v = sb.bf16(128)
    b2v = sb.f32(64)
    one1 = sb.f32(32)
    hb = sb.f32(4)
    wv = sb.bf16(8 * 512)
    NW = 4
    wbs = [sb.bf16(1024) for _ in range(NW)]
    hx = sb.f32(512)
    hy = sb.f32(512)
    hidT = sb.bf16(512)
    kcw = sb.f32(128)
    for i in range(2):
        self.wload(w1[i], W["w1"][i], 8192, ("w1", i))
        A("pool", lambda e, i=i: e.dma_start(out=peT[i], in_=W["peT"][i]), writes=[("peT", i)], dkey=("peT", i))
    A("pool", lambda e: e.dma_start(out=w2k, in_=W["w2k"]), writes=["w2k"], dkey="w2k")
    A("pool", lambda e: e.dma_start(out=w2v, in_=W["w2v"]), writes=["w2v"], dkey="w2v")
    A("sp", lambda e: e.dma_start(out=b2v[0:1, :], in_=W["b2v"]), writes=["b2v"], dkey="b2v")
    A("pool", lambda e: e.memset(one1[0:1, :], 1.0), writes=["one1"])
    self.wload(wv, W["wv"], 4096, "wv")
    for r_ in range(8):
        A("pool", lambda e, r_=r_: e.memset(kraw[r_], 0.0), writes=[("kraw", r_)])
    pb6 = self.ps[:, 6 * 512:6 * 512 + 4]
    for i in range(2):
        for hh in range(2):
            col = i * 2 + hh
            for l in range(32):
                A("pe", lambda e, i=i, hh=hh, l=l, col=col: e.matmul(
                    out=pb6[:, col:col + 1], lhsT=w1[i][0:64, l * 256 + hh * 128:l * 256 + (hh + 1) * 128],
                    rhs=peT[i][0:64, l:l + 1], start=(l == 0), stop=(l == 31), skip_group_check=True),
                    reads=[("w1", i), ("peT", i)], writes=[("ps", 6)])
    A("dve", lambda e: e.tensor_tensor(out=hb, in0=pb6, in1=ncs["hb1"], op=ALU.add), reads=[("ps", 6), "ncs"], writes=["hb"])
    S.stop_at("P1")

    wc = [0]

    def tileA(t):
        self.load_x_tile(src, x32, "x32", t, T)
        self.rmsnorm_tile(x32, "x32", xn, "xn", gam, "gam", T, sq, rstd, 7, "n")
        self.rope_tables(pos_d, t, T, cosb, sinb, rwk, ncs["inv"], "rope")
        S.stop_at("P2")
        xk = [("xn", k, 0) for k in range(8)]
        for oc in range(6):
            wi = wc[0] % NW
            wc[0] += 1
            self.wload(wbs[wi], W["wka"][oc], 1024, ("win", wi))
            pbk = oc % 2
            pt = self.bank(pbk)
            for k in range(8):
                A("pe", lambda e, k=k, pt=pt, wi=wi: e.matmul(out=pt, lhsT=wbs[wi][:, k * 128:(k + 1) * 128],
                                                            rhs=xn[:, k * T:(k + 1) * T], start=(k == 0), stop=(k == 7)),
                  reads=[("win", wi), ("xn", k, 0)], writes=[("ps", pbk)])
            if oc < 4:
                A("act", lambda e, oc=oc, pt=pt: e.copy(out=kraw[oc * 2][0:64, 16:16 + T], in_=pt[0:64, :]),
                  reads=[("ps", pbk)], writes=[("kraw", oc * 2)])
                A("dve", lambda e, oc=oc, pt=pt: e.tensor_copy(out=kraw[oc * 2 + 1][64:128, 16:16 + T], in_=pt[64:128, :]),
                  reads=[("ps", pbk)], writes=[("kraw", oc * 2 + 1)])
            else:
                pp = oc % 2
                o_ap = ksT[:, pp * SQ + t * T:pp * SQ + (t + 1) * T]
                self.headnorm_rope(pt, ("ps", pbk), ncs["kw3"][:, 1:2], cosb, sinb, "hn",
                                   [(o_ap, True, ("ksT", pp, t))], 1.0, hwk, ncs, T)
        S.stop_at("P3")
        for blk in range(4):
            kt = t * 4 + blk
            pv = self.bank(4, 256)
            for k in range(8):
                A("pe", lambda e, k=k, blk=blk, pv=pv: e.matmul(
                    out=pv, lhsT=xn[:, k * T + blk * 128:k * T + (blk + 1) * 128], rhs=wv[:, k * 512:k * 512 + 256],
                    start=(k == 0), stop=(k == 7)), reads=[("xn", k, 0), "wv"], writes=[("ps", 4)])
            for j_, (st_, nm) in enumerate(((vsS, "vsS"),)):
                for g in range(4):
                    off = kt * VW + VOFF[g] + (64 if g % 2 else 0)
                    eng = "act" if (g + j_) % 2 == 0 else "dve"
                    fn = (lambda e, st_=st_, off=off, g=g, j_=j_, pv=pv: e.copy(
                        out=st_[:, off:off + 64], in_=pv[:, j_ * 256 + g * 64:j_ * 256 + (g + 1) * 64])) if eng == "act" else \
                        (lambda e, st_=st_, off=off, g=g, j_=j_, pv=pv: e.tensor_copy(
                            out=st_[:, off:off + 64], in_=pv[:, j_ * 256 + g * 64:j_ * 256 + (g + 1) * 64]))
                    A(eng, fn, reads=[("ps", 4)], writes=[(nm, kt)])
        S.stop_at("P4")
        p5 = self.bank(5)
        for i in range(2):
            for hh in range(2):
                for g in range(4):
                    pp = g // 2
                    col = ((i * 2 + hh) * 4 + g) * 32
                    ri = (i * 2 + pp) * 2 + g % 2
                    src_ = kraw[ri]
                    for l in range(32):
                        A("pe", lambda e, i=i, hh=hh, l=l, col=col, src_=src_: e.matmul(
                            out=p5[:, col:col + 32], lhsT=w1[i][:, l * 256 + hh * 128:l * 256 + (hh + 1) * 128],
                            rhs=src_[:, l:l + 16 * 31 + 1:16], start=(l == 0), stop=(l == 31), skip_group_check=True),
                            reads=[("w1", i), ("kraw", ri)], writes=[("ps", 5)])
        for r_ in range(8):
            A("pool", lambda e, r_=r_: e.tensor_copy(out=kraw[r_][:, 0:16], in_=kraw[r_][:, T:T + 16]),
              reads=[("kraw", r_)], writes=[("kraw", r_)])
        S.stop_at("P5")
        for q_ in range(4):
            A("act", lambda e, q_=q_: e.activation(out=hx[:, q_ * 128:(q_ + 1) * 128], in_=p5[:, q_ * 128:(q_ + 1) * 128],
                                                   func=AF.Identity, bias=hb[:, q_:q_ + 1]),
              reads=[("ps", 5), "hb"], writes=["hx"])
        A("dve", lambda e: e.tensor_tensor(out=hy, in0=hx, in1=hx, op=ALU.mult), reads=["hx"], writes=["hy"])
        A("dve", lambda e: e.tensor_scalar(out=hy, in0=hy, scalar1=0.044715, scalar2=1.0, op0=ALU.mult, op1=ALU.add),
          reads=["hy"], writes=["hy"])
        A("dve", lambda e: e.tensor_tensor(out=hy, in0=hy, in1=hx, op=ALU.mult), reads=["hy", "hx"], writes=["hy"])
        A("act", lambda e: e.activation(out=hy, in_=hy, func=AF.Tanh, scale=0.7978845608028654), reads=["hy"], writes=["hy"])
        A("dve", lambda e: e.tensor_scalar(out=hy, in0=hy, scalar1=0.5, scalar2=0.5, op0=ALU.mult, op1=ALU.add),
          reads=["hy"], writes=["hy"])
        A("dve", lambda e: e.tensor_tensor(out=hidT, in0=hy, in1=hx, op=ALU.mult), reads=["hy", "hx"], writes=["hidT"])
        p6 = self.ps[:, 6 * 512:6 * 512 + 128]
        for g in range(4):
            for hh in range(2):
                col = ((0 * 2 + hh) * 4 + g) * 32
                A("pe", lambda e, g=g, hh=hh, col=col: e.matmul(out=p6[:, g * 32:(g + 1) * 32], lhsT=w2k[:, hh * 128:(hh + 1) * 128],
                                                                rhs=hidT[:, col:col + 32], start=(hh == 0), stop=(hh == 1),
                                                                skip_group_check=True),
                  reads=["hidT", "w2k"], writes=[("ps", 6)])
        A("act", lambda e: e.activation(out=kcw, in_=p6, func=AF.Identity, bias=ncs["b2k"]), reads=[("ps", 6), "ncs"], writes=["kcw"])
        A("act", lambda e: e.activation(out=hwk[0][:, 0:128], in_=kcw, func=AF.Square), reads=["kcw"], writes=["kcsq"])
        A("pe", lambda e: e.matmul(out=self.bank(2, 128), lhsT=ncs["ones_bd"], rhs=hwk[0][:, 0:128], start=True, stop=True),
          reads=["kcsq", "ncs"], writes=[("ps", 2)])
        A("act", lambda e: e.activation(out=hwk[1][:, 0:128], in_=self.bank(2, 128), func=AF.Sqrt, bias=self.epsc, scale=1.0 / 64),
          reads=[("ps", 2)], writes=["kcrs"])
        A("dve", lambda e: e.reciprocal(out=hwk[1][:, 0:128], in_=hwk[1][:, 0:128]), reads=["kcrs"], writes=["kcrs"])
        kc3 = kcT.rearrange("p (g n) -> p g n", g=4)[:, :, t * 32:(t + 1) * 32]
        A("dve", lambda e, kc3=kc3: e.scalar_tensor_tensor(out=kc3, in0=v3(kcw, 4), scalar=ncs["kw3"][:, 0:1],
                                                            in1=v3(hwk[1][:, 0:128], 4), op0=ALU.mult, op1=ALU.mult),
          reads=["kcw", "kcrs", "ncs"], writes=[("kcT", t)])
        p6v = self.ps[0:32, 6 * 512 + 128:6 * 512 + 128 + 256]
        for g in range(4):
            for hh in range(2):
                col = ((1 * 2 + hh) * 4 + g) * 32
                A("pe", lambda e, g=g, hh=hh, col=col: e.matmul(out=p6v[:, g * 64:(g + 1) * 64], lhsT=hidT[:, col:col + 32],
                                                                rhs=w2v[:, hh * 64:(hh + 1) * 64], start=(hh == 0), stop=False,
                                                                skip_group_check=True),
                  reads=["hidT", "w2v"], writes=[("ps", 6)])
            A("pe", lambda e, g=g: e.matmul(out=p6v[:, g * 64:(g + 1) * 64], lhsT=one1[0:1, :], rhs=b2v[0:1, :], start=False, stop=True,
                                            skip_group_check=True), reads=["one1", "b2v"], writes=[("ps", 6)])
        for g in range(4):
            off = t * VW + VOFF[g] + (64 if g % 2 else 0)
            A("act", lambda e, g=g, off=off: e.copy(out=vcS[0:32, off:off + 64], in_=p6v[:, g * 64:(g + 1) * 64]),
              reads=[("ps", 6)], writes=[("vcS", t)])

    for t in range(NT):
        tileA(t)
        S.stop_at("P6a")
    S.stop_at("P6")
    self.nsa_st = dict(ncs=ncs, ksT=ksT, kwT=kwT, vsS=vsS, vwS=vwS, kcT=kcT, vcS=vcS, gam=gam)
    self.dbg = dict(ksT=ksT, kwT=kwT, vsS=vsS, vwS=vwS, kcT=kcT, vcS=vcS)
    sb.release(mB)
    S.barrier()
    nsa_queries(self, src, dst, W, pos_d, T, NT, SQ)
    sb.release(m)


K.nsa_phase = nsa_phase


def nsa_queries(self, src, dst, W, pos_d, T, NT, SQ):
    S, sb = self.S, self.sb
    A = S.add
    st = self.nsa_st
    ncs, ksT, kwT, vsS, vwS, kcT, vcS, gam = (st[k_] for k_ in ("ncs", "ksT", "kwT", "vsS", "vwS", "kcT", "vcS", "gam"))

    def v3(ap, a):
        return ap.rearrange("p (a b) -> p a b", a=a)

    nbc = sb.bf16(NBC_W)
    self.wload(nbc, W["nbc"], NBC_W, "nbc")
    nb = {n: nbc[:, o:o + w] for n, (o, w) in NBC.items()}
    x32 = sb.f32(8 * T)
    xn = sb.bf16(8 * T)
    sq = [sb.f32(512) for _ in range(2)]
    rstd = sb.f32(512)
    cosb, sinb = sb.f32(T), sb.f32(T)
    hwk = [sb.f32(T) for _ in range(4)]
    wv = sb.bf16(8 * 512)
    self.wload(wv, W["wv"], 4096, "wv")
    qnT = [[sb.bf16(T) for _ in range(4)] for _ in range(2)]
    qrT = [[sb.bf16(T) for _ in range(4)] for _ in range(2)]
    for hf in range(2):
        for i in range(4):
            A("pool", lambda e, hf=hf, i=i: e.memset(qnT[hf][i], 0.0), writes=[("qn", i, hf)])
            A("pool", lambda e, hf=hf, i=i: e.memset(qrT[hf][i], 0.0), writes=[("qr", i, hf)])
    gt = [sb.bf16(T) for _ in range(12)]
    ogacc = [sb.f32(T) for _ in range(4)]
    ogb = sb.bf16(8 * T)
    impT = sb.f32(T)
    selc = sb.f32(T)
    sc = sb.f32(256)
    sc2 = sb.f32(64)
    m8 = sb.f32(16)
    bm = sb.bf16(4 * 128)
    selbT = sb.bf16(T)
    pT = [sb.bf16(T) for _ in range(3)]
    rz = sb.f32(T)
    rzb = sb.f32(T)
    otmp = sb.f32(T)
    imptmp = sb.f32(T)
    rwk = [rzb, otmp, imptmp]
    rwkeys = ["rzb", "otmp", "imptmp"]
    NW = 3
    wbs = [sb.bf16(1024) for _ in range(NW)]
    wob = [sb.bf16(1024) for _ in range(2)]
    A("pool", lambda e: e.memset(bm, 0.0), writes=["bm"])
    A("pool", lambda e: e.memset(rz, 0.0), writes=["rz"])
    wc = [0, 0]
    pcnt = [0]
    SCB = (2, 3)

    def head_branch(kind, i, g, t, first):
        pp, hf = g // 2, g % 2
        q_ap = (qnT if kind == "cmp" else qrT)[hf][i]
        qkey = ("qn" if kind == "cmp" else "qr", i, hf)
        even = (g % 2 == 0)
        M = 65 if even else 128
        voff = VOFF[g]
        oacc = self.ps[0:M, 4 * 512:5 * 512]
        if kind == "cmp":
            tiles = list(range(t + 1))
            KP = 32
        elif kind == "sel":
            tiles = list(range(4 * t + 4))
            KP = 128
        else:
            tiles = list(range(max(0, 4 * t - 4), 4 * t + 4))
            KP = 128
        nt_ = len(tiles)
        for n_, kt in enumerate(tiles):
            sbk = SCB[pcnt[0] % 2]
            pbuf = pT[pcnt[0] % 3]
            pkey = ("pT", pcnt[0] % 3)
            pcnt[0] += 1
            ps_s = self.ps[0:KP, sbk * 512:(sbk + 1) * 512]
            mm = []
            if kind == "cmp":
                mm.append((kcT[:, g * 32 * NT + kt * 32:g * 32 * NT + (kt + 1) * 32], q_ap, [("kcT", kt), qkey]))
                if kt == t:
                    mm.append((nb["id128"][:, 0:32], (nb["cmpb0"] if t == 0 else nb["cmpb"]), ["nbc"]))
                elif kt == 0:
                    mm.append((nb["id128"][:, 0:32], nb["cmpr0"], ["nbc"]))
            elif kind == "sel":
                mm.append((ksT[:, pp * SQ + kt * 128:pp * SQ + (kt + 1) * 128], q_ap, [("ksT", pp, kt // 4), qkey]))
                mm.append((nb["efull"][:, kt * 128:(kt + 1) * 128], selbT, ["nbc", "selbT"]))
                if kt >= 4 * t:
                    dd = kt - 4 * t
                    mm.append((nb["id128"], nb["causb"][:, dd * 512:(dd + 1) * 512], ["nbc"]))
            else:
                slot = (kt // 4) % 2
                ko = pp * 1024 + slot * 512 + (kt % 4) * 128
                mm.append((kwT[:, ko:ko + 128], q_ap, [("kwT", pp, slot), qkey]))
                dd = kt - 4 * t
                mask = nb["causb"][:, dd * 512:(dd + 1) * 512] if dd >= 0 else nb["bandb"][:, (dd + 4) * 512:(dd + 5) * 512]
                mm.append((nb["id128"], mask, ["nbc"]))
            for j_, (l_, r_, rd) in enumerate(mm):
                A("pe", lambda e, l_=l_, r_=r_, j_=j_, ps_s=ps_s, last=(j_ == len(mm) - 1): e.matmul(
                    out=ps_s, lhsT=l_, rhs=r_, start=(j_ == 0), stop=last), reads=rd, writes=[("ps", sbk)])
            A("act", lambda e, pbuf=pbuf, ps_s=ps_s, KP=KP: e.activation(out=pbuf[0:KP, :], in_=ps_s, func=AF.Exp),
              reads=[("ps", sbk)], writes=[pkey])
            store, snm = {"cmp": (vcS, "vcS"), "sel": (vsS, "vsS"), "win": (vwS, "vwS")}[kind]
            vi = kt if kind != "win" else ((kt // 4) % 2) * 4 + kt % 4
            vl = store[0:KP, vi * VW + voff:vi * VW + voff + M]
            A("pe", lambda e, vl=vl, pbuf=pbuf, KP=KP, n_=n_, nt_=nt_: e.matmul(
                out=oacc, lhsT=vl, rhs=pbuf[0:KP, :], start=(n_ == 0), stop=(n_ == nt_ - 1)),
                reads=[pkey, (snm, vi)], writes=[("ps", 4)])
            if kind == "cmp":
                A("pe", lambda e, kt=kt, pbuf=pbuf, n_=n_, nt_=nt_: e.matmul(
                    out=self.ps[0:64, 5 * 512:6 * 512], lhsT=nb["ovl"][0:32, kt * 64:(kt + 1) * 64], rhs=pbuf[0:32, :],
                    start=(n_ == 0), stop=(n_ == nt_ - 1)), reads=[pkey, "nbc"], writes=[("ps", 5)])
        zr = 64 if even else 0
        A("dve", lambda e, zr=zr: e.tensor_scalar(out=rz[zr:zr + 1, :], in0=self.ps[zr:zr + 1, 4 * 512:5 * 512], scalar1=1e-30,
                                                  scalar2=None, op0=ALU.add), reads=[("ps", 4)], writes=["rz"])
        A("dve", lambda e, zr=zr: e.reciprocal(out=rz[zr:zr + 1, :], in_=rz[zr:zr + 1, :]), reads=["rz"], writes=["rz"])
        A("pe", lambda e, zr=zr: e.matmul(out=self.bank(6), lhsT=ncs["sel64" if zr == 64 else "sel0"], rhs=rz, start=True, stop=True),
          reads=["rz", "ncs"], writes=[("ps", 6)])
        A("act", lambda e: e.copy(out=rzb, in_=self.bank(6)), reads=[("ps", 6)], writes=["rzb"])
        r_ = {"cmp": 0, "sel": 1, "win": 2}[kind]
        gtile = gt[r_ * 4 + i]
        orow = slice(0, 64) if even else slice(64, 128)
        A("dve", lambda e, orow=orow: e.tensor_tensor(out=otmp[orow, :], in0=self.ps[orow, 4 * 512:5 * 512], in1=rzb[orow, :], op=ALU.mult),
          reads=[("ps", 4), "rzb"], writes=["otmp"])
        if first:
            A("dve", lambda e, orow=orow, gtile=gtile, i=i: e.tensor_tensor(out=ogacc[i][orow, :], in0=otmp[orow, :], in1=gtile[orow, :], op=ALU.mult),
              reads=["otmp", ("gt", r_ * 4 + i)], writes=[("og", i, g % 2)])
        else:
            A("dve", lambda e, orow=orow, gtile=gtile: e.tensor_tensor(out=otmp[orow, :], in0=otmp[orow, :], in1=gtile[orow, :], op=ALU.mult),
              reads=["otmp", ("gt", r_ * 4 + i)], writes=["otmp"])
            A("pool", lambda e, orow=orow, i=i: e.tensor_tensor(out=ogacc[i][orow, :], in0=ogacc[i][orow, :], in1=otmp[orow, :], op=ALU.add),
              reads=["otmp", ("og", i, g % 2)], writes=[("og", i, g % 2)])
        if kind == "cmp":
            if i == 0:
                A("dve", lambda e: e.tensor_tensor(out=impT[0:64, :], in0=self.ps[0:64, 5 * 512:6 * 512], in1=rzb[0:64, :], op=ALU.mult),
                  reads=[("ps", 5), "rzb"], writes=["impT"])
            else:
                A("dve", lambda e: e.tensor_tensor(out=imptmp[0:64, :], in0=self.ps[0:64, 5 * 512:6 * 512],
                                                   in1=rzb[0:64, :], op=ALU.mult), reads=[("ps", 5), "rzb"], writes=["imptmp"])
                A("pool", lambda e: e.tensor_tensor(out=impT[0:64, :], in0=impT[0:64, :], in1=imptmp[0:64, :], op=ALU.add),
                  reads=["imptmp", "impT"], writes=["impT"])

    def sel_mask(g, t):
        pm = self.ps[:, 5 * 512:5 * 512 + 256]
        for blk in range(4):
            A("pe", lambda e, blk=blk: e.matmul(out=pm[:, blk * 64:(blk + 1) * 64], lhsT=impT[0:64, blk * 128:(blk + 1) * 128],
                                                rhs=ncs["id128"][0:64, 0:64], start=True, stop=True, skip_group_check=True),
              reads=["impT", "ncs"], writes=[("ps", 5)])
        valid = selc[:, 0:256]
        addm = selc[:, 256:512]
        A("dve", lambda e: e.tensor_tensor(out=sc, in0=pm, in1=valid, op=ALU.mult), reads=[("ps", 5), "selc"], writes=["sc"])
        A("dve", lambda e: e.tensor_tensor(out=sc, in0=sc, in1=addm, op=ALU.add), reads=["sc", "selc"], writes=["sc"])
        for blk in range(4):
            sblk = sc[:, blk * 64:(blk + 1) * 64]
            A("dve", lambda e, sblk=sblk: e.max(out=m8[:, 0:8], in_=sblk), reads=["sc"], writes=["m8"])
            A("dve", lambda e, sblk=sblk: e.match_replace(out=sc2, in_to_replace=m8[:, 0:8], in_values=sblk, imm_value=-1e30),
              reads=["sc", "m8"], writes=["sc2"])
            A("dve", lambda e: e.max(out=m8[:, 8:16], in_=sc2), reads=["sc2"], writes=["m8"])
            A("dve", lambda e, sblk=sblk: e.tensor_scalar(out=sc2, in0=sblk, scalar1=m8[:, 15:16], scalar2=None, op0=ALU.is_ge),
              reads=["sc", "m8"], writes=["sc2"])
            A("dve", lambda e, blk=blk: e.tensor_tensor(out=sc2, in0=sc2, in1=valid[:, blk * 64:(blk + 1) * 64], op=ALU.mult),
              reads=["sc2", "selc"], writes=["sc2"])
            A("dve", lambda e, blk=blk: e.tensor_scalar(out=bm[:, blk * 128:blk * 128 + 64], in0=sc2, scalar1=-NEGB, scalar2=NEGB,
                                                        op0=ALU.mult, op1=ALU.add), reads=["sc2"], writes=["bm"])
        pm2 = self.ps[:, 5 * 512:6 * 512]
        for blk in range(4):
            A("pe", lambda e, blk=blk: e.matmul(out=pm2[:, blk * 128:(blk + 1) * 128], lhsT=bm[:, blk * 128:(blk + 1) * 128],
                                                rhs=nb["id128"], start=True, stop=True, skip_group_check=True),
              reads=["bm", "nbc"], writes=[("ps", 5)])
        A("act", lambda e: e.copy(out=selbT, in_=pm2), reads=[("ps", 5)], writes=["selbT"])

    def tileB(t):
        self.load_x_tile(src, x32, "x32", t, T)
        self.rmsnorm_tile(x32, "x32", xn, "xn", gam, "gam", T, sq, rstd, 7, "n")
        self.rope_tables(pos_d, t, T, cosb, sinb, rwk, ncs["inv"], "rope", rwkeys)
        A("sp", lambda e: e.dma_start(out=selc[:, 0:512], in_=W["selc"][t]), writes=["selc"], dkey="selc")
        slot = t % 2
        for pp in range(2):
            wi = wc[0] % NW
            wc[0] += 1
            self.wload(wbs[wi], W["wka"][6 + pp], 1024, ("win", wi))
            pbk = pp % 2
            pt = self.bank(pbk)
            for k in range(8):
                A("pe", lambda e, k=k, pt=pt, wi=wi: e.matmul(out=pt, lhsT=wbs[wi][:, k * 128:(k + 1) * 128],
                                                            rhs=xn[:, k * T:(k + 1) * T], start=(k == 0), stop=(k == 7)),
                  reads=[("win", wi), ("xn", k, 0)], writes=[("ps", pbk)])
            o_ap = kwT[:, pp * 1024 + slot * 512:pp * 1024 + (slot + 1) * 512]
            self.headnorm_rope(pt, ("ps", pbk), ncs["kw3"][:, 2:3], cosb, sinb, "hn",
                               [(o_ap, True, ("kwT", pp, slot))], 1.0, hwk, ncs, T)
        for blk in range(4):
            vi = slot * 4 + blk
            pv = self.bank(4, 256)
            for k in range(8):
                A("pe", lambda e, k=k, blk=blk, pv=pv: e.matmul(
                    out=pv, lhsT=xn[:, k * T + blk * 128:k * T + (blk + 1) * 128], rhs=wv[:, k * 512 + 256:(k + 1) * 512],
                    start=(k == 0), stop=(k == 7)), reads=[("xn", k, 0), "wv"], writes=[("ps", 4)])
            for g in range(4):
                off = vi * VW + VOFF[g] + (64 if g % 2 else 0)
                A("act", lambda e, off=off, g=g, pv=pv: e.copy(out=vwS[:, off:off + 64], in_=pv[:, g * 64:(g + 1) * 64]),
                  reads=[("ps", 4)], writes=[("vwS", vi)])
        for pp in range(2):
            for i in range(4):
                wi = wc[0] % NW
                wc[0] += 1
                self.wload(wbs[wi], W["wq"][pp * 4 + i], 1024, ("win", wi))
                pbk = i % 2
                pt = self.bank(pbk)
                for k in range(8):
                    A("pe", lambda e, k=k, pt=pt, wi=wi: e.matmul(out=pt, lhsT=wbs[wi][:, k * 128:(k + 1) * 128],
                                                                rhs=xn[:, k * T:(k + 1) * T], start=(k == 0), stop=(k == 7)),
                      reads=[("win", wi), ("xn", k, 0)], writes=[("ps", pbk)])
                self.headnorm_rope(pt, ("ps", pbk), ncs["qw"], cosb, sinb, "hn",
                                   [(qnT[0][i], False, ("qn", i, 0), slice(0, 64)), (qnT[1][i], False, ("qn", i, 1), slice(64, 128)),
                                    (qrT[0][i], True, ("qr", i, 0), slice(0, 64)), (qrT[1][i], True, ("qr", i, 1), slice(64, 128))],
                                   0.125, hwk, ncs, T)
            for r_ in range(3):
                for i in range(4):
                    wi = wc[0] % NW
                    wc[0] += 1
                    self.wload(wbs[wi], W["wg"][pp * 12 + r_ * 4 + i], 1024, ("win", wi))
                    pbk = i % 2
                    pt = self.bank(pbk)
                    for k in range(8):
                        A("pe", lambda e, k=k, pt=pt, wi=wi: e.matmul(out=pt, lhsT=wbs[wi][:, k * 128:(k + 1) * 128],
                                                                    rhs=xn[:, k * T:(k + 1) * T], start=(k == 0), stop=(k == 7)),
                          reads=[("win", wi), ("xn", k, 0)], writes=[("ps", pbk)])
                    A("act", lambda e, r_=r_, i=i, pt=pt: e.activation(out=gt[r_ * 4 + i], in_=pt, func=AF.Sigmoid),
                      reads=[("ps", pbk)], writes=[("gt", r_ * 4 + i)])
            S.stop_at("P7")
            for g in (2 * pp, 2 * pp + 1):
                for i in range(4):
                    head_branch("cmp", i, g, t, True)
                    S.stop_at("P8")
                sel_mask(g, t)
                S.stop_at("P9")
                for i in range(4):
                    head_branch("sel", i, g, t, False)
                    S.stop_at("P10")
                for i in range(4):
                    head_branch("win", i, g, t, False)
                    S.stop_at("P11")
            for i in range(4):
                A("act", lambda e, pp=pp, i=i: e.copy(out=ogb[:, (pp * 4 + i) * T:(pp * 4 + i + 1) * T], in_=ogacc[i]),
                  reads=[("og", i, 0), ("og", i, 1)], writes=[("ogb", pp * 4 + i)])
        for o in range(8):
            wi = wc[1] % 2
            wc[1] += 1
            self.wload(wob[wi], W["wo"][o], 1024, ("wout", wi))
            pbk = o % 2
            pt = self.bank(pbk)
            for k in range(8):
                A("pe", lambda e, k=k, pt=pt, wi=wi: e.matmul(out=pt, lhsT=wob[wi][:, k * 128:(k + 1) * 128],
                                                            rhs=ogb[:, k * T:(k + 1) * T], start=(k == 0), stop=(k == 7)),
                  reads=[("wout", wi), ("ogb", k)], writes=[("ps", pbk)])
            xs = x32[:, o * T:(o + 1) * T]
            A("dve", lambda e, xs=xs, pt=pt: e.tensor_tensor(out=xs, in0=pt, in1=xs, op=ALU.add),
              reads=[("ps", pbk), ("x32", o)], writes=[("x32", o)])
            A("act", lambda e, o=o, t=t: e.dma_start(out=dst[o * 128:(o + 1) * 128, t * T:(t + 1) * T], in_=x32[:, o * T:(o + 1) * T]),
              reads=[("x32", o)], dkey=("st", o))

    for t in range(NT):
        tileB(t)
```

```python
import contextlib
import numpy as np
import concourse.bass as bass
import concourse.mybir as mybir
from concourse.bass_utils import run_bass_kernel_spmd

F32 = mybir.dt.float32
BF16 = mybir.dt.bfloat16
I32 = mybir.dt.int32
AF = mybir.ActivationFunctionType
ALU = mybir.AluOpType
AX = mybir.AxisListType

D = 1024
SEQ = 4096
DEPTH = 4
DFF = 2816
NCH = DFF // 128
EPS = 1e-6

ENGS = ("pe", "act", "dve", "pool", "sp")


class Op:
    __slots__ = ("eng", "fn", "deps", "sig", "cnt", "dkey", "dcnt", "reads", "writes", "bar")

    def __init__(self, eng, fn, reads, writes, dkey):
        self.eng = eng
        self.fn = fn
        self.reads = reads
        self.writes = writes
        self.dkey = dkey
        self.deps = []
        self.sig = False
        self.cnt = 0
        self.dcnt = 0
        self.bar = None


class Sched:
    def __init__(self, nc):
        self.nc = nc
        self.ops = {e: [] for e in ENGS}
        self.res = {}
        self.dma_cnt = {}
        self.nops = 0
        self.cut = False

    def add(self, eng, fn, reads=(), writes=(), dkey=None, ndma=1):
        if self.cut:
            return None
        reads = tuple(reads)
        writes = tuple(writes)
        op = Op(eng, fn, reads, writes, dkey)
        deps = {}
        for k in reads:
            r = self.res.get(k)
            if r is not None and r[0] is not None:
                deps[id(r[0])] = r[0]
        for k in writes:
            r = self.res.get(k)
            if r is not None:
                if r[0] is not None:
                    deps[id(r[0])] = r[0]
                for q in r[1]:
                    deps[id(q)] = q
        final = []
        rset = set(reads)
        wset = set(writes)
        for d in deps.values():
            if d is op:
                continue
            if d.dkey is None and dkey is None and d.eng == eng:
                if eng == "pe":
                    continue
                if eng != "pool" and not (rset.intersection(d.writes)) and not (wset.intersection(d.writes)):
                    continue
            final.append(d)
            if d.dkey is None:
                d.sig = True
        op.deps = final
        for k in reads:
            r = self.res.get(k)
            if r is None:
                self.res[k] = [None, [op]]
            else:
                r[1].append(op)
        for k in writes:
            self.res[k] = [op, []]
        if dkey is not None:
            self.dma_cnt[dkey] = self.dma_cnt.get(dkey, 0) + 16 * ndma
            op.dcnt = self.dma_cnt[dkey]
        self.ops[eng].append(op)
        self.nops += 1
        return op

    def stop_at(self, name):
        import os
        if os.environ.get("NSA_STOP") == name:
            self.cut = True

    def barrier(self):
        if self.cut:
            return
        snap_ops = {}
        for e in ENGS:
            last = None
            for o in reversed(self.ops[e]):
                if o.bar is None and o.dkey is None:
                    last = o
                    break
            if last is not None:
                last.sig = True
                snap_ops[e] = last
        dsnap = dict(self.dma_cnt)
        for e in ENGS:
            op = Op(e, None, (), (), None)
            op.bar = (dict(snap_ops), dsnap)
            self.ops[e].append(op)
        self.res = {}

    def emit(self):
        nc = self.nc
        with contextlib.ExitStack() as st:
            esem = {e: st.enter_context(nc.semaphore("s_" + e)) for e in ENGS}
            dsem = {k: st.enter_context(nc.semaphore("d_%d" % i)) for i, k in enumerate(self.dma_cnt)}
            for e in ENGS:
                c = 0
                for o in self.ops[e]:
                    if o.sig:
                        c += 1
                    o.cnt = c
            block = st.enter_context(nc.Block())
            final_d = dict(self.dma_cnt)

            def run(e, eng):
                seen = {}

                def wait(sem, key, val):
                    if val <= 0 or seen.get(key, 0) >= val:
                        return
                    seen[key] = val
                    eng.wait_ge(sem, val)

                for o in self.ops[e]:
                    if o.bar is not None:
                        so, ds = o.bar
                        for e2, lo in so.items():
                            if e2 != e:
                                wait(esem[e2], ("e", e2), lo.cnt)
                        for k, v in ds.items():
                            wait(dsem[k], ("d", k), v)
                        continue
                    need = {}
                    for d in o.deps:
                        if d.dkey is not None:
                            key, v, sem = ("d", d.dkey), d.dcnt, dsem[d.dkey]
                        else:
                            key, v, sem = ("e", d.eng), d.cnt, esem[d.eng]
                        if need.get(key, (None, 0))[1] < v:
                            need[key] = (sem, v)
                    for key, (sem, v) in need.items():
                        wait(sem, key, v)
                    r = o.fn(eng)
                    if o.dkey is not None:
                        if not isinstance(r, (list, tuple)):
                            r = [r]
                        for ins in r:
                            ins.then_inc(dsem[o.dkey], 16)
                    elif o.sig:
                        r.then_inc(esem[e], 1)
                if e == "sp":
                    for k, v in final_d.items():
                        wait(dsem[k], ("d", k), v)
                    for e2 in ENGS:
                        if e2 != e:
                            for o in reversed(self.ops[e2]):
                                if o.sig:
                                    wait(esem[e2], ("e", e2), o.cnt)
                                    break

            @block.tensor
            def _(eng):
                run("pe", eng)

            @block.scalar
            def _(eng):
                run("act", eng)

            @block.vector
            def _(eng):
                run("dve", eng)

            @block.gpsimd
            def _(eng):
                run("pool", eng)

            @block.sync
            def _(eng):
                run("sp", eng)


class SBAlloc:
    def __init__(self, big, ncols):
        self.big = big
        self.ncols = ncols
        self.top = 0
        self.peak = 0

    def mark(self):
        return self.top

    def release(self, m):
        self.top = m

    def f32(self, n):
        o = self.top
        self.top += n
        assert self.top <= self.ncols, "SBUF overflow %d > %d" % (self.top, self.ncols)
        self.peak = max(self.peak, self.top)
        return self.big[:, o:o + n]

    def bf16(self, n):
        w = (n + 1) // 2
        return self.f32(w).bitcast(BF16)[:, 0:n]


SB_COLS = 53000


class K:
    def __init__(self, ntok=SEQ, T=1024):
        self.ntok = ntok
        self.T = T
        self.nc = bass.Bass("TRN2", target_bir_lowering=False)
        self.st = contextlib.ExitStack()
        self.dram = {}

    def din(self, name, shape, dt=F32):
        ap = self.nc.dram_tensor(name, list(shape), dt, kind="ExternalInput").ap()
        self.dram[name] = ap
        return ap

    def dout(self, name, shape, dt=F32):
        ap = self.nc.dram_tensor(name, list(shape), dt, kind="ExternalOutput").ap()
        self.dram[name] = ap
        return ap

    def begin(self):
        nc = self.nc
        big = self.st.enter_context(nc.sbuf_tensor("SB", [128, SB_COLS], F32))
        self.ps = self.st.enter_context(nc.psum_tensor("PS", [128, 8 * 512], F32))
        self.sb = SBAlloc(big, SB_COLS)
        self.S = Sched(nc)
        S = self.S
        sb = self.sb
        self.ones32 = sb.f32(128)
        self.epsc = sb.f32(1)
        S.add("pool", lambda e: e.memset(self.ones32, 1.0), writes=["ones32"])
        S.add("pool", lambda e: e.memset(self.epsc, EPS), writes=["epsc"])

    def bank(self, b, n=512):
        return self.ps[:, b * 512:b * 512 + n]

    def finish(self):
        self.S.emit()
        self.st.close()
        return self.nc

    def rmsnorm_tile(self, x32, xkey, xn, xnkey, gamma, gkey, T, sq, rstd, psb, tag):
        S = self.S
        for s in range(T // 512):
            pt = self.bank(psb)
            for k in range(8):
                xs = x32[:, k * T + s * 512:k * T + (s + 1) * 512]
                q = sq[k % 2]
                S.add("act", lambda e, q=q, xs=xs: e.activation(out=q, in_=xs, func=AF.Square),
                      reads=[(xkey, k)], writes=[(tag + "sq", k % 2)])
                S.add("pe", lambda e, q=q, k=k, pt=pt: e.matmul(out=pt, lhsT=self.ones32, rhs=q,
                                                                start=(k == 0), stop=(k == 7)),
                      reads=[(tag + "sq", k % 2), "ones32"], writes=[("ps", psb)])
            S.add("act", lambda e, pt=pt: e.activation(out=rstd, in_=pt, func=AF.Sqrt, bias=self.epsc, scale=1.0 / D),
                  reads=[("ps", psb), "epsc"], writes=[tag + "rstd"])
            S.add("dve", lambda e: e.reciprocal(out=rstd, in_=rstd), reads=[tag + "rstd"], writes=[tag + "rstd"])
            for k in range(8):
                xs = x32[:, k * T + s * 512:k * T + (s + 1) * 512]
                xo = xn[:, k * T + s * 512:k * T + (s + 1) * 512]
                S.add("dve", lambda e, xs=xs, xo=xo, k=k: e.scalar_tensor_tensor(
                    out=xo, in0=xs, scalar=gamma[:, k:k + 1], in1=rstd, op0=ALU.mult, op1=ALU.mult),
                    reads=[(xkey, k), tag + "rstd", gkey], writes=[(xnkey, k, s)])

    def ffn_phase(self, src, dst, wgu, wd, gamma_col):
        S, sb = self.S, self.sb
        S.barrier()
        m = sb.mark()
        T = self.T
        NT = self.ntok // T
        NS = T // 512
        x32 = [sb.f32(8 * T) for _ in range(2)]
        xn = [sb.bf16(8 * T) for _ in range(2)]
        h = sb.bf16(NCH * T)
        NWG = 3
        wgb = [sb.bf16(2 * 8 * 128) for _ in range(NWG)]
        wdb = [sb.bf16(NCH * 128) for _ in range(2)]
        sq = [sb.f32(512) for _ in range(2)]
        rstd = sb.f32(512)
        sg = [sb.f32(512) for _ in range(2)]
        gam = sb.f32(8)
        S.add("sp", lambda e: e.dma_start(out=gam, in_=gamma_col), writes=["gam"], dkey="gam")

        def load_x(t):
            b = t % 2
            for k in range(8):
                S.add("sp", lambda e, k=k, b=b, t=t: e.dma_start(
                    out=x32[b][:, k * T:(k + 1) * T], in_=src[k * 128:(k + 1) * 128, t * T:(t + 1) * T]),
                    writes=[("x32_%d" % b, k)], dkey=("x32", b, k))

        def norm(t):
            b = t % 2
            self.rmsnorm_tile(x32[b], "x32_%d" % b, xn[b], "xn_%d" % b, gam, "gam", T, sq, rstd, 6, "f")

        wcount = [0, 0]

        def gateup(t):
            b = t % 2
            for c in range(NCH):
                wi = wcount[0] % NWG
                wcount[0] += 1
                wb = wgb[wi]
                S.add("pool", lambda e, c=c, wb=wb: [
                    e.dma_start(out=wb[:, j * 1024:(j + 1) * 1024], in_=wgu[c, :, j * 1024:(j + 1) * 1024])
                    for j in range(2)], writes=[("wg", wi)], dkey=("wg", wi), ndma=2)
                if c == 3 and t + 1 < NT and self.pipe and stage != 4:
                    load_x(t + 1)
                for s in range(NS):
                    gb = 0 + (s % 2) * 2
                    ub = 1 + (s % 2) * 2
                    pg, pu = self.bank(gb), self.bank(ub)
                    for j, pt, pb in ((0, pg, gb), (1, pu, ub)):
                        for k in range(8):
                            S.add("pe", lambda e, j=j, k=k, pt=pt, wb=wb, s=s: e.matmul(
                                out=pt, lhsT=wb[:, (j * 8 + k) * 128:(j * 8 + k + 1) * 128],
                                rhs=xn[b][:, k * T + s * 512:k * T + (s + 1) * 512],
                                start=(k == 0), stop=(k == 7)),
                                reads=[("wg", wi), ("xn_%d" % b, k, s)], writes=[("ps", pb)])
                    sgt = sg[s % 2]
                    S.add("act", lambda e, sgt=sgt, pg=pg: e.activation(out=sgt, in_=pg, func=AF.Silu),
                          reads=[("ps", gb)], writes=[("sg", s % 2)])
                    ho = h[:, c * T + s * 512:c * T + (s + 1) * 512]
                    S.add("dve", lambda e, ho=ho, sgt=sgt, pu=pu: e.tensor_tensor(out=ho, in0=pu, in1=sgt, op=ALU.mult),
                          reads=[("ps", ub), ("sg", s % 2)], writes=[("h", c, s)])

        def down(t):
            b = t % 2
            for o in range(8):
                wi = wcount[1] % 2
                wcount[1] += 1
                wb = wdb[wi]
                S.add("pool", lambda e, o=o, wb=wb: [
                    e.dma_start(out=wb[:, j * 1408:(j + 1) * 1408], in_=wd[o, :, j * 1408:(j + 1) * 1408])
                    for j in range(2)], writes=[("wd", wi)], dkey=("wd", wi), ndma=2)
                for s in range(NS):
                    pb = 4 + (o * NS + s) % 2
                    pt = self.bank(pb)
                    for c in range(NCH):
                        S.add("pe", lambda e, c=c, pt=pt, wb=wb, s=s: e.matmul(
                            out=pt, lhsT=wb[:, c * 128:(c + 1) * 128],
                            rhs=h[:, c * T + s * 512:c * T + (s + 1) * 512],
                            start=(c == 0), stop=(c == NCH - 1)),
                            reads=[("wd", wi), ("h", c, s)], writes=[("ps", pb)])
                    xs = x32[b][:, o * T + s * 512:o * T + (s + 1) * 512]
                    S.add("dve", lambda e, xs=xs, pt=pt: e.scalar_tensor_tensor(
                        out=xs, in0=pt, scalar=0.5, in1=xs, op0=ALU.mult, op1=ALU.add),
                        reads=[("ps", pb), ("x32_%d" % b, o)], writes=[("x32_%d" % b, o)])
                S.add("act", lambda e, o=o, b=b, t=t: e.dma_start(
                    out=dst[o * 128:(o + 1) * 128, t * T:(t + 1) * T], in_=x32[b][:, o * T:(o + 1) * T]),
                    reads=[("x32_%d" % b, o)], dkey=("st", b, o))

        import os
        stage = int(os.environ.get("FFN_STAGE", "9"))
        load_x(0)
        norm(0)
        if stage == 0:
            for k in range(8):
                S.add("dve", lambda e, k=k: e.tensor_copy(out=x32[0][:, k * T:(k + 1) * T], in_=xn[0][:, k * T:(k + 1) * T]),
                      reads=[("xn_0", k, 0), ("xn_0", k, 1)], writes=[("x32_0", k)])
                S.add("act", lambda e, k=k: e.dma_start(out=dst[k * 128:(k + 1) * 128, 0:T], in_=x32[0][:, k * T:(k + 1) * T]),
                      reads=[("x32_0", k)], dkey=("st", 0, k))
            sb.release(m)
            return
        self.pipe = stage != 2
        for t in range(NT):
            if not self.pipe and t > 0:
                load_x(t)
                norm(t)
            gateup(t)
            if stage == 1:
                for k in range(8):
                    S.add("dve", lambda e, k=k: e.tensor_copy(out=x32[0][:, k * T:(k + 1) * T], in_=h[:, k * T:(k + 1) * T]),
                          reads=[("h", k, 0), ("h", k, 1)], writes=[("x32_0", k)])
                    S.add("act", lambda e, k=k: e.dma_start(out=dst[k * 128:(k + 1) * 128, 0:T], in_=x32[0][:, k * T:(k + 1) * T]),
                          reads=[("x32_0", k)], dkey=("st", 0, k))
                sb.release(m)
                return
            if stage == 4 and t + 1 < NT:
                load_x(t + 1)
            if t + 1 < NT and self.pipe and stage != 3:
                norm(t + 1)
            down(t)
            if t + 1 < NT and stage == 3:
                norm(t + 1)
        sb.release(m)


    def wload(self, wb, src2d, n, key):
        S = self.S
        pieces = [(a, min(a + 2048, n)) for a in range(0, n, 2048)]
        S.add("pool", lambda e: [e.dma_start(out=wb[:, a:b], in_=src2d[:, a:b]) for a, b in pieces],
              writes=[key], dkey=key, ndma=len(pieces))

    def load_x_tile(self, src, x32, xkey, t, T, eng="sp"):
        for k in range(8):
            self.S.add(eng, lambda e, k=k: e.dma_start(
                out=x32[:, k * T:(k + 1) * T], in_=src[k * 128:(k + 1) * 128, t * T:(t + 1) * T]),
                writes=[(xkey, k)], dkey=(xkey, k))

    def sc_phase(self, src, dst, w_in, w_out, cw_d, gamma_col):
        S, sb = self.S, self.sb
        S.barrier()
        m = sb.mark()
        T = self.T
        NT = self.ntok // T
        NS = T // 512
        x32 = sb.f32(8 * T)
        xn = sb.bf16(8 * T)
        zb = sb.f32(8 * (T + 2))
        v = sb.bf16(8 * T)
        NW = 6
        wbs = [sb.bf16(1024) for _ in range(NW)]
        wob = [sb.bf16(1024) for _ in range(2)]
        sq = [sb.f32(512) for _ in range(2)]
        rstd = sb.f32(512)
        csb = [sb.f32(512) for _ in range(2)]
        ysb = [sb.f32(512) for _ in range(2)]
        gam = sb.f32(8)
        cw = sb.f32(24)
        S.add("sp", lambda e: e.dma_start(out=gam, in_=gamma_col), writes=["gam"], dkey="gam")
        S.add("sp", lambda e: e.dma_start(out=cw, in_=cw_d), writes=["cw"], dkey="cw")
        for j in range(8):
            S.add("pool", lambda e, j=j: e.memset(zb[:, j * (T + 2):j * (T + 2) + 2], 0.0), writes=[("zh", j)])
        wc = [0, 0]
        for t in range(NT):
            self.load_x_tile(src, x32, "x32", t, T)
            self.rmsnorm_tile(x32, "x32", xn, "xn", gam, "gam", T, sq, rstd, 6, "s")
            for j in range(8):
                wl = []
                for q in range(3):
                    oc = (1, 2, 0)[q] * 8 + j
                    wi = wc[0] % NW
                    wc[0] += 1
                    self.wload(wbs[wi], w_in[oc], 1024, ("win", wi))
                    wl.append(wi)
                z0 = j * (T + 2)
                for s_ in range(NS):
                    banks = (0 + 3 * (s_ % 2), 1 + 3 * (s_ % 2), 2 + 3 * (s_ % 2))
                    for q in range(3):
                        pt = self.bank(banks[q])
                        wb = wbs[wl[q]]
                        for k in range(8):
                            S.add("pe", lambda e, k=k, pt=pt, wb=wb, s_=s_: e.matmul(
                                out=pt, lhsT=wb[:, k * 128:(k + 1) * 128],
                                rhs=xn[:, k * T + s_ * 512:k * T + (s_ + 1) * 512],
                                start=(k == 0), stop=(k == 7)),
                                reads=[("win", wl[q]), ("xn", k, s_)], writes=[("ps", banks[q])])
                    pc, px, pbg = self.bank(banks[0]), self.bank(banks[1]), self.bank(banks[2])
                    cs, ys = csb[s_ % 2], ysb[s_ % 2]
                    zc = zb[:, z0 + 2 + s_ * 512:z0 + 2 + (s_ + 1) * 512]
                    zm1 = zb[:, z0 + 1 + s_ * 512:z0 + 1 + (s_ + 1) * 512]
                    zm2 = zb[:, z0 + s_ * 512:z0 + (s_ + 1) * 512]
                    S.add("act", lambda e, cs=cs, pc=pc: e.copy(out=cs, in_=pc),
                          reads=[("ps", banks[0])], writes=[("cs", s_ % 2)])
                    S.add("dve", lambda e, zc=zc, px=px, cs=cs: e.tensor_tensor(out=zc, in0=px, in1=cs, op=ALU.mult),
                          reads=[("ps", banks[1]), ("cs", s_ % 2)], writes=[("z", j, s_)])
                    S.add("act", lambda e, ys=ys, zc=zc, j=j: e.activation(out=ys, in_=zc, func=AF.Identity,
                                                                        scale=cw[:, j * 3 + 2:j * 3 + 3]),
                          reads=[("z", j, s_), "cw"], writes=[("ys", s_ % 2)])
                    hk = [("z", j, s_ - 1)] if s_ > 0 else [("zh", j)]
                    S.add("dve", lambda e, ys=ys, zm1=zm1, j=j: e.scalar_tensor_tensor(
                        out=ys, in0=zm1, scalar=cw[:, j * 3 + 1:j * 3 + 2], in1=ys, op0=ALU.mult, op1=ALU.add),
                        reads=[("z", j, s_), ("ys", s_ % 2), "cw"] + hk, writes=[("ys", s_ % 2)])
                    S.add("dve", lambda e, ys=ys, zm2=zm2, j=j: e.scalar_tensor_tensor(
                        out=ys, in0=zm2, scalar=cw[:, j * 3:j * 3 + 1], in1=ys, op0=ALU.mult, op1=ALU.add),
                        reads=[("z", j, s_), ("ys", s_ % 2), "cw"] + hk, writes=[("ys", s_ % 2)])
                    vo = v[:, j * T + s_ * 512:j * T + (s_ + 1) * 512]
                    S.add("dve", lambda e, vo=vo, pbg=pbg, ys=ys: e.tensor_tensor(out=vo, in0=pbg, in1=ys, op=ALU.mult),
                          reads=[("ps", banks[2]), ("ys", s_ % 2)], writes=[("v", j, s_)])
                S.add("pool", lambda e, z0=z0: e.tensor_copy(out=zb[:, z0:z0 + 2], in_=zb[:, z0 + T:z0 + T + 2]),
                      reads=[("z", j, s2) for s2 in range(NS)], writes=[("zh", j)])
            for o in range(8):
                wi = wc[1] % 2
                wc[1] += 1
                self.wload(wob[wi], w_out[o], 1024, ("wout", wi))
                for s_ in range(NS):
                    pb = 6 + (o * NS + s_) % 2
                    pt = self.bank(pb)
                    for k in range(8):
                        S.add("pe", lambda e, k=k, pt=pt, wi=wi, s_=s_: e.matmul(
                            out=pt, lhsT=wob[wi][:, k * 128:(k + 1) * 128],
                            rhs=v[:, k * T + s_ * 512:k * T + (s_ + 1) * 512],
                            start=(k == 0), stop=(k == 7)),
                            reads=[("wout", wi), ("v", k, s_)], writes=[("ps", pb)])
                    xs = x32[:, o * T + s_ * 512:o * T + (s_ + 1) * 512]
                    S.add("dve", lambda e, xs=xs, pt=pt: e.tensor_tensor(out=xs, in0=pt, in1=xs, op=ALU.add),
                          reads=[("ps", pb), ("x32", o)], writes=[("x32", o)])
                S.add("act", lambda e, o=o, t=t: e.dma_start(
                    out=dst[o * 128:(o + 1) * 128, t * T:(t + 1) * T], in_=x32[:, o * T:(o + 1) * T]),
                    reads=[("x32", o)], dkey=("st", o))
        sb.release(m)


def prep_proj(w, kch=8):
    Kd, N = w.shape
    assert Kd == kch * 128 and N % 128 == 0
    a = w.reshape(kch, 128, N // 128, 128)
    return np.ascontiguousarray(a.transpose(2, 1, 0, 3)).reshape(N // 128, 128, kch * 128)


def prep_ffn_weights(w_gate_up, w_down):
    w = w_gate_up.reshape(8, 128, 2, NCH, 128)
    wgu = np.ascontiguousarray(w.transpose(3, 1, 2, 0, 4)).reshape(NCH, 128, 2 * 8 * 128)
    w2 = w_down.reshape(NCH, 128, 8, 128)
    wd = np.ascontiguousarray(w2.transpose(2, 1, 0, 3)).reshape(8, 128, NCH * 128)
    return wgu, wd


def norm_cols(w):
    return np.ascontiguousarray(np.asarray(w, np.float32).reshape(8, 128).T)


N_NORM = DEPTH * 3
def nsa_shapes():
    return dict(wka=(8, 128, 1024), wv=(128, 4096), w1=(2, 128, 8192), peT=(2, 128, 32), w2k=(128, 256), w2v=(128, 128),
                b2v=(1, 64), ncs=(128, NCS_W), nbc=(128, NBC_W), selc=(8, 128, 512), wq=(8, 128, 1024),
                wg=(24, 128, 1024), wo=(8, 128, 1024))


def build_program(ntok=SEQ, layers=DEPTH, skip=()):
    k = K(ntok=ntok)
    xT = k.din("xT", [D, ntok])
    yT = k.dout("yT", [D, ntok])
    gam = k.din("gam", [128, 8 * N_NORM])
    wgu = [[k.din("wgu_%d_%d" % (l, f), [NCH, 128, 2048]) for f in range(2)] for l in range(layers)]
    wd = [[k.din("wd_%d_%d" % (l, f), [8, 128, NCH * 128]) for f in range(2)] for l in range(layers)]
    sc_in = k.din("sc_w_in", [24, 128, 1024])
    sc_out = k.din("sc_w_out", [8, 128, 1024])
    sc_cw = k.din("sc_cw", [128, 24])
    cst = k.din("cst", [128, CST_W])
    gd = []
    for j in range(2):
        p = "gdn%d_" % j
        gd.append(dict(w_in=k.din(p + "w_in", [32, 128, 1024]), wab=k.din(p + "wab", [128, 128]), cw=k.din(p + "cw", [128, 96]),
                       alog=k.din(p + "alog", [128, 8]), dtb=k.din(p + "dtb", [128, 8]), onw=k.din(p + "onw", [128, 1]),
                       w_out=k.din(p + "w_out", [8, 128, 1024])))
    pos = k.din("pos", [1, ntok], I32)
    nsaW = {n: k.din("nsa_" + n, list(shp)) for n, shp in nsa_shapes().items()}
    k.begin()

    def g(i):
        return gam[:, i * 8:(i + 1) * 8]

    for l in range(layers):
        k.ffn_phase(xT if l == 0 else yT, yT, wgu[l][0], wd[l][0], g(l * 3 + 0))
        kind = l % 3
        if kind == 1 and "sc" not in skip:
            k.sc_phase(yT, yT, sc_in, sc_out, sc_cw, g(l * 3 + 1))
        if kind == 2 and "nsa" not in skip:
            k.nsa_phase(yT, yT, nsaW, pos, g(l * 3 + 1))
        if kind == 0 and "gdn" not in skip:
            q = gd[l // 3]
            k.gdn_phase(yT, yT, q["w_in"], q["wab"], q["cw"], q["alog"], q["dtb"], q["onw"], q["w_out"], cst, g(l * 3 + 1))
        k.ffn_phase(yT, yT, wgu[l][1], wd[l][1], g(l * 3 + 2))
    nc = k.finish()
    return k, nc


def prep_inputs(inp, layers=DEPTH):
    f = lambda a: np.asarray(a, dtype=np.float32)
    com = {}
    gcols = []
    for l in range(DEPTH):
        gcols += [norm_cols(f(inp["ffn_norm"])[l, 0]), norm_cols(f(inp["mixer_norm"])[l]), norm_cols(f(inp["ffn_norm"])[l, 1])]
    com["gam"] = np.ascontiguousarray(np.concatenate(gcols, axis=1))
    for l in range(layers):
        for ff in range(2):
            a, b = prep_ffn_weights(f(inp["ffn_w_gate_up"])[l, ff], f(inp["ffn_w_down"])[l, ff])
            com["wgu_%d_%d" % (l, ff)] = a
            com["wd_%d_%d" % (l, ff)] = b
    com["sc_w_in"] = prep_proj(f(inp["sc_w_in"])[0])
    com["sc_w_out"] = prep_proj(f(inp["sc_w_out"])[0])
    com["cst"] = make_consts()
    for j in range(2):
        d = prep_gdn(f(inp["gdn_w_in"])[j], f(inp["gdn_conv_w"])[j], f(inp["gdn_A_log"])[j], f(inp["gdn_dt_bias"])[j],
                     f(inp["gdn_out_norm"])[j], f(inp["gdn_w_out"])[j])
        for kk_, vv_ in d.items():
            com["gdn%d_%s" % (j, kk_)] = vv_
    for n_, a_ in prep_nsa(inp).items():
        assert tuple(a_.shape) == tuple(nsa_shapes()[n_]), (n_, a_.shape)
        com["nsa_" + n_] = a_
    cwt = f(inp["sc_conv_w"])[0]
    com["sc_cw"] = np.ascontiguousarray(cwt.reshape(3, 8, 128).transpose(2, 1, 0)).reshape(128, 24)
    return com


def kernel(**inputs):
    x = np.asarray(inputs["x"], dtype=np.float32)
    B = x.shape[0]
    com = prep_inputs(inputs)
    k, nc = build_program()
    in_maps = []
    for b in range(B):
        m = dict(com)
        m["xT"] = np.ascontiguousarray(x[b].T)
        m["pos"] = np.ascontiguousarray(np.asarray(inputs["positions"])[b].astype(np.int32).reshape(1, -1))
        in_maps.append(m)
    res = run_bass_kernel_spmd(nc, in_maps, core_ids=list(range(B)))
    out = np.stack([np.ascontiguousarray(res.results[b]["yT"].T) for b in range(B)], axis=0)
    return out.astype(np.float32)


GC = 64
CST_OFF = {}
_o = 0
for _n, _w in (("id128", 128), ("LT", 64), ("mincl", 512), ("mstrict", 512), ("sel63", 128), ("ones64", 64), ("idrep", 512)):
    CST_OFF[_n] = (_o, _w)
    _o += _w
CST_W = _o


def make_consts():
    c = np.zeros((128, CST_W), np.float32)

    def put(name, arr):
        o, w = CST_OFF[name]
        c[:arr.shape[0], o:o + w] = arr

    put("id128", np.eye(128, dtype=np.float32))
    i = np.arange(64)
    put("LT", (i[:, None] <= i[None, :]).astype(np.float32))
    mincl = (i[None, :] <= i[:, None]).astype(np.float32)
    mstr = (i[None, :] < i[:, None]).astype(np.float32)
    put("mincl", np.tile(mincl, (1, 8)))
    put("mstrict", np.tile(mstr, (1, 8)))
    s = np.zeros((64, 128), np.float32)
    s[63, :] = 1.0
    put("sel63", s)
    put("ones64", np.ones((64, 64), np.float32))
    put("idrep", np.tile(np.eye(64, dtype=np.float32), (1, 8)))
    return c


def gdn_phase(self, src, dst, w_in, wab_d, cw_d, alog_d, dtb_d, onw_d, w_out, cst_d, gamma_col):
    S, sb = self.S, self.sb
    S.barrier()
    m = sb.mark()
    T = 512
    NT = self.ntok // T
    C = GC
    NCK = T // C
    H = 8
    A = S.add
    cst = sb.f32(CST_W)
    A("sp", lambda e: e.dma_start(out=cst, in_=cst_d), writes=["cst"], dkey="cst")

    def cs_(name, rows=64):
        o, w = CST_OFF[name]
        return cst[0:rows, o:o + w]

    id128 = cs_("id128", 128)
    id64 = cst[0:64, CST_OFF["id128"][0]:CST_OFF["id128"][0] + 64]
    LT, mincl, mstrict, sel63, ones64, idrep = (cs_("LT"), cs_("mincl"), cs_("mstrict"), cs_("sel63"),
                                                 cs_("ones64"), cs_("idrep"))
    x32 = sb.f32(8 * T)
    xn = sb.bf16(8 * T)
    qkv = sb.f32(24 * T)
    gs = sb.f32(8 * T)
    og = sb.bf16(8 * T)
    St = sb.f32(H * 128)
    halo = sb.f32(24 * 3)
    pre = [sb.f32(T + 3) for _ in range(2)]
    yb = [sb.f32(T) for _ in range(2)]
    sq = [sb.f32(512) for _ in range(2)]
    rstd = sb.f32(512)
    gam = sb.f32(8)
    cw = sb.f32(96)
    wab = sb.bf16(128)
    alog = sb.f32(8)
    dtb = sb.f32(8)
    nega = sb.f32(8)
    onw = sb.f32(1)
    NW = 4
    wbs = [sb.bf16(1024) for _ in range(NW)]
    wob = [sb.bf16(1024) for _ in range(2)]
    g_t = sb.f32(NCK * 8)
    be_t = sb.f32(NCK * 8)
    tmp_ab = sb.f32(NCK * 8)
    gcs = [sb.f32(8) for _ in range(2)]
    egl = [sb.f32(8) for _ in range(2)]
    ekd = [sb.f32(8) for _ in range(2)]
    egc = [sb.f32(8) for _ in range(2)]
    bege = [sb.f32(8) for _ in range(2)]
    rrhs = sb.f32(512)
    Em = [sb.f32(512) for _ in range(2)]
    ETm = [sb.f32(512) for _ in range(2)]
    Pm = [sb.f32(512) for _ in range(2)]
    PTm = [sb.f32(512) for _ in range(2)]
    attT = sb.f32(512)
    Bm = sb.f32(H * 256)
    kdec = sb.f32(H * 128)
    wT = sb.f32(512)
    vnew = sb.f32(H * 128)
    om = sb.f32(H * 128)
    osq = sb.f32(H * 128)
    ss8 = sb.f32(8)

    A("sp", lambda e: e.dma_start(out=gam, in_=gamma_col), writes=["gam"], dkey="gam")
    A("sp", lambda e: e.dma_start(out=cw, in_=cw_d), writes=["cw"], dkey="cw")
    A("sp", lambda e: e.dma_start(out=alog, in_=alog_d), writes=["alog"], dkey="alog")
    A("sp", lambda e: e.dma_start(out=dtb, in_=dtb_d), writes=["dtb"], dkey="dtb")
    A("sp", lambda e: e.dma_start(out=onw, in_=onw_d), writes=["onw"], dkey="onw")
    A("pool", lambda e: e.dma_start(out=wab, in_=wab_d), writes=["wab"], dkey="wab")
    A("pool", lambda e: e.memset(halo, 0.0), writes=["halo"])
    A("pool", lambda e: e.memset(St, 0.0), writes=[("S", 0), ("S", 1)])
    A("act", lambda e: e.activation(out=nega[0:64, :], in_=alog[0:64, :], func=AF.Exp), reads=["alog"], writes=["nega"])
    A("dve", lambda e: e.tensor_scalar(out=nega[0:64, :], in0=nega[0:64, :], scalar1=-1.0, scalar2=None, op0=ALU.mult),
      reads=["nega"], writes=["nega"])

    def bc(ap2, n):
        return ap2.unsqueeze(2).broadcast_to([ap2.shape[0], ap2.shape[1], n])

    def v3(ap, a):
        return ap.rearrange("p (a b) -> p a b", a=a)

    wc = [0, 0]
    for t in range(NT):
        self.load_x_tile(src, x32, "x32", t, T)
        self.rmsnorm_tile(x32, "x32", xn, "xn", gam, "gam", T, sq, rstd, 7, "g")
        xnk = [("xn", k, 0) for k in range(8)]
        pab = self.ps[0:64, 6 * 512:6 * 512 + NCK * 16]
        for c in range(NCK):
            for k in range(8):
                A("pe", lambda e, c=c, k=k: e.matmul(
                    out=pab[:, c * 16:(c + 1) * 16], lhsT=xn[:, k * T + c * C:k * T + (c + 1) * C],
                    rhs=wab[:, k * 16:(k + 1) * 16], start=(k == 0), stop=(k == 7), skip_group_check=True),
                    reads=[("xn", k, 0), "wab"], writes=[("ps", 6)])
        pab3 = pab.rearrange("p (c n) -> p c n", n=16)
        g3, be3, tm3 = v3(g_t[0:64, :], NCK), v3(be_t[0:64, :], NCK), v3(tmp_ab[0:64, :], NCK)
        dtb3 = dtb[0:64, :].unsqueeze(1).broadcast_to([64, NCK, 8])
        nega3 = nega[0:64, :].unsqueeze(1).broadcast_to([64, NCK, 8])
        A("dve", lambda e: e.tensor_tensor(out=tm3, in0=pab3[:, :, 0:8], in1=dtb3, op=ALU.add),
          reads=[("ps", 6), "dtb"], writes=["tmp_ab"])
        A("act", lambda e: e.activation(out=be3, in_=pab3[:, :, 8:16], func=AF.Sigmoid), reads=[("ps", 6)], writes=["be_t"])
        A("act", lambda e: e.activation(out=tm3, in_=tm3, func=AF.Exp), reads=["tmp_ab"], writes=["tmp_ab"])
        A("act", lambda e: e.activation(out=tm3, in_=tm3, func=AF.Ln, bias=1.0), reads=["tmp_ab"], writes=["tmp_ab"])
        A("dve", lambda e: e.tensor_tensor(out=g3, in0=tm3, in1=nega3, op=ALU.mult), reads=["tmp_ab", "nega"], writes=["g_t"])
        for oc in range(32):
            wi = wc[0] % NW
            wc[0] += 1
            self.wload(wbs[wi], w_in[oc], 1024, ("win", wi))
            pb = oc % 2
            pt = self.bank(pb)
            for k in range(8):
                A("pe", lambda e, k=k, pt=pt, wi=wi: e.matmul(out=pt, lhsT=wbs[wi][:, k * 128:(k + 1) * 128],
                                                            rhs=xn[:, k * T:(k + 1) * T], start=(k == 0), stop=(k == 7)),
                  reads=[("win", wi), ("xn", k, 0)], writes=[("ps", pb)])
            if oc >= 24:
                h = oc - 24
                A("act", lambda e, h=h, pt=pt: e.activation(out=gs[:, h * T:(h + 1) * T], in_=pt, func=AF.Silu),
                  reads=[("ps", pb)], writes=[("gs", h)])
                continue
            pr, y = pre[oc % 2], yb[oc % 2]
            pk, yk = ("pre", oc % 2), ("yb", oc % 2)
            A("pool", lambda e, pr=pr, oc=oc: e.tensor_copy(out=pr[:, 0:3], in_=halo[:, oc * 3:oc * 3 + 3]),
              reads=["halo"], writes=[pk])
            A("act", lambda e, pr=pr, pt=pt: e.copy(out=pr[:, 3:3 + T], in_=pt), reads=[("ps", pb)], writes=[pk])
            A("act", lambda e, y=y, pr=pr, oc=oc: e.activation(out=y, in_=pr[:, 3:3 + T], func=AF.Identity,
                                                               scale=cw[:, oc * 4 + 3:oc * 4 + 4]),
              reads=[pk, "cw"], writes=[yk])
            for tap in (2, 1, 0):
                A("dve", lambda e, y=y, pr=pr, oc=oc, tap=tap: e.scalar_tensor_tensor(
                    out=y, in0=pr[:, tap:tap + T], scalar=cw[:, oc * 4 + tap:oc * 4 + tap + 1], in1=y,
                    op0=ALU.mult, op1=ALU.add), reads=[pk, yk, "cw"], writes=[yk])
            A("pool", lambda e, pr=pr, oc=oc: e.tensor_copy(out=halo[:, oc * 3:oc * 3 + 3], in_=pr[:, T:T + 3]),
              reads=[pk], writes=["halo"])
            qo = qkv[:, oc * T:(oc + 1) * T]
            A("act", lambda e, qo=qo, y=y: e.activation(out=qo, in_=y, func=AF.Silu), reads=[yk], writes=[("qkv", oc)])
            if oc < 16:
                q2 = sq[oc % 2]
                A("act", lambda e, q2=q2, qo=qo: e.activation(out=q2, in_=qo, func=AF.Square),
                  reads=[("qkv", oc)], writes=[("gsq", oc % 2)])
                A("pe", lambda e, q2=q2: e.matmul(out=self.bank(7), lhsT=self.ones32, rhs=q2, start=True, stop=True),
                  reads=[("gsq", oc % 2)], writes=[("ps", 7)])
                A("act", lambda e: e.activation(out=rstd, in_=self.bank(7), func=AF.Sqrt, bias=self.epsc, scale=1.0),
                  reads=[("ps", 7)], writes=["grstd"])
                A("dve", lambda e: e.reciprocal(out=rstd, in_=rstd), reads=["grstd"], writes=["grstd"])
                sc_ = (128.0 ** -0.5) if oc < 8 else 1.0
                A("dve", lambda e, qo=qo, sc_=sc_: e.scalar_tensor_tensor(out=qo, in0=qo, scalar=sc_, in1=rstd,
                                                                        op0=ALU.mult, op1=ALU.mult),
                  reads=[("qkv", oc), "grstd"], writes=[("qkv", oc)])
        import os
        if os.environ.get('GDN_MAXC'):
            A('pool', lambda e: e.memset(og, 0.0), writes=[('og', c_, g_) for c_ in range(NCK) for g_ in range(2)])
        NG = 2
        HG = H // NG

        def common(c):
            cb = c % 2
            g_c = g_t[0:64, c * 8:(c + 1) * 8]
            be_c = be_t[0:64, c * 8:(c + 1) * 8]
            psm = self.ps[:, 7 * 512:8 * 512]
            gcs_, egl_, ekd_, egc_, bege_, Em_, ETm_ = gcs[cb], egl[cb], ekd[cb], egc[cb], bege[cb], Em[cb], ETm[cb]
            ck = lambda n: (n, cb)
            A("pe", lambda e: e.matmul(out=psm[0:64, 0:8], lhsT=LT, rhs=g_c, start=True, stop=True, skip_group_check=True),
              reads=["g_t", "cst"], writes=[("ps", 7)])
            A("act", lambda e: e.copy(out=gcs_[0:64, :], in_=psm[0:64, 0:8]), reads=[("ps", 7)], writes=[ck("gcs")])
            A("act", lambda e: e.activation(out=egc_[0:64, :], in_=psm[0:64, 0:8], func=AF.Exp), reads=[("ps", 7)], writes=[ck("egc")])
            yield
            A("pe", lambda e: e.matmul(out=psm[:, 8:16], lhsT=sel63, rhs=gcs_[0:64, :], start=True, stop=True, skip_group_check=True),
              reads=[ck("gcs"), "cst"], writes=[("ps", 7)])
            A("act", lambda e: e.activation(out=egl_, in_=psm[:, 8:16], func=AF.Exp), reads=[("ps", 7)], writes=[ck("egl")])
            A("dve", lambda e: e.tensor_tensor(out=ekd_[0:64, :], in0=psm[0:64, 8:16], in1=gcs_[0:64, :], op=ALU.subtract),
              reads=[("ps", 7), ck("gcs")], writes=[ck("ekd")])
            A("act", lambda e: e.activation(out=ekd_[0:64, :], in_=ekd_[0:64, :], func=AF.Exp), reads=[ck("ekd")], writes=[ck("ekd")])
            A("dve", lambda e: e.tensor_tensor(out=bege_[0:64, :], in0=egc_[0:64, :], in1=be_c, op=ALU.mult),
              reads=[ck("egc"), "be_t"], writes=[ck("bege")])
            yield
            A("dve", lambda e: e.tensor_tensor(out=v3(rrhs[0:64, :], 8), in0=v3(idrep, 8), in1=bc(gcs_[0:64, :], 64), op=ALU.mult),
              reads=[ck("gcs"), "cst"], writes=["rrhs"])
            b6 = self.ps[0:64, 6 * 512:7 * 512]
            A("pe", lambda e: e.matmul(out=b6, lhsT=ones64, rhs=rrhs[0:64, :], start=True, stop=True),
              reads=["rrhs", "cst"], writes=[("ps", 6)])
            A("dve", lambda e: e.tensor_tensor(out=v3(Em_[0:64, :], 8), in0=v3(b6, 8), in1=bc(gcs_[0:64, :], 64), op=ALU.subtract),
              reads=[("ps", 6), ck("gcs")], writes=[ck("E")])
            yield
            A("dve", lambda e: e.tensor_scalar(out=Em_[0:64, :], in0=Em_[0:64, :], scalar1=0.0, scalar2=None, op0=ALU.max),
              reads=[ck("E")], writes=[ck("E")])
            A("act", lambda e: e.activation(out=Em_[0:64, :], in_=Em_[0:64, :], func=AF.Exp, scale=-1.0), reads=[ck("E")], writes=[ck("E")])
            A("dve", lambda e: e.tensor_tensor(out=Em_[0:64, :], in0=Em_[0:64, :], in1=mincl, op=ALU.mult),
              reads=[ck("E"), "cst"], writes=[ck("E")])
            yield
            for h in range(H):
                A("pe", lambda e, h=h: e.matmul(out=b6[:, h * 64:(h + 1) * 64], lhsT=Em_[0:64, h * 64:(h + 1) * 64], rhs=id64,
                                                start=True, stop=True, skip_group_check=True),
                  reads=[ck("E"), "cst"], writes=[("ps", 6)])
            A("act", lambda e: e.copy(out=ETm_[0:64, :], in_=b6), reads=[("ps", 6)], writes=[ck("ET")])
            yield

        def chain(c, g):
            cb = c % 2
            hs = list(range(g * HG, (g + 1) * HG))
            h0 = hs[0]
            W64 = slice(h0 * 64, (h0 + HG) * 64)
            W128 = slice(h0 * 128, (h0 + HG) * 128)
            W256 = slice(h0 * 256, (h0 + HG) * 256)
            hsl = slice(h0, h0 + HG)
            bA, bB = 2 * g, 2 * g + 1
            ck = lambda n: (n, cb)
            gk = lambda n: (n, g)
            be_c = be_t[0:64, c * 8:(c + 1) * 8]
            gcs_, egl_, ekd_, egc_, bege_, Em_, ETm_ = gcs[cb], egl[cb], ekd[cb], egc[cb], bege[cb], Em[cb], ETm[cb]

            def qT(h):
                return qkv[:, h * T + c * C:h * T + (c + 1) * C]

            def kT(h):
                return qkv[:, (8 + h) * T + c * C:(8 + h) * T + (c + 1) * C]

            def vT(h):
                return qkv[:, (16 + h) * T + c * C:(16 + h) * T + (c + 1) * C]

            qk_keys = [("qkv", o_) for o_ in range(24)]
            sbk = 4 + g
            p4 = self.ps[0:64, sbk * 512:sbk * 512 + 256]
            p5 = self.ps[0:64, sbk * 512 + 256:(sbk + 1) * 512]
            p4f = self.ps[:, sbk * 512:sbk * 512 + 256]
            p5f = self.ps[:, sbk * 512 + 256:(sbk + 1) * 512]
            k4 = k5 = ("ps", sbk)
            LW = slice(0, HG * 64)
            pA = self.ps[0:64, bA * 512:(bA + 1) * 512]
            pB = self.ps[0:64, bB * 512:(bB + 1) * 512]
            pBf = self.ps[:, bB * 512:(bB + 1) * 512]
            kA, kB = ("ps", bA), ("ps", bB)
            for j, h in enumerate(hs):
                A("pe", lambda e, h=h: e.matmul(out=p4[:, (h - h0) * 64:(h - h0 + 1) * 64], lhsT=kT(h), rhs=kT(h), start=True, stop=True,
                                                skip_group_check=True), reads=qk_keys, writes=[k4])
            P0, PT0 = Pm[0], PTm[0]
            A("dve", lambda e: e.tensor_tensor(out=P0[0:64, W64], in0=p4[:, LW], in1=Em_[0:64, W64], op=ALU.mult),
              reads=[k4, ck("E")], writes=[gk("P0")])
            A("dve", lambda e: e.tensor_tensor(out=P0[0:64, W64], in0=P0[0:64, W64], in1=mstrict[:, W64], op=ALU.mult),
              reads=[gk("P0"), "cst"], writes=[gk("P0")])
            A("dve", lambda e: e.tensor_tensor(out=v3(P0[0:64, W64], HG), in0=v3(P0[0:64, W64], HG), in1=bc(be_c[:, hsl], 64), op=ALU.mult),
              reads=[gk("P0"), "be_t"], writes=[gk("P0")])
            yield
            for h in hs:
                A("pe", lambda e, h=h: e.matmul(out=p5[:, (h - h0) * 64:(h - h0 + 1) * 64], lhsT=P0[0:64, h * 64:(h + 1) * 64], rhs=id64,
                                                start=True, stop=True, skip_group_check=True),
                  reads=[gk("P0"), "cst"], writes=[k5])
            A("act", lambda e: e.copy(out=PT0[0:64, W64], in_=p5[:, LW]), reads=[k5], writes=[gk("PT0")])
            yield
            for h in hs:
                A("pe", lambda e, h=h: e.matmul(out=p4[:, (h - h0) * 64:(h - h0 + 1) * 64], lhsT=kT(h), rhs=qT(h), start=True, stop=True,
                                                skip_group_check=True), reads=qk_keys, writes=[k4])
            A("dve", lambda e: e.tensor_tensor(out=attT[0:64, W64], in0=p4[:, LW], in1=ETm_[0:64, W64], op=ALU.mult),
              reads=[k4, ck("ET")], writes=[gk("attT")])
            yield
            B4 = v3(Bm[0:64, :], 8)
            for j, h in enumerate(hs):
                A("pe", lambda e, h=h, j=j: e.matmul(out=pA[:, j * 128:(j + 1) * 128], lhsT=kT(h), rhs=id128,
                                                     start=True, stop=True, skip_group_check=True), reads=qk_keys + ["cst"], writes=[kA])
            for j, h in enumerate(hs):
                A("pe", lambda e, h=h, j=j: e.matmul(out=pB[:, j * 128:(j + 1) * 128], lhsT=vT(h), rhs=id128,
                                                     start=True, stop=True, skip_group_check=True), reads=qk_keys + ["cst"], writes=[kB])
            A("dve", lambda e: e.tensor_tensor(out=B4[:, hsl, 128:256], in0=v3(pA, HG), in1=bc(bege_[0:64, hsl], 128), op=ALU.mult),
              reads=[kA, ck("bege")], writes=[gk("B")])
            A("dve", lambda e: e.tensor_tensor(out=v3(kdec[0:64, :], 8)[:, hsl, :], in0=v3(pA, HG), in1=bc(ekd_[0:64, hsl], 128), op=ALU.mult),
              reads=[kA, ck("ekd")], writes=[gk("kdec")])
            A("dve", lambda e: e.tensor_tensor(out=B4[:, hsl, 0:128], in0=v3(pB, HG), in1=bc(be_c[:, hsl], 128), op=ALU.mult),
              reads=[kB, "be_t"], writes=[gk("B")])
            yield
            cur = 0
            for lvl in range(6):
                P, PT = Pm[cur], PTm[cur]
                pk, ptk = gk("P%d" % cur), gk("PT%d" % cur)
                for j, h in enumerate(hs):
                    pb_, kb_ = (pA, kA) if j < 2 else (pB, kB)
                    A("pe", lambda e, h=h, j=j, pb_=pb_, PT=PT: e.matmul(
                        out=pb_[:, (j % 2) * 256:(j % 2 + 1) * 256], lhsT=PT[0:64, h * 64:(h + 1) * 64],
                        rhs=Bm[0:64, h * 256:(h + 1) * 256], start=True, stop=True, skip_group_check=True),
                        reads=[ptk, gk("B")], writes=[kb_])
                for half, (pb_, kb_) in enumerate(((pA, kA), (pB, kB))):
                    bs = Bm[0:64, (h0 + 2 * half) * 256:(h0 + 2 * half + 2) * 256]
                    A("dve", lambda e, bs=bs, pb_=pb_, lvl=lvl: e.tensor_tensor(
                        out=bs, in0=bs, in1=pb_, op=(ALU.subtract if lvl == 0 else ALU.add)), reads=[kb_, gk("B")], writes=[gk("B")])
                if lvl < 5:
                    nxt = 1 - cur
                    for h in hs:
                        A("pe", lambda e, h=h, P=P, PT=PT: e.matmul(out=p4[:, (h - h0) * 64:(h - h0 + 1) * 64], lhsT=PT[0:64, h * 64:(h + 1) * 64],
                                                                    rhs=P[0:64, h * 64:(h + 1) * 64], start=True, stop=True,
                                                                    skip_group_check=True), reads=[pk, ptk], writes=[k4])
                    for h in hs:
                        A("pe", lambda e, h=h, P=P, PT=PT: e.matmul(out=p5[:, (h - h0) * 64:(h - h0 + 1) * 64], lhsT=P[0:64, h * 64:(h + 1) * 64],
                                                                    rhs=PT[0:64, h * 64:(h + 1) * 64], start=True, stop=True,
                                                                    skip_group_check=True), reads=[pk, ptk], writes=[k5])
                    A("act", lambda e, nxt=nxt: e.copy(out=Pm[nxt][0:64, W64], in_=p4[:, LW]), reads=[k4], writes=[gk("P%d" % nxt)])
                    A("act", lambda e, nxt=nxt: e.copy(out=PTm[nxt][0:64, W64], in_=p5[:, LW]), reads=[k5], writes=[gk("PT%d" % nxt)])
                    cur = nxt
                yield
            for h in hs:
                A("pe", lambda e, h=h: e.matmul(out=p4f[:, (h - h0) * 64:(h - h0 + 1) * 64], lhsT=Bm[0:64, h * 256 + 128:h * 256 + 256], rhs=id64,
                                                start=True, stop=True, skip_group_check=True), reads=[gk("B"), "cst"], writes=[k4])
            A("act", lambda e: e.copy(out=wT[:, W64], in_=p4f[:, LW]), reads=[k4], writes=[gk("wT")])
            yield
            for j, h in enumerate(hs):
                A("pe", lambda e, h=h, j=j: e.matmul(out=pA[:, j * 128:(j + 1) * 128], lhsT=wT[:, h * 64:(h + 1) * 64],
                                                     rhs=St[:, h * 128:(h + 1) * 128], start=True, stop=True, skip_group_check=True),
                  reads=[gk("wT"), gk("S")], writes=[kA])
            vn3 = v3(vnew[0:64, :], 8)
            A("dve", lambda e: e.tensor_tensor(out=vn3[:, hsl, :], in0=B4[:, hsl, 0:128], in1=v3(pA, HG), op=ALU.subtract),
              reads=[kA, gk("B")], writes=[gk("vnew")])
            for j, h in enumerate(hs):
                A("pe", lambda e, h=h, j=j: e.matmul(out=pB[:, j * 128:(j + 1) * 128], lhsT=qT(h), rhs=St[:, h * 128:(h + 1) * 128],
                                                     start=True, stop=True, skip_group_check=True), reads=qk_keys + [gk("S")], writes=[kB])
            o3 = v3(om[0:64, :], 8)
            A("dve", lambda e: e.tensor_tensor(out=o3[:, hsl, :], in0=v3(pB, HG), in1=bc(egc_[0:64, hsl], 128), op=ALU.mult),
              reads=[kB, ck("egc")], writes=[gk("o")])
            yield
            for j, h in enumerate(hs):
                A("pe", lambda e, h=h, j=j: e.matmul(out=pA[:, j * 128:(j + 1) * 128], lhsT=attT[0:64, h * 64:(h + 1) * 64],
                                                     rhs=vnew[0:64, h * 128:(h + 1) * 128], start=True, stop=True, skip_group_check=True),
                  reads=[gk("attT"), gk("vnew")], writes=[kA])
            A("dve", lambda e: e.tensor_tensor(out=o3[:, hsl, :], in0=o3[:, hsl, :], in1=v3(pA, HG), op=ALU.add),
              reads=[kA, gk("o")], writes=[gk("o")])
            for j, h in enumerate(hs):
                A("pe", lambda e, h=h, j=j: e.matmul(out=pBf[:, j * 128:(j + 1) * 128], lhsT=kdec[0:64, h * 128:(h + 1) * 128],
                                                     rhs=vnew[0:64, h * 128:(h + 1) * 128], start=True, stop=True, skip_group_check=True),
                  reads=[gk("kdec"), gk("vnew")], writes=[kB])
            A("dve", lambda e: e.tensor_tensor(out=v3(St[:, W128], HG), in0=v3(St[:, W128], HG), in1=bc(egl_[:, hsl], 128), op=ALU.mult),
              reads=[gk("S"), ck("egl")], writes=[gk("S")])
            A("dve", lambda e: e.tensor_tensor(out=St[:, W128], in0=St[:, W128], in1=pBf, op=ALU.add),
              reads=[kB, gk("S")], writes=[gk("S")])
            yield
            A("pool", lambda e: e.tensor_tensor(out=osq[0:64, W128], in0=om[0:64, W128], in1=om[0:64, W128], op=ALU.mult),
              reads=[gk("o")], writes=[gk("osq")])
            A("dve", lambda e: e.tensor_reduce(out=ss8[0:64, hsl], in_=v3(osq[0:64, W128], HG), axis=AX.X, op=ALU.add),
              reads=[gk("osq")], writes=[gk("ss8")])
            A("act", lambda e: e.activation(out=ss8[0:64, hsl], in_=ss8[0:64, hsl], func=AF.Sqrt, bias=self.epsc[0:64, :], scale=1.0 / 128),
              reads=[gk("ss8")], writes=[gk("ss8")])
            A("dve", lambda e: e.reciprocal(out=ss8[0:64, hsl], in_=ss8[0:64, hsl]), reads=[gk("ss8")], writes=[gk("ss8")])
            A("dve", lambda e: e.tensor_tensor(out=o3[:, hsl, :], in0=o3[:, hsl, :], in1=bc(ss8[0:64, hsl], 128), op=ALU.mult),
              reads=[gk("ss8"), gk("o")], writes=[gk("o")])
            yield
            for h in hs:
                A("pe", lambda e, h=h: e.matmul(out=p5f[:, (h - h0) * 64:(h - h0 + 1) * 64], lhsT=om[0:64, h * 128:(h + 1) * 128], rhs=id64,
                                                start=True, stop=True, skip_group_check=True), reads=[gk("o"), "cst"], writes=[k5])
            og3 = v3(og, 8)[:, hsl, c * C:(c + 1) * C]
            gs3 = v3(gs, 8)[:, hsl, c * C:(c + 1) * C]
            A("dve", lambda e: e.scalar_tensor_tensor(out=og3, in0=v3(p5f[:, LW], HG), scalar=onw[:, 0:1], in1=gs3,
                                                      op0=ALU.mult, op1=ALU.mult),
              reads=[k5, "onw"] + [("gs", h) for h in hs], writes=[("og", c, g)])
            yield

        def run_gens(gens):
            gens = list(gens)
            while gens:
                for gname in list(gens):
                    try:
                        next(gname)
                    except StopIteration:
                        gens.remove(gname)

        ncs_ = min(NCK, int(os.environ.get('GDN_MAXC', '99')))
        run_gens([common(0)])
        for c in range(ncs_):
            gl = [chain(c, g) for g in range(NG)]
            if c + 1 < ncs_:
                gl.append(common(c + 1))
            run_gens(gl)
        for o in range(8):
            wi = wc[1] % 2
            wc[1] += 1
            self.wload(wob[wi], w_out[o], 1024, ("wout", wi))
            pb = o % 2
            pt = self.bank(pb)
            for k in range(8):
                A("pe", lambda e, k=k, pt=pt, wi=wi: e.matmul(out=pt, lhsT=wob[wi][:, k * 128:(k + 1) * 128],
                                                            rhs=og[:, k * T:(k + 1) * T], start=(k == 0), stop=(k == 7)),
                  reads=[("wout", wi)] + [("og", c, g_) for c in range(NCK) for g_ in range(2)], writes=[("ps", pb)])
            xs = x32[:, o * T:(o + 1) * T]
            A("dve", lambda e, xs=xs, pt=pt: e.tensor_tensor(out=xs, in0=pt, in1=xs, op=ALU.add),
              reads=[("ps", pb), ("x32", o)], writes=[("x32", o)])
            A("act", lambda e, o=o, t=t: e.dma_start(out=dst[o * 128:(o + 1) * 128, t * T:(t + 1) * T], in_=x32[:, o * T:(o + 1) * T]),
              reads=[("x32", o)], dkey=("st", o))
    self.dbg = dict(wT=wT, qkv=qkv, gs=gs, g_t=g_t, be_t=be_t, gcs=gcs[1], egl=egl[1], ekd=ekd[1], egc=egc[1], Em=Em[1], ETm=ETm[1], P0=Pm[0], P1=Pm[1], attT=attT, Bm=Bm, kdec=kdec, vnew=vnew, om=om, St=St, og=og, xn=xn, x32=x32)
    sb.release(m)


K.gdn_phase = gdn_phase


def prep_gdn(w_in, conv_w, A_log, dt_bias, out_norm, w_out):
    d = {}
    d["w_in"] = prep_proj(np.ascontiguousarray(w_in[:, :4096]))
    wab = w_in[:, 4096:4112].reshape(8, 128, 16)
    d["wab"] = np.ascontiguousarray(wab.transpose(1, 0, 2)).reshape(128, 128)
    d["cw"] = np.ascontiguousarray(conv_w.reshape(4, 24, 128).transpose(2, 1, 0)).reshape(128, 96)
    d["alog"] = np.ascontiguousarray(np.broadcast_to(A_log[None, :], (128, 8)))
    d["dtb"] = np.ascontiguousarray(np.broadcast_to(dt_bias[None, :], (128, 8)))
    d["onw"] = np.ascontiguousarray(out_norm.reshape(128, 1))
    d["w_out"] = prep_proj(w_out)
    return d


NEGB = -30000.0
VW = 386
VOFF = (0, 65, 193, 258)
NCS = {}
_o = 0
for _n, _w in (("ones_bd", 128), ("rperm", 128), ("id128", 128), ("inv", 1), ("qw", 1), ("kw3", 3), ("b2k", 1),
               ("hb1", 4), ("sel0", 128), ("sel64", 128)):
    NCS[_n] = (_o, _w)
    _o += _w
NCS_W = _o
NBC = {}
_o = 0
for _n, _w in (("efull", 4096), ("id128", 128), ("causb", 4 * 512), ("bandb", 4 * 512), ("cmpb", 512), ("cmpb0", 512),
               ("ovl", 8 * 64), ("cmpr0", 512)):
    NBC[_n] = (_o, _w)
    _o += _w
NBC_W = _o


def prep_nsa(inp):
    f = lambda a: np.asarray(a, dtype=np.float32)
    w = f(inp["nsa_w_in"])[0]
    d = {}
    colsA = np.concatenate([np.arange(1024, 1536), np.arange(1536, 1792), np.arange(2048, 2304)])
    d["wka"] = prep_proj(np.ascontiguousarray(w[:, colsA]))
    colsV = np.concatenate([np.arange(1792, 2048), np.arange(2304, 2560)])
    wv = w[:, colsV].reshape(8, 128, 512)
    d["wv"] = np.ascontiguousarray(wv.transpose(1, 0, 2)).reshape(128, 8 * 512)
    W1 = f(inp["nsa_cmp_w1"])[0]
    w1 = W1.reshape(2, 32, 64, 256).transpose(0, 2, 1, 3).reshape(2, 64, 32 * 256)
    d["w1"] = np.ascontiguousarray(np.concatenate([w1, w1], axis=1))
    pe = f(inp["nsa_cmp_pe"])[0]
    peT = pe.transpose(0, 2, 1)
    d["peT"] = np.ascontiguousarray(np.concatenate([peT, peT], axis=1))
    W2 = f(inp["nsa_cmp_w2"])[0]
    w2k = W2[0].reshape(2, 128, 64).transpose(1, 0, 2)
    d["w2k"] = np.ascontiguousarray(np.concatenate([w2k, w2k], axis=2)).reshape(128, 256)
    d["w2v"] = np.ascontiguousarray(W2[1].reshape(2, 128, 64).transpose(1, 0, 2)).reshape(128, 128)
    b1 = f(inp["nsa_cmp_b1"])[0]
    b2 = f(inp["nsa_cmp_b2"])[0]
    d["b2v"] = np.ascontiguousarray(b2[1].reshape(1, 64))
    c = np.zeros((128, NCS_W), np.float32)

    def put(name, arr):
        o, wd_ = NCS[name]
        c[:arr.shape[0], o:o + wd_] = arr

    ob = np.zeros((128, 128), np.float32)
    ob[:64, :64] = 1
    ob[64:, 64:] = 1
    put("ones_bd", ob)
    rp = np.zeros((128, 128), np.float32)
    for blk in (0, 64):
        for m_ in range(32):
            rp[blk + m_ + 32, blk + m_] = -1.0
            rp[blk + m_, blk + m_ + 32] = 1.0
    put("rperm", rp)
    put("id128", np.eye(128, dtype=np.float32))
    inv = (1.0 / (10000.0 ** (np.arange(0, 64, 2, dtype=np.float32) / 64))).astype(np.float32)
    put("inv", np.tile(inv, 4).reshape(128, 1))
    put("qw", np.tile(f(inp["nsa_q_norm"])[0], 2).reshape(128, 1))
    kn = f(inp["nsa_k_norm"])[0]
    put("kw3", np.tile(kn.T, (2, 1)))
    put("b2k", np.tile(b2[0], 2).reshape(128, 1))
    put("hb1", b1.reshape(2, 2, 128).transpose(2, 0, 1).reshape(128, 4))
    s0 = np.zeros((128, 128), np.float32)
    s0[0, :] = 1.0
    put("sel0", s0)
    s64 = np.zeros((128, 128), np.float32)
    s64[64, :] = 1.0
    put("sel64", s64)
    d["ncs"] = c
    bc_ = np.zeros((128, NBC_W), np.float32)

    def putb(name, arr):
        o, wd_ = NBC[name]
        bc_[:arr.shape[0], o:o + wd_] = arr

    keys = np.arange(4096)
    putb("efull", (keys[None, :] // 64 == np.arange(64)[:, None]).astype(np.float32))
    putb("id128", np.eye(128, dtype=np.float32))
    kk = np.arange(128)[:, None]
    qq = np.arange(512)[None, :]
    putb("causb", np.concatenate([np.where(dd * 128 + kk > qq, NEGB, 0.0) for dd in range(4)], axis=1))
    putb("bandb", np.concatenate([np.where(kk + e_ * 128 <= qq, NEGB, 0.0) for e_ in range(4)], axis=1))
    jj = np.arange(32)[:, None]
    cm = np.where(16 * jj + 15 > qq, NEGB, 0.0)
    putb("cmpb", cm)
    cm0 = cm.copy()
    cm0[0, :] = NEGB
    putb("cmpb0", cm0)
    r0 = np.zeros((32, 512), np.float32)
    r0[0, :] = NEGB
    ov = np.zeros((32, 8, 64), np.float32)
    for tp in range(8):
        for j in range(32):
            n = 32 * tp + j - 1
            if n < 0:
                continue
            for s_ in range(64):
                lo = max(16 * n, 64 * s_)
                hi = min(16 * n + 32, 64 * s_ + 64)
                ov[j, tp, s_] = max(hi - lo, 0) / 32.0
    putb("ovl", ov.reshape(32, 512))
    putb("cmpr0", r0)
    d["nbc"] = bc_
    sm = np.zeros((8, 128, 2, 4, 64), np.float32)
    for t in range(8):
        for blk in range(4):
            tq = t * 512 + blk * 128 + np.arange(128)[:, None]
            s_ = np.arange(64)[None, :]
            valid = (s_ * 64 <= tq)
            dist = tq // 64 - s_
            forced = (s_ == 0) | ((dist >= 0) & (dist < 2))
            sm[t, :, 0, blk, :] = valid
            sm[t, :, 1, blk, :] = np.where(valid & forced, 1e9, 0.0) + np.where(valid, 0.0, -1.0)
    d["selc"] = sm.reshape(8, 128, 512)
    qcols = []
    for pp in range(2):
        for i in range(4):
            for g in (2 * pp, 2 * pp + 1):
                h = g * 4 + i
                qcols.append(np.arange(h * 64, (h + 1) * 64))
    gcols = []
    for pp in range(2):
        for r in range(3):
            for i in range(4):
                for g in (2 * pp, 2 * pp + 1):
                    h = g * 4 + i
                    gcols.append(np.full(64, 2560 + h * 3 + r))
    d["wq"] = prep_proj(np.ascontiguousarray(w[:, np.concatenate(qcols)]))
    d["wg"] = prep_proj(np.ascontiguousarray(w[:, np.concatenate(gcols)]))
    wo = f(inp["nsa_w_out"])[0]
    d["wo"] = prep_proj(np.ascontiguousarray(wo[np.concatenate(qcols), :]))
    return d


import math
TWO_PI = 2.0 * math.pi
CW1 = 6.28125
CW2 = TWO_PI - CW1


def rope_tables(self, pos_d, t, T, cosb, sinb, wk, inv_col, tag, wkeys=None):
    A = self.S.add
    ti = wk[0].bitcast(I32)
    ang, kf = wk[1], wk[2]
    if wkeys is None:
        wkeys = [tag + "w0", tag + "w1", tag + "w2"]
    K0, K1, K2 = wkeys
    A("sp", lambda e: e.dma_start(out=ti, in_=pos_d[0:1, t * T:(t + 1) * T].broadcast_to([128, T])),
      writes=[K0], dkey=tag + "pos")
    A("dve", lambda e: e.tensor_copy(out=ang, in_=ti), reads=[K0], writes=[K1])
    A("dve", lambda e: e.tensor_scalar(out=ang, in0=ang, scalar1=inv_col, scalar2=None, op0=ALU.mult),
      reads=[K1, "ncs"], writes=[K1])
    A("dve", lambda e: e.tensor_scalar(out=ti, in0=ang, scalar1=1.0 / TWO_PI, scalar2=None, op0=ALU.mult),
      reads=[K1], writes=[K0])
    A("dve", lambda e: e.tensor_copy(out=kf, in_=ti), reads=[K0], writes=[K2])
    A("dve", lambda e: e.scalar_tensor_tensor(out=ang, in0=kf, scalar=-CW1, in1=ang, op0=ALU.mult, op1=ALU.add),
      reads=[K1, K2], writes=[K1])
    A("dve", lambda e: e.scalar_tensor_tensor(out=ang, in0=kf, scalar=-CW2, in1=ang, op0=ALU.mult, op1=ALU.add),
      reads=[K1, K2], writes=[K1])

    def wrap(x, key):
        A("dve", lambda e: e.tensor_scalar(out=kf, in0=x, scalar1=math.pi, scalar2=-TWO_PI, op0=ALU.is_gt, op1=ALU.mult),
          reads=[key], writes=[K2])
        A("dve", lambda e: e.tensor_tensor(out=x, in0=x, in1=kf, op=ALU.add), reads=[key, K2], writes=[key])
        A("dve", lambda e: e.tensor_scalar(out=kf, in0=x, scalar1=-math.pi, scalar2=TWO_PI, op0=ALU.is_lt, op1=ALU.mult),
          reads=[key], writes=[K2])
        A("dve", lambda e: e.tensor_tensor(out=x, in0=x, in1=kf, op=ALU.add), reads=[key, K2], writes=[key])

    wrap(ang, K1)
    A("act", lambda e: e.activation(out=sinb, in_=ang, func=AF.Sin), reads=[K1], writes=[tag + "sin"])
    A("dve", lambda e: e.tensor_scalar(out=ang, in0=ang, scalar1=math.pi / 2, scalar2=None, op0=ALU.add),
      reads=[K1], writes=[K1])
    wrap(ang, K1)
    A("act", lambda e: e.activation(out=cosb, in_=ang, func=AF.Sin), reads=[K1], writes=[tag + "cos"])


K.rope_tables = rope_tables


def headnorm_rope(self, pt, pkey, wcol, cosb, sinb, tag, outs, scale, wk, ncs, T):
    A = self.S.add
    sqv, rs, xnr, t1 = wk
    ones_bd, rperm = ncs["ones_bd"], ncs["rperm"]
    A("act", lambda e: e.activation(out=sqv, in_=pt, func=AF.Square), reads=[pkey], writes=[tag + "sq"])
    A("pe", lambda e: e.matmul(out=self.bank(2, T), lhsT=ones_bd, rhs=sqv, start=True, stop=True),
      reads=[tag + "sq", "ncs"], writes=[("ps", 2)])
    A("act", lambda e: e.activation(out=rs, in_=self.bank(2, T), func=AF.Sqrt, bias=self.epsc, scale=1.0 / 64),
      reads=[("ps", 2)], writes=[tag + "rs"])
    A("dve", lambda e: e.reciprocal(out=rs, in_=rs), reads=[tag + "rs"], writes=[tag + "rs"])
    A("dve", lambda e: e.scalar_tensor_tensor(out=xnr, in0=pt, scalar=wcol, in1=rs, op0=ALU.mult, op1=ALU.mult),
      reads=[pkey, tag + "rs", "ncs"], writes=[tag + "xn"])
    outs = [o_ if len(o_) == 4 else (o_[0], o_[1], o_[2], slice(0, 128)) for o_ in outs]
    need_rope = any(o_[1] for o_ in outs)
    if need_rope:
        A("pe", lambda e: e.matmul(out=self.bank(3, T), lhsT=rperm, rhs=xnr, start=True, stop=True),
          reads=[tag + "xn", "ncs"], writes=[("ps", 3)])
    roped = False
    for o_ap, rope, okey, rows in outs:
        if not rope:
            A("act", lambda e, o_ap=o_ap, rows=rows: e.activation(out=o_ap[rows, :], in_=xnr[rows, :], func=AF.Copy, scale=scale),
              reads=[tag + "xn"], writes=[okey])
        else:
            if not roped:
                A("pool", lambda e: e.tensor_tensor(out=t1, in0=xnr, in1=cosb, op=ALU.mult),
                  reads=[tag + "xn", "ropecos"], writes=[tag + "t1"])
                A("dve", lambda e: e.tensor_tensor(out=rs, in0=self.bank(3, T), in1=sinb, op=ALU.mult),
                  reads=[("ps", 3), "ropesin", tag + "rs"], writes=[tag + "rs"])
                A("dve", lambda e: e.tensor_tensor(out=t1, in0=t1, in1=rs, op=ALU.add),
                  reads=[tag + "t1", tag + "rs"], writes=[tag + "t1"])
                roped = True
            A("act", lambda e, o_ap=o_ap, rows=rows: e.activation(out=o_ap[rows, :], in_=t1[rows, :], func=AF.Copy, scale=scale),
              reads=[tag + "t1"], writes=[okey])


K.headnorm_rope = headnorm_rope


def nsa_phase(self, src, dst, W, pos_d, gamma_col):
    S, sb = self.S, self.sb
    S.barrier()
    m = sb.mark()
    A = S.add
    T = 512
    NT = self.ntok // T
    SQ = self.ntok
    NKT = SQ // 128

    def v3(ap, a):
        return ap.rearrange("p (a b) -> p a b", a=a)

    ncs_t = sb.f32(NCS_W)
    A("sp", lambda e: e.dma_start(out=ncs_t, in_=W["ncs"]), writes=["ncs"], dkey="ncs")
    ncs = {n: ncs_t[:, o:o + w] for n, (o, w) in NCS.items()}
    ksT = sb.bf16(2 * SQ)
    kwT = sb.bf16(2 * 1024)
    vsS = sb.bf16(NKT * VW)
    vwS = sb.bf16(8 * VW)
    kcT = sb.bf16(4 * 32 * NT)
    vcS = sb.bf16(NT * VW)
    gam = sb.f32(8)
    A("sp", lambda e: e.dma_start(out=gam, in_=gamma_col), writes=["gam"], dkey="gam")
    for st_, nm in ((vsS, "vsS"), (vwS, "vwS"), (vcS, "vcS")):
        A("pool", lambda e, st_=st_: e.memset(st_, 0.0), writes=[nm])
        n_t = st_.shape[1] // VW
        s3 = st_.rearrange("p (t w) -> p t w", w=VW)
        for col in (64, 65, 257, 258):
            A("pool", lambda e, s3=s3, col=col: e.memset(s3[:, :, col:col + 1], 1.0), writes=[nm])
    S.barrier()
    mB = sb.mark()

    x32 = sb.f32(8 * T)
    xn = sb.bf16(8 * T)
    sq = [sb.f32(512) for _ in range(2)]
    rstd = sb.f32(512)
    cosb, sinb = sb.f32(T), sb.f32(T)
    rwk = [sb.f32(T) for _ in range(3)]
    hwk = [sb.f32(T) for _ in range(4)]
    kraw = [sb.bf16(16 + T) for _ in range(8)]
    w1 = [sb.bf16(8192) for _ in range(2)]
    peT = [sb.bf16(32) for _ in range(2)]
    w2k = sb.bf16(256)
    w2v = sb.bf16(128)
    b2v = sb.f32(64)
    one1 = sb.f32(32)
    hb = sb.f32(4)
    wv = sb.bf16(8 * 512)
    NW = 4
    wbs = [sb.bf16(1024) for _ in range(NW)]
    hx = sb.f32(512)
    hy = sb.f32(512)
    hidT = sb.bf16(512)
    kcw = sb.f32(128)
    for i in range(2):
        self.wload(w1[i], W["w1"][i], 8192, ("w1", i))
        A("pool", lambda e, i=i: e.dma_start(out=peT[i], in_=W["peT"][i]), writes=[("peT", i)], dkey=("peT", i))
    A("pool", lambda e: e.dma_start(out=w2k, in_=W["w2k"]), writes=["w2k"], dkey="w2k")
    A("pool", lambda e: e.dma_start(out=w2v, in_=W["w2v"]), writes=["w2v"], dkey="w2v")
    A("sp", lambda e: e.dma_start(out=b2v[0:1, :], in_=W["b2v"]), writes=["b2v"], dkey="b2v")
    A("pool", lambda e: e.memset(one1[0:1, :], 1.0), writes=["one1"])
    self.wload(wv, W["wv"], 4096, "wv")
    for r_ in range(8):
        A("pool", lambda e, r_=r_: e.memset(kraw[r_], 0.0), writes=[("kraw", r_)])
    pb6 = self.ps[:, 6 * 512:6 * 512 + 4]
    for i in range(2):
        for hh in range(2):
            col = i * 2 + hh
            for l in range(32):
                A("pe", lambda e, i=i, hh=hh, l=l, col=col: e.matmul(
                    out=pb6[:, col:col + 1], lhsT=w1[i][0:64, l * 256 + hh * 128:l * 256 + (hh + 1) * 128],
                    rhs=peT[i][0:64, l:l + 1], start=(l == 0), stop=(l == 31), skip_group_check=True),
                    reads=[("w1", i), ("peT", i)], writes=[("ps", 6)])
    A("dve", lambda e: e.tensor_tensor(out=hb, in0=pb6, in1=ncs["hb1"], op=ALU.add), reads=[("ps", 6), "ncs"], writes=["hb"])
    S.stop_at("P1")

    wc = [0]

    def tileA(t):
        self.load_x_tile(src, x32, "x32", t, T)
        self.rmsnorm_tile(x32, "x32", xn, "xn", gam, "gam", T, sq, rstd, 7, "n")
        self.rope_tables(pos_d, t, T, cosb, sinb, rwk, ncs["inv"], "rope")
        S.stop_at("P2")
        xk = [("xn", k, 0) for k in range(8)]
        for oc in range(6):
            wi = wc[0] % NW
            wc[0] += 1
            self.wload(wbs[wi], W["wka"][oc], 1024, ("win", wi))
            pbk = oc % 2
            pt = self.bank(pbk)
            for k in range(8):
                A("pe", lambda e, k=k, pt=pt, wi=wi: e.matmul(out=pt, lhsT=wbs[wi][:, k * 128:(k + 1) * 128],
                                                            rhs=xn[:, k * T:(k + 1) * T], start=(k == 0), stop=(k == 7)),
                  reads=[("win", wi), ("xn", k, 0)], writes=[("ps", pbk)])
            if oc < 4:
                A("act", lambda e, oc=oc, pt=pt: e.copy(out=kraw[oc * 2][0:64, 16:16 + T], in_=pt[0:64, :]),
                  reads=[("ps", pbk)], writes=[("kraw", oc * 2)])
                A("dve", lambda e, oc=oc, pt=pt: e.tensor_copy(out=kraw[oc * 2 + 1][64:128, 16:16 + T], in_=pt[64:128, :]),
                  reads=[("ps", pbk)], writes=[("kraw", oc * 2 + 1)])
            else:
                pp = oc % 2
                o_ap = ksT[:, pp * SQ + t * T:pp * SQ + (t + 1) * T]
                self.headnorm_rope(pt, ("ps", pbk), ncs["kw3"][:, 1:2], cosb, sinb, "hn",
                                   [(o_ap, True, ("ksT", pp, t))], 1.0, hwk, ncs, T)
        S.stop_at("P3")
        for blk in range(4):
            kt = t * 4 + blk
            pv = self.bank(4, 256)
            for k in range(8):
                A("pe", lambda e, k=k, blk=blk, pv=pv: e.matmul(
                    out=pv, lhsT=xn[:, k * T + blk * 128:k * T + (blk + 1) * 128], rhs=wv[:, k * 512:k * 512 + 256],
                    start=(k == 0), stop=(k == 7)), reads=[("xn", k, 0), "wv"], writes=[("ps", 4)])
            for j_, (st_, nm) in enumerate(((vsS, "vsS"),)):
                for g in range(4):
                    off = kt * VW + VOFF[g] + (64 if g % 2 else 0)
                    eng = "act" if (g + j_) % 2 == 0 else "dve"
                    fn = (lambda e, st_=st_, off=off, g=g, j_=j_, pv=pv: e.copy(
                        out=st_[:, off:off + 64], in_=pv[:, j_ * 256 + g * 64:j_ * 256 + (g + 1) * 64])) if eng == "act" else \
                        (lambda e, st_=st_, off=off, g=g, j_=j_, pv=pv: e.tensor_copy(
                            out=st_[:, off:off + 64], in_=pv[:, j_ * 256 + g * 64:j_ * 256 + (g + 1) * 64]))
                    A(eng, fn, reads=[("ps", 4)], writes=[(nm, kt)])
        S.stop_at("P4")
        p5 = self.bank(5)
        for i in range(2):
            for hh in range(2):
                for g in range(4):
                    pp = g // 2
                    col = ((i * 2 + hh) * 4 + g) * 32
                    ri = (i * 2 + pp) * 2 + g % 2
                    src_ = kraw[ri]
                    for l in range(32):
                        A("pe", lambda e, i=i, hh=hh, l=l, col=col, src_=src_: e.matmul(
                            out=p5[:, col:col + 32], lhsT=w1[i][:, l * 256 + hh * 128:l * 256 + (hh + 1) * 128],
                            rhs=src_[:, l:l + 16 * 31 + 1:16], start=(l == 0), stop=(l == 31), skip_group_check=True),
                            reads=[("w1", i), ("kraw", ri)], writes=[("ps", 5)])
        for r_ in range(8):
            A("pool", lambda e, r_=r_: e.tensor_copy(out=kraw[r_][:, 0:16], in_=kraw[r_][:, T:T + 16]),
              reads=[("kraw", r_)], writes=[("kraw", r_)])
        S.stop_at("P5")
        for q_ in range(4):
            A("act", lambda e, q_=q_: e.activation(out=hx[:, q_ * 128:(q_ + 1) * 128], in_=p5[:, q_ * 128:(q_ + 1) * 128],
                                                   func=AF.Identity, bias=hb[:, q_:q_ + 1]),
              reads=[("ps", 5), "hb"], writes=["hx"])
        A("dve", lambda e: e.tensor_tensor(out=hy, in0=hx, in1=hx, op=ALU.mult), reads=["hx"], writes=["hy"])
        A("dve", lambda e: e.tensor_scalar(out=hy, in0=hy, scalar1=0.044715, scalar2=1.0, op0=ALU.mult, op1=ALU.add),
          reads=["hy"], writes=["hy"])
        A("dve", lambda e: e.tensor_tensor(out=hy, in0=hy, in1=hx, op=ALU.mult), reads=["hy", "hx"], writes=["hy"])
        A("act", lambda e: e.activation(out=hy, in_=hy, func=AF.Tanh, scale=0.7978845608028654), reads=["hy"], writes=["hy"])
        A("dve", lambda e: e.tensor_scalar(out=hy, in0=hy, scalar1=0.5, scalar2=0.5, op0=ALU.mult, op1=ALU.add),
          reads=["hy"], writes=["hy"])
        A("dve", lambda e: e.tensor_tensor(out=hidT, in0=hy, in1=hx, op=ALU.mult), reads=["hy", "hx"], writes=["hidT"])
        p6 = self.ps[:, 6 * 512:6 * 512 + 128]
        for g in range(4):
            for hh in range(2):
                col = ((0 * 2 + hh) * 4 + g) * 32
                A("pe", lambda e, g=g, hh=hh, col=col: e.matmul(out=p6[:, g * 32:(g + 1) * 32], lhsT=w2k[:, hh * 128:(hh + 1) * 128],
                                                                rhs=hidT[:, col:col + 32], start=(hh == 0), stop=(hh == 1),
                                                                skip_group_check=True),
                  reads=["hidT", "w2k"], writes=[("ps", 6)])
        A("act", lambda e: e.activation(out=kcw, in_=p6, func=AF.Identity, bias=ncs["b2k"]), reads=[("ps", 6), "ncs"], writes=["kcw"])
        A("act", lambda e: e.activation(out=hwk[0][:, 0:128], in_=kcw, func=AF.Square), reads=["kcw"], writes=["kcsq"])
        A("pe", lambda e: e.matmul(out=self.bank(2, 128), lhsT=ncs["ones_bd"], rhs=hwk[0][:, 0:128], start=True, stop=True),
          reads=["kcsq", "ncs"], writes=[("ps", 2)])
        A("act", lambda e: e.activation(out=hwk[1][:, 0:128], in_=self.bank(2, 128), func=AF.Sqrt, bias=self.epsc, scale=1.0 / 64),
          reads=[("ps", 2)], writes=["kcrs"])
        A("dve", lambda e: e.reciprocal(out=hwk[1][:, 0:128], in_=hwk[1][:, 0:128]), reads=["kcrs"], writes=["kcrs"])
        kc3 = kcT.rearrange("p (g n) -> p g n", g=4)[:, :, t * 32:(t + 1) * 32]
        A("dve", lambda e, kc3=kc3: e.scalar_tensor_tensor(out=kc3, in0=v3(kcw, 4), scalar=ncs["kw3"][:, 0:1],
                                                            in1=v3(hwk[1][:, 0:128], 4), op0=ALU.mult, op1=ALU.mult),
          reads=["kcw", "kcrs", "ncs"], writes=[("kcT", t)])
        p6v = self.ps[0:32, 6 * 512 + 128:6 * 512 + 128 + 256]
        for g in range(4):
            for hh in range(2):
                col = ((1 * 2 + hh) * 4 + g) * 32
                A("pe", lambda e, g=g, hh=hh, col=col: e.matmul(out=p6v[:, g * 64:(g + 1) * 64], lhsT=hidT[:, col:col + 32],
                                                                rhs=w2v[:, hh * 64:(hh + 1) * 64], start=(hh == 0), stop=False,
                                                                skip_group_check=True),
                  reads=["hidT", "w2v"], writes=[("ps", 6)])
            A("pe", lambda e, g=g: e.matmul(out=p6v[:, g * 64:(g + 1) * 64], lhsT=one1[0:1, :], rhs=b2v[0:1, :], start=False, stop=True,
                                            skip_group_check=True), reads=["one1", "b2v"], writes=[("ps", 6)])
        for g in range(4):
            off = t * VW + VOFF[g] + (64 if g % 2 else 0)
            A("act", lambda e, g=g, off=off: e.copy(out=vcS[0:32, off:off + 64], in_=p6v[:, g * 64:(g + 1) * 64]),
              reads=[("ps", 6)], writes=[("vcS", t)])

    for t in range(NT):
        tileA(t)
        S.stop_at("P6a")
    S.stop_at("P6")
    self.nsa_st = dict(ncs=ncs, ksT=ksT, kwT=kwT, vsS=vsS, vwS=vwS, kcT=kcT, vcS=vcS, gam=gam)
    self.dbg = dict(ksT=ksT, kwT=kwT, vsS=vsS, vwS=vwS, kcT=kcT, vcS=vcS)
    sb.release(mB)
    S.barrier()
    nsa_queries(self, src, dst, W, pos_d, T, NT, SQ)
    sb.release(m)


K.nsa_phase = nsa_phase


def nsa_queries(self, src, dst, W, pos_d, T, NT, SQ):
    S, sb = self.S, self.sb
    A = S.add
    st = self.nsa_st
    ncs, ksT, kwT, vsS, vwS, kcT, vcS, gam = (st[k_] for k_ in ("ncs", "ksT", "kwT", "vsS", "vwS", "kcT", "vcS", "gam"))

    def v3(ap, a):
        return ap.rearrange("p (a b) -> p a b", a=a)

    nbc = sb.bf16(NBC_W)
    self.wload(nbc, W["nbc"], NBC_W, "nbc")
    nb = {n: nbc[:, o:o + w] for n, (o, w) in NBC.items()}
    x32 = sb.f32(8 * T)
    xn = sb.bf16(8 * T)
    sq = [sb.f32(512) for _ in range(2)]
    rstd = sb.f32(512)
    cosb, sinb = sb.f32(T), sb.f32(T)
    hwk = [sb.f32(T) for _ in range(4)]
    wv = sb.bf16(8 * 512)
    self.wload(wv, W["wv"], 4096, "wv")
    qnT = [[sb.bf16(T) for _ in range(4)] for _ in range(2)]
    qrT = [[sb.bf16(T) for _ in range(4)] for _ in range(2)]
    for hf in range(2):
        for i in range(4):
            A("pool", lambda e, hf=hf, i=i: e.memset(qnT[hf][i], 0.0), writes=[("qn", i, hf)])
            A("pool", lambda e, hf=hf, i=i: e.memset(qrT[hf][i], 0.0), writes=[("qr", i, hf)])
    gt = [sb.bf16(T) for _ in range(12)]
    ogacc = [sb.f32(T) for _ in range(4)]
    ogb = sb.bf16(8 * T)
    impT = sb.f32(T)
    selc = sb.f32(T)
    sc = sb.f32(256)
    sc2 = sb.f32(64)
    m8 = sb.f32(16)
    bm = sb.bf16(4 * 128)
    selbT = sb.bf16(T)
    pT = [sb.bf16(T) for _ in range(3)]
    rz = sb.f32(T)
    rzb = sb.f32(T)
    otmp = sb.f32(T)
    imptmp = sb.f32(T)
    rwk = [rzb, otmp, imptmp]
    rwkeys = ["rzb", "otmp", "imptmp"]
    NW = 3
    wbs = [sb.bf16(1024) for _ in range(NW)]
    wob = [sb.bf16(1024) for _ in range(2)]
    A("pool", lambda e: e.memset(bm, 0.0), writes=["bm"])
    A("pool", lambda e: e.memset(rz, 0.0), writes=["rz"])
    wc = [0, 0]
    pcnt = [0]
    SCB = (2, 3)

    hbc = [0]

    def head_branch(kind, i, g, t, first):
        pp, hf = g // 2, g % 2
        q_ap = (qnT if kind == "cmp" else qrT)[hf][i]
        qkey = ("qn" if kind == "cmp" else "qr", i, hf)
        even = (g % 2 == 0)
        M = 65 if even else 128
        voff = VOFF[g]
        ob = 4 + hbc[0] % 2
        hbc[0] += 1
        okey = ("ps", ob)
        oacc = self.ps[0:M, ob * 512:(ob + 1) * 512]
        if kind == "cmp":
            tiles = list(range(t + 1))
            KP = 32
        elif kind == "sel":
            tiles = list(range(4 * t + 4))
            KP = 128
        else:
            tiles = list(range(max(0, 4 * t - 4), 4 * t + 4))
            KP = 128
        nt_ = len(tiles)
        store, snm = {"cmp": (vcS, "vcS"), "sel": (vsS, "vsS"), "win": (vwS, "vwS")}[kind]

        def emit_pv(n_, kt, pbuf, pkey):
            vi = kt if kind != "win" else ((kt // 4) % 2) * 4 + kt % 4
            vl = store[0:KP, vi * VW + voff:vi * VW + voff + M]
            A("pe", lambda e, vl=vl, pbuf=pbuf, n_=n_: e.matmul(
                out=oacc, lhsT=vl, rhs=pbuf[0:KP, :], start=(n_ == 0), stop=(n_ == nt_ - 1)),
                reads=[pkey, (snm, vi)], writes=[okey])
            if kind == "cmp":
                A("pe", lambda e, kt=kt, pbuf=pbuf, n_=n_: e.matmul(
                    out=self.ps[0:64, 6 * 512:7 * 512], lhsT=nb["ovl"][0:32, kt * 64:(kt + 1) * 64], rhs=pbuf[0:32, :],
                    start=(n_ == 0), stop=(n_ == nt_ - 1)), reads=[pkey, "nbc"], writes=[("ps", 6)])

        pend = None
        for n_, kt in enumerate(tiles):
            sbk = SCB[pcnt[0] % 2]
            pbuf = pT[pcnt[0] % 3]
            pkey = ("pT", pcnt[0] % 3)
            pcnt[0] += 1
            ps_s = self.ps[0:KP, sbk * 512:(sbk + 1) * 512]
            mm = []
            if kind == "cmp":
                mm.append((kcT[:, g * 32 * NT + kt * 32:g * 32 * NT + (kt + 1) * 32], q_ap, [("kcT", kt), qkey]))
                if kt == t:
                    mm.append((nb["id128"][:, 0:32], (nb["cmpb0"] if t == 0 else nb["cmpb"]), ["nbc"]))
                elif kt == 0:
                    mm.append((nb["id128"][:, 0:32], nb["cmpr0"], ["nbc"]))
            elif kind == "sel":
                mm.append((ksT[:, pp * SQ + kt * 128:pp * SQ + (kt + 1) * 128], q_ap, [("ksT", pp, kt // 4), qkey]))
                mm.append((nb["efull"][:, kt * 128:(kt + 1) * 128], selbT, ["nbc", "selbT"]))
                if kt >= 4 * t:
                    dd = kt - 4 * t
                    mm.append((nb["id128"], nb["causb"][:, dd * 512:(dd + 1) * 512], ["nbc"]))
            else:
                slot = (kt // 4) % 2
                ko = pp * 1024 + slot * 512 + (kt % 4) * 128
                mm.append((kwT[:, ko:ko + 128], q_ap, [("kwT", pp, slot), qkey]))
                dd = kt - 4 * t
                mask = nb["causb"][:, dd * 512:(dd + 1) * 512] if dd >= 0 else nb["bandb"][:, (dd + 4) * 512:(dd + 5) * 512]
                mm.append((nb["id128"], mask, ["nbc"]))
            for j_, (l_, r_, rd) in enumerate(mm):
                A("pe", lambda e, l_=l_, r_=r_, j_=j_, ps_s=ps_s, last=(j_ == len(mm) - 1): e.matmul(
                    out=ps_s, lhsT=l_, rhs=r_, start=(j_ == 0), stop=last), reads=rd, writes=[("ps", sbk)])
            A("act", lambda e, pbuf=pbuf, ps_s=ps_s: e.activation(out=pbuf[0:KP, :], in_=ps_s, func=AF.Exp),
              reads=[("ps", sbk)], writes=[pkey])
            if pend is not None:
                emit_pv(*pend)
            pend = (n_, kt, pbuf, pkey)
        emit_pv(*pend)
        zr = 64 if even else 0
        A("act", lambda e, zr=zr: e.activation(out=rz[zr:zr + 1, :], in_=self.ps[zr:zr + 1, ob * 512:(ob + 1) * 512], func=AF.Ln, bias=1e-18),
          reads=[okey], writes=["rz"])
        A("pe", lambda e, zr=zr: e.matmul(out=self.bank(7), lhsT=ncs["sel64" if zr == 64 else "sel0"], rhs=rz, start=True, stop=True),
          reads=["rz", "ncs"], writes=[("ps", 7)])
        A("act", lambda e: e.activation(out=rzb, in_=self.bank(7), func=AF.Exp, scale=-1.0), reads=[("ps", 7)], writes=["rzb"])
        r_ = {"cmp": 0, "sel": 1, "win": 2}[kind]
        gtile = gt[r_ * 4 + i]
        orow = slice(0, 64) if even else slice(64, 128)
        A("dve", lambda e, orow=orow: e.tensor_tensor(out=otmp[orow, :], in0=self.ps[orow, ob * 512:(ob + 1) * 512], in1=rzb[orow, :], op=ALU.mult),
          reads=[okey, "rzb"], writes=["otmp"])
        if first:
            A("dve", lambda e, orow=orow, gtile=gtile, i=i: e.tensor_tensor(out=ogacc[i][orow, :], in0=otmp[orow, :], in1=gtile[orow, :], op=ALU.mult),
              reads=["otmp", ("gt", r_ * 4 + i)], writes=[("og", i, g % 2)])
        else:
            A("dve", lambda e, orow=orow, gtile=gtile: e.tensor_tensor(out=otmp[orow, :], in0=otmp[orow, :], in1=gtile[orow, :], op=ALU.mult),
              reads=["otmp", ("gt", r_ * 4 + i)], writes=["otmp"])
            A("pool", lambda e, orow=orow, i=i: e.tensor_tensor(out=ogacc[i][orow, :], in0=ogacc[i][orow, :], in1=otmp[orow, :], op=ALU.add),
              reads=["otmp", ("og", i, g % 2)], writes=[("og", i, g % 2)])
        if kind == "cmp":
            if i == 0:
                A("dve", lambda e: e.tensor_tensor(out=impT[0:64, :], in0=self.ps[0:64, 6 * 512:7 * 512], in1=rzb[0:64, :], op=ALU.mult),
                  reads=[("ps", 6), "rzb"], writes=["impT"])
            else:
                A("dve", lambda e: e.tensor_tensor(out=imptmp[0:64, :], in0=self.ps[0:64, 6 * 512:7 * 512],
                                                   in1=rzb[0:64, :], op=ALU.mult), reads=[("ps", 6), "rzb"], writes=["imptmp"])
                A("pool", lambda e: e.tensor_tensor(out=impT[0:64, :], in0=impT[0:64, :], in1=imptmp[0:64, :], op=ALU.add),
                  reads=["imptmp", "impT"], writes=["impT"])

    def sel_mask(g, t):
        pm = self.ps[:, 6 * 512:6 * 512 + 256]
        for blk in range(4):
            A("pe", lambda e, blk=blk: e.matmul(out=pm[:, blk * 64:(blk + 1) * 64], lhsT=impT[0:64, blk * 128:(blk + 1) * 128],
                                                rhs=ncs["id128"][0:64, 0:64], start=True, stop=True, skip_group_check=True),
              reads=["impT", "ncs"], writes=[("ps", 6)])
        valid = selc[:, 0:256]
        addm = selc[:, 256:512]
        A("dve", lambda e: e.tensor_tensor(out=sc, in0=pm, in1=valid, op=ALU.mult), reads=[("ps", 6), "selc"], writes=["sc"])
        A("dve", lambda e: e.tensor_tensor(out=sc, in0=sc, in1=addm, op=ALU.add), reads=["sc", "selc"], writes=["sc"])
        for blk in range(4):
            sblk = sc[:, blk * 64:(blk + 1) * 64]
            A("dve", lambda e, sblk=sblk: e.max(out=m8[:, 0:8], in_=sblk), reads=["sc"], writes=["m8"])
            A("dve", lambda e, sblk=sblk: e.match_replace(out=sc2, in_to_replace=m8[:, 0:8], in_values=sblk, imm_value=-1e30),
              reads=["sc", "m8"], writes=["sc2"])
            A("dve", lambda e: e.max(out=m8[:, 8:16], in_=sc2), reads=["sc2"], writes=["m8"])
            A("dve", lambda e, sblk=sblk: e.tensor_scalar(out=sc2, in0=sblk, scalar1=m8[:, 15:16], scalar2=None, op0=ALU.is_ge),
              reads=["sc", "m8"], writes=["sc2"])
            A("dve", lambda e, blk=blk: e.tensor_tensor(out=sc2, in0=sc2, in1=valid[:, blk * 64:(blk + 1) * 64], op=ALU.mult),
              reads=["sc2", "selc"], writes=["sc2"])
            A("dve", lambda e, blk=blk: e.tensor_scalar(out=bm[:, blk * 128:blk * 128 + 64], in0=sc2, scalar1=-NEGB, scalar2=NEGB,
                                                        op0=ALU.mult, op1=ALU.add), reads=["sc2"], writes=["bm"])
        pm2 = self.ps[:, 6 * 512:7 * 512]
        for blk in range(4):
            A("pe", lambda e, blk=blk: e.matmul(out=pm2[:, blk * 128:(blk + 1) * 128], lhsT=bm[:, blk * 128:(blk + 1) * 128],
                                                rhs=nb["id128"], start=True, stop=True, skip_group_check=True),
              reads=["bm", "nbc"], writes=[("ps", 6)])
        A("act", lambda e: e.copy(out=selbT, in_=pm2), reads=[("ps", 6)], writes=["selbT"])

    def tileB(t):
        self.load_x_tile(src, x32, "x32", t, T)
        self.rmsnorm_tile(x32, "x32", xn, "xn", gam, "gam", T, sq, rstd, 7, "n")
        self.rope_tables(pos_d, t, T, cosb, sinb, rwk, ncs["inv"], "rope", rwkeys)
        A("sp", lambda e: e.dma_start(out=selc[:, 0:512], in_=W["selc"][t]), writes=["selc"], dkey="selc")
        slot = t % 2
        for pp in range(2):
            wi = wc[0] % NW
            wc[0] += 1
            self.wload(wbs[wi], W["wka"][6 + pp], 1024, ("win", wi))
            pbk = pp % 2
            pt = self.bank(pbk)
            for k in range(8):
                A("pe", lambda e, k=k, pt=pt, wi=wi: e.matmul(out=pt, lhsT=wbs[wi][:, k * 128:(k + 1) * 128],
                                                            rhs=xn[:, k * T:(k + 1) * T], start=(k == 0), stop=(k == 7)),
                  reads=[("win", wi), ("xn", k, 0)], writes=[("ps", pbk)])
            o_ap = kwT[:, pp * 1024 + slot * 512:pp * 1024 + (slot + 1) * 512]
            self.headnorm_rope(pt, ("ps", pbk), ncs["kw3"][:, 2:3], cosb, sinb, "hn",
                               [(o_ap, True, ("kwT", pp, slot))], 1.0, hwk, ncs, T)
        for blk in range(4):
            vi = slot * 4 + blk
            pv = self.bank(4, 256)
            for k in range(8):
                A("pe", lambda e, k=k, blk=blk, pv=pv: e.matmul(
                    out=pv, lhsT=xn[:, k * T + blk * 128:k * T + (blk + 1) * 128], rhs=wv[:, k * 512 + 256:(k + 1) * 512],
                    start=(k == 0), stop=(k == 7)), reads=[("xn", k, 0), "wv"], writes=[("ps", 4)])
            for g in range(4):
                off = vi * VW + VOFF[g] + (64 if g % 2 else 0)
                A("act", lambda e, off=off, g=g, pv=pv: e.copy(out=vwS[:, off:off + 64], in_=pv[:, g * 64:(g + 1) * 64]),
                  reads=[("ps", 4)], writes=[("vwS", vi)])
        for pp in range(2):
            for i in range(4):
                wi = wc[0] % NW
                wc[0] += 1
                self.wload(wbs[wi], W["wq"][pp * 4 + i], 1024, ("win", wi))
                pbk = i % 2
                pt = self.bank(pbk)
                for k in range(8):
                    A("pe", lambda e, k=k, pt=pt, wi=wi: e.matmul(out=pt, lhsT=wbs[wi][:, k * 128:(k + 1) * 128],
                                                                rhs=xn[:, k * T:(k + 1) * T], start=(k == 0), stop=(k == 7)),
                      reads=[("win", wi), ("xn", k, 0)], writes=[("ps", pbk)])
                self.headnorm_rope(pt, ("ps", pbk), ncs["qw"], cosb, sinb, "hn",
                                   [(qnT[0][i], False, ("qn", i, 0), slice(0, 64)), (qnT[1][i], False, ("qn", i, 1), slice(64, 128)),
                                    (qrT[0][i], True, ("qr", i, 0), slice(0, 64)), (qrT[1][i], True, ("qr", i, 1), slice(64, 128))],
                                   0.125, hwk, ncs, T)
            for r_ in range(3):
                for i in range(4):
                    wi = wc[0] % NW
                    wc[0] += 1
                    self.wload(wbs[wi], W["wg"][pp * 12 + r_ * 4 + i], 1024, ("win", wi))
                    pbk = i % 2
                    pt = self.bank(pbk)
                    for k in range(8):
                        A("pe", lambda e, k=k, pt=pt, wi=wi: e.matmul(out=pt, lhsT=wbs[wi][:, k * 128:(k + 1) * 128],
                                                                    rhs=xn[:, k * T:(k + 1) * T], start=(k == 0), stop=(k == 7)),
                          reads=[("win", wi), ("xn", k, 0)], writes=[("ps", pbk)])
                    A("act", lambda e, r_=r_, i=i, pt=pt: e.activation(out=gt[r_ * 4 + i], in_=pt, func=AF.Sigmoid),
                      reads=[("ps", pbk)], writes=[("gt", r_ * 4 + i)])
            S.stop_at("P7")
            for g in (2 * pp, 2 * pp + 1):
                for i in range(4):
                    head_branch("cmp", i, g, t, True)
                    S.stop_at("P8")
                sel_mask(g, t)
                S.stop_at("P9")
                for i in range(4):
                    head_branch("sel", i, g, t, False)
                    S.stop_at("P10")
                for i in range(4):
                    head_branch("win", i, g, t, False)
                    S.stop_at("P11")
            for i in range(4):
                A("act", lambda e, pp=pp, i=i: e.copy(out=ogb[:, (pp * 4 + i) * T:(pp * 4 + i + 1) * T], in_=ogacc[i]),
                  reads=[("og", i, 0), ("og", i, 1)], writes=[("ogb", pp * 4 + i)])
        for o in range(8):
            wi = wc[1] % 2
            wc[1] += 1
            self.wload(wob[wi], W["wo"][o], 1024, ("wout", wi))
            pbk = o % 2
            pt = self.bank(pbk)
            for k in range(8):
                A("pe", lambda e, k=k, pt=pt, wi=wi: e.matmul(out=pt, lhsT=wob[wi][:, k * 128:(k + 1) * 128],
                                                            rhs=ogb[:, k * T:(k + 1) * T], start=(k == 0), stop=(k == 7)),
                  reads=[("wout", wi), ("ogb", k)], writes=[("ps", pbk)])
            xs = x32[:, o * T:(o + 1) * T]
            A("dve", lambda e, xs=xs, pt=pt: e.tensor_tensor(out=xs, in0=pt, in1=xs, op=ALU.add),
              reads=[("ps", pbk), ("x32", o)], writes=[("x32", o)])
            A("act", lambda e, o=o, t=t: e.dma_start(out=dst[o * 128:(o + 1) * 128, t * T:(t + 1) * T], in_=x32[:, o * T:(o + 1) * T]),
              reads=[("x32", o)], dkey=("st", o))

    for t in range(NT):
        tileB(t)
```

```python
import contextlib
import numpy as np
import concourse.bass as bass
import concourse.mybir as mybir
from concourse.bass_utils import run_bass_kernel_spmd

F32 = mybir.dt.float32
BF16 = mybir.dt.bfloat16
I32 = mybir.dt.int32
AF = mybir.ActivationFunctionType
ALU = mybir.AluOpType
AX = mybir.AxisListType

D = 1024
SEQ = 4096
DEPTH = 4
DFF = 2816
NCH = DFF // 128
EPS = 1e-6

ENGS = ("pe", "act", "dve", "pool", "sp")


class Op:
    __slots__ = ("eng", "fn", "deps", "sig", "cnt", "dkey", "dcnt", "reads", "writes", "bar")

    def __init__(self, eng, fn, reads, writes, dkey):
        self.eng = eng
        self.fn = fn
        self.reads = reads
        self.writes = writes
        self.dkey = dkey
        self.deps = []
        self.sig = False
        self.cnt = 0
        self.dcnt = 0
        self.bar = None


class Sched:
    def __init__(self, nc):
        self.nc = nc
        self.ops = {e: [] for e in ENGS}
        self.res = {}
        self.dma_cnt = {}
        self.nops = 0
        self.cut = False

    def add(self, eng, fn, reads=(), writes=(), dkey=None, ndma=1):
        if self.cut:
            return None
        reads = tuple(reads)
        writes = tuple(writes)
        op = Op(eng, fn, reads, writes, dkey)
        deps = {}
        for k in reads:
            r = self.res.get(k)
            if r is not None and r[0] is not None:
                deps[id(r[0])] = r[0]
        for k in writes:
            r = self.res.get(k)
            if r is not None:
                if r[0] is not None:
                    deps[id(r[0])] = r[0]
                for q in r[1]:
                    deps[id(q)] = q
        final = []
        rset = set(reads)
        wset = set(writes)
        for d in deps.values():
            if d is op:
                continue
            if d.dkey is None and dkey is None and d.eng == eng:
                if eng == "pe":
                    continue
                if eng != "pool" and not (rset.intersection(d.writes)) and not (wset.intersection(d.writes)):
                    continue
            final.append(d)
            if d.dkey is None:
                d.sig = True
        op.deps = final
        for k in reads:
            r = self.res.get(k)
            if r is None:
                self.res[k] = [None, [op]]
            else:
                r[1].append(op)
        for k in writes:
            self.res[k] = [op, []]
        if dkey is not None:
            self.dma_cnt[dkey] = self.dma_cnt.get(dkey, 0) + 16 * ndma
            op.dcnt = self.dma_cnt[dkey]
        self.ops[eng].append(op)
        self.nops += 1
        return op

    def stop_at(self, name):
        import os
        if os.environ.get("NSA_STOP") == name:
            self.cut = True

    def barrier(self):
        if self.cut:
            return
        snap_ops = {}
        for e in ENGS:
            last = None
            for o in reversed(self.ops[e]):
                if o.bar is None and o.dkey is None:
                    last = o
                    break
            if last is not None:
                last.sig = True
                snap_ops[e] = last
        dsnap = dict(self.dma_cnt)
        for e in ENGS:
            op = Op(e, None, (), (), None)
            op.bar = (dict(snap_ops), dsnap)
            self.ops[e].append(op)
        self.res = {}

    def emit(self):
        nc = self.nc
        with contextlib.ExitStack() as st:
            esem = {e: st.enter_context(nc.semaphore("s_" + e)) for e in ENGS}
            dsem = {k: st.enter_context(nc.semaphore("d_%d" % i)) for i, k in enumerate(self.dma_cnt)}
            for e in ENGS:
                c = 0
                for o in self.ops[e]:
                    if o.sig:
                        c += 1
                    o.cnt = c
            block = st.enter_context(nc.Block())
            final_d = dict(self.dma_cnt)

            def run(e, eng):
                seen = {}

                def wait(sem, key, val):
                    if val <= 0 or seen.get(key, 0) >= val:
                        return
                    seen[key] = val
                    eng.wait_ge(sem, val)

                for o in self.ops[e]:
                    if o.bar is not None:
                        so, ds = o.bar
                        for e2, lo in so.items():
                            if e2 != e:
                                wait(esem[e2], ("e", e2), lo.cnt)
                        for k, v in ds.items():
                            wait(dsem[k], ("d", k), v)
                        continue
                    need = {}
                    for d in o.deps:
                        if d.dkey is not None:
                            key, v, sem = ("d", d.dkey), d.dcnt, dsem[d.dkey]
                        else:
                            key, v, sem = ("e", d.eng), d.cnt, esem[d.eng]
                        if need.get(key, (None, 0))[1] < v:
                            need[key] = (sem, v)
                    for key, (sem, v) in need.items():
                        wait(sem, key, v)
                    r = o.fn(eng)
                    if o.dkey is not None:
                        if not isinstance(r, (list, tuple)):
                            r = [r]
                        for ins in r:
                            ins.then_inc(dsem[o.dkey], 16)
                    elif o.sig:
                        r.then_inc(esem[e], 1)
                if e == "sp":
                    for k, v in final_d.items():
                        wait(dsem[k], ("d", k), v)
                    for e2 in ENGS:
                        if e2 != e:
                            for o in reversed(self.ops[e2]):
                                if o.sig:
                                    wait(esem[e2], ("e", e2), o.cnt)
                                    break

            @block.tensor
            def _(eng):
                run("pe", eng)

            @block.scalar
            def _(eng):
                run("act", eng)

            @block.vector
            def _(eng):
                run("dve", eng)

            @block.gpsimd
            def _(eng):
                run("pool", eng)

            @block.sync
            def _(eng):
                run("sp", eng)


class SBAlloc:
    def __init__(self, big, ncols):
        self.big = big
        self.ncols = ncols
        self.top = 0
        self.peak = 0

    def mark(self):
        return self.top

    def release(self, m):
        self.top = m

    def f32(self, n):
        o = self.top
        self.top += n
        assert self.top <= self.ncols, "SBUF overflow %d > %d" % (self.top, self.ncols)
        self.peak = max(self.peak, self.top)
        return self.big[:, o:o + n]

    def bf16(self, n):
        w = (n + 1) // 2
        return self.f32(w).bitcast(BF16)[:, 0:n]


SB_COLS = 53000


class K:
    def __init__(self, ntok=SEQ, T=1024):
        self.ntok = ntok
        self.T = T
        self.nc = bass.Bass("TRN2", target_bir_lowering=False)
        self.st = contextlib.ExitStack()
        self.dram = {}

    def din(self, name, shape, dt=F32):
        ap = self.nc.dram_tensor(name, list(shape), dt, kind="ExternalInput").ap()
        self.dram[name] = ap
        return ap

    def dout(self, name, shape, dt=F32):
        ap = self.nc.dram_tensor(name, list(shape), dt, kind="ExternalOutput").ap()
        self.dram[name] = ap
        return ap

    def begin(self):
        nc = self.nc
        big = self.st.enter_context(nc.sbuf_tensor("SB", [128, SB_COLS], F32))
        self.ps = self.st.enter_context(nc.psum_tensor("PS", [128, 8 * 512], F32))
        self.sb = SBAlloc(big, SB_COLS)
        self.S = Sched(nc)
        S = self.S
        sb = self.sb
        self.ones32 = sb.f32(128)
        self.epsc = sb.f32(1)
        S.add("pool", lambda e: e.memset(self.ones32, 1.0), writes=["ones32"])
        S.add("pool", lambda e: e.memset(self.epsc, EPS), writes=["epsc"])

    def bank(self, b, n=512):
        return self.ps[:, b * 512:b * 512 + n]

    def finish(self):
        self.S.emit()
        self.st.close()
        return self.nc

    def rmsnorm_tile(self, x32, xkey, xn, xnkey, gamma, gkey, T, sq, rstd, psb, tag):
        S = self.S
        for s in range(T // 512):
            pt = self.bank(psb)
            for k in range(8):
                xs = x32[:, k * T + s * 512:k * T + (s + 1) * 512]
                q = sq[k % 2]
                S.add("act", lambda e, q=q, xs=xs: e.activation(out=q, in_=xs, func=AF.Square),
                      reads=[(xkey, k)], writes=[(tag + "sq", k % 2)])
                S.add("pe", lambda e, q=q, k=k, pt=pt: e.matmul(out=pt, lhsT=self.ones32, rhs=q,
                                                                start=(k == 0), stop=(k == 7)),
                      reads=[(tag + "sq", k % 2), "ones32"], writes=[("ps", psb)])
            S.add("act", lambda e, pt=pt: e.activation(out=rstd, in_=pt, func=AF.Sqrt, bias=self.epsc, scale=1.0 / D),
                  reads=[("ps", psb), "epsc"], writes=[tag + "rstd"])
            S.add("dve", lambda e: e.reciprocal(out=rstd, in_=rstd), reads=[tag + "rstd"], writes=[tag + "rstd"])
            for k in range(8):
                xs = x32[:, k * T + s * 512:k * T + (s + 1) * 512]
                xo = xn[:, k * T + s * 512:k * T + (s + 1) * 512]
                S.add("dve", lambda e, xs=xs, xo=xo, k=k: e.scalar_tensor_tensor(
                    out=xo, in0=xs, scalar=gamma[:, k:k + 1], in1=rstd, op0=ALU.mult, op1=ALU.mult),
                    reads=[(xkey, k), tag + "rstd", gkey], writes=[(xnkey, k, s)])

    def ffn_phase(self, src, dst, wgu, wd, gamma_col):
        S, sb = self.S, self.sb
        S.barrier()
        m = sb.mark()
        T = self.T
        NT = self.ntok // T
        NS = T // 512
        x32 = [sb.f32(8 * T) for _ in range(2)]
        xn = [sb.bf16(8 * T) for _ in range(2)]
        h = sb.bf16(NCH * T)
        NWG = 3
        wgb = [sb.bf16(2 * 8 * 128) for _ in range(NWG)]
        wdb = [sb.bf16(NCH * 128) for _ in range(2)]
        sq = [sb.f32(512) for _ in range(2)]
        rstd = sb.f32(512)
        sg = [sb.f32(512) for _ in range(2)]
        gam = sb.f32(8)
        S.add("sp", lambda e: e.dma_start(out=gam, in_=gamma_col), writes=["gam"], dkey="gam")

        def load_x(t):
            b = t % 2
            for k in range(8):
                S.add("sp", lambda e, k=k, b=b, t=t: e.dma_start(
                    out=x32[b][:, k * T:(k + 1) * T], in_=src[k * 128:(k + 1) * 128, t * T:(t + 1) * T]),
                    writes=[("x32_%d" % b, k)], dkey=("x32", b, k))

        def norm(t):
            b = t % 2
            self.rmsnorm_tile(x32[b], "x32_%d" % b, xn[b], "xn_%d" % b, gam, "gam", T, sq, rstd, 6, "f")

        wcount = [0, 0]

        def gateup(t):
            b = t % 2
            for c in range(NCH):
                wi = wcount[0] % NWG
                wcount[0] += 1
                wb = wgb[wi]
                S.add("pool", lambda e, c=c, wb=wb: [
                    e.dma_start(out=wb[:, j * 1024:(j + 1) * 1024], in_=wgu[c, :, j * 1024:(j + 1) * 1024])
                    for j in range(2)], writes=[("wg", wi)], dkey=("wg", wi), ndma=2)
                if c == 3 and t + 1 < NT and self.pipe and stage != 4:
                    load_x(t + 1)
                for s in range(NS):
                    gb = 0 + (s % 2) * 2
                    ub = 1 + (s % 2) * 2
                    pg, pu = self.bank(gb), self.bank(ub)
                    for j, pt, pb in ((0, pg, gb), (1, pu, ub)):
                        for k in range(8):
                            S.add("pe", lambda e, j=j, k=k, pt=pt, wb=wb, s=s: e.matmul(
                                out=pt, lhsT=wb[:, (j * 8 + k) * 128:(j * 8 + k + 1) * 128],
                                rhs=xn[b][:, k * T + s * 512:k * T + (s + 1) * 512],
                                start=(k == 0), stop=(k == 7)),
                                reads=[("wg", wi), ("xn_%d" % b, k, s)], writes=[("ps", pb)])
                    sgt = sg[s % 2]
                    S.add("act", lambda e, sgt=sgt, pg=pg: e.activation(out=sgt, in_=pg, func=AF.Silu),
                          reads=[("ps", gb)], writes=[("sg", s % 2)])
                    ho = h[:, c * T + s * 512:c * T + (s + 1) * 512]
                    S.add("dve", lambda e, ho=ho, sgt=sgt, pu=pu: e.tensor_tensor(out=ho, in0=pu, in1=sgt, op=ALU.mult),
                          reads=[("ps", ub), ("sg", s % 2)], writes=[("h", c, s)])

        def down(t):
            b = t % 2
            for o in range(8):
                wi = wcount[1] % 2
                wcount[1] += 1
                wb = wdb[wi]
                S.add("pool", lambda e, o=o, wb=wb: [
                    e.dma_start(out=wb[:, j * 1408:(j + 1) * 1408], in_=wd[o, :, j * 1408:(j + 1) * 1408])
                    for j in range(2)], writes=[("wd", wi)], dkey=("wd", wi), ndma=2)
                for s in range(NS):
                    pb = 4 + (o * NS + s) % 2
                    pt = self.bank(pb)
                    for c in range(NCH):
                        S.add("pe", lambda e, c=c, pt=pt, wb=wb, s=s: e.matmul(
                            out=pt, lhsT=wb[:, c * 128:(c + 1) * 128],
                            rhs=h[:, c * T + s * 512:c * T + (s + 1) * 512],
                            start=(c == 0), stop=(c == NCH - 1)),
                            reads=[("wd", wi), ("h", c, s)], writes=[("ps", pb)])
                    xs = x32[b][:, o * T + s * 512:o * T + (s + 1) * 512]
                    S.add("dve", lambda e, xs=xs, pt=pt: e.scalar_tensor_tensor(
                        out=xs, in0=pt, scalar=0.5, in1=xs, op0=ALU.mult, op1=ALU.add),
                        reads=[("ps", pb), ("x32_%d" % b, o)], writes=[("x32_%d" % b, o)])
                S.add("act", lambda e, o=o, b=b, t=t: e.dma_start(
                    out=dst[o * 128:(o + 1) * 128, t * T:(t + 1) * T], in_=x32[b][:, o * T:(o + 1) * T]),
                    reads=[("x32_%d" % b, o)], dkey=("st", b, o))

        import os
        stage = int(os.environ.get("FFN_STAGE", "9"))
        load_x(0)
        norm(0)
        if stage == 0:
            for k in range(8):
                S.add("dve", lambda e, k=k: e.tensor_copy(out=x32[0][:, k * T:(k + 1) * T], in_=xn[0][:, k * T:(k + 1) * T]),
                      reads=[("xn_0", k, 0), ("xn_0", k, 1)], writes=[("x32_0", k)])
                S.add("act", lambda e, k=k: e.dma_start(out=dst[k * 128:(k + 1) * 128, 0:T], in_=x32[0][:, k * T:(k + 1) * T]),
                      reads=[("x32_0", k)], dkey=("st", 0, k))
            sb.release(m)
            return
        self.pipe = stage != 2
        for t in range(NT):
            if not self.pipe and t > 0:
                load_x(t)
                norm(t)
            gateup(t)
            if stage == 1:
                for k in range(8):
                    S.add("dve", lambda e, k=k: e.tensor_copy(out=x32[0][:, k * T:(k + 1) * T], in_=h[:, k * T:(k + 1) * T]),
                          reads=[("h", k, 0), ("h", k, 1)], writes=[("x32_0", k)])
                    S.add("act", lambda e, k=k: e.dma_start(out=dst[k * 128:(k + 1) * 128, 0:T], in_=x32[0][:, k * T:(k + 1) * T]),
                          reads=[("x32_0", k)], dkey=("st", 0, k))
                sb.release(m)
                return
            if stage == 4 and t + 1 < NT:
                load_x(t + 1)
            if t + 1 < NT and self.pipe and stage != 3:
                norm(t + 1)
            down(t)
            if t + 1 < NT and stage == 3:
                norm(t + 1)
        sb.release(m)


    def wload(self, wb, src2d, n, key):
        S = self.S
        pieces = [(a, min(a + 2048, n)) for a in range(0, n, 2048)]
        S.add("pool", lambda e: [e.dma_start(out=wb[:, a:b], in_=src2d[:, a:b]) for a, b in pieces],
              writes=[key], dkey=key, ndma=len(pieces))

    def load_x_tile(self, src, x32, xkey, t, T, eng="sp"):
        for k in range(8):
            self.S.add(eng, lambda e, k=k: e.dma_start(
                out=x32[:, k * T:(k + 1) * T], in_=src[k * 128:(k + 1) * 128, t * T:(t + 1) * T]),
                writes=[(xkey, k)], dkey=(xkey, k))

    def sc_phase(self, src, dst, w_in, w_out, cw_d, gamma_col):
        S, sb = self.S, self.sb
        S.barrier()
        m = sb.mark()
        T = self.T
        NT = self.ntok // T
        NS = T // 512
        x32 = sb.f32(8 * T)
        xn = sb.bf16(8 * T)
        zb = sb.f32(8 * (T + 2))
        v = sb.bf16(8 * T)
        NW = 6
        wbs = [sb.bf16(1024) for _ in range(NW)]
        wob = [sb.bf16(1024) for _ in range(2)]
        sq = [sb.f32(512) for _ in range(2)]
        rstd = sb.f32(512)
        csb = [sb.f32(512) for _ in range(2)]
        ysb = [sb.f32(512) for _ in range(2)]
        gam = sb.f32(8)
        cw = sb.f32(24)
        S.add("sp", lambda e: e.dma_start(out=gam, in_=gamma_col), writes=["gam"], dkey="gam")
        S.add("sp", lambda e: e.dma_start(out=cw, in_=cw_d), writes=["cw"], dkey="cw")
        for j in range(8):
            S.add("pool", lambda e, j=j: e.memset(zb[:, j * (T + 2):j * (T + 2) + 2], 0.0), writes=[("zh", j)])
        wc = [0, 0]
        for t in range(NT):
            self.load_x_tile(src, x32, "x32", t, T)
            self.rmsnorm_tile(x32, "x32", xn, "xn", gam, "gam", T, sq, rstd, 6, "s")
            for j in range(8):
                wl = []
                for q in range(3):
                    oc = (1, 2, 0)[q] * 8 + j
                    wi = wc[0] % NW
                    wc[0] += 1
                    self.wload(wbs[wi], w_in[oc], 1024, ("win", wi))
                    wl.append(wi)
                z0 = j * (T + 2)
                for s_ in range(NS):
                    banks = (0 + 3 * (s_ % 2), 1 + 3 * (s_ % 2), 2 + 3 * (s_ % 2))
                    for q in range(3):
                        pt = self.bank(banks[q])
                        wb = wbs[wl[q]]
                        for k in range(8):
                            S.add("pe", lambda e, k=k, pt=pt, wb=wb, s_=s_: e.matmul(
                                out=pt, lhsT=wb[:, k * 128:(k + 1) * 128],
                                rhs=xn[:, k * T + s_ * 512:k * T + (s_ + 1) * 512],
                                start=(k == 0), stop=(k == 7)),
                                reads=[("win", wl[q]), ("xn", k, s_)], writes=[("ps", banks[q])])
                    pc, px, pbg = self.bank(banks[0]), self.bank(banks[1]), self.bank(banks[2])
                    cs, ys = csb[s_ % 2], ysb[s_ % 2]
                    zc = zb[:, z0 + 2 + s_ * 512:z0 + 2 + (s_ + 1) * 512]
                    zm1 = zb[:, z0 + 1 + s_ * 512:z0 + 1 + (s_ + 1) * 512]
                    zm2 = zb[:, z0 + s_ * 512:z0 + (s_ + 1) * 512]
                    S.add("act", lambda e, cs=cs, pc=pc: e.copy(out=cs, in_=pc),
                          reads=[("ps", banks[0])], writes=[("cs", s_ % 2)])
                    S.add("dve", lambda e, zc=zc, px=px, cs=cs: e.tensor_tensor(out=zc, in0=px, in1=cs, op=ALU.mult),
                          reads=[("ps", banks[1]), ("cs", s_ % 2)], writes=[("z", j, s_)])
                    S.add("act", lambda e, ys=ys, zc=zc, j=j: e.activation(out=ys, in_=zc, func=AF.Identity,
                                                                        scale=cw[:, j * 3 + 2:j * 3 + 3]),
                          reads=[("z", j, s_), "cw"], writes=[("ys", s_ % 2)])
                    hk = [("z", j, s_ - 1)] if s_ > 0 else [("zh", j)]
                    S.add("dve", lambda e, ys=ys, zm1=zm1, j=j: e.scalar_tensor_tensor(
                        out=ys, in0=zm1, scalar=cw[:, j * 3 + 1:j * 3 + 2], in1=ys, op0=ALU.mult, op1=ALU.add),
                        reads=[("z", j, s_), ("ys", s_ % 2), "cw"] + hk, writes=[("ys", s_ % 2)])
                    S.add("dve", lambda e, ys=ys, zm2=zm2, j=j: e.scalar_tensor_tensor(
                        out=ys, in0=zm2, scalar=cw[:, j * 3:j * 3 + 1], in1=ys, op0=ALU.mult, op1=ALU.add),
                        reads=[("z", j, s_), ("ys", s_ % 2), "cw"] + hk, writes=[("ys", s_ % 2)])
                    vo = v[:, j * T + s_ * 512:j * T + (s_ + 1) * 512]
                    S.add("dve", lambda e, vo=vo, pbg=pbg, ys=ys: e.tensor_tensor(out=vo, in0=pbg, in1=ys, op=ALU.mult),
                          reads=[("ps", banks[2]), ("ys", s_ % 2)], writes=[("v", j, s_)])
                S.add("pool", lambda e, z0=z0: e.tensor_copy(out=zb[:, z0:z0 + 2], in_=zb[:, z0 + T:z0 + T + 2]),
                      reads=[("z", j, s2) for s2 in range(NS)], writes=[("zh", j)])
            for o in range(8):
                wi = wc[1] % 2
                wc[1] += 1
                self.wload(wob[wi], w_out[o], 1024, ("wout", wi))
                for s_ in range(NS):
                    pb = 6 + (o * NS + s_) % 2
                    pt = self.bank(pb)
                    for k in range(8):
                        S.add("pe", lambda e, k=k, pt=pt, wi=wi, s_=s_: e.matmul(
                            out=pt, lhsT=wob[wi][:, k * 128:(k + 1) * 128],
                            rhs=v[:, k * T + s_ * 512:k * T + (s_ + 1) * 512],
                            start=(k == 0), stop=(k == 7)),
                            reads=[("wout", wi), ("v", k, s_)], writes=[("ps", pb)])
                    xs = x32[:, o * T + s_ * 512:o * T + (s_ + 1) * 512]
                    S.add("dve", lambda e, xs=xs, pt=pt: e.tensor_tensor(out=xs, in0=pt, in1=xs, op=ALU.add),
                          reads=[("ps", pb), ("x32", o)], writes=[("x32", o)])
                S.add("act", lambda e, o=o, t=t: e.dma_start(
                    out=dst[o * 128:(o + 1) * 128, t * T:(t + 1) * T], in_=x32[:, o * T:(o + 1) * T]),
                    reads=[("x32", o)], dkey=("st", o))
        sb.release(m)


def prep_proj(w, kch=8):
    Kd, N = w.shape
    assert Kd == kch * 128 and N % 128 == 0
    a = w.reshape(kch, 128, N // 128, 128)
    return np.ascontiguousarray(a.transpose(2, 1, 0, 3)).reshape(N // 128, 128, kch * 128)


def prep_ffn_weights(w_gate_up, w_down):
    w = w_gate_up.reshape(8, 128, 2, NCH, 128)
    wgu = np.ascontiguousarray(w.transpose(3, 1, 2, 0, 4)).reshape(NCH, 128, 2 * 8 * 128)
    w2 = w_down.reshape(NCH, 128, 8, 128)
    wd = np.ascontiguousarray(w2.transpose(2, 1, 0, 3)).reshape(8, 128, NCH * 128)
    return wgu, wd


def norm_cols(w):
    return np.ascontiguousarray(np.asarray(w, np.float32).reshape(8, 128).T)


N_NORM = DEPTH * 3
def nsa_shapes():
    return dict(wka=(8, 128, 1024), wv=(128, 4096), w1=(2, 128, 8192), peT=(2, 128, 32), w2k=(128, 256), w2v=(128, 128),
                b2v=(1, 64), ncs=(128, NCS_W), nbc=(128, NBC_W), selc=(8, 128, 512), wq=(8, 128, 1024),
                wg=(24, 128, 1024), wo=(8, 128, 1024))


def build_program(ntok=SEQ, layers=DEPTH, skip=()):
    k = K(ntok=ntok)
    xT = k.din("xT", [D, ntok])
    yT = k.dout("yT", [D, ntok])
    gam = k.din("gam", [128, 8 * N_NORM])
    wgu = [[k.din("wgu_%d_%d" % (l, f), [NCH, 128, 2048]) for f in range(2)] for l in range(layers)]
    wd = [[k.din("wd_%d_%d" % (l, f), [8, 128, NCH * 128]) for f in range(2)] for l in range(layers)]
    sc_in = k.din("sc_w_in", [24, 128, 1024])
    sc_out = k.din("sc_w_out", [8, 128, 1024])
    sc_cw = k.din("sc_cw", [128, 24])
    cst = k.din("cst", [128, CST_W])
    gd = []
    for j in range(2):
        p = "gdn%d_" % j
        gd.append(dict(w_in=k.din(p + "w_in", [32, 128, 1024]), wab=k.din(p + "wab", [128, 128]), cw=k.din(p + "cw", [128, 96]),
                       alog=k.din(p + "alog", [128, 8]), dtb=k.din(p + "dtb", [128, 8]), onw=k.din(p + "onw", [128, 1]),
                       w_out=k.din(p + "w_out", [8, 128, 1024])))
    pos = k.din("pos", [1, ntok], I32)
    nsaW = {n: k.din("nsa_" + n, list(shp)) for n, shp in nsa_shapes().items()}
    k.begin()

    def g(i):
        return gam[:, i * 8:(i + 1) * 8]

    for l in range(layers):
        k.ffn_phase(xT if l == 0 else yT, yT, wgu[l][0], wd[l][0], g(l * 3 + 0))
        kind = l % 3
        if kind == 1 and "sc" not in skip:
            k.sc_phase(yT, yT, sc_in, sc_out, sc_cw, g(l * 3 + 1))
        if kind == 2 and "nsa" not in skip:
            k.nsa_phase(yT, yT, nsaW, pos, g(l * 3 + 1))
        if kind == 0 and "gdn" not in skip:
            q = gd[l // 3]
            k.gdn_phase(yT, yT, q["w_in"], q["wab"], q["cw"], q["alog"], q["dtb"], q["onw"], q["w_out"], cst, g(l * 3 + 1))
        k.ffn_phase(yT, yT, wgu[l][1], wd[l][1], g(l * 3 + 2))
    nc = k.finish()
    return k, nc


def prep_inputs(inp, layers=DEPTH):
    f = lambda a: np.asarray(a, dtype=np.float32)
    com = {}
    gcols = []
    for l in range(DEPTH):
        gcols += [norm_cols(f(inp["ffn_norm"])[l, 0]), norm_cols(f(inp["mixer_norm"])[l]), norm_cols(f(inp["ffn_norm"])[l, 1])]
    com["gam"] = np.ascontiguousarray(np.concatenate(gcols, axis=1))
    for l in range(layers):
        for ff in range(2):
            a, b = prep_ffn_weights(f(inp["ffn_w_gate_up"])[l, ff], f(inp["ffn_w_down"])[l, ff])
            com["wgu_%d_%d" % (l, ff)] = a
            com["wd_%d_%d" % (l, ff)] = b
    com["sc_w_in"] = prep_proj(f(inp["sc_w_in"])[0])
    com["sc_w_out"] = prep_proj(f(inp["sc_w_out"])[0])
    com["cst"] = make_consts()
    for j in range(2):
        d = prep_gdn(f(inp["gdn_w_in"])[j], f(inp["gdn_conv_w"])[j], f(inp["gdn_A_log"])[j], f(inp["gdn_dt_bias"])[j],
                     f(inp["gdn_out_norm"])[j], f(inp["gdn_w_out"])[j])
        for kk_, vv_ in d.items():
            com["gdn%d_%s" % (j, kk_)] = vv_
    for n_, a_ in prep_nsa(inp).items():
        assert tuple(a_.shape) == tuple(nsa_shapes()[n_]), (n_, a_.shape)
        com["nsa_" + n_] = a_
    cwt = f(inp["sc_conv_w"])[0]
    com["sc_cw"] = np.ascontiguousarray(cwt.reshape(3, 8, 128).transpose(2, 1, 0)).reshape(128, 24)
    return com


def kernel(**inputs):
    x = np.asarray(inputs["x"], dtype=np.float32)
    B = x.shape[0]
    com = prep_inputs(inputs)
    k, nc = build_program()
    in_maps = []
    for b in range(B):
        m = dict(com)
        m["xT"] = np.ascontiguousarray(x[b].T)
        m["pos"] = np.ascontiguousarray(np.asarray(inputs["positions"])[b].astype(np.int32).reshape(1, -1))
        in_maps.append(m)
    res = run_bass_kernel_spmd(nc, in_maps, core_ids=list(range(B)))
    out = np.stack([np.ascontiguousarray(res.results[b]["yT"].T) for b in range(B)], axis=0)
    return out.astype(np.float32)


GC = 64
CST_OFF = {}
_o = 0
for _n, _w in (("id128", 128), ("LT", 64), ("mincl", 512), ("mstrict", 512), ("sel63", 128), ("ones64", 64), ("idrep", 512)):
    CST_OFF[_n] = (_o, _w)
    _o += _w
CST_W = _o


def make_consts():
    c = np.zeros((128, CST_W), np.float32)

    def put(name, arr):
        o, w = CST_OFF[name]
        c[:arr.shape[0], o:o + w] = arr

    put("id128", np.eye(128, dtype=np.float32))
    i = np.arange(64)
    put("LT", (i[:, None] <= i[None, :]).astype(np.float32))
    mincl = (i[None, :] <= i[:, None]).astype(np.float32)
    mstr = (i[None, :] < i[:, None]).astype(np.float32)
    put("mincl", np.tile(mincl, (1, 8)))
    put("mstrict", np.tile(mstr, (1, 8)))
    s = np.zeros((64, 128), np.float32)
    s[63, :] = 1.0
    put("sel63", s)
    put("ones64", np.ones((64, 64), np.float32))
    put("idrep", np.tile(np.eye(64, dtype=np.float32), (1, 8)))
    return c


def gdn_phase(self, src, dst, w_in, wab_d, cw_d, alog_d, dtb_d, onw_d, w_out, cst_d, gamma_col):
    S, sb = self.S, self.sb
    S.barrier()
    m = sb.mark()
    T = 512
    NT = self.ntok // T
    C = GC
    NCK = T // C
    H = 8
    A = S.add
    cst = sb.f32(CST_W)
    A("sp", lambda e: e.dma_start(out=cst, in_=cst_d), writes=["cst"], dkey="cst")

    def cs_(name, rows=64):
        o, w = CST_OFF[name]
        return cst[0:rows, o:o + w]

    id128 = cs_("id128", 128)
    id64 = cst[0:64, CST_OFF["id128"][0]:CST_OFF["id128"][0] + 64]
    idb64 = None
    LT, mincl, mstrict, sel63, ones64, idrep = (cs_("LT"), cs_("mincl"), cs_("mstrict"), cs_("sel63"),
                                                 cs_("ones64"), cs_("idrep"))
    x32 = sb.f32(8 * T)
    xn = sb.bf16(8 * T)
    qkv = sb.bf16(24 * T)
    gs = sb.f32(8 * T)
    og = sb.bf16(8 * T)
    St = sb.f32(H * 128)
    halo = sb.f32(24 * 3)
    pre = [sb.f32(T + 3) for _ in range(2)]
    yb = [sb.f32(T) for _ in range(2)]
    sq = [sb.f32(512) for _ in range(2)]
    rstd = sb.f32(512)
    gam = sb.f32(8)
    cw = sb.f32(96)
    wab = sb.bf16(128)
    alog = sb.f32(8)
    dtb = sb.f32(8)
    nega = sb.f32(8)
    onw = sb.f32(1)
    NW = 4
    wbs = [sb.bf16(1024) for _ in range(NW)]
    wob = [sb.bf16(1024) for _ in range(2)]
    g_t = sb.f32(NCK * 8)
    be_t = sb.f32(NCK * 8)
    tmp_ab = sb.f32(NCK * 8)
    gcs = [sb.f32(8) for _ in range(2)]
    egl = [sb.f32(8) for _ in range(2)]
    ekd = [sb.f32(8) for _ in range(2)]
    egc = [sb.f32(8) for _ in range(2)]
    bege = [sb.f32(8) for _ in range(2)]
    rrhs = sb.f32(512)
    Em = [sb.f32(512) for _ in range(2)]
    ETm = [sb.f32(512) for _ in range(2)]
    Pm = [sb.bf16(512) for _ in range(2)]
    PTm = [sb.bf16(512) for _ in range(2)]
    Ptmp = sb.f32(512)
    attT = sb.bf16(512)
    Bm = sb.bf16(H * 256)
    kdec = sb.bf16(H * 128)
    wT = sb.bf16(512)
    XTm = sb.bf16(512)
    vnew = sb.bf16(H * 128)
    omb = sb.bf16(H * 128)
    Sb = sb.bf16(H * 128)
    Eb = [sb.bf16(512) for _ in range(2)]
    idb = sb.bf16(128)
    om = sb.f32(H * 128)
    osq = sb.f32(H * 128)
    ss8 = sb.f32(8)

    A("sp", lambda e: e.dma_start(out=gam, in_=gamma_col), writes=["gam"], dkey="gam")
    A("sp", lambda e: e.dma_start(out=cw, in_=cw_d), writes=["cw"], dkey="cw")
    A("sp", lambda e: e.dma_start(out=alog, in_=alog_d), writes=["alog"], dkey="alog")
    A("sp", lambda e: e.dma_start(out=dtb, in_=dtb_d), writes=["dtb"], dkey="dtb")
    A("sp", lambda e: e.dma_start(out=onw, in_=onw_d), writes=["onw"], dkey="onw")
    A("pool", lambda e: e.dma_start(out=wab, in_=wab_d), writes=["wab"], dkey="wab")
    A("pool", lambda e: e.memset(halo, 0.0), writes=["halo"])
    A("pool", lambda e: e.memset(St, 0.0), writes=[("S", 0), ("S", 1)])
    A("pool", lambda e: e.memset(Sb, 0.0), writes=[("Sb", 0), ("Sb", 1)])
    A("pool", lambda e: e.tensor_copy(out=idb, in_=id128), reads=["cst"], writes=["idb"])
    A("pool", lambda e: e.memset(XTm, 0.0), writes=[("XT", 0), ("XT", 1)])
    A("pool", lambda e: e.memset(Bm, 0.0), writes=[("B", 0), ("B", 1)])
    A("act", lambda e: e.activation(out=nega[0:64, :], in_=alog[0:64, :], func=AF.Exp), reads=["alog"], writes=["nega"])
    A("dve", lambda e: e.tensor_scalar(out=nega[0:64, :], in0=nega[0:64, :], scalar1=-1.0, scalar2=None, op0=ALU.mult),
      reads=["nega"], writes=["nega"])

    def bc(ap2, n):
        return ap2.unsqueeze(2).broadcast_to([ap2.shape[0], ap2.shape[1], n])

    def v3(ap, a):
        return ap.rearrange("p (a b) -> p a b", a=a)

    wc = [0, 0]
    for t in range(NT):
        self.load_x_tile(src, x32, "x32", t, T)
        self.rmsnorm_tile(x32, "x32", xn, "xn", gam, "gam", T, sq, rstd, 7, "g")
        xnk = [("xn", k, 0) for k in range(8)]
        pab = self.ps[0:64, 6 * 512:6 * 512 + NCK * 16]
        for c in range(NCK):
            for k in range(8):
                A("pe", lambda e, c=c, k=k: e.matmul(
                    out=pab[:, c * 16:(c + 1) * 16], lhsT=xn[:, k * T + c * C:k * T + (c + 1) * C],
                    rhs=wab[:, k * 16:(k + 1) * 16], start=(k == 0), stop=(k == 7), skip_group_check=True),
                    reads=[("xn", k, 0), "wab"], writes=[("ps", 6)])
        pab3 = pab.rearrange("p (c n) -> p c n", n=16)
        g3, be3, tm3 = v3(g_t[0:64, :], NCK), v3(be_t[0:64, :], NCK), v3(tmp_ab[0:64, :], NCK)
        dtb3 = dtb[0:64, :].unsqueeze(1).broadcast_to([64, NCK, 8])
        nega3 = nega[0:64, :].unsqueeze(1).broadcast_to([64, NCK, 8])
        A("dve", lambda e: e.tensor_tensor(out=tm3, in0=pab3[:, :, 0:8], in1=dtb3, op=ALU.add),
          reads=[("ps", 6), "dtb"], writes=["tmp_ab"])
        A("act", lambda e: e.activation(out=be3, in_=pab3[:, :, 8:16], func=AF.Sigmoid), reads=[("ps", 6)], writes=["be_t"])
        A("act", lambda e: e.activation(out=tm3, in_=tm3, func=AF.Exp), reads=["tmp_ab"], writes=["tmp_ab"])
        A("act", lambda e: e.activation(out=tm3, in_=tm3, func=AF.Ln, bias=1.0), reads=["tmp_ab"], writes=["tmp_ab"])
        A("dve", lambda e: e.tensor_tensor(out=g3, in0=tm3, in1=nega3, op=ALU.mult), reads=["tmp_ab", "nega"], writes=["g_t"])
        for oc in range(32):
            wi = wc[0] % NW
            wc[0] += 1
            self.wload(wbs[wi], w_in[oc], 1024, ("win", wi))
            pb = oc % 2
            pt = self.bank(pb)
            for k in range(8):
                A("pe", lambda e, k=k, pt=pt, wi=wi: e.matmul(out=pt, lhsT=wbs[wi][:, k * 128:(k + 1) * 128],
                                                            rhs=xn[:, k * T:(k + 1) * T], start=(k == 0), stop=(k == 7)),
                  reads=[("win", wi), ("xn", k, 0)], writes=[("ps", pb)])
            if oc >= 24:
                h = oc - 24
                A("act", lambda e, h=h, pt=pt: e.activation(out=gs[:, h * T:(h + 1) * T], in_=pt, func=AF.Silu),
                  reads=[("ps", pb)], writes=[("gs", h)])
                continue
            pr, y = pre[oc % 2], yb[oc % 2]
            pk, yk = ("pre", oc % 2), ("yb", oc % 2)
            A("pool", lambda e, pr=pr, oc=oc: e.tensor_copy(out=pr[:, 0:3], in_=halo[:, oc * 3:oc * 3 + 3]),
              reads=["halo"], writes=[pk])
            A("act", lambda e, pr=pr, pt=pt: e.copy(out=pr[:, 3:3 + T], in_=pt), reads=[("ps", pb)], writes=[pk])
            A("act", lambda e, y=y, pr=pr, oc=oc: e.activation(out=y, in_=pr[:, 3:3 + T], func=AF.Identity,
                                                               scale=cw[:, oc * 4 + 3:oc * 4 + 4]),
              reads=[pk, "cw"], writes=[yk])
            for tap in (2, 1, 0):
                A("dve", lambda e, y=y, pr=pr, oc=oc, tap=tap: e.scalar_tensor_tensor(
                    out=y, in0=pr[:, tap:tap + T], scalar=cw[:, oc * 4 + tap:oc * 4 + tap + 1], in1=y,
                    op0=ALU.mult, op1=ALU.add), reads=[pk, yk, "cw"], writes=[yk])
            A("pool", lambda e, pr=pr, oc=oc: e.tensor_copy(out=halo[:, oc * 3:oc * 3 + 3], in_=pr[:, T:T + 3]),
              reads=[pk], writes=["halo"])
            qo = qkv[:, oc * T:(oc + 1) * T]
            if oc >= 16:
                A("act", lambda e, qo=qo, y=y: e.activation(out=qo, in_=y, func=AF.Silu), reads=[yk], writes=[("qkv", oc)])
            if oc < 16:
                qf = pr[:, 3:3 + T]
                A("act", lambda e, qf=qf, y=y: e.activation(out=qf, in_=y, func=AF.Silu), reads=[yk], writes=[pk])
                q2 = sq[oc % 2]
                A("act", lambda e, q2=q2, qf=qf: e.activation(out=q2, in_=qf, func=AF.Square),
                  reads=[pk], writes=[("gsq", oc % 2)])
                A("pe", lambda e, q2=q2: e.matmul(out=self.bank(7), lhsT=self.ones32, rhs=q2, start=True, stop=True),
                  reads=[("gsq", oc % 2)], writes=[("ps", 7)])
                A("act", lambda e: e.activation(out=rstd, in_=self.bank(7), func=AF.Ln, bias=self.epsc, scale=1.0),
                  reads=[("ps", 7)], writes=["grstd"])
                A("act", lambda e: e.activation(out=rstd, in_=rstd, func=AF.Exp, scale=-0.5), reads=["grstd"], writes=["grstd"])
                sc_ = (128.0 ** -0.5) if oc < 8 else 1.0
                A("dve", lambda e, qo=qo, qf=qf, sc_=sc_: e.scalar_tensor_tensor(out=qo, in0=qf, scalar=sc_, in1=rstd,
                                                                               op0=ALU.mult, op1=ALU.mult),
                  reads=[pk, "grstd"], writes=[("qkv", oc)])
        import os
        if os.environ.get('GDN_MAXC'):
            A('pool', lambda e: e.memset(og, 0.0), writes=[('og', c_, g_) for c_ in range(NCK) for g_ in range(2)])
        NG = 2
        HG = H // NG

        def common(c):
            cb = c % 2
            g_c = g_t[0:64, c * 8:(c + 1) * 8]
            be_c = be_t[0:64, c * 8:(c + 1) * 8]
            psm = self.ps[:, 7 * 512:8 * 512]
            gcs_, egl_, ekd_, egc_, bege_, Em_, ETm_ = gcs[cb], egl[cb], ekd[cb], egc[cb], bege[cb], Em[cb], ETm[cb]
            ck = lambda n: (n, cb)
            A("pe", lambda e: e.matmul(out=psm[0:64, 0:8], lhsT=LT, rhs=g_c, start=True, stop=True, skip_group_check=True),
              reads=["g_t", "cst"], writes=[("ps", 7)])
            A("act", lambda e: e.copy(out=gcs_[0:64, :], in_=psm[0:64, 0:8]), reads=[("ps", 7)], writes=[ck("gcs")])
            A("act", lambda e: e.activation(out=egc_[0:64, :], in_=psm[0:64, 0:8], func=AF.Exp), reads=[("ps", 7)], writes=[ck("egc")])
            yield
            A("pe", lambda e: e.matmul(out=psm[:, 8:16], lhsT=sel63, rhs=gcs_[0:64, :], start=True, stop=True, skip_group_check=True),
              reads=[ck("gcs"), "cst"], writes=[("ps", 7)])
            A("act", lambda e: e.activation(out=egl_, in_=psm[:, 8:16], func=AF.Exp), reads=[("ps", 7)], writes=[ck("egl")])
            A("dve", lambda e: e.tensor_tensor(out=ekd_[0:64, :], in0=psm[0:64, 8:16], in1=gcs_[0:64, :], op=ALU.subtract),
              reads=[("ps", 7), ck("gcs")], writes=[ck("ekd")])
            A("act", lambda e: e.activation(out=ekd_[0:64, :], in_=ekd_[0:64, :], func=AF.Exp), reads=[ck("ekd")], writes=[ck("ekd")])
            A("dve", lambda e: e.tensor_tensor(out=bege_[0:64, :], in0=egc_[0:64, :], in1=be_c, op=ALU.mult),
              reads=[ck("egc"), "be_t"], writes=[ck("bege")])
            yield
            A("dve", lambda e: e.tensor_tensor(out=v3(rrhs[0:64, :], 8), in0=v3(idrep, 8), in1=bc(gcs_[0:64, :], 64), op=ALU.mult),
              reads=[ck("gcs"), "cst"], writes=["rrhs"])
            b6 = self.ps[0:64, 6 * 512:7 * 512]
            A("pe", lambda e: e.matmul(out=b6, lhsT=ones64, rhs=rrhs[0:64, :], start=True, stop=True),
              reads=["rrhs", "cst"], writes=[("ps", 6)])
            A("dve", lambda e: e.tensor_tensor(out=v3(Em_[0:64, :], 8), in0=v3(b6, 8), in1=bc(gcs_[0:64, :], 64), op=ALU.subtract),
              reads=[("ps", 6), ck("gcs")], writes=[ck("E")])
            yield
            A("dve", lambda e: e.tensor_scalar(out=Em_[0:64, :], in0=Em_[0:64, :], scalar1=0.0, scalar2=None, op0=ALU.max),
              reads=[ck("E")], writes=[ck("E")])
            A("act", lambda e: e.activation(out=Em_[0:64, :], in_=Em_[0:64, :], func=AF.Exp, scale=-1.0), reads=[ck("E")], writes=[ck("E")])
            A("dve", lambda e: e.tensor_tensor(out=Em_[0:64, :], in0=Em_[0:64, :], in1=mincl, op=ALU.mult),
              reads=[ck("E"), "cst"], writes=[ck("E")])
            yield
            Eb_ = Eb[cb]
            A("pool", lambda e: e.tensor_copy(out=Eb_[0:64, :], in_=Em_[0:64, :]), reads=[ck("E")], writes=[ck("Eb")])
            for h in range(H):
                A("pe", lambda e, h=h: e.matmul(out=b6[:, h * 64:(h + 1) * 64], lhsT=Eb_[0:64, h * 64:(h + 1) * 64], rhs=idb[0:64, 0:64],
                                                start=True, stop=True, skip_group_check=True),
                  reads=[ck("Eb"), "idb"], writes=[("ps", 6)])
            A("act", lambda e: e.copy(out=ETm_[0:64, :], in_=b6), reads=[("ps", 6)], writes=[ck("ET")])
            yield

        def chain(c, g):
            cb = c % 2
            hs = list(range(g * HG, (g + 1) * HG))
            h0 = hs[0]
            W64 = slice(h0 * 64, (h0 + HG) * 64)
            W128 = slice(h0 * 128, (h0 + HG) * 128)
            W256 = slice(h0 * 256, (h0 + HG) * 256)
            hsl = slice(h0, h0 + HG)
            bA, bB = 2 * g, 2 * g + 1
            ck = lambda n: (n, cb)
            gk = lambda n: (n, g)
            be_c = be_t[0:64, c * 8:(c + 1) * 8]
            gcs_, egl_, ekd_, egc_, bege_, Em_, ETm_ = gcs[cb], egl[cb], ekd[cb], egc[cb], bege[cb], Em[cb], ETm[cb]

            def qT(h):
                return qkv[:, h * T + c * C:h * T + (c + 1) * C]

            def kT(h):
                return qkv[:, (8 + h) * T + c * C:(8 + h) * T + (c + 1) * C]

            def vT(h):
                return qkv[:, (16 + h) * T + c * C:(16 + h) * T + (c + 1) * C]

            qk_keys = [("qkv", o_) for o_ in range(24)]
            sbk = 4 + g
            p4 = self.ps[0:64, sbk * 512:sbk * 512 + 256]
            p5 = self.ps[0:64, sbk * 512 + 256:(sbk + 1) * 512]
            p4f = self.ps[:, sbk * 512:sbk * 512 + 256]
            p5f = self.ps[:, sbk * 512 + 256:(sbk + 1) * 512]
            k4 = k5 = ("ps", sbk)
            LW = slice(0, HG * 64)
            pA = self.ps[0:64, bA * 512:(bA + 1) * 512]
            pB = self.ps[0:64, bB * 512:(bB + 1) * 512]
            pBf = self.ps[:, bB * 512:(bB + 1) * 512]
            pAf = self.ps[:, bA * 512:(bA + 1) * 512]
            kA, kB = ("ps", bA), ("ps", bB)
            for j, h in enumerate(hs):
                A("pe", lambda e, h=h: e.matmul(out=p4[:, (h - h0) * 64:(h - h0 + 1) * 64], lhsT=kT(h), rhs=kT(h), start=True, stop=True,
                                                skip_group_check=True), reads=qk_keys, writes=[k4])
            P0, PT0 = Pm[0], PTm[0]
            A("dve", lambda e: e.tensor_tensor(out=Ptmp[0:64, W64], in0=p4[:, LW], in1=Em_[0:64, W64], op=ALU.mult),
              reads=[k4, ck("E")], writes=[gk("Ptmp")])
            A("dve", lambda e: e.tensor_tensor(out=Ptmp[0:64, W64], in0=Ptmp[0:64, W64], in1=mstrict[:, W64], op=ALU.mult),
              reads=[gk("Ptmp"), "cst"], writes=[gk("Ptmp")])
            A("dve", lambda e: e.tensor_tensor(out=v3(P0[0:64, W64], HG), in0=v3(Ptmp[0:64, W64], HG), in1=bc(be_c[:, hsl], 64), op=ALU.mult),
              reads=[gk("Ptmp"), "be_t"], writes=[gk("P0")])
            yield
            for h in hs:
                A("pe", lambda e, h=h: e.matmul(out=p5[:, (h - h0) * 64:(h - h0 + 1) * 64], lhsT=P0[0:64, h * 64:(h + 1) * 64], rhs=idb[0:64, 0:64],
                                                start=True, stop=True, skip_group_check=True),
                  reads=[gk("P0"), "idb"], writes=[k5])
            A("act", lambda e: e.copy(out=PT0[0:64, W64], in_=p5[:, LW]), reads=[k5], writes=[gk("PT0")])
            A("dve", lambda e: e.tensor_tensor(out=XTm[0:64, W64], in0=idrep[:, W64], in1=PT0[0:64, W64], op=ALU.subtract),
              reads=[gk("PT0"), "cst"], writes=[gk("XT")])
            yield
            if g == 1:
                S.stop_at('G3')
            for h in hs:
                A("pe", lambda e, h=h: e.matmul(out=p4[:, (h - h0) * 64:(h - h0 + 1) * 64], lhsT=kT(h), rhs=qT(h), start=True, stop=True,
                                                skip_group_check=True), reads=qk_keys, writes=[k4])
            A("dve", lambda e: e.tensor_tensor(out=attT[0:64, W64], in0=p4[:, LW], in1=ETm_[0:64, W64], op=ALU.mult),
              reads=[k4, ck("ET")], writes=[gk("attT")])
            yield
            B4 = v3(Bm[0:64, :], 8)
            for j, h in enumerate(hs):
                A("pe", lambda e, h=h, j=j: e.matmul(out=pA[:, j * 128:(j + 1) * 128], lhsT=kT(h), rhs=idb,
                                                     start=True, stop=True, skip_group_check=True), reads=qk_keys + ["idb"], writes=[kA])
            for j, h in enumerate(hs):
                A("pe", lambda e, h=h, j=j: e.matmul(out=pB[:, j * 128:(j + 1) * 128], lhsT=vT(h), rhs=idb,
                                                     start=True, stop=True, skip_group_check=True), reads=qk_keys + ["idb"], writes=[kB])
            A("dve", lambda e: e.tensor_tensor(out=B4[:, hsl, 128:256], in0=v3(pA, HG), in1=bc(bege_[0:64, hsl], 128), op=ALU.mult),
              reads=[kA, ck("bege")], writes=[gk("B")])
            A("dve", lambda e: e.tensor_tensor(out=v3(kdec[0:64, :], 8)[:, hsl, :], in0=v3(pA, HG), in1=bc(ekd_[0:64, hsl], 128), op=ALU.mult),
              reads=[kA, ck("ekd")], writes=[gk("kdec")])
            A("dve", lambda e: e.tensor_tensor(out=B4[:, hsl, 0:128], in0=v3(pB, HG), in1=bc(be_c[:, hsl], 128), op=ALU.mult),
              reads=[kB, "be_t"], writes=[gk("B")])
            yield
            if g == 1:
                S.stop_at('G5')
            def squares(cur, need_pt):
                P, PT = Pm[cur], PTm[cur]
                pk, ptk = gk("P%d" % cur), gk("PT%d" % cur)
                for h in hs:
                    A("pe", lambda e, h=h, P=P, PT=PT: e.matmul(out=p4[:, (h - h0) * 64:(h - h0 + 1) * 64], lhsT=PT[0:64, h * 64:(h + 1) * 64],
                                                                rhs=P[0:64, h * 64:(h + 1) * 64], start=True, stop=True,
                                                                skip_group_check=True), reads=[pk, ptk], writes=[k4])
                if need_pt:
                    for h in hs:
                        A("pe", lambda e, h=h, P=P, PT=PT: e.matmul(out=p5[:, (h - h0) * 64:(h - h0 + 1) * 64], lhsT=P[0:64, h * 64:(h + 1) * 64],
                                                                    rhs=PT[0:64, h * 64:(h + 1) * 64], start=True, stop=True,
                                                                    skip_group_check=True), reads=[pk, ptk], writes=[k5])

            def sq_copies(nxt, need_pt):
                A("act", lambda e: e.copy(out=Pm[nxt][0:64, W64], in_=p4[:, LW]), reads=[k4], writes=[gk("P%d" % nxt)])
                if need_pt:
                    A("act", lambda e: e.copy(out=PTm[nxt][0:64, W64], in_=p5[:, LW]), reads=[k5], writes=[gk("PT%d" % nxt)])

            squares(0, True)
            sq_copies(1, True)
            cur = 1
            yield
            for lvl in range(1, 6):
                for j, h in enumerate(hs):
                    A("pe", lambda e, h=h, j=j, cur=cur: e.matmul(out=pA[:, j * 64:(j + 1) * 64], lhsT=Pm[cur][0:64, h * 64:(h + 1) * 64],
                                                                  rhs=XTm[0:64, h * 64:(h + 1) * 64], start=True, stop=True,
                                                                  skip_group_check=True), reads=[gk("P%d" % cur), gk("XT")], writes=[kA])
                if lvl < 5:
                    squares(cur, lvl < 4)
                A("dve", lambda e: e.tensor_tensor(out=XTm[0:64, W64], in0=XTm[0:64, W64], in1=pA[:, 0:HG * 64], op=ALU.add),
                  reads=[kA, gk("XT")], writes=[gk("XT")])
                if lvl < 5:
                    sq_copies(1 - cur, lvl < 4)
                    cur = 1 - cur
                yield
            if g == 1:
                S.stop_at('G6')
            for h in hs:
                A("pe", lambda e, h=h: e.matmul(out=p4f[:, (h - h0) * 64:(h - h0 + 1) * 64], lhsT=Bm[0:64, h * 256 + 128:h * 256 + 256],
                                                rhs=XTm[0:64, h * 64:(h + 1) * 64], start=True, stop=True, skip_group_check=True),
                  reads=[gk("B"), gk("XT")], writes=[k4])
            A("act", lambda e: e.activation(out=wT[:, W64], in_=p4f[:, LW], func=AF.Copy, scale=-1.0), reads=[k4], writes=[gk("wT")])
            yield
            if g == 1:
                S.stop_at('G7')
            for j, h in enumerate(hs):
                A("pe", lambda e, h=h, j=j: e.matmul(out=pB[:, j * 128:(j + 1) * 128], lhsT=XTm[:, h * 64:(h + 1) * 64],
                                                     rhs=Bm[:, h * 256:h * 256 + 128], start=True, stop=False, skip_group_check=True),
                  reads=[gk("XT"), gk("B")], writes=[kB])
                A("pe", lambda e, h=h, j=j: e.matmul(out=pB[:, j * 128:(j + 1) * 128], lhsT=wT[:, h * 64:(h + 1) * 64],
                                                     rhs=Sb[:, h * 128:(h + 1) * 128], start=False, stop=True, skip_group_check=True),
                  reads=[gk("wT"), gk("Sb")], writes=[kB])
            vn3 = v3(vnew[0:64, :], 8)
            A("act", lambda e: e.copy(out=vnew[0:64, W128], in_=pB), reads=[kB], writes=[gk("vnew")])
            yield
            if g == 1:
                S.stop_at('G8')
            for j, h in enumerate(hs):
                A("pe", lambda e, h=h, j=j: e.matmul(out=pA[:, j * 128:(j + 1) * 128], lhsT=qT(h), rhs=Sb[:, h * 128:(h + 1) * 128],
                                                     start=True, stop=True, skip_group_check=True), reads=qk_keys + [gk("Sb")], writes=[kA])
            o3 = v3(om[0:64, :], 8)
            A("dve", lambda e: e.tensor_tensor(out=o3[:, hsl, :], in0=v3(pA, HG), in1=bc(egc_[0:64, hsl], 128), op=ALU.mult),
              reads=[kA, ck("egc")], writes=[gk("o")])
            yield
            if g == 1:
                S.stop_at('G9')
            for j, h in enumerate(hs):
                A("pe", lambda e, h=h, j=j: e.matmul(out=pB[:, j * 128:(j + 1) * 128], lhsT=attT[0:64, h * 64:(h + 1) * 64],
                                                     rhs=vnew[0:64, h * 128:(h + 1) * 128], start=True, stop=True, skip_group_check=True),
                  reads=[gk("attT"), gk("vnew")], writes=[kB])
            A("dve", lambda e: e.tensor_tensor(out=o3[:, hsl, :], in0=o3[:, hsl, :], in1=v3(pB, HG), op=ALU.add),
              reads=[kB, gk("o")], writes=[gk("o")])
            for j, h in enumerate(hs):
                A("pe", lambda e, h=h, j=j: e.matmul(out=pAf[:, j * 128:(j + 1) * 128], lhsT=kdec[0:64, h * 128:(h + 1) * 128],
                                                     rhs=vnew[0:64, h * 128:(h + 1) * 128], start=True, stop=True, skip_group_check=True),
                  reads=[gk("kdec"), gk("vnew")], writes=[kA])
            A("dve", lambda e: e.tensor_tensor(out=v3(St[:, W128], HG), in0=v3(St[:, W128], HG), in1=bc(egl_[:, hsl], 128), op=ALU.mult),
              reads=[gk("S"), ck("egl")], writes=[gk("S")])
            A("dve", lambda e: e.tensor_tensor(out=St[:, W128], in0=St[:, W128], in1=pAf, op=ALU.add),
              reads=[kA, gk("S")], writes=[gk("S")])
            A("pool", lambda e: e.tensor_copy(out=Sb[:, W128], in_=St[:, W128]), reads=[gk("S")], writes=[gk("Sb")])
            yield
            A("pool", lambda e: e.tensor_tensor(out=osq[0:64, W128], in0=om[0:64, W128], in1=om[0:64, W128], op=ALU.mult),
              reads=[gk("o")], writes=[gk("osq")])
            A("dve", lambda e: e.tensor_reduce(out=ss8[0:64, hsl], in_=v3(osq[0:64, W128], HG), axis=AX.X, op=ALU.add),
              reads=[gk("osq")], writes=[gk("ss8")])
            A("act", lambda e: e.activation(out=ss8[0:64, hsl], in_=ss8[0:64, hsl], func=AF.Ln, bias=self.epsc[0:64, :], scale=1.0 / 128),
              reads=[gk("ss8")], writes=[gk("ss8")])
            A("act", lambda e: e.activation(out=ss8[0:64, hsl], in_=ss8[0:64, hsl], func=AF.Exp, scale=-0.5), reads=[gk("ss8")], writes=[gk("ss8")])
            A("dve", lambda e: e.tensor_tensor(out=v3(omb[0:64, :], 8)[:, hsl, :], in0=o3[:, hsl, :], in1=bc(ss8[0:64, hsl], 128), op=ALU.mult),
              reads=[gk("ss8"), gk("o")], writes=[gk("omb")])
            yield
            for h in hs:
                A("pe", lambda e, h=h: e.matmul(out=p5f[:, (h - h0) * 64:(h - h0 + 1) * 64], lhsT=omb[0:64, h * 128:(h + 1) * 128], rhs=idb[0:64, 0:64],
                                                start=True, stop=True, skip_group_check=True), reads=[gk("omb"), "idb"], writes=[k5])
            og3 = v3(og, 8)[:, hsl, c * C:(c + 1) * C]
            gs3 = v3(gs, 8)[:, hsl, c * C:(c + 1) * C]
            A("dve", lambda e: e.scalar_tensor_tensor(out=og3, in0=v3(p5f[:, LW], HG), scalar=onw[:, 0:1], in1=gs3,
                                                      op0=ALU.mult, op1=ALU.mult),
              reads=[k5, "onw"] + [("gs", h) for h in hs], writes=[("og", c, g)])
            yield

        def run_gens(gens):
            gens = list(gens)
            while gens:
                for gname in list(gens):
                    try:
                        next(gname)
                    except StopIteration:
                        gens.remove(gname)

        ncs_ = min(NCK, int(os.environ.get('GDN_MAXC', '99')))
        run_gens([common(0)])
        for c in range(ncs_):
            gl = [chain(c, g) for g in range(NG)]
            if c + 1 < ncs_:
                gl.append(common(c + 1))
            run_gens(gl)
        for o in range(8):
            wi = wc[1] % 2
            wc[1] += 1
            self.wload(wob[wi], w_out[o], 1024, ("wout", wi))
            pb = o % 2
            pt = self.bank(pb)
            for k in range(8):
                A("pe", lambda e, k=k, pt=pt, wi=wi: e.matmul(out=pt, lhsT=wob[wi][:, k * 128:(k + 1) * 128],
                                                            rhs=og[:, k * T:(k + 1) * T], start=(k == 0), stop=(k == 7)),
                  reads=[("wout", wi)] + [("og", c, g_) for c in range(NCK) for g_ in range(2)], writes=[("ps", pb)])
            xs = x32[:, o * T:(o + 1) * T]
            A("dve", lambda e, xs=xs, pt=pt: e.tensor_tensor(out=xs, in0=pt, in1=xs, op=ALU.add),
              reads=[("ps", pb), ("x32", o)], writes=[("x32", o)])
            A("act", lambda e, o=o, t=t: e.dma_start(out=dst[o * 128:(o + 1) * 128, t * T:(t + 1) * T], in_=x32[:, o * T:(o + 1) * T]),
              reads=[("x32", o)], dkey=("st", o))
    self.dbg = dict(wT=wT, qkv=qkv, gs=gs, g_t=g_t, be_t=be_t, gcs=gcs[1], egl=egl[1], ekd=ekd[1], egc=egc[1], Em=Em[1], ETm=ETm[1], P0=Pm[0], P1=Pm[1], attT=attT, Bm=Bm, kdec=kdec, vnew=vnew, om=om, St=St, og=og, xn=xn, x32=x32)
    sb.release(m)


K.gdn_phase = gdn_phase


def prep_gdn(w_in, conv_w, A_log, dt_bias, out_norm, w_out):
    d = {}
    d["w_in"] = prep_proj(np.ascontiguousarray(w_in[:, :4096]))
    wab = w_in[:, 4096:4112].reshape(8, 128, 16)
    d["wab"] = np.ascontiguousarray(wab.transpose(1, 0, 2)).reshape(128, 128)
    d["cw"] = np.ascontiguousarray(conv_w.reshape(4, 24, 128).transpose(2, 1, 0)).reshape(128, 96)
    d["alog"] = np.ascontiguousarray(np.broadcast_to(A_log[None, :], (128, 8)))
    d["dtb"] = np.ascontiguousarray(np.broadcast_to(dt_bias[None, :], (128, 8)))
    d["onw"] = np.ascontiguousarray(out_norm.reshape(128, 1))
    d["w_out"] = prep_proj(w_out)
    return d


NEGB = -30000.0
VW = 386
VOFF = (0, 65, 193, 258)
NCS = {}
_o = 0
for _n, _w in (("ones_bd", 128), ("rperm", 128), ("id128", 128), ("inv", 1), ("qw", 1), ("kw3", 3), ("b2k", 1),
               ("hb1", 4), ("sel0", 128), ("sel64", 128)):
    NCS[_n] = (_o, _w)
    _o += _w
NCS_W = _o
NBC = {}
_o = 0
for _n, _w in (("efull", 4096), ("id128", 128), ("causb", 4 * 512), ("bandb", 4 * 512), ("cmpb", 512), ("cmpb0", 512),
               ("ovl", 8 * 64), ("cmpr0", 512)):
    NBC[_n] = (_o, _w)
    _o += _w
NBC_W = _o


def prep_nsa(inp):
    f = lambda a: np.asarray(a, dtype=np.float32)
    w = f(inp["nsa_w_in"])[0]
    d = {}
    colsA = np.concatenate([np.arange(1024, 1536), np.arange(1536, 1792), np.arange(2048, 2304)])
    d["wka"] = prep_proj(np.ascontiguousarray(w[:, colsA]))
    colsV = np.concatenate([np.arange(1792, 2048), np.arange(2304, 2560)])
    wv = w[:, colsV].reshape(8, 128, 512)
    d["wv"] = np.ascontiguousarray(wv.transpose(1, 0, 2)).reshape(128, 8 * 512)
    W1 = f(inp["nsa_cmp_w1"])[0]
    w1 = W1.reshape(2, 32, 64, 256).transpose(0, 2, 1, 3).reshape(2, 64, 32 * 256)
    d["w1"] = np.ascontiguousarray(np.concatenate([w1, w1], axis=1))
    pe = f(inp["nsa_cmp_pe"])[0]
    peT = pe.transpose(0, 2, 1)
    d["peT"] = np.ascontiguousarray(np.concatenate([peT, peT], axis=1))
    W2 = f(inp["nsa_cmp_w2"])[0]
    w2k = W2[0].reshape(2, 128, 64).transpose(1, 0, 2)
    d["w2k"] = np.ascontiguousarray(np.concatenate([w2k, w2k], axis=2)).reshape(128, 256)
    d["w2v"] = np.ascontiguousarray(W2[1].reshape(2, 128, 64).transpose(1, 0, 2)).reshape(128, 128)
    b1 = f(inp["nsa_cmp_b1"])[0]
    b2 = f(inp["nsa_cmp_b2"])[0]
    d["b2v"] = np.ascontiguousarray(b2[1].reshape(1, 64))
    c = np.zeros((128, NCS_W), np.float32)

    def put(name, arr):
        o, wd_ = NCS[name]
        c[:arr.shape[0], o:o + wd_] = arr

    ob = np.zeros((128, 128), np.float32)
    ob[:64, :64] = 1
    ob[64:, 64:] = 1
    put("ones_bd", ob)
    rp = np.zeros((128, 128), np.float32)
    for blk in (0, 64):
        for m_ in range(32):
            rp[blk + m_ + 32, blk + m_] = -1.0
            rp[blk + m_, blk + m_ + 32] = 1.0
    put("rperm", rp)
    put("id128", np.eye(128, dtype=np.float32))
    inv = (1.0 / (10000.0 ** (np.arange(0, 64, 2, dtype=np.float32) / 64))).astype(np.float32)
    put("inv", np.tile(inv, 4).reshape(128, 1))
    put("qw", np.tile(f(inp["nsa_q_norm"])[0], 2).reshape(128, 1))
    kn = f(inp["nsa_k_norm"])[0]
    put("kw3", np.tile(kn.T, (2, 1)))
    put("b2k", np.tile(b2[0], 2).reshape(128, 1))
    put("hb1", b1.reshape(2, 2, 128).transpose(2, 0, 1).reshape(128, 4))
    s0 = np.zeros((128, 128), np.float32)
    s0[0, :] = 1.0
    put("sel0", s0)
    s64 = np.zeros((128, 128), np.float32)
    s64[64, :] = 1.0
    put("sel64", s64)
    d["ncs"] = c
    bc_ = np.zeros((128, NBC_W), np.float32)

    def putb(name, arr):
        o, wd_ = NBC[name]
        bc_[:arr.shape[0], o:o + wd_] = arr

    keys = np.arange(4096)
    putb("efull", (keys[None, :] // 64 == np.arange(64)[:, None]).astype(np.float32))
    putb("id128", np.eye(128, dtype=np.float32))
    kk = np.arange(128)[:, None]
    qq = np.arange(512)[None, :]
    putb("causb", np.concatenate([np.where(dd * 128 + kk > qq, NEGB, 0.0) for dd in range(4)], axis=1))
    putb("bandb", np.concatenate([np.where(kk + e_ * 128 <= qq, NEGB, 0.0) for e_ in range(4)], axis=1))
    jj = np.arange(32)[:, None]
    cm = np.where(16 * jj + 15 > qq, NEGB, 0.0)
    putb("cmpb", cm)
    cm0 = cm.copy()
    cm0[0, :] = NEGB
    putb("cmpb0", cm0)
    r0 = np.zeros((32, 512), np.float32)
    r0[0, :] = NEGB
    ov = np.zeros((32, 8, 64), np.float32)
    for tp in range(8):
        for j in range(32):
            n = 32 * tp + j - 1
            if n < 0:
                continue
            for s_ in range(64):
                lo = max(16 * n, 64 * s_)
                hi = min(16 * n + 32, 64 * s_ + 64)
                ov[j, tp, s_] = max(hi - lo, 0) / 32.0
    putb("ovl", ov.reshape(32, 512))
    putb("cmpr0", r0)
    d["nbc"] = bc_
    sm = np.zeros((8, 128, 2, 4, 64), np.float32)
    for t in range(8):
        for blk in range(4):
            tq = t * 512 + blk * 128 + np.arange(128)[:, None]
            s_ = np.arange(64)[None, :]
            valid = (s_ * 64 <= tq)
            dist = tq // 64 - s_
            forced = (s_ == 0) | ((dist >= 0) & (dist < 2))
            sm[t, :, 0, blk, :] = valid
            sm[t, :, 1, blk, :] = np.where(valid & forced, 1e9, 0.0) + np.where(valid, 0.0, -1.0)
    d["selc"] = sm.reshape(8, 128, 512)
    qcols = []
    for pp in range(2):
        for i in range(4):
            for g in (2 * pp, 2 * pp + 1):
                h = g * 4 + i
                qcols.append(np.arange(h * 64, (h + 1) * 64))
    gcols = []
    for pp in range(2):
        for r in range(3):
            for i in range(4):
                for g in (2 * pp, 2 * pp + 1):
                    h = g * 4 + i
                    gcols.append(np.full(64, 2560 + h * 3 + r))
    d["wq"] = prep_proj(np.ascontiguousarray(w[:, np.concatenate(qcols)]))
    d["wg"] = prep_proj(np.ascontiguousarray(w[:, np.concatenate(gcols)]))
    wo = f(inp["nsa_w_out"])[0]
    d["wo"] = prep_proj(np.ascontiguousarray(wo[np.concatenate(qcols), :]))
    return d


import math
TWO_PI = 2.0 * math.pi
CW1 = 6.28125
CW2 = TWO_PI - CW1


def rope_tables(self, pos_d, t, T, cosb, sinb, wk, inv_col, tag, wkeys=None):
    A = self.S.add
    ti = wk[0].bitcast(I32)
    ang, kf = wk[1], wk[2]
    if wkeys is None:
        wkeys = [tag + "w0", tag + "w1", tag + "w2"]
    K0, K1, K2 = wkeys
    A("sp", lambda e: e.dma_start(out=ti, in_=pos_d[0:1, t * T:(t + 1) * T].broadcast_to([128, T])),
      writes=[K0], dkey=tag + "pos")
    A("dve", lambda e: e.tensor_copy(out=ang, in_=ti), reads=[K0], writes=[K1])
    A("dve", lambda e: e.tensor_scalar(out=ang, in0=ang, scalar1=inv_col, scalar2=None, op0=ALU.mult),
      reads=[K1, "ncs"], writes=[K1])
    A("dve", lambda e: e.tensor_scalar(out=ti, in0=ang, scalar1=1.0 / TWO_PI, scalar2=None, op0=ALU.mult),
      reads=[K1], writes=[K0])
    A("dve", lambda e: e.tensor_copy(out=kf, in_=ti), reads=[K0], writes=[K2])
    A("dve", lambda e: e.scalar_tensor_tensor(out=ang, in0=kf, scalar=-CW1, in1=ang, op0=ALU.mult, op1=ALU.add),
      reads=[K1, K2], writes=[K1])
    A("dve", lambda e: e.scalar_tensor_tensor(out=ang, in0=kf, scalar=-CW2, in1=ang, op0=ALU.mult, op1=ALU.add),
      reads=[K1, K2], writes=[K1])

    def wrap(x, key):
        A("dve", lambda e: e.tensor_scalar(out=kf, in0=x, scalar1=math.pi, scalar2=-TWO_PI, op0=ALU.is_gt, op1=ALU.mult),
          reads=[key], writes=[K2])
        A("dve", lambda e: e.tensor_tensor(out=x, in0=x, in1=kf, op=ALU.add), reads=[key, K2], writes=[key])
        A("dve", lambda e: e.tensor_scalar(out=kf, in0=x, scalar1=-math.pi, scalar2=TWO_PI, op0=ALU.is_lt, op1=ALU.mult),
          reads=[key], writes=[K2])
        A("dve", lambda e: e.tensor_tensor(out=x, in0=x, in1=kf, op=ALU.add), reads=[key, K2], writes=[key])

    wrap(ang, K1)
    A("act", lambda e: e.activation(out=sinb, in_=ang, func=AF.Sin), reads=[K1], writes=[tag + "sin"])
    A("dve", lambda e: e.tensor_scalar(out=ang, in0=ang, scalar1=math.pi / 2, scalar2=None, op0=ALU.add),
      reads=[K1], writes=[K1])
    wrap(ang, K1)
    A("act", lambda e: e.activation(out=cosb, in_=ang, func=AF.Sin), reads=[K1], writes=[tag + "cos"])


K.rope_tables = rope_tables


def headnorm_rope(self, pt, pkey, wcol, cosb, sinb, tag, outs, scale, wk, ncs, T):
    A = self.S.add
    sqv, rs, xnr, t1 = wk
    ones_bd, rperm = ncs["ones_bd"], ncs["rperm"]
    A("act", lambda e: e.activation(out=sqv, in_=pt, func=AF.Square), reads=[pkey], writes=[tag + "sq"])
    A("pe", lambda e: e.matmul(out=self.bank(2, T), lhsT=ones_bd, rhs=sqv, start=True, stop=True),
      reads=[tag + "sq", "ncs"], writes=[("ps", 2)])
    A("act", lambda e: e.activation(out=rs, in_=self.bank(2, T), func=AF.Ln, bias=self.epsc, scale=1.0 / 64),
      reads=[("ps", 2)], writes=[tag + "rs"])
    A("act", lambda e: e.activation(out=rs, in_=rs, func=AF.Exp, scale=-0.5), reads=[tag + "rs"], writes=[tag + "rs"])
    A("dve", lambda e: e.scalar_tensor_tensor(out=xnr, in0=pt, scalar=wcol, in1=rs, op0=ALU.mult, op1=ALU.mult),
      reads=[pkey, tag + "rs", "ncs"], writes=[tag + "xn"])
    outs = [o_ if len(o_) == 4 else (o_[0], o_[1], o_[2], slice(0, 128)) for o_ in outs]
    need_rope = any(o_[1] for o_ in outs)
    if need_rope:
        A("pe", lambda e: e.matmul(out=self.bank(3, T), lhsT=rperm, rhs=xnr, start=True, stop=True),
          reads=[tag + "xn", "ncs"], writes=[("ps", 3)])
    roped = False
    for o_ap, rope, okey, rows in outs:
        if not rope:
            A("act", lambda e, o_ap=o_ap, rows=rows: e.activation(out=o_ap[rows, :], in_=xnr[rows, :], func=AF.Copy, scale=scale),
              reads=[tag + "xn"], writes=[okey])
        else:
            if not roped:
                A("pool", lambda e: e.tensor_tensor(out=t1, in0=xnr, in1=cosb, op=ALU.mult),
                  reads=[tag + "xn", "ropecos"], writes=[tag + "t1"])
                A("dve", lambda e: e.tensor_tensor(out=rs, in0=self.bank(3, T), in1=sinb, op=ALU.mult),
                  reads=[("ps", 3), "ropesin", tag + "rs"], writes=[tag + "rs"])
                A("dve", lambda e: e.tensor_tensor(out=t1, in0=t1, in1=rs, op=ALU.add),
                  reads=[tag + "t1", tag + "rs"], writes=[tag + "t1"])
                roped = True
            A("act", lambda e, o_ap=o_ap, rows=rows: e.activation(out=o_ap[rows, :], in_=t1[rows, :], func=AF.Copy, scale=scale),
              reads=[tag + "t1"], writes=[okey])


K.headnorm_rope = headnorm_rope


def nsa_phase(self, src, dst, W, pos_d, gamma_col):
    S, sb = self.S, self.sb
    S.barrier()
    m = sb.mark()
    A = S.add
    T = 512
    NT = self.ntok // T
    SQ = self.ntok
    NKT = SQ // 128

    def v3(ap, a):
        return ap.rearrange("p (a b) -> p a b", a=a)

    ncs_t = sb.f32(NCS_W)
    A("sp", lambda e: e.dma_start(out=ncs_t, in_=W["ncs"]), writes=["ncs"], dkey="ncs")
    ncs = {n: ncs_t[:, o:o + w] for n, (o, w) in NCS.items()}
    ksT = sb.bf16(2 * SQ)
    kwT = sb.bf16(2 * 1024)
    vsS = sb.bf16(NKT * VW)
    vwS = sb.bf16(8 * VW)
    kcT = sb.bf16(4 * 32 * NT)
    vcS = sb.bf16(NT * VW)
    gam = sb.f32(8)
    A("sp", lambda e: e.dma_start(out=gam, in_=gamma_col), writes=["gam"], dkey="gam")
    for st_, nm in ((vsS, "vsS"), (vwS, "vwS"), (vcS, "vcS")):
        A("pool", lambda e, st_=st_: e.memset(st_, 0.0), writes=[nm])
        n_t = st_.shape[1] // VW
        s3 = st_.rearrange("p (t w) -> p t w", w=VW)
        for col in (64, 65, 257, 258):
            A("pool", lambda e, s3=s3, col=col: e.memset(s3[:, :, col:col + 1], 1.0), writes=[nm])
    S.barrier()
    mB = sb.mark()

    x32 = sb.f32(8 * T)
    xn = sb.bf16(8 * T)
    sq = [sb.f32(512) for _ in range(2)]
    rstd = sb.f32(512)
    cosb, sinb = sb.f32(T), sb.f32(T)
    rwk = [sb.f32(T) for _ in range(3)]
    hwk = [sb.f32(T) for _ in range(4)]
    kraw = [sb.bf16(16 + T) for _ in range(8)]
    w1 = [sb.bf16(8192) for _ in range(2)]
    peT = [sb.bf16(32) for _ in range(2)]
    w2k = sb.bf16(256)
    w2v = sb.bf16(128)
    b2v = sb.f32(64)
    one1 = sb.f32(32)
    hb = sb.f32(4)
    wv = sb.bf16(8 * 512)
    NW = 4
    wbs = [sb.bf16(1024) for _ in range(NW)]
    hx = sb.f32(512)
    hy = sb.f32(512)
    hidT = sb.bf16(512)
    kcw = sb.f32(128)
    for i in range(2):
        self.wload(w1[i], W["w1"][i], 8192, ("w1", i))
        A("pool", lambda e, i=i: e.dma_start(out=peT[i], in_=W["peT"][i]), writes=[("peT", i)], dkey=("peT", i))
    A("pool", lambda e: e.dma_start(out=w2k, in_=W["w2k"]), writes=["w2k"], dkey="w2k")
    A("pool", lambda e: e.dma_start(out=w2v, in_=W["w2v"]), writes=["w2v"], dkey="w2v")
    A("sp", lambda e: e.dma_start(out=b2v[0:1, :], in_=W["b2v"]), writes=["b2v"], dkey="b2v")
    A("pool", lambda e: e.memset(one1[0:1, :], 1.0), writes=["one1"])
    self.wload(wv, W["wv"], 4096, "wv")
    for r_ in range(8):
        A("pool", lambda e, r_=r_: e.memset(kraw[r_], 0.0), writes=[("kraw", r_)])
    pb6 = self.ps[:, 6 * 512:6 * 512 + 4]
    for i in range(2):
        for hh in range(2):
            col = i * 2 + hh
            for l in range(32):
                A("pe", lambda e, i=i, hh=hh, l=l, col=col: e.matmul(
                    out=pb6[:, col:col + 1], lhsT=w1[i][0:64, l * 256 + hh * 128:l * 256 + (hh + 1) * 128],
                    rhs=peT[i][0:64, l:l + 1], start=(l == 0), stop=(l == 31), skip_group_check=True),
                    reads=[("w1", i), ("peT", i)], writes=[("ps", 6)])
    A("dve", lambda e: e.tensor_tensor(out=hb, in0=pb6, in1=ncs["hb1"], op=ALU.add), reads=[("ps", 6), "ncs"], writes=["hb"])
    S.stop_at("P1")

    wc = [0]

    def tileA(t):
        self.load_x_tile(src, x32, "x32", t, T)
        self.rmsnorm_tile(x32, "x32", xn, "xn", gam, "gam", T, sq, rstd, 7, "n")
        self.rope_tables(pos_d, t, T, cosb, sinb, rwk, ncs["inv"], "rope")
        S.stop_at("P2")
        xk = [("xn", k, 0) for k in range(8)]
        for oc in range(6):
            wi = wc[0] % NW
            wc[0] += 1
            self.wload(wbs[wi], W["wka"][oc], 1024, ("win", wi))
            pbk = oc % 2
            pt = self.bank(pbk)
            for k in range(8):
                A("pe", lambda e, k=k, pt=pt, wi=wi: e.matmul(out=pt, lhsT=wbs[wi][:, k * 128:(k + 1) * 128],
                                                            rhs=xn[:, k * T:(k + 1) * T], start=(k == 0), stop=(k == 7)),
                  reads=[("win", wi), ("xn", k, 0)], writes=[("ps", pbk)])
            if oc < 4:
                A("act", lambda e, oc=oc, pt=pt: e.copy(out=kraw[oc * 2][0:64, 16:16 + T], in_=pt[0:64, :]),
                  reads=[("ps", pbk)], writes=[("kraw", oc * 2)])
                A("dve", lambda e, oc=oc, pt=pt: e.tensor_copy(out=kraw[oc * 2 + 1][64:128, 16:16 + T], in_=pt[64:128, :]),
                  reads=[("ps", pbk)], writes=[("kraw", oc * 2 + 1)])
            else:
                pp = oc % 2
                o_ap = ksT[:, pp * SQ + t * T:pp * SQ + (t + 1) * T]
                self.headnorm_rope(pt, ("ps", pbk), ncs["kw3"][:, 1:2], cosb, sinb, "hn",
                                   [(o_ap, True, ("ksT", pp, t))], 1.0, hwk, ncs, T)
        S.stop_at("P3")
        for blk in range(4):
            kt = t * 4 + blk
            pv = self.bank(4, 256)
            for k in range(8):
                A("pe", lambda e, k=k, blk=blk, pv=pv: e.matmul(
                    out=pv, lhsT=xn[:, k * T + blk * 128:k * T + (blk + 1) * 128], rhs=wv[:, k * 512:k * 512 + 256],
                    start=(k == 0), stop=(k == 7)), reads=[("xn", k, 0), "wv"], writes=[("ps", 4)])
            for j_, (st_, nm) in enumerate(((vsS, "vsS"),)):
                for g in range(4):
                    off = kt * VW + VOFF[g] + (64 if g % 2 else 0)
                    eng = "act" if (g + j_) % 2 == 0 else "dve"
                    fn = (lambda e, st_=st_, off=off, g=g, j_=j_, pv=pv: e.copy(
                        out=st_[:, off:off + 64], in_=pv[:, j_ * 256 + g * 64:j_ * 256 + (g + 1) * 64])) if eng == "act" else \
                        (lambda e, st_=st_, off=off, g=g, j_=j_, pv=pv: e.tensor_copy(
                            out=st_[:, off:off + 64], in_=pv[:, j_ * 256 + g * 64:j_ * 256 + (g + 1) * 64]))
                    A(eng, fn, reads=[("ps", 4)], writes=[(nm, kt)])
        S.stop_at("P4")
        p5 = self.bank(5)
        for i in range(2):
            for hh in range(2):
                for g in range(4):
                    pp = g // 2
                    col = ((i * 2 + hh) * 4 + g) * 32
                    ri = (i * 2 + pp) * 2 + g % 2
                    src_ = kraw[ri]
                    for l in range(32):
                        A("pe", lambda e, i=i, hh=hh, l=l, col=col, src_=src_: e.matmul(
                            out=p5[:, col:col + 32], lhsT=w1[i][:, l * 256 + hh * 128:l * 256 + (hh + 1) * 128],
                            rhs=src_[:, l:l + 16 * 31 + 1:16], start=(l == 0), stop=(l == 31), skip_group_check=True),
                            reads=[("w1", i), ("kraw", ri)], writes=[("ps", 5)])
        for r_ in range(8):
            A("pool", lambda e, r_=r_: e.tensor_copy(out=kraw[r_][:, 0:16], in_=kraw[r_][:, T:T + 16]),
              reads=[("kraw", r_)], writes=[("kraw", r_)])
        S.stop_at("P5")
        for q_ in range(4):
            A("act", lambda e, q_=q_: e.activation(out=hx[:, q_ * 128:(q_ + 1) * 128], in_=p5[:, q_ * 128:(q_ + 1) * 128],
                                                   func=AF.Identity, bias=hb[:, q_:q_ + 1]),
              reads=[("ps", 5), "hb"], writes=["hx"])
        A("dve", lambda e: e.tensor_tensor(out=hy, in0=hx, in1=hx, op=ALU.mult), reads=["hx"], writes=["hy"])
        A("dve", lambda e: e.tensor_scalar(out=hy, in0=hy, scalar1=0.044715, scalar2=1.0, op0=ALU.mult, op1=ALU.add),
          reads=["hy"], writes=["hy"])
        A("dve", lambda e: e.tensor_tensor(out=hy, in0=hy, in1=hx, op=ALU.mult), reads=["hy", "hx"], writes=["hy"])
        A("act", lambda e: e.activation(out=hy, in_=hy, func=AF.Tanh, scale=0.7978845608028654), reads=["hy"], writes=["hy"])
        A("dve", lambda e: e.tensor_scalar(out=hy, in0=hy, scalar1=0.5, scalar2=0.5, op0=ALU.mult, op1=ALU.add),
          reads=["hy"], writes=["hy"])
        A("dve", lambda e: e.tensor_tensor(out=hidT, in0=hy, in1=hx, op=ALU.mult), reads=["hy", "hx"], writes=["hidT"])
        p6 = self.ps[:, 6 * 512:6 * 512 + 128]
        for g in range(4):
            for hh in range(2):
                col = ((0 * 2 + hh) * 4 + g) * 32
                A("pe", lambda e, g=g, hh=hh, col=col: e.matmul(out=p6[:, g * 32:(g + 1) * 32], lhsT=w2k[:, hh * 128:(hh + 1) * 128],
                                                                rhs=hidT[:, col:col + 32], start=(hh == 0), stop=(hh == 1),
                                                                skip_group_check=True),
                  reads=["hidT", "w2k"], writes=[("ps", 6)])
        A("act", lambda e: e.activation(out=kcw, in_=p6, func=AF.Identity, bias=ncs["b2k"]), reads=[("ps", 6), "ncs"], writes=["kcw"])
        A("act", lambda e: e.activation(out=hwk[0][:, 0:128], in_=kcw, func=AF.Square), reads=["kcw"], writes=["kcsq"])
        A("pe", lambda e: e.matmul(out=self.bank(2, 128), lhsT=ncs["ones_bd"], rhs=hwk[0][:, 0:128], start=True, stop=True),
          reads=["kcsq", "ncs"], writes=[("ps", 2)])
        A("act", lambda e: e.activation(out=hwk[1][:, 0:128], in_=self.bank(2, 128), func=AF.Sqrt, bias=self.epsc, scale=1.0 / 64),
          reads=[("ps", 2)], writes=["kcrs"])
        A("dve", lambda e: e.reciprocal(out=hwk[1][:, 0:128], in_=hwk[1][:, 0:128]), reads=["kcrs"], writes=["kcrs"])
        kc3 = kcT.rearrange("p (g n) -> p g n", g=4)[:, :, t * 32:(t + 1) * 32]
        A("dve", lambda e, kc3=kc3: e.scalar_tensor_tensor(out=kc3, in0=v3(kcw, 4), scalar=ncs["kw3"][:, 0:1],
                                                            in1=v3(hwk[1][:, 0:128], 4), op0=ALU.mult, op1=ALU.mult),
          reads=["kcw", "kcrs", "ncs"], writes=[("kcT", t)])
        p6v = self.ps[0:32, 6 * 512 + 128:6 * 512 + 128 + 256]
        for g in range(4):
            for hh in range(2):
                col = ((1 * 2 + hh) * 4 + g) * 32
                A("pe", lambda e, g=g, hh=hh, col=col: e.matmul(out=p6v[:, g * 64:(g + 1) * 64], lhsT=hidT[:, col:col + 32],
                                                                rhs=w2v[:, hh * 64:(hh + 1) * 64], start=(hh == 0), stop=False,
                                                                skip_group_check=True),
                  reads=["hidT", "w2v"], writes=[("ps", 6)])
            A("pe", lambda e, g=g: e.matmul(out=p6v[:, g * 64:(g + 1) * 64], lhsT=one1[0:1, :], rhs=b2v[0:1, :], start=False, stop=True,
                                            skip_group_check=True), reads=["one1", "b2v"], writes=[("ps", 6)])
        for g in range(4):
            off = t * VW + VOFF[g] + (64 if g % 2 else 0)
            A("act", lambda e, g=g, off=off: e.copy(out=vcS[0:32, off:off + 64], in_=p6v[:, g * 64:(g + 1) * 64]),
              reads=[("ps", 6)], writes=[("vcS", t)])

    for t in range(NT):
        tileA(t)
        S.stop_at("P6a")
    S.stop_at("P6")
    self.nsa_st = dict(ncs=ncs, ksT=ksT, kwT=kwT, vsS=vsS, vwS=vwS, kcT=kcT, vcS=vcS, gam=gam)
    self.dbg = dict(ksT=ksT, kwT=kwT, vsS=vsS, vwS=vwS, kcT=kcT, vcS=vcS)
    sb.release(mB)
    S.barrier()
    nsa_queries(self, src, dst, W, pos_d, T, NT, SQ)
    sb.release(m)


K.nsa_phase = nsa_phase


def nsa_queries(self, src, dst, W, pos_d, T, NT, SQ):
    S, sb = self.S, self.sb
    A = S.add
    st = self.nsa_st
    ncs, ksT, kwT, vsS, vwS, kcT, vcS, gam = (st[k_] for k_ in ("ncs", "ksT", "kwT", "vsS", "vwS", "kcT", "vcS", "gam"))

    def v3(ap, a):
        return ap.rearrange("p (a b) -> p a b", a=a)

    nbc = sb.bf16(NBC_W)
    self.wload(nbc, W["nbc"], NBC_W, "nbc")
    nb = {n: nbc[:, o:o + w] for n, (o, w) in NBC.items()}
    x32 = sb.f32(8 * T)
    xn = sb.bf16(8 * T)
    sq = [sb.f32(512) for _ in range(2)]
    rstd = sb.f32(512)
    cosb, sinb = sb.f32(T), sb.f32(T)
    hwk = [sb.f32(T) for _ in range(4)]
    wv = sb.bf16(8 * 512)
    self.wload(wv, W["wv"], 4096, "wv")
    qnT = [[sb.bf16(T) for _ in range(4)] for _ in range(2)]
    qrT = [[sb.bf16(T) for _ in range(4)] for _ in range(2)]
    for hf in range(2):
        for i in range(4):
            A("pool", lambda e, hf=hf, i=i: e.memset(qnT[hf][i], 0.0), writes=[("qn", i, hf)])
            A("pool", lambda e, hf=hf, i=i: e.memset(qrT[hf][i], 0.0), writes=[("qr", i, hf)])
    gt = [sb.bf16(T) for _ in range(12)]
    ogacc = [sb.f32(T) for _ in range(4)]
    ogb = sb.bf16(8 * T)
    impT = sb.f32(T)
    selc = sb.f32(T)
    sc = sb.f32(256)
    sc2 = sb.f32(64)
    m8 = sb.f32(16)
    bm = sb.bf16(4 * 128)
    selbT = sb.bf16(T)
    pT = [sb.bf16(T) for _ in range(4)]
    rz = sb.f32(T)
    rzb = sb.f32(T)
    otmp = sb.f32(T)
    imptmp = sb.f32(T)
    rwk = [rzb, otmp, imptmp]
    rwkeys = ["rzb", "otmp", "imptmp"]
    NW = 3
    wbs = [sb.bf16(1024) for _ in range(NW)]
    wob = [sb.bf16(1024) for _ in range(2)]
    A("pool", lambda e: e.memset(bm, 0.0), writes=["bm"])
    A("pool", lambda e: e.memset(rz, 0.0), writes=["rz"])
    wc = [0, 0]
    pcnt = [0, 0]

    hbc = [0]

    def head_branch(kind, i, g, t, first, ch=0):
        pp, hf = g // 2, g % 2
        q_ap = (qnT if kind == "cmp" else qrT)[hf][i]
        qkey = ("qn" if kind == "cmp" else "qr", i, hf)
        even = (g % 2 == 0)
        M = 65 if even else 128
        voff = VOFF[g]
        ob = 4 + ch
        scb = (2, 3) if ch == 0 else (0, 1)
        okey = ("ps", ob)
        oacc = self.ps[0:M, ob * 512:(ob + 1) * 512]
        if kind == "cmp":
            tiles = list(range(t + 1))
            KP = 32
        elif kind == "sel":
            tiles = list(range(4 * t + 4))
            KP = 128
        else:
            tiles = list(range(max(0, 4 * t - 4), 4 * t + 4))
            KP = 128
        nt_ = len(tiles)
        store, snm = {"cmp": (vcS, "vcS"), "sel": (vsS, "vsS"), "win": (vwS, "vwS")}[kind]

        def emit_pv(n_, kt, pbuf, pkey):
            vi = kt if kind != "win" else ((kt // 4) % 2) * 4 + kt % 4
            vl = store[0:KP, vi * VW + voff:vi * VW + voff + M]
            A("pe", lambda e, vl=vl, pbuf=pbuf, n_=n_: e.matmul(
                out=oacc, lhsT=vl, rhs=pbuf[0:KP, :], start=(n_ == 0), stop=(n_ == nt_ - 1)),
                reads=[pkey, (snm, vi)], writes=[okey])
            if kind == "cmp":
                A("pe", lambda e, kt=kt, pbuf=pbuf, n_=n_: e.matmul(
                    out=self.ps[0:64, 6 * 512:7 * 512], lhsT=nb["ovl"][0:32, kt * 64:(kt + 1) * 64], rhs=pbuf[0:32, :],
                    start=(n_ == 0), stop=(n_ == nt_ - 1)), reads=[pkey, "nbc"], writes=[("ps", 6)])

        pend = None
        for n_, kt in enumerate(tiles):
            sbk = scb[pcnt[ch] % 2]
            pbuf = pT[ch * 2 + pcnt[ch] % 2]
            pkey = ("pT", ch * 2 + pcnt[ch] % 2)
            pcnt[ch] += 1
            ps_s = self.ps[0:KP, sbk * 512:(sbk + 1) * 512]
            mm = []
            if kind == "cmp":
                mm.append((kcT[:, g * 32 * NT + kt * 32:g * 32 * NT + (kt + 1) * 32], q_ap, [("kcT", kt), qkey]))
                if kt == t:
                    mm.append((nb["id128"][:, 0:32], (nb["cmpb0"] if t == 0 else nb["cmpb"]), ["nbc"]))
                elif kt == 0:
                    mm.append((nb["id128"][:, 0:32], nb["cmpr0"], ["nbc"]))
            elif kind == "sel":
                mm.append((ksT[:, pp * SQ + kt * 128:pp * SQ + (kt + 1) * 128], q_ap, [("ksT", pp, kt // 4), qkey]))
                mm.append((nb["efull"][:, kt * 128:(kt + 1) * 128], selbT, ["nbc", "selbT"]))
                if kt >= 4 * t:
                    dd = kt - 4 * t
                    mm.append((nb["id128"], nb["causb"][:, dd * 512:(dd + 1) * 512], ["nbc"]))
            else:
                slot = (kt // 4) % 2
                ko = pp * 1024 + slot * 512 + (kt % 4) * 128
                mm.append((kwT[:, ko:ko + 128], q_ap, [("kwT", pp, slot), qkey]))
                dd = kt - 4 * t
                mask = nb["causb"][:, dd * 512:(dd + 1) * 512] if dd >= 0 else nb["bandb"][:, (dd + 4) * 512:(dd + 5) * 512]
                mm.append((nb["id128"], mask, ["nbc"]))
            for j_, (l_, r_, rd) in enumerate(mm):
                A("pe", lambda e, l_=l_, r_=r_, j_=j_, ps_s=ps_s, last=(j_ == len(mm) - 1): e.matmul(
                    out=ps_s, lhsT=l_, rhs=r_, start=(j_ == 0), stop=last), reads=rd, writes=[("ps", sbk)])
            A("act", lambda e, pbuf=pbuf, ps_s=ps_s: e.activation(out=pbuf[0:KP, :], in_=ps_s, func=AF.Exp),
              reads=[("ps", sbk)], writes=[pkey])
            if pend is not None:
                emit_pv(*pend)
            pend = (n_, kt, pbuf, pkey)
            yield
        emit_pv(*pend)
        yield
        zr = 64 if even else 0
        A("act", lambda e, zr=zr: e.activation(out=rz[zr:zr + 1, :], in_=self.ps[zr:zr + 1, ob * 512:(ob + 1) * 512], func=AF.Ln, bias=1e-18),
          reads=[okey], writes=["rz"])
        A("pe", lambda e, zr=zr: e.matmul(out=self.bank(7), lhsT=ncs["sel64" if zr == 64 else "sel0"], rhs=rz, start=True, stop=True),
          reads=["rz", "ncs"], writes=[("ps", 7)])
        A("act", lambda e: e.activation(out=rzb, in_=self.bank(7), func=AF.Exp, scale=-1.0), reads=[("ps", 7)], writes=["rzb"])
        r_ = {"cmp": 0, "sel": 1, "win": 2}[kind]
        gtile = gt[r_ * 4 + i]
        orow = slice(0, 64) if even else slice(64, 128)
        A("dve", lambda e, orow=orow: e.tensor_tensor(out=otmp[orow, :], in0=self.ps[orow, ob * 512:(ob + 1) * 512], in1=rzb[orow, :], op=ALU.mult),
          reads=[okey, "rzb"], writes=["otmp"])
        if first:
            A("dve", lambda e, orow=orow, gtile=gtile, i=i: e.tensor_tensor(out=ogacc[i][orow, :], in0=otmp[orow, :], in1=gtile[orow, :], op=ALU.mult),
              reads=["otmp", ("gt", r_ * 4 + i)], writes=[("og", i, g % 2)])
        else:
            A("dve", lambda e, orow=orow, gtile=gtile: e.tensor_tensor(out=otmp[orow, :], in0=otmp[orow, :], in1=gtile[orow, :], op=ALU.mult),
              reads=["otmp", ("gt", r_ * 4 + i)], writes=["otmp"])
            A("pool", lambda e, orow=orow, i=i: e.tensor_tensor(out=ogacc[i][orow, :], in0=ogacc[i][orow, :], in1=otmp[orow, :], op=ALU.add),
              reads=["otmp", ("og", i, g % 2)], writes=[("og", i, g % 2)])
        if kind == "cmp":
            if i == 0:
                A("dve", lambda e: e.tensor_tensor(out=impT[0:64, :], in0=self.ps[0:64, 6 * 512:7 * 512], in1=rzb[0:64, :], op=ALU.mult),
                  reads=[("ps", 6), "rzb"], writes=["impT"])
            else:
                A("dve", lambda e: e.tensor_tensor(out=imptmp[0:64, :], in0=self.ps[0:64, 6 * 512:7 * 512],
                                                   in1=rzb[0:64, :], op=ALU.mult), reads=[("ps", 6), "rzb"], writes=["imptmp"])
                A("pool", lambda e: e.tensor_tensor(out=impT[0:64, :], in0=impT[0:64, :], in1=imptmp[0:64, :], op=ALU.add),
                  reads=["imptmp", "impT"], writes=["impT"])

    def sel_mask(g, t):
        pm = self.ps[:, 6 * 512:6 * 512 + 256]
        for blk in range(4):
            A("pe", lambda e, blk=blk: e.matmul(out=pm[:, blk * 64:(blk + 1) * 64], lhsT=impT[0:64, blk * 128:(blk + 1) * 128],
                                                rhs=ncs["id128"][0:64, 0:64], start=True, stop=True, skip_group_check=True),
              reads=["impT", "ncs"], writes=[("ps", 6)])
        valid = selc[:, 0:256]
        addm = selc[:, 256:512]
        A("dve", lambda e: e.tensor_tensor(out=sc, in0=pm, in1=valid, op=ALU.mult), reads=[("ps", 6), "selc"], writes=["sc"])
        A("dve", lambda e: e.tensor_tensor(out=sc, in0=sc, in1=addm, op=ALU.add), reads=["sc", "selc"], writes=["sc"])
        for blk in range(4):
            sblk = sc[:, blk * 64:(blk + 1) * 64]
            A("dve", lambda e, sblk=sblk: e.max(out=m8[:, 0:8], in_=sblk), reads=["sc"], writes=["m8"])
            A("dve", lambda e, sblk=sblk: e.match_replace(out=sc2, in_to_replace=m8[:, 0:8], in_values=sblk, imm_value=-1e30),
              reads=["sc", "m8"], writes=["sc2"])
            A("dve", lambda e: e.max(out=m8[:, 8:16], in_=sc2), reads=["sc2"], writes=["m8"])
            A("dve", lambda e, sblk=sblk: e.tensor_scalar(out=sc2, in0=sblk, scalar1=m8[:, 15:16], scalar2=None, op0=ALU.is_ge),
              reads=["sc", "m8"], writes=["sc2"])
            A("dve", lambda e, blk=blk: e.tensor_tensor(out=sc2, in0=sc2, in1=valid[:, blk * 64:(blk + 1) * 64], op=ALU.mult),
              reads=["sc2", "selc"], writes=["sc2"])
            A("dve", lambda e, blk=blk: e.tensor_scalar(out=bm[:, blk * 128:blk * 128 + 64], in0=sc2, scalar1=-NEGB, scalar2=NEGB,
                                                        op0=ALU.mult, op1=ALU.add), reads=["sc2"], writes=["bm"])
        pm2 = self.ps[:, 6 * 512:7 * 512]
        for blk in range(4):
            A("pe", lambda e, blk=blk: e.matmul(out=pm2[:, blk * 128:(blk + 1) * 128], lhsT=bm[:, blk * 128:(blk + 1) * 128],
                                                rhs=nb["id128"], start=True, stop=True, skip_group_check=True),
              reads=["bm", "nbc"], writes=[("ps", 6)])
        A("act", lambda e: e.copy(out=selbT, in_=pm2), reads=[("ps", 6)], writes=["selbT"])

    def tileB(t):
        self.load_x_tile(src, x32, "x32", t, T)
        self.rmsnorm_tile(x32, "x32", xn, "xn", gam, "gam", T, sq, rstd, 7, "n")
        self.rope_tables(pos_d, t, T, cosb, sinb, rwk, ncs["inv"], "rope", rwkeys)
        A("sp", lambda e: e.dma_start(out=selc[:, 0:512], in_=W["selc"][t]), writes=["selc"], dkey="selc")
        slot = t % 2
        for pp in range(2):
            wi = wc[0] % NW
            wc[0] += 1
            self.wload(wbs[wi], W["wka"][6 + pp], 1024, ("win", wi))
            pbk = pp % 2
            pt = self.bank(pbk)
            for k in range(8):
                A("pe", lambda e, k=k, pt=pt, wi=wi: e.matmul(out=pt, lhsT=wbs[wi][:, k * 128:(k + 1) * 128],
                                                            rhs=xn[:, k * T:(k + 1) * T], start=(k == 0), stop=(k == 7)),
                  reads=[("win", wi), ("xn", k, 0)], writes=[("ps", pbk)])
            o_ap = kwT[:, pp * 1024 + slot * 512:pp * 1024 + (slot + 1) * 512]
            self.headnorm_rope(pt, ("ps", pbk), ncs["kw3"][:, 2:3], cosb, sinb, "hn",
                               [(o_ap, True, ("kwT", pp, slot))], 1.0, hwk, ncs, T)
        for blk in range(4):
            vi = slot * 4 + blk
            pv = self.bank(4, 256)
            for k in range(8):
                A("pe", lambda e, k=k, blk=blk, pv=pv: e.matmul(
                    out=pv, lhsT=xn[:, k * T + blk * 128:k * T + (blk + 1) * 128], rhs=wv[:, k * 512 + 256:(k + 1) * 512],
                    start=(k == 0), stop=(k == 7)), reads=[("xn", k, 0), "wv"], writes=[("ps", 4)])
            for g in range(4):
                off = vi * VW + VOFF[g] + (64 if g % 2 else 0)
                A("act", lambda e, off=off, g=g, pv=pv: e.copy(out=vwS[:, off:off + 64], in_=pv[:, g * 64:(g + 1) * 64]),
                  reads=[("ps", 4)], writes=[("vwS", vi)])
        for pp in range(2):
            for i in range(4):
                wi = wc[0] % NW
                wc[0] += 1
                self.wload(wbs[wi], W["wq"][pp * 4 + i], 1024, ("win", wi))
                pbk = i % 2
                pt = self.bank(pbk)
                for k in range(8):
                    A("pe", lambda e, k=k, pt=pt, wi=wi: e.matmul(out=pt, lhsT=wbs[wi][:, k * 128:(k + 1) * 128],
                                                                rhs=xn[:, k * T:(k + 1) * T], start=(k == 0), stop=(k == 7)),
                      reads=[("win", wi), ("xn", k, 0)], writes=[("ps", pbk)])
                self.headnorm_rope(pt, ("ps", pbk), ncs["qw"], cosb, sinb, "hn",
                                   [(qnT[0][i], False, ("qn", i, 0), slice(0, 64)), (qnT[1][i], False, ("qn", i, 1), slice(64, 128)),
                                    (qrT[0][i], True, ("qr", i, 0), slice(0, 64)), (qrT[1][i], True, ("qr", i, 1), slice(64, 128))],
                                   0.125, hwk, ncs, T)
            for r_ in range(3):
                for i in range(4):
                    wi = wc[0] % NW
                    wc[0] += 1
                    self.wload(wbs[wi], W["wg"][pp * 12 + r_ * 4 + i], 1024, ("win", wi))
                    pbk = i % 2
                    pt = self.bank(pbk)
                    for k in range(8):
                        A("pe", lambda e, k=k, pt=pt, wi=wi: e.matmul(out=pt, lhsT=wbs[wi][:, k * 128:(k + 1) * 128],
                                                                    rhs=xn[:, k * T:(k + 1) * T], start=(k == 0), stop=(k == 7)),
                          reads=[("win", wi), ("xn", k, 0)], writes=[("ps", pbk)])
                    A("act", lambda e, r_=r_, i=i, pt=pt: e.activation(out=gt[r_ * 4 + i], in_=pt, func=AF.Sigmoid),
                      reads=[("ps", pbk)], writes=[("gt", r_ * 4 + i)])
            S.stop_at("P7")
            def run_gens(gens):
                gens = list(gens)
                while gens:
                    for gn in list(gens):
                        try:
                            next(gn)
                        except StopIteration:
                            gens.remove(gn)

            for g in (2 * pp, 2 * pp + 1):
                for i in range(4):
                    run_gens([head_branch("cmp", i, g, t, True, i % 2)])
                sel_mask(g, t)
                for i in (0, 2):
                    run_gens([head_branch("sel", i, g, t, False, 0), head_branch("sel", i + 1, g, t, False, 1)])
                for i in (0, 2):
                    run_gens([head_branch("win", i, g, t, False, 0), head_branch("win", i + 1, g, t, False, 1)])
            for i in range(4):
                A("act", lambda e, pp=pp, i=i: e.copy(out=ogb[:, (pp * 4 + i) * T:(pp * 4 + i + 1) * T], in_=ogacc[i]),
                  reads=[("og", i, 0), ("og", i, 1)], writes=[("ogb", pp * 4 + i)])
        for o in range(8):
            wi = wc[1] % 2
            wc[1] += 1
            self.wload(wob[wi], W["wo"][o], 1024, ("wout", wi))
            pbk = o % 2
            pt = self.bank(pbk)
            for k in range(8):
                A("pe", lambda e, k=k, pt=pt, wi=wi: e.matmul(out=pt, lhsT=wob[wi][:, k * 128:(k + 1) * 128],
                                                            rhs=ogb[:, k * T:(k + 1) * T], start=(k == 0), stop=(k == 7)),
                  reads=[("wout", wi), ("ogb", k)], writes=[("ps", pbk)])
            xs = x32[:, o * T:(o + 1) * T]
            A("dve", lambda e, xs=xs, pt=pt: e.tensor_tensor(out=xs, in0=pt, in1=xs, op=ALU.add),
              reads=[("ps", pbk), ("x32", o)], writes=[("x32", o)])
            A("act", lambda e, o=o, t=t: e.dma_start(out=dst[o * 128:(o + 1) * 128, t * T:(t + 1) * T], in_=x32[:, o * T:(o + 1) * T]),
              reads=[("x32", o)], dkey=("st", o))

    for t in range(NT):
        tileB(t)
```

```python
import contextlib
import numpy as np
import concourse.bass as bass
import concourse.mybir as mybir
from concourse.bass_utils import run_bass_kernel_spmd

F32 = mybir.dt.float32
BF16 = mybir.dt.bfloat16
I32 = mybir.dt.int32
AF = mybir.ActivationFunctionType
ALU = mybir.AluOpType
AX = mybir.AxisListType

D = 1024
SEQ = 4096
DEPTH = 4
DFF = 2816
NCH = DFF // 128
EPS = 1e-6

ENGS = ("pe", "act", "dve", "pool", "sp")


class Op:
    __slots__ = ("eng", "fn", "deps", "sig", "cnt", "dkey", "dcnt", "reads", "writes", "bar")

    def __init__(self, eng, fn, reads, writes, dkey):
        self.eng = eng
        self.fn = fn
        self.reads = reads
        self.writes = writes
        self.dkey = dkey
        self.deps = []
        self.sig = False
        self.cnt = 0
        self.dcnt = 0
        self.bar = None


class Sched:
    def __init__(self, nc):
        self.nc = nc
        self.ops = {e: [] for e in ENGS}
        self.res = {}
        self.dma_cnt = {}
        self.nops = 0
        self.cut = False

    def add(self, eng, fn, reads=(), writes=(), dkey=None, ndma=1):
        if self.cut:
            return None
        reads = tuple(reads)
        writes = tuple(writes)
        op = Op(eng, fn, reads, writes, dkey)
        deps = {}
        for k in reads:
            r = self.res.get(k)
            if r is not None and r[0] is not None:
                deps[id(r[0])] = r[0]
        for k in writes:
            r = self.res.get(k)
            if r is not None:
                if r[0] is not None:
                    deps[id(r[0])] = r[0]
                for q in r[1]:
                    deps[id(q)] = q
        final = []
        rset = set(reads)
        wset = set(writes)
        for d in deps.values():
            if d is op:
                continue
            if d.dkey is None and dkey is None and d.eng == eng:
                if eng == "pe":
                    continue
                if eng != "pool" and not (rset.intersection(d.writes)) and not (wset.intersection(d.writes)):
                    continue
            final.append(d)
            if d.dkey is None:
                d.sig = True
        op.deps = final
        for k in reads:
            r = self.res.get(k)
            if r is None:
                self.res[k] = [None, [op]]
            else:
                r[1].append(op)
        for k in writes:
            self.res[k] = [op, []]
        if dkey is not None:
            self.dma_cnt[dkey] = self.dma_cnt.get(dkey, 0) + 16 * ndma
            op.dcnt = self.dma_cnt[dkey]
        self.ops[eng].append(op)
        self.nops += 1
        return op

    def stop_at(self, name):
        import os
        if os.environ.get("NSA_STOP") == name:
            self.cut = True

    def barrier(self):
        if self.cut:
            return
        snap_ops = {}
        for e in ENGS:
            last = None
            for o in reversed(self.ops[e]):
                if o.bar is None and o.dkey is None:
                    last = o
                    break
            if last is not None:
                last.sig = True
                snap_ops[e] = last
        dsnap = dict(self.dma_cnt)
        for e in ENGS:
            op = Op(e, None, (), (), None)
            op.bar = (dict(snap_ops), dsnap)
            self.ops[e].append(op)
        self.res = {}

    def emit(self):
        nc = self.nc
        with contextlib.ExitStack() as st:
            esem = {e: st.enter_context(nc.semaphore("s_" + e)) for e in ENGS}
            dsem = {k: st.enter_context(nc.semaphore("d_%d" % i)) for i, k in enumerate(self.dma_cnt)}
            for e in ENGS:
                c = 0
                for o in self.ops[e]:
                    if o.sig:
                        c += 1
                    o.cnt = c
            block = st.enter_context(nc.Block())
            final_d = dict(self.dma_cnt)

            def run(e, eng):
                seen = {}

                def wait(sem, key, val):
                    if val <= 0 or seen.get(key, 0) >= val:
                        return
                    seen[key] = val
                    eng.wait_ge(sem, val)

                for o in self.ops[e]:
                    if o.bar is not None:
                        so, ds = o.bar
                        for e2, lo in so.items():
                            if e2 != e:
                                wait(esem[e2], ("e", e2), lo.cnt)
                        for k, v in ds.items():
                            wait(dsem[k], ("d", k), v)
                        continue
                    need = {}
                    for d in o.deps:
                        if d.dkey is not None:
                            key, v, sem = ("d", d.dkey), d.dcnt, dsem[d.dkey]
                        else:
                            key, v, sem = ("e", d.eng), d.cnt, esem[d.eng]
                        if need.get(key, (None, 0))[1] < v:
                            need[key] = (sem, v)
                    for key, (sem, v) in need.items():
                        wait(sem, key, v)
                    r = o.fn(eng)
                    if o.dkey is not None:
                        if not isinstance(r, (list, tuple)):
                            r = [r]
                        for ins in r:
                            ins.then_inc(dsem[o.dkey], 16)
                    elif o.sig:
                        r.then_inc(esem[e], 1)
                if e == "sp":
                    for k, v in final_d.items():
                        wait(dsem[k], ("d", k), v)
                    for e2 in ENGS:
                        if e2 != e:
                            for o in reversed(self.ops[e2]):
                                if o.sig:
                                    wait(esem[e2], ("e", e2), o.cnt)
                                    break

            @block.tensor
            def _(eng):
                run("pe", eng)

            @block.scalar
            def _(eng):
                run("act", eng)

            @block.vector
            def _(eng):
                run("dve", eng)

            @block.gpsimd
            def _(eng):
                run("pool", eng)

            @block.sync
            def _(eng):
                run("sp", eng)


class SBAlloc:
    def __init__(self, big, ncols):
        self.big = big
        self.ncols = ncols
        self.top = 0
        self.peak = 0

    def mark(self):
        return self.top

    def release(self, m):
        self.top = m

    def f32(self, n):
        o = self.top
        self.top += n
        assert self.top <= self.ncols, "SBUF overflow %d > %d" % (self.top, self.ncols)
        self.peak = max(self.peak, self.top)
        return self.big[:, o:o + n]

    def bf16(self, n):
        w = (n + 1) // 2
        return self.f32(w).bitcast(BF16)[:, 0:n]


SB_COLS = 53000


class K:
    def __init__(self, ntok=SEQ, T=1024):
        self.ntok = ntok
        self.T = T
        self.nc = bass.Bass("TRN2", target_bir_lowering=False)
        self.st = contextlib.ExitStack()
        self.dram = {}

    def din(self, name, shape, dt=F32):
        ap = self.nc.dram_tensor(name, list(shape), dt, kind="ExternalInput").ap()
        self.dram[name] = ap
        return ap

    def dout(self, name, shape, dt=F32):
        ap = self.nc.dram_tensor(name, list(shape), dt, kind="ExternalOutput").ap()
        self.dram[name] = ap
        return ap

    def begin(self):
        nc = self.nc
        big = self.st.enter_context(nc.sbuf_tensor("SB", [128, SB_COLS], F32))
        self.ps = self.st.enter_context(nc.psum_tensor("PS", [128, 8 * 512], F32))
        self.sb = SBAlloc(big, SB_COLS)
        self.S = Sched(nc)
        S = self.S
        sb = self.sb
        self.ones32 = sb.f32(128)
        self.epsc = sb.f32(1)
        S.add("pool", lambda e: e.memset(self.ones32, 1.0), writes=["ones32"])
        S.add("pool", lambda e: e.memset(self.epsc, EPS), writes=["epsc"])

    def bank(self, b, n=512):
        return self.ps[:, b * 512:b * 512 + n]

    def finish(self):
        self.S.emit()
        self.st.close()
        return self.nc

    def rmsnorm_tile(self, x32, xkey, xn, xnkey, gamma, gkey, T, sq, rstd, psb, tag):
        S = self.S
        for s in range(T // 512):
            pt = self.bank(psb)
            for k in range(8):
                xs = x32[:, k * T + s * 512:k * T + (s + 1) * 512]
                q = sq[k % 2]
                S.add("act", lambda e, q=q, xs=xs: e.activation(out=q, in_=xs, func=AF.Square),
                      reads=[(xkey, k)], writes=[(tag + "sq", k % 2)])
                S.add("pe", lambda e, q=q, k=k, pt=pt: e.matmul(out=pt, lhsT=self.ones32, rhs=q,
                                                                start=(k == 0), stop=(k == 7)),
                      reads=[(tag + "sq", k % 2), "ones32"], writes=[("ps", psb)])
            S.add("act", lambda e, pt=pt: e.activation(out=rstd, in_=pt, func=AF.Sqrt, bias=self.epsc, scale=1.0 / D),
                  reads=[("ps", psb), "epsc"], writes=[tag + "rstd"])
            S.add("dve", lambda e: e.reciprocal(out=rstd, in_=rstd), reads=[tag + "rstd"], writes=[tag + "rstd"])
            for k in range(8):
                xs = x32[:, k * T + s * 512:k * T + (s + 1) * 512]
                xo = xn[:, k * T + s * 512:k * T + (s + 1) * 512]
                S.add("dve", lambda e, xs=xs, xo=xo, k=k: e.scalar_tensor_tensor(
                    out=xo, in0=xs, scalar=gamma[:, k:k + 1], in1=rstd, op0=ALU.mult, op1=ALU.mult),
                    reads=[(xkey, k), tag + "rstd", gkey], writes=[(xnkey, k, s)])

    def ffn_phase(self, src, dst, wgu, wd, gamma_col):
        S, sb = self.S, self.sb
        S.barrier()
        m = sb.mark()
        T = self.T
        NT = self.ntok // T
        NS = T // 512
        x32 = [sb.f32(8 * T) for _ in range(2)]
        xn = [sb.bf16(8 * T) for _ in range(2)]
        h = sb.bf16(NCH * T)
        NWG = 3
        wgb = [sb.bf16(2 * 8 * 128) for _ in range(NWG)]
        wdb = [sb.bf16(NCH * 128) for _ in range(2)]
        sq = [sb.f32(512) for _ in range(2)]
        rstd = sb.f32(512)
        sg = [sb.f32(512) for _ in range(2)]
        gam = sb.f32(8)
        S.add("sp", lambda e: e.dma_start(out=gam, in_=gamma_col), writes=["gam"], dkey="gam")

        def load_x(t):
            b = t % 2
            for k in range(8):
                S.add("sp", lambda e, k=k, b=b, t=t: e.dma_start(
                    out=x32[b][:, k * T:(k + 1) * T], in_=src[k * 128:(k + 1) * 128, t * T:(t + 1) * T]),
                    writes=[("x32_%d" % b, k)], dkey=("x32", b, k))

        def norm(t):
            b = t % 2
            self.rmsnorm_tile(x32[b], "x32_%d" % b, xn[b], "xn_%d" % b, gam, "gam", T, sq, rstd, 6, "f")

        wcount = [0, 0]

        def gateup(t):
            b = t % 2
            for c in range(NCH):
                wi = wcount[0] % NWG
                wcount[0] += 1
                wb = wgb[wi]
                S.add("pool", lambda e, c=c, wb=wb: [
                    e.dma_start(out=wb[:, j * 1024:(j + 1) * 1024], in_=wgu[c, :, j * 1024:(j + 1) * 1024])
                    for j in range(2)], writes=[("wg", wi)], dkey=("wg", wi), ndma=2)
                if c == 3 and t + 1 < NT and self.pipe and stage != 4:
                    load_x(t + 1)
                for s in range(NS):
                    gb = 0 + (s % 2) * 2
                    ub = 1 + (s % 2) * 2
                    pg, pu = self.bank(gb), self.bank(ub)
                    for j, pt, pb in ((0, pg, gb), (1, pu, ub)):
                        for k in range(8):
                            S.add("pe", lambda e, j=j, k=k, pt=pt, wb=wb, s=s: e.matmul(
                                out=pt, lhsT=wb[:, (j * 8 + k) * 128:(j * 8 + k + 1) * 128],
                                rhs=xn[b][:, k * T + s * 512:k * T + (s + 1) * 512],
                                start=(k == 0), stop=(k == 7)),
                                reads=[("wg", wi), ("xn_%d" % b, k, s)], writes=[("ps", pb)])
                    sgt = sg[s % 2]
                    S.add("act", lambda e, sgt=sgt, pg=pg: e.activation(out=sgt, in_=pg, func=AF.Silu),
                          reads=[("ps", gb)], writes=[("sg", s % 2)])
                    ho = h[:, c * T + s * 512:c * T + (s + 1) * 512]
                    S.add("dve", lambda e, ho=ho, sgt=sgt, pu=pu: e.tensor_tensor(out=ho, in0=pu, in1=sgt, op=ALU.mult),
                          reads=[("ps", ub), ("sg", s % 2)], writes=[("h", c, s)])

        def down(t):
            b = t % 2
            for o in range(8):
                wi = wcount[1] % 2
                wcount[1] += 1
                wb = wdb[wi]
                S.add("pool", lambda e, o=o, wb=wb: [
                    e.dma_start(out=wb[:, j * 1408:(j + 1) * 1408], in_=wd[o, :, j * 1408:(j + 1) * 1408])
                    for j in range(2)], writes=[("wd", wi)], dkey=("wd", wi), ndma=2)
                for s in range(NS):
                    pb = 4 + (o * NS + s) % 2
                    pt = self.bank(pb)
                    for c in range(NCH):
                        S.add("pe", lambda e, c=c, pt=pt, wb=wb, s=s: e.matmul(
                            out=pt, lhsT=wb[:, c * 128:(c + 1) * 128],
                            rhs=h[:, c * T + s * 512:c * T + (s + 1) * 512],
                            start=(c == 0), stop=(c == NCH - 1)),
                            reads=[("wd", wi), ("h", c, s)], writes=[("ps", pb)])
                    xs = x32[b][:, o * T + s * 512:o * T + (s + 1) * 512]
                    S.add("dve", lambda e, xs=xs, pt=pt: e.scalar_tensor_tensor(
                        out=xs, in0=pt, scalar=0.5, in1=xs, op0=ALU.mult, op1=ALU.add),
                        reads=[("ps", pb), ("x32_%d" % b, o)], writes=[("x32_%d" % b, o)])
                S.add("act", lambda e, o=o, b=b, t=t: e.dma_start(
                    out=dst[o * 128:(o + 1) * 128, t * T:(t + 1) * T], in_=x32[b][:, o * T:(o + 1) * T]),
                    reads=[("x32_%d" % b, o)], dkey=("st", b, o))

        import os
        stage = int(os.environ.get("FFN_STAGE", "9"))
        load_x(0)
        norm(0)
        if stage == 0:
            for k in range(8):
                S.add("dve", lambda e, k=k: e.tensor_copy(out=x32[0][:, k * T:(k + 1) * T], in_=xn[0][:, k * T:(k + 1) * T]),
                      reads=[("xn_0", k, 0), ("xn_0", k, 1)], writes=[("x32_0", k)])
                S.add("act", lambda e, k=k: e.dma_start(out=dst[k * 128:(k + 1) * 128, 0:T], in_=x32[0][:, k * T:(k + 1) * T]),
                      reads=[("x32_0", k)], dkey=("st", 0, k))
            sb.release(m)
            return
        self.pipe = stage != 2
        for t in range(NT):
            if not self.pipe and t > 0:
                load_x(t)
                norm(t)
            gateup(t)
            if stage == 1:
                for k in range(8):
                    S.add("dve", lambda e, k=k: e.tensor_copy(out=x32[0][:, k * T:(k + 1) * T], in_=h[:, k * T:(k + 1) * T]),
                          reads=[("h", k, 0), ("h", k, 1)], writes=[("x32_0", k)])
                    S.add("act", lambda e, k=k: e.dma_start(out=dst[k * 128:(k + 1) * 128, 0:T], in_=x32[0][:, k * T:(k + 1) * T]),
                          reads=[("x32_0", k)], dkey=("st", 0, k))
                sb.release(m)
                return
            if stage == 4 and t + 1 < NT:
                load_x(t + 1)
            if t + 1 < NT and self.pipe and stage != 3:
                norm(t + 1)
            down(t)
            if t + 1 < NT and stage == 3:
                norm(t + 1)
        sb.release(m)


    def wload(self, wb, src2d, n, key):
        S = self.S
        pieces = [(a, min(a + 2048, n)) for a in range(0, n, 2048)]
        S.add("pool", lambda e: [e.dma_start(out=wb[:, a:b], in_=src2d[:, a:b]) for a, b in pieces],
              writes=[key], dkey=key, ndma=len(pieces))

    def load_x_tile(self, src, x32, xkey, t, T, eng="sp"):
        for k in range(8):
            self.S.add(eng, lambda e, k=k: e.dma_start(
                out=x32[:, k * T:(k + 1) * T], in_=src[k * 128:(k + 1) * 128, t * T:(t + 1) * T]),
                writes=[(xkey, k)], dkey=(xkey, k))

    def sc_phase(self, src, dst, w_in, w_out, cw_d, gamma_col):
        S, sb = self.S, self.sb
        S.barrier()
        m = sb.mark()
        T = self.T
        NT = self.ntok // T
        NS = T // 512
        x32 = sb.f32(8 * T)
        xn = sb.bf16(8 * T)
        zb = sb.f32(8 * (T + 2))
        v = sb.bf16(8 * T)
        NW = 6
        wbs = [sb.bf16(1024) for _ in range(NW)]
        wob = [sb.bf16(1024) for _ in range(2)]
        sq = [sb.f32(512) for _ in range(2)]
        rstd = sb.f32(512)
        csb = [sb.f32(512) for _ in range(2)]
        ysb = [sb.f32(512) for _ in range(2)]
        gam = sb.f32(8)
        cw = sb.f32(24)
        S.add("sp", lambda e: e.dma_start(out=gam, in_=gamma_col), writes=["gam"], dkey="gam")
        S.add("sp", lambda e: e.dma_start(out=cw, in_=cw_d), writes=["cw"], dkey="cw")
        for j in range(8):
            S.add("pool", lambda e, j=j: e.memset(zb[:, j * (T + 2):j * (T + 2) + 2], 0.0), writes=[("zh", j)])
        wc = [0, 0]
        for t in range(NT):
            self.load_x_tile(src, x32, "x32", t, T)
            self.rmsnorm_tile(x32, "x32", xn, "xn", gam, "gam", T, sq, rstd, 6, "s")
            for j in range(8):
                wl = []
                for q in range(3):
                    oc = (1, 2, 0)[q] * 8 + j
                    wi = wc[0] % NW
                    wc[0] += 1
                    self.wload(wbs[wi], w_in[oc], 1024, ("win", wi))
                    wl.append(wi)
                z0 = j * (T + 2)
                for s_ in range(NS):
                    banks = (0 + 3 * (s_ % 2), 1 + 3 * (s_ % 2), 2 + 3 * (s_ % 2))
                    for q in range(3):
                        pt = self.bank(banks[q])
                        wb = wbs[wl[q]]
                        for k in range(8):
                            S.add("pe", lambda e, k=k, pt=pt, wb=wb, s_=s_: e.matmul(
                                out=pt, lhsT=wb[:, k * 128:(k + 1) * 128],
                                rhs=xn[:, k * T + s_ * 512:k * T + (s_ + 1) * 512],
                                start=(k == 0), stop=(k == 7)),
                                reads=[("win", wl[q]), ("xn", k, s_)], writes=[("ps", banks[q])])
                    pc, px, pbg = self.bank(banks[0]), self.bank(banks[1]), self.bank(banks[2])
                    cs, ys = csb[s_ % 2], ysb[s_ % 2]
                    zc = zb[:, z0 + 2 + s_ * 512:z0 + 2 + (s_ + 1) * 512]
                    zm1 = zb[:, z0 + 1 + s_ * 512:z0 + 1 + (s_ + 1) * 512]
                    zm2 = zb[:, z0 + s_ * 512:z0 + (s_ + 1) * 512]
                    S.add("act", lambda e, cs=cs, pc=pc: e.copy(out=cs, in_=pc),
                          reads=[("ps", banks[0])], writes=[("cs", s_ % 2)])
                    S.add("dve", lambda e, zc=zc, px=px, cs=cs: e.tensor_tensor(out=zc, in0=px, in1=cs, op=ALU.mult),
                          reads=[("ps", banks[1]), ("cs", s_ % 2)], writes=[("z", j, s_)])
                    S.add("act", lambda e, ys=ys, zc=zc, j=j: e.activation(out=ys, in_=zc, func=AF.Identity,
                                                                        scale=cw[:, j * 3 + 2:j * 3 + 3]),
                          reads=[("z", j, s_), "cw"], writes=[("ys", s_ % 2)])
                    hk = [("z", j, s_ - 1)] if s_ > 0 else [("zh", j)]
                    S.add("dve", lambda e, ys=ys, zm1=zm1, j=j: e.scalar_tensor_tensor(
                        out=ys, in0=zm1, scalar=cw[:, j * 3 + 1:j * 3 + 2], in1=ys, op0=ALU.mult, op1=ALU.add),
                        reads=[("z", j, s_), ("ys", s_ % 2), "cw"] + hk, writes=[("ys", s_ % 2)])
                    S.add("dve", lambda e, ys=ys, zm2=zm2, j=j: e.scalar_tensor_tensor(
                        out=ys, in0=zm2, scalar=cw[:, j * 3:j * 3 + 1], in1=ys, op0=ALU.mult, op1=ALU.add),
                        reads=[("z", j, s_), ("ys", s_ % 2), "cw"] + hk, writes=[("ys", s_ % 2)])
                    vo = v[:, j * T + s_ * 512:j * T + (s_ + 1) * 512]
                    S.add("dve", lambda e, vo=vo, pbg=pbg, ys=ys: e.tensor_tensor(out=vo, in0=pbg, in1=ys, op=ALU.mult),
                          reads=[("ps", banks[2]), ("ys", s_ % 2)], writes=[("v", j, s_)])
                S.add("pool", lambda e, z0=z0: e.tensor_copy(out=zb[:, z0:z0 + 2], in_=zb[:, z0 + T:z0 + T + 2]),
                      reads=[("z", j, s2) for s2 in range(NS)], writes=[("zh", j)])
            for o in range(8):
                wi = wc[1] % 2
                wc[1] += 1
                self.wload(wob[wi], w_out[o], 1024, ("wout", wi))
                for s_ in range(NS):
                    pb = 6 + (o * NS + s_) % 2
                    pt = self.bank(pb)
                    for k in range(8):
                        S.add("pe", lambda e, k=k, pt=pt, wi=wi, s_=s_: e.matmul(
                            out=pt, lhsT=wob[wi][:, k * 128:(k + 1) * 128],
                            rhs=v[:, k * T + s_ * 512:k * T + (s_ + 1) * 512],
                            start=(k == 0), stop=(k == 7)),
                            reads=[("wout", wi), ("v", k, s_)], writes=[("ps", pb)])
                    xs = x32[:, o * T + s_ * 512:o * T + (s_ + 1) * 512]
                    S.add("dve", lambda e, xs=xs, pt=pt: e.tensor_tensor(out=xs, in0=pt, in1=xs, op=ALU.add),
                          reads=[("ps", pb), ("x32", o)], writes=[("x32", o)])
                S.add("act", lambda e, o=o, t=t: e.dma_start(
                    out=dst[o * 128:(o + 1) * 128, t * T:(t + 1) * T], in_=x32[:, o * T:(o + 1) * T]),
                    reads=[("x32", o)], dkey=("st", o))
        sb.release(m)


def prep_proj(w, kch=8):
    Kd, N = w.shape
    assert Kd == kch * 128 and N % 128 == 0
    a = w.reshape(kch, 128, N // 128, 128)
    return np.ascontiguousarray(a.transpose(2, 1, 0, 3)).reshape(N // 128, 128, kch * 128)


def prep_ffn_weights(w_gate_up, w_down):
    w = w_gate_up.reshape(8, 128, 2, NCH, 128)
    wgu = np.ascontiguousarray(w.transpose(3, 1, 2, 0, 4)).reshape(NCH, 128, 2 * 8 * 128)
    w2 = w_down.reshape(NCH, 128, 8, 128)
    wd = np.ascontiguousarray(w2.transpose(2, 1, 0, 3)).reshape(8, 128, NCH * 128)
    return wgu, wd


def norm_cols(w):
    return np.ascontiguousarray(np.asarray(w, np.float32).reshape(8, 128).T)


N_NORM = DEPTH * 3
def nsa_shapes():
    return dict(wka=(8, 128, 1024), wv=(128, 4096), w1=(2, 128, 8192), peT=(2, 128, 32), w2k=(128, 256), w2v=(128, 128),
                b2v=(1, 64), ncs=(128, NCS_W), nbc=(128, NBC_W), selc=(8, 128, 512), wq=(8, 128, 1024),
                wg=(24, 128, 1024), wo=(8, 128, 1024))


def build_program(ntok=SEQ, layers=DEPTH, skip=()):
    k = K(ntok=ntok)
    xT = k.din("xT", [D, ntok])
    yT = k.dout("yT", [D, ntok])
    gam = k.din("gam", [128, 8 * N_NORM])
    wgu = [[k.din("wgu_%d_%d" % (l, f), [NCH, 128, 2048]) for f in range(2)] for l in range(layers)]
    wd = [[k.din("wd_%d_%d" % (l, f), [8, 128, NCH * 128]) for f in range(2)] for l in range(layers)]
    sc_in = k.din("sc_w_in", [24, 128, 1024])
    sc_out = k.din("sc_w_out", [8, 128, 1024])
    sc_cw = k.din("sc_cw", [128, 24])
    cst = k.din("cst", [128, CST_W])
    gd = []
    for j in range(2):
        p = "gdn%d_" % j
        gd.append(dict(w_in=k.din(p + "w_in", [32, 128, 1024]), wab=k.din(p + "wab", [128, 128]), cw=k.din(p + "cw", [128, 96]),
                       alog=k.din(p + "alog", [128, 8]), dtb=k.din(p + "dtb", [128, 8]), onw=k.din(p + "onw", [128, 1]),
                       w_out=k.din(p + "w_out", [8, 128, 1024])))
    pos = k.din("pos", [1, ntok], I32)
    nsaW = {n: k.din("nsa_" + n, list(shp)) for n, shp in nsa_shapes().items()}
    k.begin()

    def g(i):
        return gam[:, i * 8:(i + 1) * 8]

    for l in range(layers):
        k.ffn_phase(xT if l == 0 else yT, yT, wgu[l][0], wd[l][0], g(l * 3 + 0))
        kind = l % 3
        if kind == 1 and "sc" not in skip:
            k.sc_phase(yT, yT, sc_in, sc_out, sc_cw, g(l * 3 + 1))
        if kind == 2 and "nsa" not in skip:
            k.nsa_phase(yT, yT, nsaW, pos, g(l * 3 + 1))
        if kind == 0 and "gdn" not in skip:
            q = gd[l // 3]
            k.gdn_phase(yT, yT, q["w_in"], q["wab"], q["cw"], q["alog"], q["dtb"], q["onw"], q["w_out"], cst, g(l * 3 + 1))
        k.ffn_phase(yT, yT, wgu[l][1], wd[l][1], g(l * 3 + 2))
    nc = k.finish()
    return k, nc


def prep_inputs(inp, layers=DEPTH):
    f = lambda a: np.asarray(a, dtype=np.float32)
    com = {}
    gcols = []
    for l in range(DEPTH):
        gcols += [norm_cols(f(inp["ffn_norm"])[l, 0]), norm_cols(f(inp["mixer_norm"])[l]), norm_cols(f(inp["ffn_norm"])[l, 1])]
    com["gam"] = np.ascontiguousarray(np.concatenate(gcols, axis=1))
    for l in range(layers):
        for ff in range(2):
            a, b = prep_ffn_weights(f(inp["ffn_w_gate_up"])[l, ff], f(inp["ffn_w_down"])[l, ff])
            com["wgu_%d_%d" % (l, ff)] = a
            com["wd_%d_%d" % (l, ff)] = b
    com["sc_w_in"] = prep_proj(f(inp["sc_w_in"])[0])
    com["sc_w_out"] = prep_proj(f(inp["sc_w_out"])[0])
    com["cst"] = make_consts()
    for j in range(2):
        d = prep_gdn(f(inp["gdn_w_in"])[j], f(inp["gdn_conv_w"])[j], f(inp["gdn_A_log"])[j], f(inp["gdn_dt_bias"])[j],
                     f(inp["gdn_out_norm"])[j], f(inp["gdn_w_out"])[j])
        for kk_, vv_ in d.items():
            com["gdn%d_%s" % (j, kk_)] = vv_
    for n_, a_ in prep_nsa(inp).items():
        assert tuple(a_.shape) == tuple(nsa_shapes()[n_]), (n_, a_.shape)
        com["nsa_" + n_] = a_
    cwt = f(inp["sc_conv_w"])[0]
    com["sc_cw"] = np.ascontiguousarray(cwt.reshape(3, 8, 128).transpose(2, 1, 0)).reshape(128, 24)
    return com


def kernel(**inputs):
    x = np.asarray(inputs["x"], dtype=np.float32)
    B = x.shape[0]
    com = prep_inputs(inputs)
    k, nc = build_program()
    in_maps = []
    for b in range(B):
        m = dict(com)
        m["xT"] = np.ascontiguousarray(x[b].T)
        m["pos"] = np.ascontiguousarray(np.asarray(inputs["positions"])[b].astype(np.int32).reshape(1, -1))
        in_maps.append(m)
    res = run_bass_kernel_spmd(nc, in_maps, core_ids=list(range(B)))
    out = np.stack([np.ascontiguousarray(res.results[b]["yT"].T) for b in range(B)], axis=0)
    return out.astype(np.float32)


GC = 64
CST_OFF = {}
_o = 0
for _n, _w in (("id128", 128), ("LT", 64), ("mincl", 512), ("mstrict", 512), ("sel63", 128), ("ones64", 64), ("idrep", 512)):
    CST_OFF[_n] = (_o, _w)
    _o += _w
CST_W = _o


def make_consts():
    c = np.zeros((128, CST_W), np.float32)

    def put(name, arr):
        o, w = CST_OFF[name]
        c[:arr.shape[0], o:o + w] = arr

    put("id128", np.eye(128, dtype=np.float32))
    i = np.arange(64)
    put("LT", (i[:, None] <= i[None, :]).astype(np.float32))
    mincl = (i[None, :] <= i[:, None]).astype(np.float32)
    mstr = (i[None, :] < i[:, None]).astype(np.float32)
    put("mincl", np.tile(mincl, (1, 8)))
    put("mstrict", np.tile(mstr, (1, 8)))
    s = np.zeros((64, 128), np.float32)
    s[63, :] = 1.0
    put("sel63", s)
    put("ones64", np.ones((64, 64), np.float32))
    put("idrep", np.tile(np.eye(64, dtype=np.float32), (1, 8)))
    return c


def gdn_phase(self, src, dst, w_in, wab_d, cw_d, alog_d, dtb_d, onw_d, w_out, cst_d, gamma_col):
    S, sb = self.S, self.sb
    S.barrier()
    m = sb.mark()
    T = 512
    NT = self.ntok // T
    C = GC
    NCK = T // C
    H = 8
    A = S.add
    cst = sb.f32(CST_W)
    A("sp", lambda e: e.dma_start(out=cst, in_=cst_d), writes=["cst"], dkey="cst")

    def cs_(name, rows=64):
        o, w = CST_OFF[name]
        return cst[0:rows, o:o + w]

    id128 = cs_("id128", 128)
    id64 = cst[0:64, CST_OFF["id128"][0]:CST_OFF["id128"][0] + 64]
    idb64 = None
    LT, mincl, mstrict, sel63, ones64, idrep = (cs_("LT"), cs_("mincl"), cs_("mstrict"), cs_("sel63"),
                                                 cs_("ones64"), cs_("idrep"))
    x32 = sb.f32(8 * T)
    xn = sb.bf16(8 * T)
    qkv = sb.bf16(24 * T)
    gs = sb.f32(8 * T)
    og = sb.bf16(8 * T)
    St = sb.f32(H * 128)
    halo = sb.f32(24 * 3)
    pre = [sb.f32(T + 3) for _ in range(4)]
    yb = [sb.f32(T) for _ in range(4)]
    sq4 = [sb.f32(512) for _ in range(2)]
    rstd2 = [sb.f32(512) for _ in range(2)]
    sq = [sb.f32(512) for _ in range(2)]
    rstd = sb.f32(512)
    gam = sb.f32(8)
    cw = sb.f32(96)
    wab = sb.bf16(128)
    alog = sb.f32(8)
    dtb = sb.f32(8)
    nega = sb.f32(8)
    onw = sb.f32(1)
    NW = 4
    wbs = [sb.bf16(1024) for _ in range(NW)]
    wob = [sb.bf16(1024) for _ in range(2)]
    g_t = sb.f32(NCK * 8)
    be_t = sb.f32(NCK * 8)
    tmp_ab = sb.f32(NCK * 8)
    NCB = 3
    gcs = [sb.f32(8) for _ in range(NCB)]
    egl = [sb.f32(8) for _ in range(NCB)]
    ekd = [sb.f32(8) for _ in range(NCB)]
    egc = [sb.f32(8) for _ in range(NCB)]
    bege = [sb.f32(8) for _ in range(NCB)]
    rrhs = sb.f32(512)
    Em = [sb.f32(512) for _ in range(NCB)]
    ETm = [sb.f32(512) for _ in range(NCB)]
    Pm = [sb.bf16(512) for _ in range(2)]
    PTm = [sb.bf16(512) for _ in range(2)]
    Ptmp = sb.f32(512)
    attT2 = [sb.bf16(512) for _ in range(2)]
    Bm2 = [sb.bf16(H * 256) for _ in range(2)]
    kdec2 = [sb.bf16(H * 128) for _ in range(2)]
    wT2 = [sb.bf16(512) for _ in range(2)]
    XTm2 = [sb.bf16(512) for _ in range(2)]
    attT, Bm, kdec, wT, XTm = attT2[0], Bm2[0], kdec2[0], wT2[0], XTm2[0]
    vnew = sb.bf16(H * 128)
    omb = sb.bf16(H * 128)
    Sb = sb.bf16(H * 128)
    Eb = [sb.bf16(512) for _ in range(3)]
    idb = sb.bf16(128)
    om = sb.f32(H * 128)
    osq = sb.f32(H * 128)
    ss8 = sb.f32(8)

    A("sp", lambda e: e.dma_start(out=gam, in_=gamma_col), writes=["gam"], dkey="gam")
    A("sp", lambda e: e.dma_start(out=cw, in_=cw_d), writes=["cw"], dkey="cw")
    A("sp", lambda e: e.dma_start(out=alog, in_=alog_d), writes=["alog"], dkey="alog")
    A("sp", lambda e: e.dma_start(out=dtb, in_=dtb_d), writes=["dtb"], dkey="dtb")
    A("sp", lambda e: e.dma_start(out=onw, in_=onw_d), writes=["onw"], dkey="onw")
    A("pool", lambda e: e.dma_start(out=wab, in_=wab_d), writes=["wab"], dkey="wab")
    A("pool", lambda e: e.memset(halo, 0.0), writes=["halo"])
    A("pool", lambda e: e.memset(St, 0.0), writes=[("S", 0), ("S", 1)])
    A("pool", lambda e: e.memset(Sb, 0.0), writes=[("Sb", 0), ("Sb", 1)])
    A("pool", lambda e: e.tensor_copy(out=idb, in_=id128), reads=["cst"], writes=["idb"])
    for pc_ in range(2):
        A("pool", lambda e, pc_=pc_: e.memset(XTm2[pc_], 0.0), writes=[("XT", 0, pc_), ("XT", 1, pc_)])
        A("pool", lambda e, pc_=pc_: e.memset(Bm2[pc_], 0.0), writes=[("B", 0, pc_), ("B", 1, pc_)])
    A("act", lambda e: e.activation(out=nega[0:64, :], in_=alog[0:64, :], func=AF.Exp), reads=["alog"], writes=["nega"])
    A("dve", lambda e: e.tensor_scalar(out=nega[0:64, :], in0=nega[0:64, :], scalar1=-1.0, scalar2=None, op0=ALU.mult),
      reads=["nega"], writes=["nega"])

    def bc(ap2, n):
        return ap2.unsqueeze(2).broadcast_to([ap2.shape[0], ap2.shape[1], n])

    def v3(ap, a):
        return ap.rearrange("p (a b) -> p a b", a=a)

    wc = [0, 0]
    for t in range(NT):
        self.load_x_tile(src, x32, "x32", t, T)
        self.rmsnorm_tile(x32, "x32", xn, "xn", gam, "gam", T, sq, rstd, 7, "g")
        xnk = [("xn", k, 0) for k in range(8)]
        pab = self.ps[0:64, 6 * 512:6 * 512 + NCK * 16]
        for c in range(NCK):
            for k in range(8):
                A("pe", lambda e, c=c, k=k: e.matmul(
                    out=pab[:, c * 16:(c + 1) * 16], lhsT=xn[:, k * T + c * C:k * T + (c + 1) * C],
                    rhs=wab[:, k * 16:(k + 1) * 16], start=(k == 0), stop=(k == 7), skip_group_check=True),
                    reads=[("xn", k, 0), "wab"], writes=[("ps", 6)])
        pab3 = pab.rearrange("p (c n) -> p c n", n=16)
        g3, be3, tm3 = v3(g_t[0:64, :], NCK), v3(be_t[0:64, :], NCK), v3(tmp_ab[0:64, :], NCK)
        dtb3 = dtb[0:64, :].unsqueeze(1).broadcast_to([64, NCK, 8])
        nega3 = nega[0:64, :].unsqueeze(1).broadcast_to([64, NCK, 8])
        A("dve", lambda e: e.tensor_tensor(out=tm3, in0=pab3[:, :, 0:8], in1=dtb3, op=ALU.add),
          reads=[("ps", 6), "dtb"], writes=["tmp_ab"])
        A("act", lambda e: e.activation(out=be3, in_=pab3[:, :, 8:16], func=AF.Sigmoid), reads=[("ps", 6)], writes=["be_t"])
        A("act", lambda e: e.activation(out=tm3, in_=tm3, func=AF.Exp), reads=["tmp_ab"], writes=["tmp_ab"])
        A("act", lambda e: e.activation(out=tm3, in_=tm3, func=AF.Ln, bias=1.0), reads=["tmp_ab"], writes=["tmp_ab"])
        A("dve", lambda e: e.tensor_tensor(out=g3, in0=tm3, in1=nega3, op=ALU.mult), reads=["tmp_ab", "nega"], writes=["g_t"])
        def proj_gen(oc):
            wi = wc[0] % NW
            wc[0] += 1
            self.wload(wbs[wi], w_in[oc], 1024, ("win", wi))
            pb = oc % 4
            pt = self.bank(pb)
            for k in range(8):
                A("pe", lambda e, k=k, pt=pt, wi=wi: e.matmul(out=pt, lhsT=wbs[wi][:, k * 128:(k + 1) * 128],
                                                            rhs=xn[:, k * T:(k + 1) * T], start=(k == 0), stop=(k == 7)),
                  reads=[("win", wi), ("xn", k, 0)], writes=[("ps", pb)])
            yield
            if oc >= 24:
                h = oc - 24
                A("act", lambda e, h=h, pt=pt: e.activation(out=gs[:, h * T:(h + 1) * T], in_=pt, func=AF.Silu),
                  reads=[("ps", pb)], writes=[("gs", h)])
                return
            pr, y = pre[oc % 4], yb[oc % 4]
            pk, yk = ("pre", oc % 4), ("yb", oc % 4)
            A("pool", lambda e, pr=pr, oc=oc: e.tensor_copy(out=pr[:, 0:3], in_=halo[:, oc * 3:oc * 3 + 3]),
              reads=["halo"], writes=[pk])
            A("act", lambda e, pr=pr, pt=pt: e.copy(out=pr[:, 3:3 + T], in_=pt), reads=[("ps", pb)], writes=[pk])
            A("act", lambda e, y=y, pr=pr, oc=oc: e.activation(out=y, in_=pr[:, 3:3 + T], func=AF.Identity,
                                                               scale=cw[:, oc * 4 + 3:oc * 4 + 4]),
              reads=[pk, "cw"], writes=[yk])
            for tap in (2, 1, 0):
                A("dve", lambda e, y=y, pr=pr, oc=oc, tap=tap: e.scalar_tensor_tensor(
                    out=y, in0=pr[:, tap:tap + T], scalar=cw[:, oc * 4 + tap:oc * 4 + tap + 1], in1=y,
                    op0=ALU.mult, op1=ALU.add), reads=[pk, yk, "cw"], writes=[yk])
            A("pool", lambda e, pr=pr, oc=oc: e.tensor_copy(out=halo[:, oc * 3:oc * 3 + 3], in_=pr[:, T:T + 3]),
              reads=[pk], writes=["halo"])
            yield
            qo = qkv[:, oc * T:(oc + 1) * T]
            if oc >= 16:
                A("act", lambda e, qo=qo, y=y: e.activation(out=qo, in_=y, func=AF.Silu), reads=[yk], writes=[("qkv", oc)])
            if oc < 16:
                qf = pr[:, 3:3 + T]
                A("act", lambda e, qf=qf, y=y: e.activation(out=qf, in_=y, func=AF.Silu), reads=[yk], writes=[pk])
                q2 = sq4[oc % 2]
                nb_ = 6 + oc % 2
                rs_ = rstd2[oc % 2]
                A("act", lambda e, q2=q2, qf=qf: e.activation(out=q2, in_=qf, func=AF.Square),
                  reads=[pk], writes=[("gsq4", oc % 2)])
                A("pe", lambda e, q2=q2, nb_=nb_: e.matmul(out=self.bank(nb_), lhsT=self.ones32, rhs=q2, start=True, stop=True),
                  reads=[("gsq4", oc % 2)], writes=[("ps", nb_)])
                yield
                A("act", lambda e, rs_=rs_, nb_=nb_: e.activation(out=rs_, in_=self.bank(nb_), func=AF.Ln, bias=self.epsc, scale=1.0),
                  reads=[("ps", nb_)], writes=[("grstd2", oc % 2)])
                A("act", lambda e, rs_=rs_: e.activation(out=rs_, in_=rs_, func=AF.Exp, scale=-0.5), reads=[("grstd2", oc % 2)], writes=[("grstd2", oc % 2)])
                sc_ = (128.0 ** -0.5) if oc < 8 else 1.0
                A("dve", lambda e, qo=qo, qf=qf, sc_=sc_, rs_=rs_: e.scalar_tensor_tensor(out=qo, in0=qf, scalar=sc_, in1=rs_,
                                                                                        op0=ALU.mult, op1=ALU.mult),
                  reads=[pk, ("grstd2", oc % 2)], writes=[("qkv", oc)])
        _act = []
        _nxt = 0
        while _act or _nxt < 32:
            if _nxt < 32:
                _act.append(proj_gen(_nxt))
                _nxt += 1
            for _g in list(_act):
                try:
                    next(_g)
                except StopIteration:
                    _act.remove(_g)
        import os
        if os.environ.get('GDN_MAXC'):
            A('pool', lambda e: e.memset(og, 0.0), writes=[('og', c_, g_) for c_ in range(NCK) for g_ in range(2)])
        NG = 2
        HG = H // NG

        def common(c):
            cb = c % NCB
            g_c = g_t[0:64, c * 8:(c + 1) * 8]
            be_c = be_t[0:64, c * 8:(c + 1) * 8]
            psm = self.ps[:, 7 * 512:8 * 512]
            gcs_, egl_, ekd_, egc_, bege_, Em_, ETm_ = gcs[cb], egl[cb], ekd[cb], egc[cb], bege[cb], Em[cb], ETm[cb]
            ck = lambda n: (n, cb)
            A("pe", lambda e: e.matmul(out=psm[0:64, 0:8], lhsT=LT, rhs=g_c, start=True, stop=True, skip_group_check=True),
              reads=["g_t", "cst"], writes=[("ps", 7)])
            A("act", lambda e: e.copy(out=gcs_[0:64, :], in_=psm[0:64, 0:8]), reads=[("ps", 7)], writes=[ck("gcs")])
            A("act", lambda e: e.activation(out=egc_[0:64, :], in_=psm[0:64, 0:8], func=AF.Exp), reads=[("ps", 7)], writes=[ck("egc")])
            yield
            A("pe", lambda e: e.matmul(out=psm[:, 8:16], lhsT=sel63, rhs=gcs_[0:64, :], start=True, stop=True, skip_group_check=True),
              reads=[ck("gcs"), "cst"], writes=[("ps", 7)])
            A("act", lambda e: e.activation(out=egl_, in_=psm[:, 8:16], func=AF.Exp), reads=[("ps", 7)], writes=[ck("egl")])
            A("dve", lambda e: e.tensor_tensor(out=ekd_[0:64, :], in0=psm[0:64, 8:16], in1=gcs_[0:64, :], op=ALU.subtract),
              reads=[("ps", 7), ck("gcs")], writes=[ck("ekd")])
            A("act", lambda e: e.activation(out=ekd_[0:64, :], in_=ekd_[0:64, :], func=AF.Exp), reads=[ck("ekd")], writes=[ck("ekd")])
            A("dve", lambda e: e.tensor_tensor(out=bege_[0:64, :], in0=egc_[0:64, :], in1=be_c, op=ALU.mult),
              reads=[ck("egc"), "be_t"], writes=[ck("bege")])
            yield
            A("dve", lambda e: e.tensor_tensor(out=v3(rrhs[0:64, :], 8), in0=v3(idrep, 8), in1=bc(gcs_[0:64, :], 64), op=ALU.mult),
              reads=[ck("gcs"), "cst"], writes=["rrhs"])
            b6 = self.ps[0:64, 6 * 512:7 * 512]
            A("pe", lambda e: e.matmul(out=b6, lhsT=ones64, rhs=rrhs[0:64, :], start=True, stop=True),
              reads=["rrhs", "cst"], writes=[("ps", 6)])
            A("dve", lambda e: e.tensor_tensor(out=v3(Em_[0:64, :], 8), in0=v3(b6, 8), in1=bc(gcs_[0:64, :], 64), op=ALU.subtract),
              reads=[("ps", 6), ck("gcs")], writes=[ck("E")])
            yield
            A("dve", lambda e: e.tensor_scalar(out=Em_[0:64, :], in0=Em_[0:64, :], scalar1=0.0, scalar2=None, op0=ALU.max),
              reads=[ck("E")], writes=[ck("E")])
            A("act", lambda e: e.activation(out=Em_[0:64, :], in_=Em_[0:64, :], func=AF.Exp, scale=-1.0), reads=[ck("E")], writes=[ck("E")])
            A("dve", lambda e: e.tensor_tensor(out=Em_[0:64, :], in0=Em_[0:64, :], in1=mincl, op=ALU.mult),
              reads=[ck("E"), "cst"], writes=[ck("E")])
            yield
            Eb_ = Eb[cb]
            A("pool", lambda e: e.tensor_copy(out=Eb_[0:64, :], in_=Em_[0:64, :]), reads=[ck("E")], writes=[ck("Eb")])
            for h in range(H):
                A("pe", lambda e, h=h: e.matmul(out=b6[:, h * 64:(h + 1) * 64], lhsT=Eb_[0:64, h * 64:(h + 1) * 64], rhs=idb[0:64, 0:64],
                                                start=True, stop=True, skip_group_check=True),
                  reads=[ck("Eb"), "idb"], writes=[("ps", 6)])
            A("act", lambda e: e.copy(out=ETm_[0:64, :], in_=b6), reads=[("ps", 6)], writes=[ck("ET")])
            yield

        def chain(c, g, part):
            cb = c % NCB
            pc = c % 2
            attT, Bm, kdec, wT, XTm = attT2[pc], Bm2[pc], kdec2[pc], wT2[pc], XTm2[pc]
            hk = lambda n: (n, g, pc)
            hs = list(range(g * HG, (g + 1) * HG))
            h0 = hs[0]
            W64 = slice(h0 * 64, (h0 + HG) * 64)
            W128 = slice(h0 * 128, (h0 + HG) * 128)
            W256 = slice(h0 * 256, (h0 + HG) * 256)
            hsl = slice(h0, h0 + HG)
            bA, bB = 2 * g, 2 * g + 1
            ck = lambda n: (n, cb)
            gk = lambda n: (n, g)
            be_c = be_t[0:64, c * 8:(c + 1) * 8]
            gcs_, egl_, ekd_, egc_, bege_, Em_, ETm_ = gcs[cb], egl[cb], ekd[cb], egc[cb], bege[cb], Em[cb], ETm[cb]

            def qT(h):
                return qkv[:, h * T + c * C:h * T + (c + 1) * C]

            def kT(h):
                return qkv[:, (8 + h) * T + c * C:(8 + h) * T + (c + 1) * C]

            def vT(h):
                return qkv[:, (16 + h) * T + c * C:(16 + h) * T + (c + 1) * C]

            qk_keys = [("qkv", o_) for o_ in range(24)]
            sbk = 4 + g
            p4 = self.ps[0:64, sbk * 512:sbk * 512 + 256]
            p5 = self.ps[0:64, sbk * 512 + 256:(sbk + 1) * 512]
            p4f = self.ps[:, sbk * 512:sbk * 512 + 256]
            p5f = self.ps[:, sbk * 512 + 256:(sbk + 1) * 512]
            k4 = k5 = ("ps", sbk)
            LW = slice(0, HG * 64)
            pA = self.ps[0:64, bA * 512:(bA + 1) * 512]
            pB = self.ps[0:64, bB * 512:(bB + 1) * 512]
            pBf = self.ps[:, bB * 512:(bB + 1) * 512]
            pAf = self.ps[:, bA * 512:(bA + 1) * 512]
            kA, kB = ("ps", bA), ("ps", bB)
            if part == 0:
                for j, h in enumerate(hs):
                    A("pe", lambda e, h=h: e.matmul(out=p4[:, (h - h0) * 64:(h - h0 + 1) * 64], lhsT=kT(h), rhs=kT(h), start=True, stop=True,
                                                    skip_group_check=True), reads=qk_keys, writes=[k4])
                P0, PT0 = Pm[0], PTm[0]
                A("dve", lambda e: e.tensor_tensor(out=Ptmp[0:64, W64], in0=p4[:, LW], in1=Em_[0:64, W64], op=ALU.mult),
                  reads=[k4, ck("E")], writes=[gk("Ptmp")])
                A("dve", lambda e: e.tensor_tensor(out=Ptmp[0:64, W64], in0=Ptmp[0:64, W64], in1=mstrict[:, W64], op=ALU.mult),
                  reads=[gk("Ptmp"), "cst"], writes=[gk("Ptmp")])
                A("dve", lambda e: e.tensor_tensor(out=v3(P0[0:64, W64], HG), in0=v3(Ptmp[0:64, W64], HG), in1=bc(be_c[:, hsl], 64), op=ALU.mult),
                  reads=[gk("Ptmp"), "be_t"], writes=[gk("P0")])
                yield
                for h in hs:
                    A("pe", lambda e, h=h: e.matmul(out=p5[:, (h - h0) * 64:(h - h0 + 1) * 64], lhsT=P0[0:64, h * 64:(h + 1) * 64], rhs=idb[0:64, 0:64],
                                                    start=True, stop=True, skip_group_check=True),
                      reads=[gk("P0"), "idb"], writes=[k5])
                A("act", lambda e: e.copy(out=PT0[0:64, W64], in_=p5[:, LW]), reads=[k5], writes=[gk("PT0")])
                A("dve", lambda e: e.tensor_tensor(out=XTm[0:64, W64], in0=idrep[:, W64], in1=PT0[0:64, W64], op=ALU.subtract),
                  reads=[gk("PT0"), "cst"], writes=[hk("XT")])
                yield
                if g == 1:
                    S.stop_at('G3')
                for h in hs:
                    A("pe", lambda e, h=h: e.matmul(out=p4[:, (h - h0) * 64:(h - h0 + 1) * 64], lhsT=kT(h), rhs=qT(h), start=True, stop=True,
                                                    skip_group_check=True), reads=qk_keys, writes=[k4])
                A("dve", lambda e: e.tensor_tensor(out=attT[0:64, W64], in0=p4[:, LW], in1=ETm_[0:64, W64], op=ALU.mult),
                  reads=[k4, ck("ET")], writes=[hk("attT")])
                yield
                B4 = v3(Bm[0:64, :], 8)
                for j, h in enumerate(hs):
                    A("pe", lambda e, h=h, j=j: e.matmul(out=pA[:, j * 128:(j + 1) * 128], lhsT=kT(h), rhs=idb,
                                                         start=True, stop=True, skip_group_check=True), reads=qk_keys + ["idb"], writes=[kA])
                for j, h in enumerate(hs):
                    A("pe", lambda e, h=h, j=j: e.matmul(out=pB[:, j * 128:(j + 1) * 128], lhsT=vT(h), rhs=idb,
                                                         start=True, stop=True, skip_group_check=True), reads=qk_keys + ["idb"], writes=[kB])
                A("dve", lambda e: e.tensor_tensor(out=B4[:, hsl, 128:256], in0=v3(pA, HG), in1=bc(bege_[0:64, hsl], 128), op=ALU.mult),
                  reads=[kA, ck("bege")], writes=[hk("B")])
                A("dve", lambda e: e.tensor_tensor(out=v3(kdec[0:64, :], 8)[:, hsl, :], in0=v3(pA, HG), in1=bc(ekd_[0:64, hsl], 128), op=ALU.mult),
                  reads=[kA, ck("ekd")], writes=[hk("kdec")])
                A("dve", lambda e: e.tensor_tensor(out=B4[:, hsl, 0:128], in0=v3(pB, HG), in1=bc(be_c[:, hsl], 128), op=ALU.mult),
                  reads=[kB, "be_t"], writes=[hk("B")])
                yield
                if g == 1:
                    S.stop_at('G5')
                def squares(cur, need_pt):
                    P, PT = Pm[cur], PTm[cur]
                    pk, ptk = gk("P%d" % cur), gk("PT%d" % cur)
                    for h in hs:
                        A("pe", lambda e, h=h, P=P, PT=PT: e.matmul(out=p4[:, (h - h0) * 64:(h - h0 + 1) * 64], lhsT=PT[0:64, h * 64:(h + 1) * 64],
                                                                    rhs=P[0:64, h * 64:(h + 1) * 64], start=True, stop=True,
                                                                    skip_group_check=True), reads=[pk, ptk], writes=[k4])
                    if need_pt:
                        for h in hs:
                            A("pe", lambda e, h=h, P=P, PT=PT: e.matmul(out=p5[:, (h - h0) * 64:(h - h0 + 1) * 64], lhsT=P[0:64, h * 64:(h + 1) * 64],
                                                                        rhs=PT[0:64, h * 64:(h + 1) * 64], start=True, stop=True,
                                                                        skip_group_check=True), reads=[pk, ptk], writes=[k5])

                def sq_copies(nxt, need_pt):
                    A("act", lambda e: e.copy(out=Pm[nxt][0:64, W64], in_=p4[:, LW]), reads=[k4], writes=[gk("P%d" % nxt)])
                    if need_pt:
                        A("act", lambda e: e.copy(out=PTm[nxt][0:64, W64], in_=p5[:, LW]), reads=[k5], writes=[gk("PT%d" % nxt)])

                squares(0, True)
                sq_copies(1, True)
                cur = 1
                yield
                for lvl in range(1, 6):
                    for j, h in enumerate(hs):
                        A("pe", lambda e, h=h, j=j, cur=cur: e.matmul(out=pA[:, j * 64:(j + 1) * 64], lhsT=Pm[cur][0:64, h * 64:(h + 1) * 64],
                                                                      rhs=XTm[0:64, h * 64:(h + 1) * 64], start=True, stop=True,
                                                                      skip_group_check=True), reads=[gk("P%d" % cur), hk("XT")], writes=[kA])
                    if lvl < 5:
                        squares(cur, lvl < 4)
                    A("dve", lambda e: e.tensor_tensor(out=XTm[0:64, W64], in0=XTm[0:64, W64], in1=pA[:, 0:HG * 64], op=ALU.add),
                      reads=[kA, hk("XT")], writes=[hk("XT")])
                    if lvl < 5:
                        sq_copies(1 - cur, lvl < 4)
                        cur = 1 - cur
                    yield
                if g == 1:
                    S.stop_at('G6')
                for h in hs:
                    A("pe", lambda e, h=h: e.matmul(out=p4f[:, (h - h0) * 64:(h - h0 + 1) * 64], lhsT=Bm[0:64, h * 256 + 128:h * 256 + 256],
                                                    rhs=XTm[0:64, h * 64:(h + 1) * 64], start=True, stop=True, skip_group_check=True),
                      reads=[hk("B"), hk("XT")], writes=[k4])
                A("act", lambda e: e.activation(out=wT[:, W64], in_=p4f[:, LW], func=AF.Copy, scale=-1.0), reads=[k4], writes=[hk("wT")])
                yield
            if part == 0:
                return
            if g == 1:
                S.stop_at('G7')
            for j, h in enumerate(hs):
                A("pe", lambda e, h=h, j=j: e.matmul(out=pB[:, j * 128:(j + 1) * 128], lhsT=XTm[:, h * 64:(h + 1) * 64],
                                                     rhs=Bm[:, h * 256:h * 256 + 128], start=True, stop=False, skip_group_check=True),
                  reads=[hk("XT"), hk("B")], writes=[kB])
                A("pe", lambda e, h=h, j=j: e.matmul(out=pB[:, j * 128:(j + 1) * 128], lhsT=wT[:, h * 64:(h + 1) * 64],
                                                     rhs=Sb[:, h * 128:(h + 1) * 128], start=False, stop=True, skip_group_check=True),
                  reads=[hk("wT"), gk("Sb")], writes=[kB])
            vn3 = v3(vnew[0:64, :], 8)
            A("act", lambda e: e.copy(out=vnew[0:64, W128], in_=pB), reads=[kB], writes=[gk("vnew")])
            yield
            if g == 1:
                S.stop_at('G8')
            for j, h in enumerate(hs):
                A("pe", lambda e, h=h, j=j: e.matmul(out=pA[:, j * 128:(j + 1) * 128], lhsT=qT(h), rhs=Sb[:, h * 128:(h + 1) * 128],
                                                     start=True, stop=True, skip_group_check=True), reads=qk_keys + [gk("Sb")], writes=[kA])
            o3 = v3(om[0:64, :], 8)
            A("dve", lambda e: e.tensor_tensor(out=o3[:, hsl, :], in0=v3(pA, HG), in1=bc(egc_[0:64, hsl], 128), op=ALU.mult),
              reads=[kA, ck("egc")], writes=[gk("o")])
            yield
            if g == 1:
                S.stop_at('G9')
            for j, h in enumerate(hs):
                A("pe", lambda e, h=h, j=j: e.matmul(out=pB[:, j * 128:(j + 1) * 128], lhsT=attT[0:64, h * 64:(h + 1) * 64],
                                                     rhs=vnew[0:64, h * 128:(h + 1) * 128], start=True, stop=True, skip_group_check=True),
                  reads=[hk("attT"), gk("vnew")], writes=[kB])
            A("dve", lambda e: e.tensor_tensor(out=o3[:, hsl, :], in0=o3[:, hsl, :], in1=v3(pB, HG), op=ALU.add),
              reads=[kB, gk("o")], writes=[gk("o")])
            for j, h in enumerate(hs):
                A("pe", lambda e, h=h, j=j: e.matmul(out=pAf[:, j * 128:(j + 1) * 128], lhsT=kdec[0:64, h * 128:(h + 1) * 128],
                                                     rhs=vnew[0:64, h * 128:(h + 1) * 128], start=True, stop=True, skip_group_check=True),
                  reads=[hk("kdec"), gk("vnew")], writes=[kA])
            A("dve", lambda e: e.tensor_tensor(out=v3(St[:, W128], HG), in0=v3(St[:, W128], HG), in1=bc(egl_[:, hsl], 128), op=ALU.mult),
              reads=[gk("S"), ck("egl")], writes=[gk("S")])
            A("dve", lambda e: e.tensor_tensor(out=St[:, W128], in0=St[:, W128], in1=pAf, op=ALU.add),
              reads=[kA, gk("S")], writes=[gk("S")])
            A("pool", lambda e: e.tensor_copy(out=Sb[:, W128], in_=St[:, W128]), reads=[gk("S")], writes=[gk("Sb")])
            yield
            A("pool", lambda e: e.tensor_tensor(out=osq[0:64, W128], in0=om[0:64, W128], in1=om[0:64, W128], op=ALU.mult),
              reads=[gk("o")], writes=[gk("osq")])
            A("dve", lambda e: e.tensor_reduce(out=ss8[0:64, hsl], in_=v3(osq[0:64, W128], HG), axis=AX.X, op=ALU.add),
              reads=[gk("osq")], writes=[gk("ss8")])
            A("act", lambda e: e.activation(out=ss8[0:64, hsl], in_=ss8[0:64, hsl], func=AF.Ln, bias=self.epsc[0:64, :], scale=1.0 / 128),
              reads=[gk("ss8")], writes=[gk("ss8")])
            A("act", lambda e: e.activation(out=ss8[0:64, hsl], in_=ss8[0:64, hsl], func=AF.Exp, scale=-0.5), reads=[gk("ss8")], writes=[gk("ss8")])
            A("dve", lambda e: e.tensor_tensor(out=v3(omb[0:64, :], 8)[:, hsl, :], in0=o3[:, hsl, :], in1=bc(ss8[0:64, hsl], 128), op=ALU.mult),
              reads=[gk("ss8"), gk("o")], writes=[gk("omb")])
            yield
            for h in hs:
                A("pe", lambda e, h=h: e.matmul(out=p5f[:, (h - h0) * 64:(h - h0 + 1) * 64], lhsT=omb[0:64, h * 128:(h + 1) * 128], rhs=idb[0:64, 0:64],
                                                start=True, stop=True, skip_group_check=True), reads=[gk("omb"), "idb"], writes=[k5])
            og3 = v3(og, 8)[:, hsl, c * C:(c + 1) * C]
            gs3 = v3(gs, 8)[:, hsl, c * C:(c + 1) * C]
            A("dve", lambda e: e.scalar_tensor_tensor(out=og3, in0=v3(p5f[:, LW], HG), scalar=onw[:, 0:1], in1=gs3,
                                                      op0=ALU.mult, op1=ALU.mult),
              reads=[k5, "onw"] + [("gs", h) for h in hs], writes=[("og", c, g)])
            yield

        def run_gens(gens):
            gens = list(gens)
            while gens:
                for gname in list(gens):
                    try:
                        next(gname)
                    except StopIteration:
                        gens.remove(gname)

        ncs_ = NCK
        run_gens([common(0)])
        gl = [chain(0, g, 0) for g in range(NG)]
        if ncs_ > 1:
            gl.append(common(1))
        run_gens(gl)
        for c in range(ncs_):
            gl = [chain(c, g, 1) for g in range(NG)]
            if c + 1 < ncs_:
                gl += [chain(c + 1, g, 0) for g in range(NG)]
            if c + 2 < ncs_:
                gl.append(common(c + 2))
            run_gens(gl)
        for o in range(8):
            wi = wc[1] % 2
            wc[1] += 1
            self.wload(wob[wi], w_out[o], 1024, ("wout", wi))
            pb = o % 2
            pt = self.bank(pb)
            for k in range(8):
                A("pe", lambda e, k=k, pt=pt, wi=wi: e.matmul(out=pt, lhsT=wob[wi][:, k * 128:(k + 1) * 128],
                                                            rhs=og[:, k * T:(k + 1) * T], start=(k == 0), stop=(k == 7)),
                  reads=[("wout", wi)] + [("og", c, g_) for c in range(NCK) for g_ in range(2)], writes=[("ps", pb)])
            xs = x32[:, o * T:(o + 1) * T]
            A("dve", lambda e, xs=xs, pt=pt: e.tensor_tensor(out=xs, in0=pt, in1=xs, op=ALU.add),
              reads=[("ps", pb), ("x32", o)], writes=[("x32", o)])
            A("act", lambda e, o=o, t=t: e.dma_start(out=dst[o * 128:(o + 1) * 128, t * T:(t + 1) * T], in_=x32[:, o * T:(o + 1) * T]),
              reads=[("x32", o)], dkey=("st", o))
    self.dbg = dict(wT=wT, qkv=qkv, gs=gs, g_t=g_t, be_t=be_t, gcs=gcs[1], egl=egl[1], ekd=ekd[1], egc=egc[1], Em=Em[1], ETm=ETm[1], P0=Pm[0], P1=Pm[1], attT=attT, Bm=Bm, kdec=kdec, vnew=vnew, om=om, St=St, og=og, xn=xn, x32=x32)
    sb.release(m)


K.gdn_phase = gdn_phase


def prep_gdn(w_in, conv_w, A_log, dt_bias, out_norm, w_out):
    d = {}
    d["w_in"] = prep_proj(np.ascontiguousarray(w_in[:, :4096]))
    wab = w_in[:, 4096:4112].reshape(8, 128, 16)
    d["wab"] = np.ascontiguousarray(wab.transpose(1, 0, 2)).reshape(128, 128)
    d["cw"] = np.ascontiguousarray(conv_w.reshape(4, 24, 128).transpose(2, 1, 0)).reshape(128, 96)
    d["alog"] = np.ascontiguousarray(np.broadcast_to(A_log[None, :], (128, 8)))
    d["dtb"] = np.ascontiguousarray(np.broadcast_to(dt_bias[None, :], (128, 8)))
    d["onw"] = np.ascontiguousarray(out_norm.reshape(128, 1))
    d["w_out"] = prep_proj(w_out)
    return d


NEGB = -30000.0
VW = 386
VOFF = (0, 65, 193, 258)
NCS = {}
_o = 0
for _n, _w in (("ones_bd", 128), ("rperm", 128), ("id128", 128), ("inv", 1), ("qw", 1), ("kw3", 3), ("b2k", 1),
               ("hb1", 4), ("sel0", 128), ("sel64", 128)):
    NCS[_n] = (_o, _w)
    _o += _w
NCS_W = _o
NBC = {}
_o = 0
for _n, _w in (("efull", 4096), ("id128", 128), ("causb", 4 * 512), ("bandb", 4 * 512), ("cmpb", 512), ("cmpb0", 512),
               ("ovl", 8 * 64), ("cmpr0", 512)):
    NBC[_n] = (_o, _w)
    _o += _w
NBC_W = _o


def prep_nsa(inp):
    f = lambda a: np.asarray(a, dtype=np.float32)
    w = f(inp["nsa_w_in"])[0]
    d = {}
    colsA = np.concatenate([np.arange(1024, 1536), np.arange(1536, 1792), np.arange(2048, 2304)])
    d["wka"] = prep_proj(np.ascontiguousarray(w[:, colsA]))
    colsV = np.concatenate([np.arange(1792, 2048), np.arange(2304, 2560)])
    wv = w[:, colsV].reshape(8, 128, 512)
    d["wv"] = np.ascontiguousarray(wv.transpose(1, 0, 2)).reshape(128, 8 * 512)
    W1 = f(inp["nsa_cmp_w1"])[0]
    w1 = W1.reshape(2, 32, 64, 256).transpose(0, 2, 1, 3).reshape(2, 64, 32 * 256)
    d["w1"] = np.ascontiguousarray(np.concatenate([w1, w1], axis=1))
    pe = f(inp["nsa_cmp_pe"])[0]
    peT = pe.transpose(0, 2, 1)
    d["peT"] = np.ascontiguousarray(np.concatenate([peT, peT], axis=1))
    W2 = f(inp["nsa_cmp_w2"])[0]
    w2k = W2[0].reshape(2, 128, 64).transpose(1, 0, 2)
    d["w2k"] = np.ascontiguousarray(np.concatenate([w2k, w2k], axis=2)).reshape(128, 256)
    d["w2v"] = np.ascontiguousarray(W2[1].reshape(2, 128, 64).transpose(1, 0, 2)).reshape(128, 128)
    b1 = f(inp["nsa_cmp_b1"])[0]
    b2 = f(inp["nsa_cmp_b2"])[0]
    d["b2v"] = np.ascontiguousarray(b2[1].reshape(1, 64))
    c = np.zeros((128, NCS_W), np.float32)

    def put(name, arr):
        o, wd_ = NCS[name]
        c[:arr.shape[0], o:o + wd_] = arr

    ob = np.zeros((128, 128), np.float32)
    ob[:64, :64] = 1
    ob[64:, 64:] = 1
    put("ones_bd", ob)
    rp = np.zeros((128, 128), np.float32)
    for blk in (0, 64):
        for m_ in range(32):
            rp[blk + m_ + 32, blk + m_] = -1.0
            rp[blk + m_, blk + m_ + 32] = 1.0
    put("rperm", rp)
    put("id128", np.eye(128, dtype=np.float32))
    inv = (1.0 / (10000.0 ** (np.arange(0, 64, 2, dtype=np.float32) / 64))).astype(np.float32)
    put("inv", np.tile(inv, 4).reshape(128, 1))
    put("qw", np.tile(f(inp["nsa_q_norm"])[0], 2).reshape(128, 1))
    kn = f(inp["nsa_k_norm"])[0]
    put("kw3", np.tile(kn.T, (2, 1)))
    put("b2k", np.tile(b2[0], 2).reshape(128, 1))
    put("hb1", b1.reshape(2, 2, 128).transpose(2, 0, 1).reshape(128, 4))
    s0 = np.zeros((128, 128), np.float32)
    s0[0, :] = 1.0
    put("sel0", s0)
    s64 = np.zeros((128, 128), np.float32)
    s64[64, :] = 1.0
    put("sel64", s64)
    d["ncs"] = c
    bc_ = np.zeros((128, NBC_W), np.float32)

    def putb(name, arr):
        o, wd_ = NBC[name]
        bc_[:arr.shape[0], o:o + wd_] = arr

    keys = np.arange(4096)
    putb("efull", (keys[None, :] // 64 == np.arange(64)[:, None]).astype(np.float32))
    putb("id128", np.eye(128, dtype=np.float32))
    kk = np.arange(128)[:, None]
    qq = np.arange(512)[None, :]
    putb("causb", np.concatenate([np.where(dd * 128 + kk > qq, NEGB, 0.0) for dd in range(4)], axis=1))
    putb("bandb", np.concatenate([np.where(kk + e_ * 128 <= qq, NEGB, 0.0) for e_ in range(4)], axis=1))
    jj = np.arange(32)[:, None]
    cm = np.where(16 * jj + 15 > qq, NEGB, 0.0)
    putb("cmpb", cm)
    cm0 = cm.copy()
    cm0[0, :] = NEGB
    putb("cmpb0", cm0)
    r0 = np.zeros((32, 512), np.float32)
    r0[0, :] = NEGB
    ov = np.zeros((32, 8, 64), np.float32)
    for tp in range(8):
        for j in range(32):
            n = 32 * tp + j - 1
            if n < 0:
                continue
            for s_ in range(64):
                lo = max(16 * n, 64 * s_)
                hi = min(16 * n + 32, 64 * s_ + 64)
                ov[j, tp, s_] = max(hi - lo, 0) / 32.0
    putb("ovl", ov.reshape(32, 512))
    putb("cmpr0", r0)
    d["nbc"] = bc_
    sm = np.zeros((8, 128, 2, 4, 64), np.float32)
    for t in range(8):
        for blk in range(4):
            tq = t * 512 + blk * 128 + np.arange(128)[:, None]
            s_ = np.arange(64)[None, :]
            valid = (s_ * 64 <= tq)
            dist = tq // 64 - s_
            forced = (s_ == 0) | ((dist >= 0) & (dist < 2))
            sm[t, :, 0, blk, :] = valid
            sm[t, :, 1, blk, :] = np.where(valid & forced, 1e9, 0.0) + np.where(valid, 0.0, -1.0)
    d["selc"] = sm.reshape(8, 128, 512)
    qcols = []
    for pp in range(2):
        for i in range(4):
            for g in (2 * pp, 2 * pp + 1):
                h = g * 4 + i
                qcols.append(np.arange(h * 64, (h + 1) * 64))
    gcols = []
    for pp in range(2):
        for r in range(3):
            for i in range(4):
                for g in (2 * pp, 2 * pp + 1):
                    h = g * 4 + i
                    gcols.append(np.full(64, 2560 + h * 3 + r))
    d["wq"] = prep_proj(np.ascontiguousarray(w[:, np.concatenate(qcols)]))
    d["wg"] = prep_proj(np.ascontiguousarray(w[:, np.concatenate(gcols)]))
    wo = f(inp["nsa_w_out"])[0]
    d["wo"] = prep_proj(np.ascontiguousarray(wo[np.concatenate(qcols), :]))
    return d


import math
TWO_PI = 2.0 * math.pi
CW1 = 6.28125
CW2 = TWO_PI - CW1


def rope_tables(self, pos_d, t, T, cosb, sinb, wk, inv_col, tag, wkeys=None):
    A = self.S.add
    ti = wk[0].bitcast(I32)
    ang, kf = wk[1], wk[2]
    if wkeys is None:
        wkeys = [tag + "w0", tag + "w1", tag + "w2"]
    K0, K1, K2 = wkeys
    A("sp", lambda e: e.dma_start(out=ti, in_=pos_d[0:1, t * T:(t + 1) * T].broadcast_to([128, T])),
      writes=[K0], dkey=tag + "pos")
    A("dve", lambda e: e.tensor_copy(out=ang, in_=ti), reads=[K0], writes=[K1])
    A("dve", lambda e: e.tensor_scalar(out=ang, in0=ang, scalar1=inv_col, scalar2=None, op0=ALU.mult),
      reads=[K1, "ncs"], writes=[K1])
    A("dve", lambda e: e.tensor_scalar(out=ti, in0=ang, scalar1=1.0 / TWO_PI, scalar2=None, op0=ALU.mult),
      reads=[K1], writes=[K0])
    A("dve", lambda e: e.tensor_copy(out=kf, in_=ti), reads=[K0], writes=[K2])
    A("dve", lambda e: e.scalar_tensor_tensor(out=ang, in0=kf, scalar=-CW1, in1=ang, op0=ALU.mult, op1=ALU.add),
      reads=[K1, K2], writes=[K1])
    A("dve", lambda e: e.scalar_tensor_tensor(out=ang, in0=kf, scalar=-CW2, in1=ang, op0=ALU.mult, op1=ALU.add),
      reads=[K1, K2], writes=[K1])

    def wrap(x, key):
        A("dve", lambda e: e.tensor_scalar(out=kf, in0=x, scalar1=math.pi, scalar2=-TWO_PI, op0=ALU.is_gt, op1=ALU.mult),
          reads=[key], writes=[K2])
        A("dve", lambda e: e.tensor_tensor(out=x, in0=x, in1=kf, op=ALU.add), reads=[key, K2], writes=[key])
        A("dve", lambda e: e.tensor_scalar(out=kf, in0=x, scalar1=-math.pi, scalar2=TWO_PI, op0=ALU.is_lt, op1=ALU.mult),
          reads=[key], writes=[K2])
        A("dve", lambda e: e.tensor_tensor(out=x, in0=x, in1=kf, op=ALU.add), reads=[key, K2], writes=[key])

    wrap(ang, K1)
    A("act", lambda e: e.activation(out=sinb, in_=ang, func=AF.Sin), reads=[K1], writes=[tag + "sin"])
    A("dve", lambda e: e.tensor_scalar(out=ang, in0=ang, scalar1=math.pi / 2, scalar2=None, op0=ALU.add),
      reads=[K1], writes=[K1])
    wrap(ang, K1)
    A("act", lambda e: e.activation(out=cosb, in_=ang, func=AF.Sin), reads=[K1], writes=[tag + "cos"])


K.rope_tables = rope_tables


def headnorm_rope(self, pt, pkey, wcol, cosb, sinb, tag, outs, scale, wk, ncs, T):
    A = self.S.add
    sqv, rs, xnr, t1 = wk
    ones_bd, rperm = ncs["ones_bd"], ncs["rperm"]
    A("act", lambda e: e.activation(out=sqv, in_=pt, func=AF.Square), reads=[pkey], writes=[tag + "sq"])
    A("pe", lambda e: e.matmul(out=self.bank(2, T), lhsT=ones_bd, rhs=sqv, start=True, stop=True),
      reads=[tag + "sq", "ncs"], writes=[("ps", 2)])
    A("act", lambda e: e.activation(out=rs, in_=self.bank(2, T), func=AF.Ln, bias=self.epsc, scale=1.0 / 64),
      reads=[("ps", 2)], writes=[tag + "rs"])
    A("act", lambda e: e.activation(out=rs, in_=rs, func=AF.Exp, scale=-0.5), reads=[tag + "rs"], writes=[tag + "rs"])
    A("dve", lambda e: e.scalar_tensor_tensor(out=xnr, in0=pt, scalar=wcol, in1=rs, op0=ALU.mult, op1=ALU.mult),
      reads=[pkey, tag + "rs", "ncs"], writes=[tag + "xn"])
    outs = [o_ if len(o_) == 4 else (o_[0], o_[1], o_[2], slice(0, 128)) for o_ in outs]
    need_rope = any(o_[1] for o_ in outs)
    if need_rope:
        A("pe", lambda e: e.matmul(out=self.bank(3, T), lhsT=rperm, rhs=xnr, start=True, stop=True),
          reads=[tag + "xn", "ncs"], writes=[("ps", 3)])
    roped = False
    for o_ap, rope, okey, rows in outs:
        if not rope:
            A("act", lambda e, o_ap=o_ap, rows=rows: e.activation(out=o_ap[rows, :], in_=xnr[rows, :], func=AF.Copy, scale=scale),
              reads=[tag + "xn"], writes=[okey])
        else:
            if not roped:
                A("pool", lambda e: e.tensor_tensor(out=t1, in0=xnr, in1=cosb, op=ALU.mult),
                  reads=[tag + "xn", "ropecos"], writes=[tag + "t1"])
                A("dve", lambda e: e.tensor_tensor(out=rs, in0=self.bank(3, T), in1=sinb, op=ALU.mult),
                  reads=[("ps", 3), "ropesin", tag + "rs"], writes=[tag + "rs"])
                A("dve", lambda e: e.tensor_tensor(out=t1, in0=t1, in1=rs, op=ALU.add),
                  reads=[tag + "t1", tag + "rs"], writes=[tag + "t1"])
                roped = True
            A("act", lambda e, o_ap=o_ap, rows=rows: e.activation(out=o_ap[rows, :], in_=t1[rows, :], func=AF.Copy, scale=scale),
              reads=[tag + "t1"], writes=[okey])


K.headnorm_rope = headnorm_rope


def nsa_phase(self, src, dst, W, pos_d, gamma_col):
    S, sb = self.S, self.sb
    S.barrier()
    m = sb.mark()
    A = S.add
    T = 512
    NT = self.ntok // T
    SQ = self.ntok
    NKT = SQ // 128

    def v3(ap, a):
        return ap.rearrange("p (a b) -> p a b", a=a)

    ncs_t = sb.f32(NCS_W)
    A("sp", lambda e: e.dma_start(out=ncs_t, in_=W["ncs"]), writes=["ncs"], dkey="ncs")
    ncs = {n: ncs_t[:, o:o + w] for n, (o, w) in NCS.items()}
    ksT = sb.bf16(2 * SQ)
    kwT = sb.bf16(2 * 1024)
    vsS = sb.bf16(NKT * VW)
    vwS = sb.bf16(8 * VW)
    kcT = sb.bf16(4 * 32 * NT)
    vcS = sb.bf16(NT * VW)
    gam = sb.f32(8)
    A("sp", lambda e: e.dma_start(out=gam, in_=gamma_col), writes=["gam"], dkey="gam")
    for st_, nm in ((vsS, "vsS"), (vwS, "vwS"), (vcS, "vcS")):
        A("pool", lambda e, st_=st_: e.memset(st_, 0.0), writes=[nm])
        n_t = st_.shape[1] // VW
        s3 = st_.rearrange("p (t w) -> p t w", w=VW)
        for col in (64, 65, 257, 258):
            A("pool", lambda e, s3=s3, col=col: e.memset(s3[:, :, col:col + 1], 1.0), writes=[nm])
    S.barrier()
    mB = sb.mark()

    x32 = sb.f32(8 * T)
    xn = sb.bf16(8 * T)
    sq = [sb.f32(512) for _ in range(2)]
    rstd = sb.f32(512)
    cosb, sinb = sb.f32(T), sb.f32(T)
    rwk = [sb.f32(T) for _ in range(3)]
    hwk = [sb.f32(T) for _ in range(4)]
    kraw = [sb.bf16(16 + T) for _ in range(8)]
    w1 = [sb.bf16(8192) for _ in range(2)]
    peT = [sb.bf16(32) for _ in range(2)]
    w2k = sb.bf16(256)
    w2v = sb.bf16(128)
    b2v = sb.f32(64)
    one1 = sb.f32(32)
    hb = sb.f32(4)
    wv = sb.bf16(8 * 512)
    NW = 4
    wbs = [sb.bf16(1024) for _ in range(NW)]
    hx = sb.f32(512)
    hy = sb.f32(512)
    hidT = sb.bf16(512)
    kcw = sb.f32(128)
    for i in range(2):
        self.wload(w1[i], W["w1"][i], 8192, ("w1", i))
        A("pool", lambda e, i=i: e.dma_start(out=peT[i], in_=W["peT"][i]), writes=[("peT", i)], dkey=("peT", i))
    A("pool", lambda e: e.dma_start(out=w2k, in_=W["w2k"]), writes=["w2k"], dkey="w2k")
    A("pool", lambda e: e.dma_start(out=w2v, in_=W["w2v"]), writes=["w2v"], dkey="w2v")
    A("sp", lambda e: e.dma_start(out=b2v[0:1, :], in_=W["b2v"]), writes=["b2v"], dkey="b2v")
    A("pool", lambda e: e.memset(one1[0:1, :], 1.0), writes=["one1"])
    self.wload(wv, W["wv"], 4096, "wv")
    for r_ in range(8):
        A("pool", lambda e, r_=r_: e.memset(kraw[r_], 0.0), writes=[("kraw", r_)])
    pb6 = self.ps[:, 6 * 512:6 * 512 + 4]
    for i in range(2):
        for hh in range(2):
            col = i * 2 + hh
            for l in range(32):
                A("pe", lambda e, i=i, hh=hh, l=l, col=col: e.matmul(
                    out=pb6[:, col:col + 1], lhsT=w1[i][0:64, l * 256 + hh * 128:l * 256 + (hh + 1) * 128],
                    rhs=peT[i][0:64, l:l + 1], start=(l == 0), stop=(l == 31), skip_group_check=True),
                    reads=[("w1", i), ("peT", i)], writes=[("ps", 6)])
    A("dve", lambda e: e.tensor_tensor(out=hb, in0=pb6, in1=ncs["hb1"], op=ALU.add), reads=[("ps", 6), "ncs"], writes=["hb"])
    S.stop_at("P1")

    wc = [0]

    def tileA(t):
        self.load_x_tile(src, x32, "x32", t, T)
        self.rmsnorm_tile(x32, "x32", xn, "xn", gam, "gam", T, sq, rstd, 7, "n")
        self.rope_tables(pos_d, t, T, cosb, sinb, rwk, ncs["inv"], "rope")
        S.stop_at("P2")
        xk = [("xn", k, 0) for k in range(8)]
        for oc in range(6):
            wi = wc[0] % NW
            wc[0] += 1
            self.wload(wbs[wi], W["wka"][oc], 1024, ("win", wi))
            pbk = oc % 2
            pt = self.bank(pbk)
            for k in range(8):
                A("pe", lambda e, k=k, pt=pt, wi=wi: e.matmul(out=pt, lhsT=wbs[wi][:, k * 128:(k + 1) * 128],
                                                            rhs=xn[:, k * T:(k + 1) * T], start=(k == 0), stop=(k == 7)),
                  reads=[("win", wi), ("xn", k, 0)], writes=[("ps", pbk)])
            if oc < 4:
                A("act", lambda e, oc=oc, pt=pt: e.copy(out=kraw[oc * 2][0:64, 16:16 + T], in_=pt[0:64, :]),
                  reads=[("ps", pbk)], writes=[("kraw", oc * 2)])
                A("dve", lambda e, oc=oc, pt=pt: e.tensor_copy(out=kraw[oc * 2 + 1][64:128, 16:16 + T], in_=pt[64:128, :]),
                  reads=[("ps", pbk)], writes=[("kraw", oc * 2 + 1)])
            else:
                pp = oc % 2
                o_ap = ksT[:, pp * SQ + t * T:pp * SQ + (t + 1) * T]
                self.headnorm_rope(pt, ("ps", pbk), ncs["kw3"][:, 1:2], cosb, sinb, "hn",
                                   [(o_ap, True, ("ksT", pp, t))], 1.0, hwk, ncs, T)
        S.stop_at("P3")
        for blk in range(4):
            kt = t * 4 + blk
            pv = self.bank(4, 256)
            for k in range(8):
                A("pe", lambda e, k=k, blk=blk, pv=pv: e.matmul(
                    out=pv, lhsT=xn[:, k * T + blk * 128:k * T + (blk + 1) * 128], rhs=wv[:, k * 512:k * 512 + 256],
                    start=(k == 0), stop=(k == 7)), reads=[("xn", k, 0), "wv"], writes=[("ps", 4)])
            for j_, (st_, nm) in enumerate(((vsS, "vsS"),)):
                for g in range(4):
                    off = kt * VW + VOFF[g] + (64 if g % 2 else 0)
                    eng = "act" if (g + j_) % 2 == 0 else "dve"
                    fn = (lambda e, st_=st_, off=off, g=g, j_=j_, pv=pv: e.copy(
                        out=st_[:, off:off + 64], in_=pv[:, j_ * 256 + g * 64:j_ * 256 + (g + 1) * 64])) if eng == "act" else \
                        (lambda e, st_=st_, off=off, g=g, j_=j_, pv=pv: e.tensor_copy(
                            out=st_[:, off:off + 64], in_=pv[:, j_ * 256 + g * 64:j_ * 256 + (g + 1) * 64]))
                    A(eng, fn, reads=[("ps", 4)], writes=[(nm, kt)])
        S.stop_at("P4")
        p5 = self.bank(5)
        for i in range(2):
            for hh in range(2):
                for g in range(4):
                    pp = g // 2
                    col = ((i * 2 + hh) * 4 + g) * 32
                    ri = (i * 2 + pp) * 2 + g % 2
                    src_ = kraw[ri]
                    for l in range(32):
                        A("pe", lambda e, i=i, hh=hh, l=l, col=col, src_=src_: e.matmul(
                            out=p5[:, col:col + 32], lhsT=w1[i][:, l * 256 + hh * 128:l * 256 + (hh + 1) * 128],
                            rhs=src_[:, l:l + 16 * 31 + 1:16], start=(l == 0), stop=(l == 31), skip_group_check=True),
                            reads=[("w1", i), ("kraw", ri)], writes=[("ps", 5)])
        for r_ in range(8):
            A("pool", lambda e, r_=r_: e.tensor_copy(out=kraw[r_][:, 0:16], in_=kraw[r_][:, T:T + 16]),
              reads=[("kraw", r_)], writes=[("kraw", r_)])
        S.stop_at("P5")
        for q_ in range(4):
            A("act", lambda e, q_=q_: e.activation(out=hx[:, q_ * 128:(q_ + 1) * 128], in_=p5[:, q_ * 128:(q_ + 1) * 128],
                                                   func=AF.Identity, bias=hb[:, q_:q_ + 1]),
              reads=[("ps", 5), "hb"], writes=["hx"])
        A("dve", lambda e: e.tensor_tensor(out=hy, in0=hx, in1=hx, op=ALU.mult), reads=["hx"], writes=["hy"])
        A("dve", lambda e: e.tensor_scalar(out=hy, in0=hy, scalar1=0.044715, scalar2=1.0, op0=ALU.mult, op1=ALU.add),
          reads=["hy"], writes=["hy"])
        A("dve", lambda e: e.tensor_tensor(out=hy, in0=hy, in1=hx, op=ALU.mult), reads=["hy", "hx"], writes=["hy"])
        A("act", lambda e: e.activation(out=hy, in_=hy, func=AF.Tanh, scale=0.7978845608028654), reads=["hy"], writes=["hy"])
        A("dve", lambda e: e.tensor_scalar(out=hy, in0=hy, scalar1=0.5, scalar2=0.5, op0=ALU.mult, op1=ALU.add),
          reads=["hy"], writes=["hy"])
        A("dve", lambda e: e.tensor_tensor(out=hidT, in0=hy, in1=hx, op=ALU.mult), reads=["hy", "hx"], writes=["hidT"])
        p6 = self.ps[:, 6 * 512:6 * 512 + 128]
        for g in range(4):
            for hh in range(2):
                col = ((0 * 2 + hh) * 4 + g) * 32
                A("pe", lambda e, g=g, hh=hh, col=col: e.matmul(out=p6[:, g * 32:(g + 1) * 32], lhsT=w2k[:, hh * 128:(hh + 1) * 128],
                                                                rhs=hidT[:, col:col + 32], start=(hh == 0), stop=(hh == 1),
                                                                skip_group_check=True),
                  reads=["hidT", "w2k"], writes=[("ps", 6)])
        A("act", lambda e: e.activation(out=kcw, in_=p6, func=AF.Identity, bias=ncs["b2k"]), reads=[("ps", 6), "ncs"], writes=["kcw"])
        A("act", lambda e: e.activation(out=hwk[0][:, 0:128], in_=kcw, func=AF.Square), reads=["kcw"], writes=["kcsq"])
        A("pe", lambda e: e.matmul(out=self.bank(2, 128), lhsT=ncs["ones_bd"], rhs=hwk[0][:, 0:128], start=True, stop=True),
          reads=["kcsq", "ncs"], writes=[("ps", 2)])
        A("act", lambda e: e.activation(out=hwk[1][:, 0:128], in_=self.bank(2, 128), func=AF.Sqrt, bias=self.epsc, scale=1.0 / 64),
          reads=[("ps", 2)], writes=["kcrs"])
        A("dve", lambda e: e.reciprocal(out=hwk[1][:, 0:128], in_=hwk[1][:, 0:128]), reads=["kcrs"], writes=["kcrs"])
        kc3 = kcT.rearrange("p (g n) -> p g n", g=4)[:, :, t * 32:(t + 1) * 32]
        A("dve", lambda e, kc3=kc3: e.scalar_tensor_tensor(out=kc3, in0=v3(kcw, 4), scalar=ncs["kw3"][:, 0:1],
                                                            in1=v3(hwk[1][:, 0:128], 4), op0=ALU.mult, op1=ALU.mult),
          reads=["kcw", "kcrs", "ncs"], writes=[("kcT", t)])
        p6v = self.ps[0:32, 6 * 512 + 128:6 * 512 + 128 + 256]
        for g in range(4):
            for hh in range(2):
                col = ((1 * 2 + hh) * 4 + g) * 32
                A("pe", lambda e, g=g, hh=hh, col=col: e.matmul(out=p6v[:, g * 64:(g + 1) * 64], lhsT=hidT[:, col:col + 32],
                                                                rhs=w2v[:, hh * 64:(hh + 1) * 64], start=(hh == 0), stop=False,
                                                                skip_group_check=True),
                  reads=["hidT", "w2v"], writes=[("ps", 6)])
            A("pe", lambda e, g=g: e.matmul(out=p6v[:, g * 64:(g + 1) * 64], lhsT=one1[0:1, :], rhs=b2v[0:1, :], start=False, stop=True,
                                            skip_group_check=True), reads=["one1", "b2v"], writes=[("ps", 6)])
        for g in range(4):
            off = t * VW + VOFF[g] + (64 if g % 2 else 0)
            A("act", lambda e, g=g, off=off: e.copy(out=vcS[0:32, off:off + 64], in_=p6v[:, g * 64:(g + 1) * 64]),
              reads=[("ps", 6)], writes=[("vcS", t)])

    for t in range(NT):
        tileA(t)
        S.stop_at("P6a")
    S.stop_at("P6")
    self.nsa_st = dict(ncs=ncs, ksT=ksT, kwT=kwT, vsS=vsS, vwS=vwS, kcT=kcT, vcS=vcS, gam=gam)
    self.dbg = dict(ksT=ksT, kwT=kwT, vsS=vsS, vwS=vwS, kcT=kcT, vcS=vcS)
    sb.release(mB)
    S.barrier()
    nsa_queries(self, src, dst, W, pos_d, T, NT, SQ)
    sb.release(m)


K.nsa_phase = nsa_phase


def nsa_queries(self, src, dst, W, pos_d, T, NT, SQ):
    S, sb = self.S, self.sb
    A = S.add
    st = self.nsa_st
    ncs, ksT, kwT, vsS, vwS, kcT, vcS, gam = (st[k_] for k_ in ("ncs", "ksT", "kwT", "vsS", "vwS", "kcT", "vcS", "gam"))

    def v3(ap, a):
        return ap.rearrange("p (a b) -> p a b", a=a)

    nbc = sb.bf16(NBC_W)
    self.wload(nbc, W["nbc"], NBC_W, "nbc")
    nb = {n: nbc[:, o:o + w] for n, (o, w) in NBC.items()}
    x32 = sb.f32(8 * T)
    xn = sb.bf16(8 * T)
    sq = [sb.f32(512) for _ in range(2)]
    rstd = sb.f32(512)
    cosb, sinb = sb.f32(T), sb.f32(T)
    hwk = [sb.f32(T) for _ in range(4)]
    wv = sb.bf16(8 * 512)
    self.wload(wv, W["wv"], 4096, "wv")
    qnT = [[sb.bf16(T) for _ in range(4)] for _ in range(2)]
    qrT = [[sb.bf16(T) for _ in range(4)] for _ in range(2)]
    for hf in range(2):
        for i in range(4):
            A("pool", lambda e, hf=hf, i=i: e.memset(qnT[hf][i], 0.0), writes=[("qn", i, hf)])
            A("pool", lambda e, hf=hf, i=i: e.memset(qrT[hf][i], 0.0), writes=[("qr", i, hf)])
    gt = [sb.bf16(T) for _ in range(12)]
    ogacc = [sb.f32(T) for _ in range(4)]
    ogb = sb.bf16(8 * T)
    impT = sb.f32(T)
    selc = sb.f32(T)
    sc = sb.f32(256)
    sc2 = sb.f32(64)
    m8 = sb.f32(16)
    bm = sb.bf16(4 * 128)
    selbT = sb.bf16(T)
    pT = [sb.bf16(T) for _ in range(4)]
    rz = sb.f32(T)
    rzb = sb.f32(T)
    otmp = sb.f32(T)
    imptmp = sb.f32(T)
    rwk = [rzb, otmp, imptmp]
    rwkeys = ["rzb", "otmp", "imptmp"]
    NW = 3
    wbs = [sb.bf16(1024) for _ in range(NW)]
    wob = [sb.bf16(1024) for _ in range(2)]
    A("pool", lambda e: e.memset(bm, 0.0), writes=["bm"])
    A("pool", lambda e: e.memset(rz, 0.0), writes=["rz"])
    wc = [0, 0]
    pcnt = [0, 0]

    hbc = [0]

    def head_branch(kind, i, g, t, first, ch=0):
        pp, hf = g // 2, g % 2
        q_ap = (qnT if kind == "cmp" else qrT)[hf][i]
        qkey = ("qn" if kind == "cmp" else "qr", i, hf)
        even = (g % 2 == 0)
        M = 65 if even else 128
        voff = VOFF[g]
        ob = 4 + ch
        scb = (2, 3) if ch == 0 else (0, 1)
        okey = ("ps", ob)
        oacc = self.ps[0:M, ob * 512:(ob + 1) * 512]
        if kind == "cmp":
            tiles = list(range(t + 1))
            KP = 32
        elif kind == "sel":
            tiles = list(range(4 * t + 4))
            KP = 128
        else:
            tiles = list(range(max(0, 4 * t - 4), 4 * t + 4))
            KP = 128
        nt_ = len(tiles)
        store, snm = {"cmp": (vcS, "vcS"), "sel": (vsS, "vsS"), "win": (vwS, "vwS")}[kind]

        def emit_pv(n_, kt, pbuf, pkey):
            vi = kt if kind != "win" else ((kt // 4) % 2) * 4 + kt % 4
            vl = store[0:KP, vi * VW + voff:vi * VW + voff + M]
            A("pe", lambda e, vl=vl, pbuf=pbuf, n_=n_: e.matmul(
                out=oacc, lhsT=vl, rhs=pbuf[0:KP, :], start=(n_ == 0), stop=(n_ == nt_ - 1)),
                reads=[pkey, (snm, vi)], writes=[okey])
            if kind == "cmp":
                A("pe", lambda e, kt=kt, pbuf=pbuf, n_=n_: e.matmul(
                    out=self.ps[0:64, 6 * 512:7 * 512], lhsT=nb["ovl"][0:32, kt * 64:(kt + 1) * 64], rhs=pbuf[0:32, :],
                    start=(n_ == 0), stop=(n_ == nt_ - 1)), reads=[pkey, "nbc"], writes=[("ps", 6)])

        pend = None
        for n_, kt in enumerate(tiles):
            sbk = scb[pcnt[ch] % 2]
            pbuf = pT[ch * 2 + pcnt[ch] % 2]
            pkey = ("pT", ch * 2 + pcnt[ch] % 2)
            pcnt[ch] += 1
            ps_s = self.ps[0:KP, sbk * 512:(sbk + 1) * 512]
            mm = []
            if kind == "cmp":
                mm.append((kcT[:, g * 32 * NT + kt * 32:g * 32 * NT + (kt + 1) * 32], q_ap, [("kcT", kt), qkey]))
                if kt == t:
                    mm.append((nb["id128"][:, 0:32], (nb["cmpb0"] if t == 0 else nb["cmpb"]), ["nbc"]))
                elif kt == 0:
                    mm.append((nb["id128"][:, 0:32], nb["cmpr0"], ["nbc"]))
            elif kind == "sel":
                mm.append((ksT[:, pp * SQ + kt * 128:pp * SQ + (kt + 1) * 128], q_ap, [("ksT", pp, kt // 4), qkey]))
                mm.append((nb["efull"][:, kt * 128:(kt + 1) * 128], selbT, ["nbc", "selbT"]))
                if kt >= 4 * t:
                    dd = kt - 4 * t
                    mm.append((nb["id128"], nb["causb"][:, dd * 512:(dd + 1) * 512], ["nbc"]))
            else:
                slot = (kt // 4) % 2
                ko = pp * 1024 + slot * 512 + (kt % 4) * 128
                mm.append((kwT[:, ko:ko + 128], q_ap, [("kwT", pp, slot), qkey]))
                dd = kt - 4 * t
                mask = nb["causb"][:, dd * 512:(dd + 1) * 512] if dd >= 0 else nb["bandb"][:, (dd + 4) * 512:(dd + 5) * 512]
                mm.append((nb["id128"], mask, ["nbc"]))
            for j_, (l_, r_, rd) in enumerate(mm):
                A("pe", lambda e, l_=l_, r_=r_, j_=j_, ps_s=ps_s, last=(j_ == len(mm) - 1): e.matmul(
                    out=ps_s, lhsT=l_, rhs=r_, start=(j_ == 0), stop=last), reads=rd, writes=[("ps", sbk)])
            A("act", lambda e, pbuf=pbuf, ps_s=ps_s: e.activation(out=pbuf[0:KP, :], in_=ps_s, func=AF.Exp),
              reads=[("ps", sbk)], writes=[pkey])
            if pend is not None:
                emit_pv(*pend)
            pend = (n_, kt, pbuf, pkey)
            yield
        emit_pv(*pend)
        yield
        zr = 64 if even else 0
        A("act", lambda e, zr=zr: e.activation(out=rz[zr:zr + 1, :], in_=self.ps[zr:zr + 1, ob * 512:(ob + 1) * 512], func=AF.Ln, bias=1e-18),
          reads=[okey], writes=["rz"])
        A("pe", lambda e, zr=zr: e.matmul(out=self.bank(7), lhsT=ncs["sel64" if zr == 64 else "sel0"], rhs=rz, start=True, stop=True),
          reads=["rz", "ncs"], writes=[("ps", 7)])
        A("act", lambda e: e.activation(out=rzb, in_=self.bank(7), func=AF.Exp, scale=-1.0), reads=[("ps", 7)], writes=["rzb"])
        r_ = {"cmp": 0, "sel": 1, "win": 2}[kind]
        gtile = gt[r_ * 4 + i]
        orow = slice(0, 64) if even else slice(64, 128)
        A("dve", lambda e, orow=orow: e.tensor_tensor(out=otmp[orow, :], in0=self.ps[orow, ob * 512:(ob + 1) * 512], in1=rzb[orow, :], op=ALU.mult),
          reads=[okey, "rzb"], writes=["otmp"])
        if first:
            A("dve", lambda e, orow=orow, gtile=gtile, i=i: e.tensor_tensor(out=ogacc[i][orow, :], in0=otmp[orow, :], in1=gtile[orow, :], op=ALU.mult),
              reads=["otmp", ("gt", r_ * 4 + i)], writes=[("og", i, g % 2)])
        else:
            A("dve", lambda e, orow=orow, gtile=gtile: e.tensor_tensor(out=otmp[orow, :], in0=otmp[orow, :], in1=gtile[orow, :], op=ALU.mult),
              reads=["otmp", ("gt", r_ * 4 + i)], writes=["otmp"])
            A("pool", lambda e, orow=orow, i=i: e.tensor_tensor(out=ogacc[i][orow, :], in0=ogacc[i][orow, :], in1=otmp[orow, :], op=ALU.add),
              reads=["otmp", ("og", i, g % 2)], writes=[("og", i, g % 2)])
        if kind == "cmp":
            if i == 0:
                A("dve", lambda e: e.tensor_tensor(out=impT[0:64, :], in0=self.ps[0:64, 6 * 512:7 * 512], in1=rzb[0:64, :], op=ALU.mult),
                  reads=[("ps", 6), "rzb"], writes=["impT"])
            else:
                A("dve", lambda e: e.tensor_tensor(out=imptmp[0:64, :], in0=self.ps[0:64, 6 * 512:7 * 512],
                                                   in1=rzb[0:64, :], op=ALU.mult), reads=[("ps", 6), "rzb"], writes=["imptmp"])
                A("pool", lambda e: e.tensor_tensor(out=impT[0:64, :], in0=impT[0:64, :], in1=imptmp[0:64, :], op=ALU.add),
                  reads=["imptmp", "impT"], writes=["impT"])

    def sel_mask(g, t):
        pm = self.ps[:, 6 * 512:6 * 512 + 256]
        for blk in range(4):
            A("pe", lambda e, blk=blk: e.matmul(out=pm[:, blk * 64:(blk + 1) * 64], lhsT=impT[0:64, blk * 128:(blk + 1) * 128],
                                                rhs=ncs["id128"][0:64, 0:64], start=True, stop=True, skip_group_check=True),
              reads=["impT", "ncs"], writes=[("ps", 6)])
        valid = selc[:, 0:256]
        addm = selc[:, 256:512]
        A("dve", lambda e: e.tensor_tensor(out=sc, in0=pm, in1=valid, op=ALU.mult), reads=[("ps", 6), "selc"], writes=["sc"])
        A("dve", lambda e: e.tensor_tensor(out=sc, in0=sc, in1=addm, op=ALU.add), reads=["sc", "selc"], writes=["sc"])
        for blk in range(4):
            sblk = sc[:, blk * 64:(blk + 1) * 64]
            A("dve", lambda e, sblk=sblk: e.max(out=m8[:, 0:8], in_=sblk), reads=["sc"], writes=["m8"])
            A("dve", lambda e, sblk=sblk: e.match_replace(out=sc2, in_to_replace=m8[:, 0:8], in_values=sblk, imm_value=-1e30),
              reads=["sc", "m8"], writes=["sc2"])
            A("dve", lambda e: e.max(out=m8[:, 8:16], in_=sc2), reads=["sc2"], writes=["m8"])
            A("dve", lambda e, sblk=sblk: e.tensor_scalar(out=sc2, in0=sblk, scalar1=m8[:, 15:16], scalar2=None, op0=ALU.is_ge),
              reads=["sc", "m8"], writes=["sc2"])
            A("dve", lambda e, blk=blk: e.tensor_tensor(out=sc2, in0=sc2, in1=valid[:, blk * 64:(blk + 1) * 64], op=ALU.mult),
              reads=["sc2", "selc"], writes=["sc2"])
            A("dve", lambda e, blk=blk: e.tensor_scalar(out=bm[:, blk * 128:blk * 128 + 64], in0=sc2, scalar1=-NEGB, scalar2=NEGB,
                                                        op0=ALU.mult, op1=ALU.add), reads=["sc2"], writes=["bm"])
        pm2 = self.ps[:, 6 * 512:7 * 512]
        for blk in range(4):
            A("pe", lambda e, blk=blk: e.matmul(out=pm2[:, blk * 128:(blk + 1) * 128], lhsT=bm[:, blk * 128:(blk + 1) * 128],
                                                rhs=nb["id128"], start=True, stop=True, skip_group_check=True),
              reads=["bm", "nbc"], writes=[("ps", 6)])
        A("act", lambda e: e.copy(out=selbT, in_=pm2), reads=[("ps", 6)], writes=["selbT"])

    def tileB(t):
        self.load_x_tile(src, x32, "x32", t, T)
        self.rmsnorm_tile(x32, "x32", xn, "xn", gam, "gam", T, sq, rstd, 7, "n")
        self.rope_tables(pos_d, t, T, cosb, sinb, rwk, ncs["inv"], "rope", rwkeys)
        A("sp", lambda e: e.dma_start(out=selc[:, 0:512], in_=W["selc"][t]), writes=["selc"], dkey="selc")
        slot = t % 2
        for pp in range(2):
            wi = wc[0] % NW
            wc[0] += 1
            self.wload(wbs[wi], W["wka"][6 + pp], 1024, ("win", wi))
            pbk = pp % 2
            pt = self.bank(pbk)
            for k in range(8):
                A("pe", lambda e, k=k, pt=pt, wi=wi: e.matmul(out=pt, lhsT=wbs[wi][:, k * 128:(k + 1) * 128],
                                                            rhs=xn[:, k * T:(k + 1) * T], start=(k == 0), stop=(k == 7)),
                  reads=[("win", wi), ("xn", k, 0)], writes=[("ps", pbk)])
            o_ap = kwT[:, pp * 1024 + slot * 512:pp * 1024 + (slot + 1) * 512]
            self.headnorm_rope(pt, ("ps", pbk), ncs["kw3"][:, 2:3], cosb, sinb, "hn",
                               [(o_ap, True, ("kwT", pp, slot))], 1.0, hwk, ncs, T)
        for blk in range(4):
            vi = slot * 4 + blk
            pv = self.bank(4, 256)
            for k in range(8):
                A("pe", lambda e, k=k, blk=blk, pv=pv: e.matmul(
                    out=pv, lhsT=xn[:, k * T + blk * 128:k * T + (blk + 1) * 128], rhs=wv[:, k * 512 + 256:(k + 1) * 512],
                    start=(k == 0), stop=(k == 7)), reads=[("xn", k, 0), "wv"], writes=[("ps", 4)])
            for g in range(4):
                off = vi * VW + VOFF[g] + (64 if g % 2 else 0)
                A("act", lambda e, off=off, g=g, pv=pv: e.copy(out=vwS[:, off:off + 64], in_=pv[:, g * 64:(g + 1) * 64]),
                  reads=[("ps", 4)], writes=[("vwS", vi)])
        for pp in range(2):
            for i in range(4):
                wi = wc[0] % NW
                wc[0] += 1
                self.wload(wbs[wi], W["wq"][pp * 4 + i], 1024, ("win", wi))
                pbk = i % 2
                pt = self.bank(pbk)
                for k in range(8):
                    A("pe", lambda e, k=k, pt=pt, wi=wi: e.matmul(out=pt, lhsT=wbs[wi][:, k * 128:(k + 1) * 128],
                                                                rhs=xn[:, k * T:(k + 1) * T], start=(k == 0), stop=(k == 7)),
                      reads=[("win", wi), ("xn", k, 0)], writes=[("ps", pbk)])
                self.headnorm_rope(pt, ("ps", pbk), ncs["qw"], cosb, sinb, "hn",
                                   [(qnT[0][i], False, ("qn", i, 0), slice(0, 64)), (qnT[1][i], False, ("qn", i, 1), slice(64, 128)),
                                    (qrT[0][i], True, ("qr", i, 0), slice(0, 64)), (qrT[1][i], True, ("qr", i, 1), slice(64, 128))],
                                   0.125, hwk, ncs, T)
            for r_ in range(3):
                for i in range(4):
                    wi = wc[0] % NW
                    wc[0] += 1
                    self.wload(wbs[wi], W["wg"][pp * 12 + r_ * 4 + i], 1024, ("win", wi))
                    pbk = i % 2
                    pt = self.bank(pbk)
                    for k in range(8):
                        A("pe", lambda e, k=k, pt=pt, wi=wi: e.matmul(out=pt, lhsT=wbs[wi][:, k * 128:(k + 1) * 128],
                                                                    rhs=xn[:, k * T:(k + 1) * T], start=(k == 0), stop=(k == 7)),
                          reads=[("win", wi), ("xn", k, 0)], writes=[("ps", pbk)])
                    A("act", lambda e, r_=r_, i=i, pt=pt: e.activation(out=gt[r_ * 4 + i], in_=pt, func=AF.Sigmoid),
                      reads=[("ps", pbk)], writes=[("gt", r_ * 4 + i)])
            S.stop_at("P7")
            def run_gens(gens):
                gens = list(gens)
                while gens:
                    for gn in list(gens):
                        try:
                            next(gn)
                        except StopIteration:
                            gens.remove(gn)

            for g in (2 * pp, 2 * pp + 1):
                for i in range(4):
                    run_gens([head_branch("cmp", i, g, t, True, i % 2)])
                sel_mask(g, t)
                for i in (0, 2):
                    run_gens([head_branch("sel", i, g, t, False, 0), head_branch("sel", i + 1, g, t, False, 1)])
                for i in (0, 2):
                    run_gens([head_branch("win", i, g, t, False, 0), head_branch("win", i + 1, g, t, False, 1)])
            for i in range(4):
                A("act", lambda e, pp=pp, i=i: e.copy(out=ogb[:, (pp * 4 + i) * T:(pp * 4 + i + 1) * T], in_=ogacc[i]),
                  reads=[("og", i, 0), ("og", i, 1)], writes=[("ogb", pp * 4 + i)])
        for o in range(8):
            wi = wc[1] % 2
            wc[1] += 1
            self.wload(wob[wi], W["wo"][o], 1024, ("wout", wi))
            pbk = o % 2
            pt = self.bank(pbk)
            for k in range(8):
                A("pe", lambda e, k=k, pt=pt, wi=wi: e.matmul(out=pt, lhsT=wob[wi][:, k * 128:(k + 1) * 128],
                                                            rhs=ogb[:, k * T:(k + 1) * T], start=(k == 0), stop=(k == 7)),
                  reads=[("wout", wi), ("ogb", k)], writes=[("ps", pbk)])
            xs = x32[:, o * T:(o + 1) * T]
            A("dve", lambda e, xs=xs, pt=pt: e.tensor_tensor(out=xs, in0=pt, in1=xs, op=ALU.add),
              reads=[("ps", pbk), ("x32", o)], writes=[("x32", o)])
            A("act", lambda e, o=o, t=t: e.dma_start(out=dst[o * 128:(o + 1) * 128, t * T:(t + 1) * T], in_=x32[:, o * T:(o + 1) * T]),
              reads=[("x32", o)], dkey=("st", o))

    for t in range(NT):
        tileB(t)
```

```python
import contextlib
import numpy as np
import concourse.bass as bass
import concourse.mybir as mybir
from concourse.bass_utils import run_bass_kernel_spmd

F32 = mybir.dt.float32
BF16 = mybir.dt.bfloat16
I32 = mybir.dt.int32
AF = mybir.ActivationFunctionType
ALU = mybir.AluOpType
AX = mybir.AxisListType

D = 1024
SEQ = 4096
DEPTH = 4
DFF = 2816
NCH = DFF // 128
EPS = 1e-6

ENGS = ("pe", "act", "dve", "pool", "sp")


class Op:
    __slots__ = ("eng", "fn", "deps", "sig", "cnt", "dkey", "dcnt", "reads", "writes", "bar")

    def __init__(self, eng, fn, reads, writes, dkey):
        self.eng = eng
        self.fn = fn
        self.reads = reads
        self.writes = writes
        self.dkey = dkey
        self.deps = []
        self.sig = False
        self.cnt = 0
        self.dcnt = 0
        self.bar = None


class Sched:
    def __init__(self, nc):
        self.nc = nc
        self.ops = {e: [] for e in ENGS}
        self.res = {}
        self.dma_cnt = {}
        self.nops = 0
        self.cut = False

    def add(self, eng, fn, reads=(), writes=(), dkey=None, ndma=1):
        if self.cut:
            return None
        reads = tuple(reads)
        writes = tuple(writes)
        op = Op(eng, fn, reads, writes, dkey)
        deps = {}
        for k in reads:
            r = self.res.get(k)
            if r is not None and r[0] is not None:
                deps[id(r[0])] = r[0]
        for k in writes:
            r = self.res.get(k)
            if r is not None:
                if r[0] is not None:
                    deps[id(r[0])] = r[0]
                for q in r[1]:
                    deps[id(q)] = q
        final = []
        rset = set(reads)
        wset = set(writes)
        for d in deps.values():
            if d is op:
                continue
            if d.dkey is None and dkey is None and d.eng == eng:
                if eng == "pe":
                    continue
                if eng != "pool" and not (rset.intersection(d.writes)) and not (wset.intersection(d.writes)):
                    continue
            final.append(d)
            if d.dkey is None:
                d.sig = True
        op.deps = final
        for k in reads:
            r = self.res.get(k)
            if r is None:
                self.res[k] = [None, [op]]
            else:
                r[1].append(op)
        for k in writes:
            self.res[k] = [op, []]
        if dkey is not None:
            self.dma_cnt[dkey] = self.dma_cnt.get(dkey, 0) + 16 * ndma
            op.dcnt = self.dma_cnt[dkey]
        self.ops[eng].append(op)
        self.nops += 1
        return op

    def stop_at(self, name):
        import os
        if os.environ.get("NSA_STOP") == name:
            self.cut = True

    def barrier(self):
        if self.cut:
            return
        snap_ops = {}
        for e in ENGS:
            last = None
            for o in reversed(self.ops[e]):
                if o.bar is None and o.dkey is None:
                    last = o
                    break
            if last is not None:
                last.sig = True
                snap_ops[e] = last
        dsnap = dict(self.dma_cnt)
        for e in ENGS:
            op = Op(e, None, (), (), None)
            op.bar = (dict(snap_ops), dsnap)
            self.ops[e].append(op)
        self.res = {}

    def emit(self):
        nc = self.nc
        with contextlib.ExitStack() as st:
            esem = {e: st.enter_context(nc.semaphore("s_" + e)) for e in ENGS}
            dsem = {k: st.enter_context(nc.semaphore("d_%d" % i)) for i, k in enumerate(self.dma_cnt)}
            for e in ENGS:
                c = 0
                for o in self.ops[e]:
                    if o.sig:
                        c += 1
                    o.cnt = c
            block = st.enter_context(nc.Block())
            final_d = dict(self.dma_cnt)

            def run(e, eng):
                seen = {}

                def wait(sem, key, val):
                    if val <= 0 or seen.get(key, 0) >= val:
                        return
                    seen[key] = val
                    eng.wait_ge(sem, val)

                for o in self.ops[e]:
                    if o.bar is not None:
                        so, ds = o.bar
                        for e2, lo in so.items():
                            if e2 != e:
                                wait(esem[e2], ("e", e2), lo.cnt)
                        for k, v in ds.items():
                            wait(dsem[k], ("d", k), v)
                        continue
                    need = {}
                    for d in o.deps:
                        if d.dkey is not None:
                            key, v, sem = ("d", d.dkey), d.dcnt, dsem[d.dkey]
                        else:
                            key, v, sem = ("e", d.eng), d.cnt, esem[d.eng]
                        if need.get(key, (None, 0))[1] < v:
                            need[key] = (sem, v)
                    for key, (sem, v) in need.items():
                        wait(sem, key, v)
                    r = o.fn(eng)
                    if o.dkey is not None:
                        if not isinstance(r, (list, tuple)):
                            r = [r]
                        for ins in r:
                            ins.then_inc(dsem[o.dkey], 16)
                    elif o.sig:
                        r.then_inc(esem[e], 1)
                if e == "sp":
                    for k, v in final_d.items():
                        wait(dsem[k], ("d", k), v)
                    for e2 in ENGS:
                        if e2 != e:
                            for o in reversed(self.ops[e2]):
                                if o.sig:
                                    wait(esem[e2], ("e", e2), o.cnt)
                                    break

            @block.tensor
            def _(eng):
                run("pe", eng)

            @block.scalar
            def _(eng):
                run("act", eng)

            @block.vector
            def _(eng):
                run("dve", eng)

            @block.gpsimd
            def _(eng):
                run("pool", eng)

            @block.sync
            def _(eng):
                run("sp", eng)


class SBAlloc:
    def __init__(self, big, ncols):
        self.big = big
        self.ncols = ncols
        self.top = 0
        self.peak = 0

    def mark(self):
        return self.top

    def release(self, m):
        self.top = m

    def f32(self, n):
        o = self.top
        self.top += n
        assert self.top <= self.ncols, "SBUF overflow %d > %d" % (self.top, self.ncols)
        self.peak = max(self.peak, self.top)
        return self.big[:, o:o + n]

    def bf16(self, n):
        w = (n + 1) // 2
        return self.f32(w).bitcast(BF16)[:, 0:n]


SB_COLS = 53000


class K:
    def __init__(self, ntok=SEQ, T=1024):
        self.ntok = ntok
        self.T = T
        self.nc = bass.Bass("TRN2", target_bir_lowering=False)
        self.st = contextlib.ExitStack()
        self.dram = {}

    def din(self, name, shape, dt=F32):
        ap = self.nc.dram_tensor(name, list(shape), dt, kind="ExternalInput").ap()
        self.dram[name] = ap
        return ap

    def dout(self, name, shape, dt=F32):
        ap = self.nc.dram_tensor(name, list(shape), dt, kind="ExternalOutput").ap()
        self.dram[name] = ap
        return ap

    def begin(self):
        nc = self.nc
        big = self.st.enter_context(nc.sbuf_tensor("SB", [128, SB_COLS], F32))
        self.ps = self.st.enter_context(nc.psum_tensor("PS", [128, 8 * 512], F32))
        self.sb = SBAlloc(big, SB_COLS)
        self.S = Sched(nc)
        S = self.S
        sb = self.sb
        self.ones32 = sb.f32(128)
        self.epsc = sb.f32(1)
        S.add("pool", lambda e: e.memset(self.ones32, 1.0), writes=["ones32"])
        S.add("pool", lambda e: e.memset(self.epsc, EPS), writes=["epsc"])

    def bank(self, b, n=512):
        return self.ps[:, b * 512:b * 512 + n]

    def finish(self):
        self.S.emit()
        self.st.close()
        return self.nc

    def rmsnorm_tile(self, x32, xkey, xn, xnkey, gamma, gkey, T, sq, rstd, psb, tag):
        S = self.S
        for s in range(T // 512):
            pt = self.bank(psb)
            for k in range(8):
                xs = x32[:, k * T + s * 512:k * T + (s + 1) * 512]
                q = sq[k % 2]
                S.add("act", lambda e, q=q, xs=xs: e.activation(out=q, in_=xs, func=AF.Square),
                      reads=[(xkey, k)], writes=[(tag + "sq", k % 2)])
                S.add("pe", lambda e, q=q, k=k, pt=pt: e.matmul(out=pt, lhsT=self.ones32, rhs=q,
                                                                start=(k == 0), stop=(k == 7)),
                      reads=[(tag + "sq", k % 2), "ones32"], writes=[("ps", psb)])
            S.add("act", lambda e, pt=pt: e.activation(out=rstd, in_=pt, func=AF.Sqrt, bias=self.epsc, scale=1.0 / D),
                  reads=[("ps", psb), "epsc"], writes=[tag + "rstd"])
            S.add("dve", lambda e: e.reciprocal(out=rstd, in_=rstd), reads=[tag + "rstd"], writes=[tag + "rstd"])
            for k in range(8):
                xs = x32[:, k * T + s * 512:k * T + (s + 1) * 512]
                xo = xn[:, k * T + s * 512:k * T + (s + 1) * 512]
                S.add("dve", lambda e, xs=xs, xo=xo, k=k: e.scalar_tensor_tensor(
                    out=xo, in0=xs, scalar=gamma[:, k:k + 1], in1=rstd, op0=ALU.mult, op1=ALU.mult),
                    reads=[(xkey, k), tag + "rstd", gkey], writes=[(xnkey, k, s)])

    def ffn_phase(self, src, dst, wgu, wd, gamma_col):
        S, sb = self.S, self.sb
        S.barrier()
        m = sb.mark()
        T = self.T
        NT = self.ntok // T
        NS = T // 512
        x32 = [sb.f32(8 * T) for _ in range(2)]
        xn = [sb.bf16(8 * T) for _ in range(2)]
        h = sb.bf16(NCH * T)
        NWG = 3
        wgb = [sb.bf16(2 * 8 * 128) for _ in range(NWG)]
        wdb = [sb.bf16(NCH * 128) for _ in range(2)]
        sq = [sb.f32(512) for _ in range(2)]
        rstd = sb.f32(512)
        sg = [sb.f32(512) for _ in range(2)]
        gam = sb.f32(8)
        S.add("sp", lambda e: e.dma_start(out=gam, in_=gamma_col), writes=["gam"], dkey="gam")

        def load_x(t):
            b = t % 2
            for k in range(8):
                S.add("sp", lambda e, k=k, b=b, t=t: e.dma_start(
                    out=x32[b][:, k * T:(k + 1) * T], in_=src[k * 128:(k + 1) * 128, t * T:(t + 1) * T]),
                    writes=[("x32_%d" % b, k)], dkey=("x32", b, k))

        def norm(t):
            b = t % 2
            self.rmsnorm_tile(x32[b], "x32_%d" % b, xn[b], "xn_%d" % b, gam, "gam", T, sq, rstd, 6, "f")

        wcount = [0, 0]

        def gateup(t):
            b = t % 2
            for c in range(NCH):
                wi = wcount[0] % NWG
                wcount[0] += 1
                wb = wgb[wi]
                S.add("pool", lambda e, c=c, wb=wb: [
                    e.dma_start(out=wb[:, j * 1024:(j + 1) * 1024], in_=wgu[c, :, j * 1024:(j + 1) * 1024])
                    for j in range(2)], writes=[("wg", wi)], dkey=("wg", wi), ndma=2)
                if c == 3 and t + 1 < NT and self.pipe and stage != 4:
                    load_x(t + 1)
                for s in range(NS):
                    gb = 0 + (s % 2) * 2
                    ub = 1 + (s % 2) * 2
                    pg, pu = self.bank(gb), self.bank(ub)
                    for j, pt, pb in ((0, pg, gb), (1, pu, ub)):
                        for k in range(8):
                            S.add("pe", lambda e, j=j, k=k, pt=pt, wb=wb, s=s: e.matmul(
                                out=pt, lhsT=wb[:, (j * 8 + k) * 128:(j * 8 + k + 1) * 128],
                                rhs=xn[b][:, k * T + s * 512:k * T + (s + 1) * 512],
                                start=(k == 0), stop=(k == 7)),
                                reads=[("wg", wi), ("xn_%d" % b, k, s)], writes=[("ps", pb)])
                    sgt = sg[s % 2]
                    S.add("act", lambda e, sgt=sgt, pg=pg: e.activation(out=sgt, in_=pg, func=AF.Silu),
                          reads=[("ps", gb)], writes=[("sg", s % 2)])
                    ho = h[:, c * T + s * 512:c * T + (s + 1) * 512]
                    S.add("dve", lambda e, ho=ho, sgt=sgt, pu=pu: e.tensor_tensor(out=ho, in0=pu, in1=sgt, op=ALU.mult),
                          reads=[("ps", ub), ("sg", s % 2)], writes=[("h", c, s)])

        def down(t):
            b = t % 2
            for o in range(8):
                wi = wcount[1] % 2
                wcount[1] += 1
                wb = wdb[wi]
                S.add("pool", lambda e, o=o, wb=wb: [
                    e.dma_start(out=wb[:, j * 1408:(j + 1) * 1408], in_=wd[o, :, j * 1408:(j + 1) * 1408])
                    for j in range(2)], writes=[("wd", wi)], dkey=("wd", wi), ndma=2)
                for s in range(NS):
                    pb = 4 + (o * NS + s) % 2
                    pt = self.bank(pb)
                    for c in range(NCH):
                        S.add("pe", lambda e, c=c, pt=pt, wb=wb, s=s: e.matmul(
                            out=pt, lhsT=wb[:, c * 128:(c + 1) * 128],
                            rhs=h[:, c * T + s * 512:c * T + (s + 1) * 512],
                            start=(c == 0), stop=(c == NCH - 1)),
                            reads=[("wd", wi), ("h", c, s)], writes=[("ps", pb)])
                    xs = x32[b][:, o * T + s * 512:o * T + (s + 1) * 512]
                    S.add("dve", lambda e, xs=xs, pt=pt: e.scalar_tensor_tensor(
                        out=xs, in0=pt, scalar=0.5, in1=xs, op0=ALU.mult, op1=ALU.add),
                        reads=[("ps", pb), ("x32_%d" % b, o)], writes=[("x32_%d" % b, o)])
                S.add("act", lambda e, o=o, b=b, t=t: e.dma_start(
                    out=dst[o * 128:(o + 1) * 128, t * T:(t + 1) * T], in_=x32[b][:, o * T:(o + 1) * T]),
                    reads=[("x32_%d" % b, o)], dkey=("st", b, o))

        import os
        stage = int(os.environ.get("FFN_STAGE", "9"))
        load_x(0)
        norm(0)
        if stage == 0:
            for k in range(8):
                S.add("dve", lambda e, k=k: e.tensor_copy(out=x32[0][:, k * T:(k + 1) * T], in_=xn[0][:, k * T:(k + 1) * T]),
                      reads=[("xn_0", k, 0), ("xn_0", k, 1)], writes=[("x32_0", k)])
                S.add("act", lambda e, k=k: e.dma_start(out=dst[k * 128:(k + 1) * 128, 0:T], in_=x32[0][:, k * T:(k + 1) * T]),
                      reads=[("x32_0", k)], dkey=("st", 0, k))
            sb.release(m)
            return
        self.pipe = stage != 2
        for t in range(NT):
            if not self.pipe and t > 0:
                load_x(t)
                norm(t)
            gateup(t)
            if stage == 1:
                for k in range(8):
                    S.add("dve", lambda e, k=k: e.tensor_copy(out=x32[0][:, k * T:(k + 1) * T], in_=h[:, k * T:(k + 1) * T]),
                          reads=[("h", k, 0), ("h", k, 1)], writes=[("x32_0", k)])
                    S.add("act", lambda e, k=k: e.dma_start(out=dst[k * 128:(k + 1) * 128, 0:T], in_=x32[0][:, k * T:(k + 1) * T]),
                          reads=[("x32_0", k)], dkey=("st", 0, k))
                sb.release(m)
                return
            if stage == 4 and t + 1 < NT:
                load_x(t + 1)
            if t + 1 < NT and self.pipe and stage != 3:
                norm(t + 1)
            down(t)
            if t + 1 < NT and stage == 3:
                norm(t + 1)
        sb.release(m)


    def wload(self, wb, src2d, n, key):
        S = self.S
        pieces = [(a, min(a + 2048, n)) for a in range(0, n, 2048)]
        S.add("pool", lambda e: [e.dma_start(out=wb[:, a:b], in_=src2d[:, a:b]) for a, b in pieces],
              writes=[key], dkey=key, ndma=len(pieces))

    def load_x_tile(self, src, x32, xkey, t, T, eng="sp"):
        for k in range(8):
            self.S.add(eng, lambda e, k=k: e.dma_start(
                out=x32[:, k * T:(k + 1) * T], in_=src[k * 128:(k + 1) * 128, t * T:(t + 1) * T]),
                writes=[(xkey, k)], dkey=(xkey, k))

    def sc_phase(self, src, dst, w_in, w_out, cw_d, gamma_col):
        S, sb = self.S, self.sb
        S.barrier()
        m = sb.mark()
        T = self.T
        NT = self.ntok // T
        NS = T // 512
        x32 = sb.f32(8 * T)
        xn = sb.bf16(8 * T)
        zb = sb.f32(8 * (T + 2))
        v = sb.bf16(8 * T)
        NW = 6
        wbs = [sb.bf16(1024) for _ in range(NW)]
        wob = [sb.bf16(1024) for _ in range(2)]
        sq = [sb.f32(512) for _ in range(2)]
        rstd = sb.f32(512)
        csb = [sb.f32(512) for _ in range(2)]
        ysb = [sb.f32(512) for _ in range(2)]
        gam = sb.f32(8)
        cw = sb.f32(24)
        S.add("sp", lambda e: e.dma_start(out=gam, in_=gamma_col), writes=["gam"], dkey="gam")
        S.add("sp", lambda e: e.dma_start(out=cw, in_=cw_d), writes=["cw"], dkey="cw")
        for j in range(8):
            S.add("pool", lambda e, j=j: e.memset(zb[:, j * (T + 2):j * (T + 2) + 2], 0.0), writes=[("zh", j)])
        wc = [0, 0]
        for t in range(NT):
            self.load_x_tile(src, x32, "x32", t, T)
            self.rmsnorm_tile(x32, "x32", xn, "xn", gam, "gam", T, sq, rstd, 6, "s")
            for j in range(8):
                wl = []
                for q in range(3):
                    oc = (1, 2, 0)[q] * 8 + j
                    wi = wc[0] % NW
                    wc[0] += 1
                    self.wload(wbs[wi], w_in[oc], 1024, ("win", wi))
                    wl.append(wi)
                z0 = j * (T + 2)
                for s_ in range(NS):
                    banks = (0 + 3 * (s_ % 2), 1 + 3 * (s_ % 2), 2 + 3 * (s_ % 2))
                    for q in range(3):
                        pt = self.bank(banks[q])
                        wb = wbs[wl[q]]
                        for k in range(8):
                            S.add("pe", lambda e, k=k, pt=pt, wb=wb, s_=s_: e.matmul(
                                out=pt, lhsT=wb[:, k * 128:(k + 1) * 128],
                                rhs=xn[:, k * T + s_ * 512:k * T + (s_ + 1) * 512],
                                start=(k == 0), stop=(k == 7)),
                                reads=[("win", wl[q]), ("xn", k, s_)], writes=[("ps", banks[q])])
                    pc, px, pbg = self.bank(banks[0]), self.bank(banks[1]), self.bank(banks[2])
                    cs, ys = csb[s_ % 2], ysb[s_ % 2]
                    zc = zb[:, z0 + 2 + s_ * 512:z0 + 2 + (s_ + 1) * 512]
                    zm1 = zb[:, z0 + 1 + s_ * 512:z0 + 1 + (s_ + 1) * 512]
                    zm2 = zb[:, z0 + s_ * 512:z0 + (s_ + 1) * 512]
                    S.add("act", lambda e, cs=cs, pc=pc: e.copy(out=cs, in_=pc),
                          reads=[("ps", banks[0])], writes=[("cs", s_ % 2)])
                    S.add("dve", lambda e, zc=zc, px=px, cs=cs: e.tensor_tensor(out=zc, in0=px, in1=cs, op=ALU.mult),
                          reads=[("ps", banks[1]), ("cs", s_ % 2)], writes=[("z", j, s_)])
                    S.add("act", lambda e, ys=ys, zc=zc, j=j: e.activation(out=ys, in_=zc, func=AF.Identity,
                                                                        scale=cw[:, j * 3 + 2:j * 3 + 3]),
                          reads=[("z", j, s_), "cw"], writes=[("ys", s_ % 2)])
                    hk = [("z", j, s_ - 1)] if s_ > 0 else [("zh", j)]
                    S.add("dve", lambda e, ys=ys, zm1=zm1, j=j: e.scalar_tensor_tensor(
                        out=ys, in0=zm1, scalar=cw[:, j * 3 + 1:j * 3 + 2], in1=ys, op0=ALU.mult, op1=ALU.add),
                        reads=[("z", j, s_), ("ys", s_ % 2), "cw"] + hk, writes=[("ys", s_ % 2)])
                    S.add("dve", lambda e, ys=ys, zm2=zm2, j=j: e.scalar_tensor_tensor(
                        out=ys, in0=zm2, scalar=cw[:, j * 3:j * 3 + 1], in1=ys, op0=ALU.mult, op1=ALU.add),
                        reads=[("z", j, s_), ("ys", s_ % 2), "cw"] + hk, writes=[("ys", s_ % 2)])
                    vo = v[:, j * T + s_ * 512:j * T + (s_ + 1) * 512]
                    S.add("dve", lambda e, vo=vo, pbg=pbg, ys=ys: e.tensor_tensor(out=vo, in0=pbg, in1=ys, op=ALU.mult),
                          reads=[("ps", banks[2]), ("ys", s_ % 2)], writes=[("v", j, s_)])
                S.add("pool", lambda e, z0=z0: e.tensor_copy(out=zb[:, z0:z0 + 2], in_=zb[:, z0 + T:z0 + T + 2]),
                      reads=[("z", j, s2) for s2 in range(NS)], writes=[("zh", j)])
            for o in range(8):
                wi = wc[1] % 2
                wc[1] += 1
                self.wload(wob[wi], w_out[o], 1024, ("wout", wi))
                for s_ in range(NS):
                    pb = 6 + (o * NS + s_) % 2
                    pt = self.bank(pb)
                    for k in range(8):
                        S.add("pe", lambda e, k=k, pt=pt, wi=wi, s_=s_: e.matmul(
                            out=pt, lhsT=wob[wi][:, k * 128:(k + 1) * 128],
                            rhs=v[:, k * T + s_ * 512:k * T + (s_ + 1) * 512],
                            start=(k == 0), stop=(k == 7)),
                            reads=[("wout", wi), ("v", k, s_)], writes=[("ps", pb)])
                    xs = x32[:, o * T + s_ * 512:o * T + (s_ + 1) * 512]
                    S.add("dve", lambda e, xs=xs, pt=pt: e.tensor_tensor(out=xs, in0=pt, in1=xs, op=ALU.add),
                          reads=[("ps", pb), ("x32", o)], writes=[("x32", o)])
                S.add("act", lambda e, o=o, t=t: e.dma_start(
                    out=dst[o * 128:(o + 1) * 128, t * T:(t + 1) * T], in_=x32[:, o * T:(o + 1) * T]),
                    reads=[("x32", o)], dkey=("st", o))
        sb.release(m)


def prep_proj(w, kch=8):
    Kd, N = w.shape
    assert Kd == kch * 128 and N % 128 == 0
    a = w.reshape(kch, 128, N // 128, 128)
    return np.ascontiguousarray(a.transpose(2, 1, 0, 3)).reshape(N // 128, 128, kch * 128)


def prep_ffn_weights(w_gate_up, w_down):
    w = w_gate_up.reshape(8, 128, 2, NCH, 128)
    wgu = np.ascontiguousarray(w.transpose(3, 1, 2, 0, 4)).reshape(NCH, 128, 2 * 8 * 128)
    w2 = w_down.reshape(NCH, 128, 8, 128)
    wd = np.ascontiguousarray(w2.transpose(2, 1, 0, 3)).reshape(8, 128, NCH * 128)
    return wgu, wd


def norm_cols(w):
    return np.ascontiguousarray(np.asarray(w, np.float32).reshape(8, 128).T)


N_NORM = DEPTH * 3
def nsa_shapes():
    return dict(wka=(12, 128, 1024), wv=(128, 4096), w1=(2, 128, 4096), peT=(2, 128, 16), w2k=(128, 256), w2v=(128, 128),
                b2v=(1, 64), ncs=(128, NCS_W), nbc=(128, NBC_W), selc=(8, 128, 512), wq=(8, 128, 1024),
                wg=(24, 128, 1024), wo=(8, 128, 1024))


def build_program(ntok=SEQ, layers=DEPTH, skip=()):
    k = K(ntok=ntok)
    xT = k.din("xT", [D, ntok])
    yT = k.dout("yT", [D, ntok])
    gam = k.din("gam", [128, 8 * N_NORM])
    wgu = [[k.din("wgu_%d_%d" % (l, f), [NCH, 128, 2048]) for f in range(2)] for l in range(layers)]
    wd = [[k.din("wd_%d_%d" % (l, f), [8, 128, NCH * 128]) for f in range(2)] for l in range(layers)]
    sc_in = k.din("sc_w_in", [24, 128, 1024])
    sc_out = k.din("sc_w_out", [8, 128, 1024])
    sc_cw = k.din("sc_cw", [128, 24])
    cst = k.din("cst", [128, CST_W])
    gd = []
    for j in range(2):
        p = "gdn%d_" % j
        gd.append(dict(w_in=k.din(p + "w_in", [32, 128, 1024]), wab=k.din(p + "wab", [128, 128]), cw=k.din(p + "cw", [128, 96]),
                       alog=k.din(p + "alog", [128, 8]), dtb=k.din(p + "dtb", [128, 8]), onw=k.din(p + "onw", [128, 1]),
                       w_out=k.din(p + "w_out", [8, 128, 1024])))
    pos = k.din("pos", [1, ntok], I32)
    nsaW = {n: k.din("nsa_" + n, list(shp)) for n, shp in nsa_shapes().items()}
    k.begin()

    def g(i):
        return gam[:, i * 8:(i + 1) * 8]

    for l in range(layers):
        k.ffn_phase(xT if l == 0 else yT, yT, wgu[l][0], wd[l][0], g(l * 3 + 0))
        kind = l % 3
        if kind == 1 and "sc" not in skip:
            k.sc_phase(yT, yT, sc_in, sc_out, sc_cw, g(l * 3 + 1))
        if kind == 2 and "nsa" not in skip:
            k.nsa_phase(yT, yT, nsaW, pos, g(l * 3 + 1))
        if kind == 0 and "gdn" not in skip:
            q = gd[l // 3]
            k.gdn_phase(yT, yT, q["w_in"], q["wab"], q["cw"], q["alog"], q["dtb"], q["onw"], q["w_out"], cst, g(l * 3 + 1))
        k.ffn_phase(yT, yT, wgu[l][1], wd[l][1], g(l * 3 + 2))
    nc = k.finish()
    return k, nc


def prep_inputs(inp, layers=DEPTH):
    f = lambda a: np.asarray(a, dtype=np.float32)
    com = {}
    gcols = []
    for l in range(DEPTH):
        gcols += [norm_cols(f(inp["ffn_norm"])[l, 0]), norm_cols(f(inp["mixer_norm"])[l]), norm_cols(f(inp["ffn_norm"])[l, 1])]
    com["gam"] = np.ascontiguousarray(np.concatenate(gcols, axis=1))
    for l in range(layers):
        for ff in range(2):
            a, b = prep_ffn_weights(f(inp["ffn_w_gate_up"])[l, ff], f(inp["ffn_w_down"])[l, ff])
            com["wgu_%d_%d" % (l, ff)] = a
            com["wd_%d_%d" % (l, ff)] = b
    com["sc_w_in"] = prep_proj(f(inp["sc_w_in"])[0])
    com["sc_w_out"] = prep_proj(f(inp["sc_w_out"])[0])
    com["cst"] = make_consts()
    for j in range(2):
        d = prep_gdn(f(inp["gdn_w_in"])[j], f(inp["gdn_conv_w"])[j], f(inp["gdn_A_log"])[j], f(inp["gdn_dt_bias"])[j],
                     f(inp["gdn_out_norm"])[j], f(inp["gdn_w_out"])[j])
        for kk_, vv_ in d.items():
            com["gdn%d_%s" % (j, kk_)] = vv_
    for n_, a_ in prep_nsa(inp).items():
        assert tuple(a_.shape) == tuple(nsa_shapes()[n_]), (n_, a_.shape)
        com["nsa_" + n_] = a_
    cwt = f(inp["sc_conv_w"])[0]
    com["sc_cw"] = np.ascontiguousarray(cwt.reshape(3, 8, 128).transpose(2, 1, 0)).reshape(128, 24)
    return com


def kernel(**inputs):
    x = np.asarray(inputs["x"], dtype=np.float32)
    B = x.shape[0]
    com = prep_inputs(inputs)
    k, nc = build_program()
    in_maps = []
    for b in range(B):
        m = dict(com)
        m["xT"] = np.ascontiguousarray(x[b].T)
        m["pos"] = np.ascontiguousarray(np.asarray(inputs["positions"])[b].astype(np.int32).reshape(1, -1))
        in_maps.append(m)
    res = run_bass_kernel_spmd(nc, in_maps, core_ids=list(range(B)))
    out = np.stack([np.ascontiguousarray(res.results[b]["yT"].T) for b in range(B)], axis=0)
    return out.astype(np.float32)


GC = 64
CST_OFF = {}
_o = 0
for _n, _w in (("id128", 128), ("LT", 64), ("mincl", 512), ("mstrict", 512), ("sel63", 128), ("ones64", 64), ("idrep", 512)):
    CST_OFF[_n] = (_o, _w)
    _o += _w
CST_W = _o


def make_consts():
    c = np.zeros((128, CST_W), np.float32)

    def put(name, arr):
        o, w = CST_OFF[name]
        c[:arr.shape[0], o:o + w] = arr

    put("id128", np.eye(128, dtype=np.float32))
    i = np.arange(64)
    put("LT", (i[:, None] <= i[None, :]).astype(np.float32))
    mincl = (i[None, :] <= i[:, None]).astype(np.float32)
    mstr = (i[None, :] < i[:, None]).astype(np.float32)
    put("mincl", np.tile(mincl, (1, 8)))
    put("mstrict", np.tile(mstr, (1, 8)))
    s = np.zeros((64, 128), np.float32)
    s[63, :] = 1.0
    put("sel63", s)
    put("ones64", np.ones((64, 64), np.float32))
    put("idrep", np.tile(np.eye(64, dtype=np.float32), (1, 8)))
    return c


def gdn_phase(self, src, dst, w_in, wab_d, cw_d, alog_d, dtb_d, onw_d, w_out, cst_d, gamma_col):
    S, sb = self.S, self.sb
    S.barrier()
    m = sb.mark()
    T = 512
    NT = self.ntok // T
    C = GC
    NCK = T // C
    H = 8
    A = S.add
    cst = sb.f32(CST_W)
    A("sp", lambda e: e.dma_start(out=cst, in_=cst_d), writes=["cst"], dkey="cst")

    def cs_(name, rows=64):
        o, w = CST_OFF[name]
        return cst[0:rows, o:o + w]

    id128 = cs_("id128", 128)
    id64 = cst[0:64, CST_OFF["id128"][0]:CST_OFF["id128"][0] + 64]
    idb64 = None
    LT, mincl, mstrict, sel63, ones64, idrep = (cs_("LT"), cs_("mincl"), cs_("mstrict"), cs_("sel63"),
                                                 cs_("ones64"), cs_("idrep"))
    x32 = sb.f32(8 * T)
    xn = sb.bf16(8 * T)
    qkv = sb.bf16(24 * T)
    gs = sb.f32(8 * T)
    og = sb.bf16(8 * T)
    St = sb.f32(H * 128)
    halo = sb.f32(24 * 3)
    pre = [sb.f32(T + 3) for _ in range(4)]
    yb = [sb.f32(T) for _ in range(4)]
    sq4 = [sb.f32(512) for _ in range(2)]
    rstd2 = [sb.f32(512) for _ in range(2)]
    sq = [sb.f32(512) for _ in range(2)]
    rstd = sb.f32(512)
    gam = sb.f32(8)
    cw = sb.f32(96)
    wab = sb.bf16(128)
    alog = sb.f32(8)
    dtb = sb.f32(8)
    nega = sb.f32(8)
    onw = sb.f32(1)
    NW = 4
    wbs = [sb.bf16(1024) for _ in range(NW)]
    wob = [sb.bf16(1024) for _ in range(2)]
    g_t = sb.f32(NCK * 8)
    be_t = sb.f32(NCK * 8)
    tmp_ab = sb.f32(NCK * 8)
    NCB = 3
    gcs = [sb.f32(8) for _ in range(NCB)]
    egl = [sb.f32(8) for _ in range(NCB)]
    ekd = [sb.f32(8) for _ in range(NCB)]
    egc = [sb.f32(8) for _ in range(NCB)]
    bege = [sb.f32(8) for _ in range(NCB)]
    rrhs = sb.f32(512)
    Em = [sb.f32(512) for _ in range(NCB)]
    ETm = [sb.f32(512) for _ in range(NCB)]
    Pm = [sb.bf16(512) for _ in range(2)]
    PTm = [sb.bf16(512) for _ in range(2)]
    Ptmp = sb.f32(512)
    attT2 = [sb.bf16(512) for _ in range(2)]
    Bm2 = [sb.bf16(H * 256) for _ in range(2)]
    kdec2 = [sb.bf16(H * 128) for _ in range(2)]
    wT2 = [sb.bf16(512) for _ in range(2)]
    XTm2 = [sb.bf16(512) for _ in range(2)]
    attT, Bm, kdec, wT, XTm = attT2[0], Bm2[0], kdec2[0], wT2[0], XTm2[0]
    vnew = sb.bf16(H * 128)
    omb = sb.bf16(H * 128)
    Sb = sb.bf16(H * 128)
    Eb = [sb.bf16(512) for _ in range(3)]
    idb = sb.bf16(128)
    om = sb.f32(H * 128)
    osq = sb.f32(H * 128)
    ss8 = sb.f32(8)

    A("sp", lambda e: e.dma_start(out=gam, in_=gamma_col), writes=["gam"], dkey="gam")
    A("sp", lambda e: e.dma_start(out=cw, in_=cw_d), writes=["cw"], dkey="cw")
    A("sp", lambda e: e.dma_start(out=alog, in_=alog_d), writes=["alog"], dkey="alog")
    A("sp", lambda e: e.dma_start(out=dtb, in_=dtb_d), writes=["dtb"], dkey="dtb")
    A("sp", lambda e: e.dma_start(out=onw, in_=onw_d), writes=["onw"], dkey="onw")
    A("pool", lambda e: e.dma_start(out=wab, in_=wab_d), writes=["wab"], dkey="wab")
    A("pool", lambda e: e.memset(halo, 0.0), writes=["halo"])
    A("pool", lambda e: e.memset(St, 0.0), writes=[("S", 0), ("S", 1)])
    A("pool", lambda e: e.memset(Sb, 0.0), writes=[("Sb", 0), ("Sb", 1)])
    A("pool", lambda e: e.tensor_copy(out=idb, in_=id128), reads=["cst"], writes=["idb"])
    for pc_ in range(2):
        A("pool", lambda e, pc_=pc_: e.memset(XTm2[pc_], 0.0), writes=[("XT", 0, pc_), ("XT", 1, pc_)])
        A("pool", lambda e, pc_=pc_: e.memset(Bm2[pc_], 0.0), writes=[("B", 0, pc_), ("B", 1, pc_)])
    A("act", lambda e: e.activation(out=nega[0:64, :], in_=alog[0:64, :], func=AF.Exp), reads=["alog"], writes=["nega"])
    A("dve", lambda e: e.tensor_scalar(out=nega[0:64, :], in0=nega[0:64, :], scalar1=-1.0, scalar2=None, op0=ALU.mult),
      reads=["nega"], writes=["nega"])

    def bc(ap2, n):
        return ap2.unsqueeze(2).broadcast_to([ap2.shape[0], ap2.shape[1], n])

    def v3(ap, a):
        return ap.rearrange("p (a b) -> p a b", a=a)

    wc = [0, 0]
    for t in range(NT):
        self.load_x_tile(src, x32, "x32", t, T)
        self.rmsnorm_tile(x32, "x32", xn, "xn", gam, "gam", T, sq, rstd, 7, "g")
        xnk = [("xn", k, 0) for k in range(8)]
        pab = self.ps[0:64, 6 * 512:6 * 512 + NCK * 16]
        for c in range(NCK):
            for k in range(8):
                A("pe", lambda e, c=c, k=k: e.matmul(
                    out=pab[:, c * 16:(c + 1) * 16], lhsT=xn[:, k * T + c * C:k * T + (c + 1) * C],
                    rhs=wab[:, k * 16:(k + 1) * 16], start=(k == 0), stop=(k == 7), skip_group_check=True),
                    reads=[("xn", k, 0), "wab"], writes=[("ps", 6)])
        pab3 = pab.rearrange("p (c n) -> p c n", n=16)
        g3, be3, tm3 = v3(g_t[0:64, :], NCK), v3(be_t[0:64, :], NCK), v3(tmp_ab[0:64, :], NCK)
        dtb3 = dtb[0:64, :].unsqueeze(1).broadcast_to([64, NCK, 8])
        nega3 = nega[0:64, :].unsqueeze(1).broadcast_to([64, NCK, 8])
        A("dve", lambda e: e.tensor_tensor(out=tm3, in0=pab3[:, :, 0:8], in1=dtb3, op=ALU.add),
          reads=[("ps", 6), "dtb"], writes=["tmp_ab"])
        A("act", lambda e: e.activation(out=be3, in_=pab3[:, :, 8:16], func=AF.Sigmoid), reads=[("ps", 6)], writes=["be_t"])
        A("act", lambda e: e.activation(out=tm3, in_=tm3, func=AF.Exp), reads=["tmp_ab"], writes=["tmp_ab"])
        A("act", lambda e: e.activation(out=tm3, in_=tm3, func=AF.Ln, bias=1.0), reads=["tmp_ab"], writes=["tmp_ab"])
        A("dve", lambda e: e.tensor_tensor(out=g3, in0=tm3, in1=nega3, op=ALU.mult), reads=["tmp_ab", "nega"], writes=["g_t"])
        def proj_gen(oc):
            wi = wc[0] % NW
            wc[0] += 1
            self.wload(wbs[wi], w_in[oc], 1024, ("win", wi))
            pb = oc % 4
            pt = self.bank(pb)
            for k in range(8):
                A("pe", lambda e, k=k, pt=pt, wi=wi: e.matmul(out=pt, lhsT=wbs[wi][:, k * 128:(k + 1) * 128],
                                                            rhs=xn[:, k * T:(k + 1) * T], start=(k == 0), stop=(k == 7)),
                  reads=[("win", wi), ("xn", k, 0)], writes=[("ps", pb)])
            yield
            if oc >= 24:
                h = oc - 24
                A("act", lambda e, h=h, pt=pt: e.activation(out=gs[:, h * T:(h + 1) * T], in_=pt, func=AF.Silu),
                  reads=[("ps", pb)], writes=[("gs", h)])
                return
            pr, y = pre[oc % 4], yb[oc % 4]
            pk, yk = ("pre", oc % 4), ("yb", oc % 4)
            A("pool", lambda e, pr=pr, oc=oc: e.tensor_copy(out=pr[:, 0:3], in_=halo[:, oc * 3:oc * 3 + 3]),
              reads=["halo"], writes=[pk])
            A("act", lambda e, pr=pr, pt=pt: e.copy(out=pr[:, 3:3 + T], in_=pt), reads=[("ps", pb)], writes=[pk])
            A("act", lambda e, y=y, pr=pr, oc=oc: e.activation(out=y, in_=pr[:, 3:3 + T], func=AF.Identity,
                                                               scale=cw[:, oc * 4 + 3:oc * 4 + 4]),
              reads=[pk, "cw"], writes=[yk])
            for tap in (2, 1, 0):
                A("dve", lambda e, y=y, pr=pr, oc=oc, tap=tap: e.scalar_tensor_tensor(
                    out=y, in0=pr[:, tap:tap + T], scalar=cw[:, oc * 4 + tap:oc * 4 + tap + 1], in1=y,
                    op0=ALU.mult, op1=ALU.add), reads=[pk, yk, "cw"], writes=[yk])
            A("pool", lambda e, pr=pr, oc=oc: e.tensor_copy(out=halo[:, oc * 3:oc * 3 + 3], in_=pr[:, T:T + 3]),
              reads=[pk], writes=["halo"])
            yield
            qo = qkv[:, oc * T:(oc + 1) * T]
            if oc >= 16:
                A("act", lambda e, qo=qo, y=y: e.activation(out=qo, in_=y, func=AF.Silu), reads=[yk], writes=[("qkv", oc)])
            if oc < 16:
                qf = pr[:, 3:3 + T]
                A("act", lambda e, qf=qf, y=y: e.activation(out=qf, in_=y, func=AF.Silu), reads=[yk], writes=[pk])
                q2 = sq4[oc % 2]
                nb_ = 6 + oc % 2
                rs_ = rstd2[oc % 2]
                A("act", lambda e, q2=q2, qf=qf: e.activation(out=q2, in_=qf, func=AF.Square),
                  reads=[pk], writes=[("gsq4", oc % 2)])
                A("pe", lambda e, q2=q2, nb_=nb_: e.matmul(out=self.bank(nb_), lhsT=self.ones32, rhs=q2, start=True, stop=True),
                  reads=[("gsq4", oc % 2)], writes=[("ps", nb_)])
                yield
                A("act", lambda e, rs_=rs_, nb_=nb_: e.activation(out=rs_, in_=self.bank(nb_), func=AF.Ln, bias=self.epsc, scale=1.0),
                  reads=[("ps", nb_)], writes=[("grstd2", oc % 2)])
                A("act", lambda e, rs_=rs_: e.activation(out=rs_, in_=rs_, func=AF.Exp, scale=-0.5), reads=[("grstd2", oc % 2)], writes=[("grstd2", oc % 2)])
                sc_ = (128.0 ** -0.5) if oc < 8 else 1.0
                A("dve", lambda e, qo=qo, qf=qf, sc_=sc_, rs_=rs_: e.scalar_tensor_tensor(out=qo, in0=qf, scalar=sc_, in1=rs_,
                                                                                        op0=ALU.mult, op1=ALU.mult),
                  reads=[pk, ("grstd2", oc % 2)], writes=[("qkv", oc)])
        _act = []
        _nxt = 0
        while _act or _nxt < 32:
            if _nxt < 32:
                _act.append(proj_gen(_nxt))
                _nxt += 1
            for _g in list(_act):
                try:
                    next(_g)
                except StopIteration:
                    _act.remove(_g)
        import os
        if os.environ.get('GDN_MAXC'):
            A('pool', lambda e: e.memset(og, 0.0), writes=[('og', c_, g_) for c_ in range(NCK) for g_ in range(2)])
        NG = 2
        HG = H // NG

        def common(c):
            cb = c % NCB
            g_c = g_t[0:64, c * 8:(c + 1) * 8]
            be_c = be_t[0:64, c * 8:(c + 1) * 8]
            psm = self.ps[:, 7 * 512:8 * 512]
            gcs_, egl_, ekd_, egc_, bege_, Em_, ETm_ = gcs[cb], egl[cb], ekd[cb], egc[cb], bege[cb], Em[cb], ETm[cb]
            ck = lambda n: (n, cb)
            A("pe", lambda e: e.matmul(out=psm[0:64, 0:8], lhsT=LT, rhs=g_c, start=True, stop=True, skip_group_check=True),
              reads=["g_t", "cst"], writes=[("ps", 7)])
            A("act", lambda e: e.copy(out=gcs_[0:64, :], in_=psm[0:64, 0:8]), reads=[("ps", 7)], writes=[ck("gcs")])
            A("act", lambda e: e.activation(out=egc_[0:64, :], in_=psm[0:64, 0:8], func=AF.Exp), reads=[("ps", 7)], writes=[ck("egc")])
            yield
            A("pe", lambda e: e.matmul(out=psm[:, 8:16], lhsT=sel63, rhs=gcs_[0:64, :], start=True, stop=True, skip_group_check=True),
              reads=[ck("gcs"), "cst"], writes=[("ps", 7)])
            A("act", lambda e: e.activation(out=egl_, in_=psm[:, 8:16], func=AF.Exp), reads=[("ps", 7)], writes=[ck("egl")])
            A("dve", lambda e: e.tensor_tensor(out=ekd_[0:64, :], in0=psm[0:64, 8:16], in1=gcs_[0:64, :], op=ALU.subtract),
              reads=[("ps", 7), ck("gcs")], writes=[ck("ekd")])
            A("act", lambda e: e.activation(out=ekd_[0:64, :], in_=ekd_[0:64, :], func=AF.Exp), reads=[ck("ekd")], writes=[ck("ekd")])
            A("dve", lambda e: e.tensor_tensor(out=bege_[0:64, :], in0=egc_[0:64, :], in1=be_c, op=ALU.mult),
              reads=[ck("egc"), "be_t"], writes=[ck("bege")])
            yield
            A("dve", lambda e: e.tensor_tensor(out=v3(rrhs[0:64, :], 8), in0=v3(idrep, 8), in1=bc(gcs_[0:64, :], 64), op=ALU.mult),
              reads=[ck("gcs"), "cst"], writes=["rrhs"])
            b6 = self.ps[0:64, 6 * 512:7 * 512]
            A("pe", lambda e: e.matmul(out=b6, lhsT=ones64, rhs=rrhs[0:64, :], start=True, stop=True),
              reads=["rrhs", "cst"], writes=[("ps", 6)])
            A("dve", lambda e: e.tensor_tensor(out=v3(Em_[0:64, :], 8), in0=v3(b6, 8), in1=bc(gcs_[0:64, :], 64), op=ALU.subtract),
              reads=[("ps", 6), ck("gcs")], writes=[ck("E")])
            yield
            A("dve", lambda e: e.tensor_scalar(out=ETm_[0:64, :], in0=Em_[0:64, :], scalar1=0.0, scalar2=None, op0=ALU.min),
              reads=[ck("E")], writes=[ck("ET")])
            A("dve", lambda e: e.tensor_scalar(out=Em_[0:64, :], in0=Em_[0:64, :], scalar1=0.0, scalar2=None, op0=ALU.max),
              reads=[ck("E")], writes=[ck("E")])
            A("act", lambda e: e.activation(out=ETm_[0:64, :], in_=ETm_[0:64, :], func=AF.Exp), reads=[ck("ET")], writes=[ck("ET")])
            A("act", lambda e: e.activation(out=Em_[0:64, :], in_=Em_[0:64, :], func=AF.Exp, scale=-1.0), reads=[ck("E")], writes=[ck("E")])
            yield
            A("dve", lambda e: e.tensor_tensor(out=v3(ETm_[0:64, :], 8), in0=v3(ETm_[0:64, :], 8),
                                               in1=LT.unsqueeze(1).broadcast_to([64, 8, 64]), op=ALU.mult),
              reads=[ck("ET"), "cst"], writes=[ck("ET")])
            A("dve", lambda e: e.tensor_tensor(out=Em_[0:64, :], in0=Em_[0:64, :], in1=mincl, op=ALU.mult),
              reads=[ck("E"), "cst"], writes=[ck("E")])
            yield

        def chain(c, g, part):
            cb = c % NCB
            pc = c % 2
            attT, Bm, kdec, wT, XTm = attT2[pc], Bm2[pc], kdec2[pc], wT2[pc], XTm2[pc]
            hk = lambda n: (n, g, pc)
            hs = list(range(g * HG, (g + 1) * HG))
            h0 = hs[0]
            W64 = slice(h0 * 64, (h0 + HG) * 64)
            W128 = slice(h0 * 128, (h0 + HG) * 128)
            W256 = slice(h0 * 256, (h0 + HG) * 256)
            hsl = slice(h0, h0 + HG)
            bA, bB = 2 * g, 2 * g + 1
            ck = lambda n: (n, cb)
            gk = lambda n: (n, g)
            be_c = be_t[0:64, c * 8:(c + 1) * 8]
            gcs_, egl_, ekd_, egc_, bege_, Em_, ETm_ = gcs[cb], egl[cb], ekd[cb], egc[cb], bege[cb], Em[cb], ETm[cb]

            def qT(h):
                return qkv[:, h * T + c * C:h * T + (c + 1) * C]

            def kT(h):
                return qkv[:, (8 + h) * T + c * C:(8 + h) * T + (c + 1) * C]

            def vT(h):
                return qkv[:, (16 + h) * T + c * C:(16 + h) * T + (c + 1) * C]

            qk_keys = [("qkv", o_) for o_ in range(24)]
            sbk = 4 + g
            p4 = self.ps[0:64, sbk * 512:sbk * 512 + 256]
            p5 = self.ps[0:64, sbk * 512 + 256:(sbk + 1) * 512]
            p4f = self.ps[:, sbk * 512:sbk * 512 + 256]
            p5f = self.ps[:, sbk * 512 + 256:(sbk + 1) * 512]
            k4 = k5 = ("ps", sbk)
            LW = slice(0, HG * 64)
            pA = self.ps[0:64, bA * 512:(bA + 1) * 512]
            pB = self.ps[0:64, bB * 512:(bB + 1) * 512]
            pBf = self.ps[:, bB * 512:(bB + 1) * 512]
            pAf = self.ps[:, bA * 512:(bA + 1) * 512]
            kA, kB = ("ps", bA), ("ps", bB)
            if part == 0:
                for j, h in enumerate(hs):
                    A("pe", lambda e, h=h: e.matmul(out=p4[:, (h - h0) * 64:(h - h0 + 1) * 64], lhsT=kT(h), rhs=kT(h), start=True, stop=True,
                                                    skip_group_check=True), reads=qk_keys, writes=[k4])
                P0, PT0 = Pm[0], PTm[0]
                A("dve", lambda e: e.tensor_tensor(out=Ptmp[0:64, W64], in0=p4[:, LW], in1=Em_[0:64, W64], op=ALU.mult),
                  reads=[k4, ck("E")], writes=[gk("Ptmp")])
                A("dve", lambda e: e.tensor_tensor(out=Ptmp[0:64, W64], in0=Ptmp[0:64, W64], in1=mstrict[:, W64], op=ALU.mult),
                  reads=[gk("Ptmp"), "cst"], writes=[gk("Ptmp")])
                A("dve", lambda e: e.tensor_tensor(out=v3(P0[0:64, W64], HG), in0=v3(Ptmp[0:64, W64], HG), in1=bc(be_c[:, hsl], 64), op=ALU.mult),
                  reads=[gk("Ptmp"), "be_t"], writes=[gk("P0")])
                yield
                for h in hs:
                    A("pe", lambda e, h=h: e.matmul(out=p5[:, (h - h0) * 64:(h - h0 + 1) * 64], lhsT=P0[0:64, h * 64:(h + 1) * 64], rhs=idb[0:64, 0:64],
                                                    start=True, stop=True, skip_group_check=True),
                      reads=[gk("P0"), "idb"], writes=[k5])
                A("act", lambda e: e.copy(out=PT0[0:64, W64], in_=p5[:, LW]), reads=[k5], writes=[gk("PT0")])
                A("dve", lambda e: e.tensor_tensor(out=XTm[0:64, W64], in0=idrep[:, W64], in1=PT0[0:64, W64], op=ALU.subtract),
                  reads=[gk("PT0"), "cst"], writes=[hk("XT")])
                yield
                if g == 1:
                    S.stop_at('G3')
                for h in hs:
                    A("pe", lambda e, h=h: e.matmul(out=p4[:, (h - h0) * 64:(h - h0 + 1) * 64], lhsT=kT(h), rhs=qT(h), start=True, stop=True,
                                                    skip_group_check=True), reads=qk_keys, writes=[k4])
                A("dve", lambda e: e.tensor_tensor(out=attT[0:64, W64], in0=p4[:, LW], in1=ETm_[0:64, W64], op=ALU.mult),
                  reads=[k4, ck("ET")], writes=[hk("attT")])
                yield
                B4 = v3(Bm[0:64, :], 8)
                for j, h in enumerate(hs):
                    A("pe", lambda e, h=h, j=j: e.matmul(out=pA[:, j * 128:(j + 1) * 128], lhsT=kT(h), rhs=idb,
                                                         start=True, stop=True, skip_group_check=True), reads=qk_keys + ["idb"], writes=[kA])
                for j, h in enumerate(hs):
                    A("pe", lambda e, h=h, j=j: e.matmul(out=pB[:, j * 128:(j + 1) * 128], lhsT=vT(h), rhs=idb,
                                                         start=True, stop=True, skip_group_check=True), reads=qk_keys + ["idb"], writes=[kB])
                A("dve", lambda e: e.tensor_tensor(out=B4[:, hsl, 128:256], in0=v3(pA, HG), in1=bc(bege_[0:64, hsl], 128), op=ALU.mult),
                  reads=[kA, ck("bege")], writes=[hk("B")])
                A("dve", lambda e: e.tensor_tensor(out=v3(kdec[0:64, :], 8)[:, hsl, :], in0=v3(pA, HG), in1=bc(ekd_[0:64, hsl], 128), op=ALU.mult),
                  reads=[kA, ck("ekd")], writes=[hk("kdec")])
                A("dve", lambda e: e.tensor_tensor(out=B4[:, hsl, 0:128], in0=v3(pB, HG), in1=bc(be_c[:, hsl], 128), op=ALU.mult),
                  reads=[kB, "be_t"], writes=[hk("B")])
                yield
                if g == 1:
                    S.stop_at('G5')
                def squares(cur, need_pt):
                    P, PT = Pm[cur], PTm[cur]
                    pk, ptk = gk("P%d" % cur), gk("PT%d" % cur)
                    for h in hs:
                        A("pe", lambda e, h=h, P=P, PT=PT: e.matmul(out=p4[:, (h - h0) * 64:(h - h0 + 1) * 64], lhsT=PT[0:64, h * 64:(h + 1) * 64],
                                                                    rhs=P[0:64, h * 64:(h + 1) * 64], start=True, stop=True,
                                                                    skip_group_check=True), reads=[pk, ptk], writes=[k4])
                    if need_pt:
                        for h in hs:
                            A("pe", lambda e, h=h, P=P, PT=PT: e.matmul(out=p5[:, (h - h0) * 64:(h - h0 + 1) * 64], lhsT=P[0:64, h * 64:(h + 1) * 64],
                                                                        rhs=PT[0:64, h * 64:(h + 1) * 64], start=True, stop=True,
                                                                        skip_group_check=True), reads=[pk, ptk], writes=[k5])

                def sq_copies(nxt, need_pt):
                    A("act", lambda e: e.copy(out=Pm[nxt][0:64, W64], in_=p4[:, LW]), reads=[k4], writes=[gk("P%d" % nxt)])
                    if need_pt:
                        A("act", lambda e: e.copy(out=PTm[nxt][0:64, W64], in_=p5[:, LW]), reads=[k5], writes=[gk("PT%d" % nxt)])

                squares(0, True)
                sq_copies(1, True)
                cur = 1
                yield
                for lvl in range(1, 6):
                    for j, h in enumerate(hs):
                        A("pe", lambda e, h=h, j=j, cur=cur: e.matmul(out=pA[:, j * 64:(j + 1) * 64], lhsT=Pm[cur][0:64, h * 64:(h + 1) * 64],
                                                                      rhs=XTm[0:64, h * 64:(h + 1) * 64], start=True, stop=True,
                                                                      skip_group_check=True), reads=[gk("P%d" % cur), hk("XT")], writes=[kA])
                    if lvl < 5:
                        squares(cur, lvl < 4)
                    A("dve", lambda e: e.tensor_tensor(out=XTm[0:64, W64], in0=XTm[0:64, W64], in1=pA[:, 0:HG * 64], op=ALU.add),
                      reads=[kA, hk("XT")], writes=[hk("XT")])
                    if lvl < 5:
                        sq_copies(1 - cur, lvl < 4)
                        cur = 1 - cur
                    yield
                if g == 1:
                    S.stop_at('G6')
                for h in hs:
                    A("pe", lambda e, h=h: e.matmul(out=p4f[:, (h - h0) * 64:(h - h0 + 1) * 64], lhsT=Bm[0:64, h * 256 + 128:h * 256 + 256],
                                                    rhs=XTm[0:64, h * 64:(h + 1) * 64], start=True, stop=True, skip_group_check=True),
                      reads=[hk("B"), hk("XT")], writes=[k4])
                A("act", lambda e: e.activation(out=wT[:, W64], in_=p4f[:, LW], func=AF.Copy, scale=-1.0), reads=[k4], writes=[hk("wT")])
                yield
            if part == 0:
                return
            if g == 1:
                S.stop_at('G7')
            for j, h in enumerate(hs):
                A("pe", lambda e, h=h, j=j: e.matmul(out=pB[:, j * 128:(j + 1) * 128], lhsT=XTm[:, h * 64:(h + 1) * 64],
                                                     rhs=Bm[:, h * 256:h * 256 + 128], start=True, stop=False, skip_group_check=True),
                  reads=[hk("XT"), hk("B")], writes=[kB])
                A("pe", lambda e, h=h, j=j: e.matmul(out=pB[:, j * 128:(j + 1) * 128], lhsT=wT[:, h * 64:(h + 1) * 64],
                                                     rhs=Sb[:, h * 128:(h + 1) * 128], start=False, stop=True, skip_group_check=True),
                  reads=[hk("wT"), gk("Sb")], writes=[kB])
            vn3 = v3(vnew[0:64, :], 8)
            A("act", lambda e: e.copy(out=vnew[0:64, W128], in_=pB), reads=[kB], writes=[gk("vnew")])
            yield
            if g == 1:
                S.stop_at('G8')
            for j, h in enumerate(hs):
                A("pe", lambda e, h=h, j=j: e.matmul(out=pA[:, j * 128:(j + 1) * 128], lhsT=qT(h), rhs=Sb[:, h * 128:(h + 1) * 128],
                                                     start=True, stop=True, skip_group_check=True), reads=qk_keys + [gk("Sb")], writes=[kA])
            o3 = v3(om[0:64, :], 8)
            A("dve", lambda e: e.tensor_tensor(out=o3[:, hsl, :], in0=v3(pA, HG), in1=bc(egc_[0:64, hsl], 128), op=ALU.mult),
              reads=[kA, ck("egc")], writes=[gk("o")])
            yield
            if g == 1:
                S.stop_at('G9')
            for j, h in enumerate(hs):
                A("pe", lambda e, h=h, j=j: e.matmul(out=pB[:, j * 128:(j + 1) * 128], lhsT=attT[0:64, h * 64:(h + 1) * 64],
                                                     rhs=vnew[0:64, h * 128:(h + 1) * 128], start=True, stop=True, skip_group_check=True),
                  reads=[hk("attT"), gk("vnew")], writes=[kB])
            A("dve", lambda e: e.tensor_tensor(out=o3[:, hsl, :], in0=o3[:, hsl, :], in1=v3(pB, HG), op=ALU.add),
              reads=[kB, gk("o")], writes=[gk("o")])
            for j, h in enumerate(hs):
                A("pe", lambda e, h=h, j=j: e.matmul(out=pAf[:, j * 128:(j + 1) * 128], lhsT=kdec[0:64, h * 128:(h + 1) * 128],
                                                     rhs=vnew[0:64, h * 128:(h + 1) * 128], start=True, stop=True, skip_group_check=True),
                  reads=[hk("kdec"), gk("vnew")], writes=[kA])
            A("dve", lambda e: e.tensor_tensor(out=v3(St[:, W128], HG), in0=v3(St[:, W128], HG), in1=bc(egl_[:, hsl], 128), op=ALU.mult),
              reads=[gk("S"), ck("egl")], writes=[gk("S")])
            A("dve", lambda e: e.tensor_tensor(out=St[:, W128], in0=St[:, W128], in1=pAf, op=ALU.add),
              reads=[kA, gk("S")], writes=[gk("S")])
            A("pool", lambda e: e.tensor_copy(out=Sb[:, W128], in_=St[:, W128]), reads=[gk("S")], writes=[gk("Sb")])
            yield
            A("pool", lambda e: e.tensor_tensor(out=osq[0:64, W128], in0=om[0:64, W128], in1=om[0:64, W128], op=ALU.mult),
              reads=[gk("o")], writes=[gk("osq")])
            A("dve", lambda e: e.tensor_reduce(out=ss8[0:64, hsl], in_=v3(osq[0:64, W128], HG), axis=AX.X, op=ALU.add),
              reads=[gk("osq")], writes=[gk("ss8")])
            A("act", lambda e: e.activation(out=ss8[0:64, hsl], in_=ss8[0:64, hsl], func=AF.Ln, bias=self.epsc[0:64, :], scale=1.0 / 128),
              reads=[gk("ss8")], writes=[gk("ss8")])
            A("act", lambda e: e.activation(out=ss8[0:64, hsl], in_=ss8[0:64, hsl], func=AF.Exp, scale=-0.5), reads=[gk("ss8")], writes=[gk("ss8")])
            A("dve", lambda e: e.tensor_tensor(out=v3(omb[0:64, :], 8)[:, hsl, :], in0=o3[:, hsl, :], in1=bc(ss8[0:64, hsl], 128), op=ALU.mult),
              reads=[gk("ss8"), gk("o")], writes=[gk("omb")])
            yield
            for h in hs:
                A("pe", lambda e, h=h: e.matmul(out=p5f[:, (h - h0) * 64:(h - h0 + 1) * 64], lhsT=omb[0:64, h * 128:(h + 1) * 128], rhs=idb[0:64, 0:64],
                                                start=True, stop=True, skip_group_check=True), reads=[gk("omb"), "idb"], writes=[k5])
            og3 = v3(og, 8)[:, hsl, c * C:(c + 1) * C]
            gs3 = v3(gs, 8)[:, hsl, c * C:(c + 1) * C]
            A("dve", lambda e: e.scalar_tensor_tensor(out=og3, in0=v3(p5f[:, LW], HG), scalar=onw[:, 0:1], in1=gs3,
                                                      op0=ALU.mult, op1=ALU.mult),
              reads=[k5, "onw"] + [("gs", h) for h in hs], writes=[("og", c, g)])
            yield

        def run_gens(gens):
            gens = list(gens)
            while gens:
                for gname in list(gens):
                    try:
                        next(gname)
                    except StopIteration:
                        gens.remove(gname)

        ncs_ = NCK
        run_gens([common(0)])
        gl = [chain(0, g, 0) for g in range(NG)]
        if ncs_ > 1:
            gl.append(common(1))
        run_gens(gl)
        for c in range(ncs_):
            gl = [chain(c, g, 1) for g in range(NG)]
            if c + 1 < ncs_:
                gl += [chain(c + 1, g, 0) for g in range(NG)]
            if c + 2 < ncs_:
                gl.append(common(c + 2))
            run_gens(gl)
        for o in range(8):
            wi = wc[1] % 2
            wc[1] += 1
            self.wload(wob[wi], w_out[o], 1024, ("wout", wi))
            pb = o % 2
            pt = self.bank(pb)
            for k in range(8):
                A("pe", lambda e, k=k, pt=pt, wi=wi: e.matmul(out=pt, lhsT=wob[wi][:, k * 128:(k + 1) * 128],
                                                            rhs=og[:, k * T:(k + 1) * T], start=(k == 0), stop=(k == 7)),
                  reads=[("wout", wi)] + [("og", c, g_) for c in range(NCK) for g_ in range(2)], writes=[("ps", pb)])
            xs = x32[:, o * T:(o + 1) * T]
            A("dve", lambda e, xs=xs, pt=pt: e.tensor_tensor(out=xs, in0=pt, in1=xs, op=ALU.add),
              reads=[("ps", pb), ("x32", o)], writes=[("x32", o)])
            A("act", lambda e, o=o, t=t: e.dma_start(out=dst[o * 128:(o + 1) * 128, t * T:(t + 1) * T], in_=x32[:, o * T:(o + 1) * T]),
              reads=[("x32", o)], dkey=("st", o))
    self.dbg = dict(wT=wT, qkv=qkv, gs=gs, g_t=g_t, be_t=be_t, gcs=gcs[1], egl=egl[1], ekd=ekd[1], egc=egc[1], Em=Em[1], ETm=ETm[1], P0=Pm[0], P1=Pm[1], attT=attT, Bm=Bm, kdec=kdec, vnew=vnew, om=om, St=St, og=og, xn=xn, x32=x32)
    sb.release(m)


K.gdn_phase = gdn_phase


def prep_gdn(w_in, conv_w, A_log, dt_bias, out_norm, w_out):
    d = {}
    d["w_in"] = prep_proj(np.ascontiguousarray(w_in[:, :4096]))
    wab = w_in[:, 4096:4112].reshape(8, 128, 16)
    d["wab"] = np.ascontiguousarray(wab.transpose(1, 0, 2)).reshape(128, 128)
    d["cw"] = np.ascontiguousarray(conv_w.reshape(4, 24, 128).transpose(2, 1, 0)).reshape(128, 96)
    d["alog"] = np.ascontiguousarray(np.broadcast_to(A_log[None, :], (128, 8)))
    d["dtb"] = np.ascontiguousarray(np.broadcast_to(dt_bias[None, :], (128, 8)))
    d["onw"] = np.ascontiguousarray(out_norm.reshape(128, 1))
    d["w_out"] = prep_proj(w_out)
    return d


NEGB = -30000.0
VW = 386
VOFF = (0, 65, 193, 258)
NCS = {}
_o = 0
for _n, _w in (("ones_bd", 128), ("rperm", 128), ("id128", 128), ("inv", 1), ("qw", 1), ("kw3", 3), ("b2k", 1),
               ("hb1", 4), ("sel0", 128), ("sel64", 128)):
    NCS[_n] = (_o, _w)
    _o += _w
NCS_W = _o
NBC = {}
_o = 0
for _n, _w in (("efull", 4096), ("id128", 128), ("causb", 4 * 512), ("bandb", 4 * 512), ("cmpb", 512), ("cmpb0", 512),
               ("ovl", 8 * 64), ("cmpr0", 512)):
    NBC[_n] = (_o, _w)
    _o += _w
NBC_W = _o


def prep_nsa(inp):
    f = lambda a: np.asarray(a, dtype=np.float32)
    w = f(inp["nsa_w_in"])[0]
    d = {}
    dup = []
    for base in (1024, 1280):
        for g in range(4):
            cg = np.arange(base + g * 64, base + (g + 1) * 64)
            dup += [cg, cg]
    colsA = np.concatenate(dup + [np.arange(1536, 1792), np.arange(2048, 2304)])
    d["wka"] = prep_proj(np.ascontiguousarray(w[:, colsA]))
    colsV = np.concatenate([np.arange(1792, 2048), np.arange(2304, 2560)])
    wv = w[:, colsV].reshape(8, 128, 512)
    d["wv"] = np.ascontiguousarray(wv.transpose(1, 0, 2)).reshape(128, 8 * 512)
    W1 = f(inp["nsa_cmp_w1"])[0]
    w1 = W1.reshape(2, 16, 2, 64, 256).transpose(0, 2, 3, 1, 4).reshape(2, 128, 16 * 256)
    d["w1"] = np.ascontiguousarray(w1)
    pe = f(inp["nsa_cmp_pe"])[0]
    d["peT"] = np.ascontiguousarray(pe.reshape(2, 16, 2, 64).transpose(0, 2, 3, 1).reshape(2, 128, 16))
    W2 = f(inp["nsa_cmp_w2"])[0]
    w2k = W2[0].reshape(2, 128, 64).transpose(1, 0, 2)
    d["w2k"] = np.ascontiguousarray(np.concatenate([w2k, w2k], axis=2)).reshape(128, 256)
    d["w2v"] = np.ascontiguousarray(W2[1].reshape(2, 128, 64).transpose(1, 0, 2)).reshape(128, 128)
    b1 = f(inp["nsa_cmp_b1"])[0]
    b2 = f(inp["nsa_cmp_b2"])[0]
    d["b2v"] = np.ascontiguousarray(b2[1].reshape(1, 64))
    c = np.zeros((128, NCS_W), np.float32)

    def put(name, arr):
        o, wd_ = NCS[name]
        c[:arr.shape[0], o:o + wd_] = arr

    ob = np.zeros((128, 128), np.float32)
    ob[:64, :64] = 1
    ob[64:, 64:] = 1
    put("ones_bd", ob)
    rp = np.zeros((128, 128), np.float32)
    for blk in (0, 64):
        for m_ in range(32):
            rp[blk + m_ + 32, blk + m_] = -1.0
            rp[blk + m_, blk + m_ + 32] = 1.0
    put("rperm", rp)
    put("id128", np.eye(128, dtype=np.float32))
    inv = (1.0 / (10000.0 ** (np.arange(0, 64, 2, dtype=np.float32) / 64))).astype(np.float32)
    put("inv", np.tile(inv, 4).reshape(128, 1))
    put("qw", np.tile(f(inp["nsa_q_norm"])[0], 2).reshape(128, 1))
    kn = f(inp["nsa_k_norm"])[0]
    put("kw3", np.tile(kn.T, (2, 1)))
    put("b2k", np.tile(b2[0], 2).reshape(128, 1))
    put("hb1", b1.reshape(2, 2, 128).transpose(2, 0, 1).reshape(128, 4))
    s0 = np.zeros((128, 128), np.float32)
    s0[0, :] = 1.0
    put("sel0", s0)
    s64 = np.zeros((128, 128), np.float32)
    s64[64, :] = 1.0
    put("sel64", s64)
    d["ncs"] = c
    bc_ = np.zeros((128, NBC_W), np.float32)

    def putb(name, arr):
        o, wd_ = NBC[name]
        bc_[:arr.shape[0], o:o + wd_] = arr

    keys = np.arange(4096)
    putb("efull", (keys[None, :] // 64 == np.arange(64)[:, None]).astype(np.float32))
    putb("id128", np.eye(128, dtype=np.float32))
    kk = np.arange(128)[:, None]
    qq = np.arange(512)[None, :]
    putb("causb", np.concatenate([np.where(dd * 128 + kk > qq, NEGB, 0.0) for dd in range(4)], axis=1))
    putb("bandb", np.concatenate([np.where(kk + e_ * 128 <= qq, NEGB, 0.0) for e_ in range(4)], axis=1))
    jj = np.arange(32)[:, None]
    cm = np.where(16 * jj + 15 > qq, NEGB, 0.0)
    putb("cmpb", cm)
    cm0 = cm.copy()
    cm0[0, :] = NEGB
    putb("cmpb0", cm0)
    r0 = np.zeros((32, 512), np.float32)
    r0[0, :] = NEGB
    ov = np.zeros((32, 8, 64), np.float32)
    for tp in range(8):
        for j in range(32):
            n = 32 * tp + j - 1
            if n < 0:
                continue
            for s_ in range(64):
                lo = max(16 * n, 64 * s_)
                hi = min(16 * n + 32, 64 * s_ + 64)
                ov[j, tp, s_] = max(hi - lo, 0) / 32.0
    putb("ovl", ov.reshape(32, 512))
    putb("cmpr0", r0)
    d["nbc"] = bc_
    sm = np.zeros((8, 128, 2, 4, 64), np.float32)
    for t in range(8):
        for blk in range(4):
            tq = t * 512 + blk * 128 + np.arange(128)[:, None]
            s_ = np.arange(64)[None, :]
            valid = (s_ * 64 <= tq)
            dist = tq // 64 - s_
            forced = (s_ == 0) | ((dist >= 0) & (dist < 2))
            sm[t, :, 0, blk, :] = valid
            sm[t, :, 1, blk, :] = np.where(valid & forced, 1e9, 0.0) + np.where(valid, 0.0, -1.0)
    d["selc"] = sm.reshape(8, 128, 512)
    qcols = []
    for pp in range(2):
        for i in range(4):
            for g in (2 * pp, 2 * pp + 1):
                h = g * 4 + i
                qcols.append(np.arange(h * 64, (h + 1) * 64))
    gcols = []
    for pp in range(2):
        for r in range(3):
            for i in range(4):
                for g in (2 * pp, 2 * pp + 1):
                    h = g * 4 + i
                    gcols.append(np.full(64, 2560 + h * 3 + r))
    d["wq"] = prep_proj(np.ascontiguousarray(w[:, np.concatenate(qcols)]))
    d["wg"] = prep_proj(np.ascontiguousarray(w[:, np.concatenate(gcols)]))
    wo = f(inp["nsa_w_out"])[0]
    d["wo"] = prep_proj(np.ascontiguousarray(wo[np.concatenate(qcols), :]))
    return d


import math
TWO_PI = 2.0 * math.pi
CW1 = 6.28125
CW2 = TWO_PI - CW1


def rope_tables(self, pos_d, t, T, cosb, sinb, wk, inv_col, tag, wkeys=None):
    A = self.S.add
    ti = wk[0].bitcast(I32)
    ang, kf = wk[1], wk[2]
    if wkeys is None:
        wkeys = [tag + "w0", tag + "w1", tag + "w2"]
    K0, K1, K2 = wkeys
    A("sp", lambda e: e.dma_start(out=ti, in_=pos_d[0:1, t * T:(t + 1) * T].broadcast_to([128, T])),
      writes=[K0], dkey=tag + "pos")
    A("dve", lambda e: e.tensor_copy(out=ang, in_=ti), reads=[K0], writes=[K1])
    A("dve", lambda e: e.tensor_scalar(out=ang, in0=ang, scalar1=inv_col, scalar2=None, op0=ALU.mult),
      reads=[K1, "ncs"], writes=[K1])
    A("dve", lambda e: e.tensor_scalar(out=ti, in0=ang, scalar1=1.0 / TWO_PI, scalar2=None, op0=ALU.mult),
      reads=[K1], writes=[K0])
    A("dve", lambda e: e.tensor_copy(out=kf, in_=ti), reads=[K0], writes=[K2])
    A("dve", lambda e: e.scalar_tensor_tensor(out=ang, in0=kf, scalar=-CW1, in1=ang, op0=ALU.mult, op1=ALU.add),
      reads=[K1, K2], writes=[K1])
    A("dve", lambda e: e.scalar_tensor_tensor(out=ang, in0=kf, scalar=-CW2, in1=ang, op0=ALU.mult, op1=ALU.add),
      reads=[K1, K2], writes=[K1])

    def wrap(x, key):
        A("dve", lambda e: e.tensor_scalar(out=kf, in0=x, scalar1=math.pi, scalar2=-TWO_PI, op0=ALU.is_gt, op1=ALU.mult),
          reads=[key], writes=[K2])
        A("dve", lambda e: e.tensor_tensor(out=x, in0=x, in1=kf, op=ALU.add), reads=[key, K2], writes=[key])
        A("dve", lambda e: e.tensor_scalar(out=kf, in0=x, scalar1=-math.pi, scalar2=TWO_PI, op0=ALU.is_lt, op1=ALU.mult),
          reads=[key], writes=[K2])
        A("dve", lambda e: e.tensor_tensor(out=x, in0=x, in1=kf, op=ALU.add), reads=[key, K2], writes=[key])

    wrap(ang, K1)
    A("act", lambda e: e.activation(out=sinb, in_=ang, func=AF.Sin), reads=[K1], writes=[tag + "sin"])
    A("dve", lambda e: e.tensor_scalar(out=ang, in0=ang, scalar1=math.pi / 2, scalar2=None, op0=ALU.add),
      reads=[K1], writes=[K1])
    wrap(ang, K1)
    A("act", lambda e: e.activation(out=cosb, in_=ang, func=AF.Sin), reads=[K1], writes=[tag + "cos"])


K.rope_tables = rope_tables


def headnorm_rope(self, pt, pkey, wcol, cosb, sinb, tag, outs, scale, wk, ncs, T):
    A = self.S.add
    sqv, rs, xnr, t1 = wk
    ones_bd, rperm = ncs["ones_bd"], ncs["rperm"]
    A("act", lambda e: e.activation(out=sqv, in_=pt, func=AF.Square), reads=[pkey], writes=[tag + "sq"])
    A("pe", lambda e: e.matmul(out=self.bank(2, T), lhsT=ones_bd, rhs=sqv, start=True, stop=True),
      reads=[tag + "sq", "ncs"], writes=[("ps", 2)])
    A("act", lambda e: e.activation(out=rs, in_=self.bank(2, T), func=AF.Ln, bias=self.epsc, scale=1.0 / 64),
      reads=[("ps", 2)], writes=[tag + "rs"])
    A("act", lambda e: e.activation(out=rs, in_=rs, func=AF.Exp, scale=-0.5), reads=[tag + "rs"], writes=[tag + "rs"])
    A("dve", lambda e: e.scalar_tensor_tensor(out=xnr, in0=pt, scalar=wcol, in1=rs, op0=ALU.mult, op1=ALU.mult),
      reads=[pkey, tag + "rs", "ncs"], writes=[tag + "xn"])
    outs = [o_ if len(o_) == 4 else (o_[0], o_[1], o_[2], slice(0, 128)) for o_ in outs]
    need_rope = any(o_[1] for o_ in outs)
    if need_rope:
        A("pe", lambda e: e.matmul(out=self.bank(3, T), lhsT=rperm, rhs=xnr, start=True, stop=True),
          reads=[tag + "xn", "ncs"], writes=[("ps", 3)])
    roped = False
    for o_ap, rope, okey, rows in outs:
        if not rope:
            A("act", lambda e, o_ap=o_ap, rows=rows: e.activation(out=o_ap[rows, :], in_=xnr[rows, :], func=AF.Copy, scale=scale),
              reads=[tag + "xn"], writes=[okey])
        else:
            if not roped:
                A("pool", lambda e: e.tensor_tensor(out=t1, in0=xnr, in1=cosb, op=ALU.mult),
                  reads=[tag + "xn", "ropecos"], writes=[tag + "t1"])
                A("dve", lambda e: e.tensor_tensor(out=rs, in0=self.bank(3, T), in1=sinb, op=ALU.mult),
                  reads=[("ps", 3), "ropesin", tag + "rs"], writes=[tag + "rs"])
                A("dve", lambda e: e.tensor_tensor(out=t1, in0=t1, in1=rs, op=ALU.add),
                  reads=[tag + "t1", tag + "rs"], writes=[tag + "t1"])
                roped = True
            A("act", lambda e, o_ap=o_ap, rows=rows: e.activation(out=o_ap[rows, :], in_=t1[rows, :], func=AF.Copy, scale=scale),
              reads=[tag + "t1"], writes=[okey])


K.headnorm_rope = headnorm_rope


def nsa_phase(self, src, dst, W, pos_d, gamma_col):
    S, sb = self.S, self.sb
    S.barrier()
    m = sb.mark()
    A = S.add
    T = 512
    NT = self.ntok // T
    SQ = self.ntok
    NKT = SQ // 128

    def v3(ap, a):
        return ap.rearrange("p (a b) -> p a b", a=a)

    ncs_t = sb.f32(NCS_W)
    A("sp", lambda e: e.dma_start(out=ncs_t, in_=W["ncs"]), writes=["ncs"], dkey="ncs")
    ncs = {n: ncs_t[:, o:o + w] for n, (o, w) in NCS.items()}
    ksT = sb.bf16(2 * SQ)
    kwT = sb.bf16(2 * 1024)
    vsS = sb.bf16(NKT * VW)
    vwS = sb.bf16(8 * VW)
    kcT = sb.bf16(4 * 32 * NT)
    vcS = sb.bf16(NT * VW)
    gam = sb.f32(8)
    A("sp", lambda e: e.dma_start(out=gam, in_=gamma_col), writes=["gam"], dkey="gam")
    for st_, nm in ((vsS, "vsS"), (vwS, "vwS"), (vcS, "vcS")):
        A("pool", lambda e, st_=st_: e.memset(st_, 0.0), writes=[nm])
        n_t = st_.shape[1] // VW
        s3 = st_.rearrange("p (t w) -> p t w", w=VW)
        for col in (64, 65, 257, 258):
            A("pool", lambda e, s3=s3, col=col: e.memset(s3[:, :, col:col + 1], 1.0), writes=[nm])
    S.barrier()
    mB = sb.mark()

    x32 = sb.f32(8 * T)
    xn = sb.bf16(8 * T)
    sq = [sb.f32(512) for _ in range(2)]
    rstd = sb.f32(512)
    cosb, sinb = sb.f32(T), sb.f32(T)
    rwk = [sb.f32(T) for _ in range(3)]
    hwk = [sb.f32(T) for _ in range(4)]
    kraw = [sb.bf16(16 + T) for _ in range(8)]
    w1 = [sb.bf16(4096) for _ in range(2)]
    peT = [sb.bf16(16) for _ in range(2)]
    w2k = sb.bf16(256)
    w2v = sb.bf16(128)
    b2v = sb.f32(64)
    one1 = sb.f32(32)
    hb = sb.f32(4)
    wv = sb.bf16(8 * 512)
    NW = 4
    wbs = [sb.bf16(1024) for _ in range(NW)]
    hx = sb.f32(512)
    hy = sb.f32(512)
    hidT = sb.bf16(512)
    kcw = sb.f32(128)
    for i in range(2):
        self.wload(w1[i], W["w1"][i], 4096, ("w1", i))
        A("pool", lambda e, i=i: e.dma_start(out=peT[i], in_=W["peT"][i]), writes=[("peT", i)], dkey=("peT", i))
    A("pool", lambda e: e.dma_start(out=w2k, in_=W["w2k"]), writes=["w2k"], dkey="w2k")
    A("pool", lambda e: e.dma_start(out=w2v, in_=W["w2v"]), writes=["w2v"], dkey="w2v")
    A("sp", lambda e: e.dma_start(out=b2v[0:1, :], in_=W["b2v"]), writes=["b2v"], dkey="b2v")
    A("pool", lambda e: e.memset(one1[0:1, :], 1.0), writes=["one1"])
    self.wload(wv, W["wv"], 4096, "wv")
    for r_ in range(8):
        A("pool", lambda e, r_=r_: e.memset(kraw[r_], 0.0), writes=[("kraw", r_)])
    pb6 = self.ps[:, 6 * 512:6 * 512 + 4]
    for i in range(2):
        for hh in range(2):
            col = i * 2 + hh
            for l in range(16):
                A("pe", lambda e, i=i, hh=hh, l=l, col=col: e.matmul(
                    out=pb6[:, col:col + 1], lhsT=w1[i][:, l * 256 + hh * 128:l * 256 + (hh + 1) * 128],
                    rhs=peT[i][:, l:l + 1], start=(l == 0), stop=(l == 15), skip_group_check=True),
                    reads=[("w1", i), ("peT", i)], writes=[("ps", 6)])
    A("dve", lambda e: e.tensor_tensor(out=hb, in0=pb6, in1=ncs["hb1"], op=ALU.add), reads=[("ps", 6), "ncs"], writes=["hb"])
    S.stop_at("P1")

    wc = [0]

    def tileA(t):
        self.load_x_tile(src, x32, "x32", t, T)
        self.rmsnorm_tile(x32, "x32", xn, "xn", gam, "gam", T, sq, rstd, 7, "n")
        self.rope_tables(pos_d, t, T, cosb, sinb, rwk, ncs["inv"], "rope")
        S.stop_at("P2")
        xk = [("xn", k, 0) for k in range(8)]
        for oc in range(10):
            wi = wc[0] % NW
            wc[0] += 1
            self.wload(wbs[wi], W["wka"][oc], 1024, ("win", wi))
            pbk = oc % 2
            pt = self.bank(pbk)
            for k in range(8):
                A("pe", lambda e, k=k, pt=pt, wi=wi: e.matmul(out=pt, lhsT=wbs[wi][:, k * 128:(k + 1) * 128],
                                                            rhs=xn[:, k * T:(k + 1) * T], start=(k == 0), stop=(k == 7)),
                  reads=[("win", wi), ("xn", k, 0)], writes=[("ps", pbk)])
            if oc < 8:
                A("act", lambda e, oc=oc, pt=pt: e.copy(out=kraw[oc][0:64, 16:16 + T], in_=pt[0:64, :]),
                  reads=[("ps", pbk)], writes=[("kraw", oc)])
                A("dve", lambda e, oc=oc, pt=pt: e.tensor_copy(out=kraw[oc][64:128, 15:15 + T], in_=pt[64:128, :]),
                  reads=[("ps", pbk)], writes=[("kraw", oc)])
            else:
                pp = oc - 8
                o_ap = ksT[:, pp * SQ + t * T:pp * SQ + (t + 1) * T]
                self.headnorm_rope(pt, ("ps", pbk), ncs["kw3"][:, 1:2], cosb, sinb, "hn",
                                   [(o_ap, True, ("ksT", pp, t))], 1.0, hwk, ncs, T)
        S.stop_at("P3")
        for blk in range(4):
            kt = t * 4 + blk
            pv = self.bank(4, 256)
            for k in range(8):
                A("pe", lambda e, k=k, blk=blk, pv=pv: e.matmul(
                    out=pv, lhsT=xn[:, k * T + blk * 128:k * T + (blk + 1) * 128], rhs=wv[:, k * 512:k * 512 + 256],
                    start=(k == 0), stop=(k == 7)), reads=[("xn", k, 0), "wv"], writes=[("ps", 4)])
            for j_, (st_, nm) in enumerate(((vsS, "vsS"),)):
                for g in range(4):
                    off = kt * VW + VOFF[g] + (64 if g % 2 else 0)
                    eng = "act" if (g + j_) % 2 == 0 else "dve"
                    fn = (lambda e, st_=st_, off=off, g=g, j_=j_, pv=pv: e.copy(
                        out=st_[:, off:off + 64], in_=pv[:, j_ * 256 + g * 64:j_ * 256 + (g + 1) * 64])) if eng == "act" else \
                        (lambda e, st_=st_, off=off, g=g, j_=j_, pv=pv: e.tensor_copy(
                            out=st_[:, off:off + 64], in_=pv[:, j_ * 256 + g * 64:j_ * 256 + (g + 1) * 64]))
                    A(eng, fn, reads=[("ps", 4)], writes=[(nm, kt)])
        S.stop_at("P4")
        p5 = self.bank(5)
        for i in range(2):
            for hh in range(2):
                for g in range(4):
                    col = ((i * 2 + hh) * 4 + g) * 32
                    ri = i * 4 + g
                    src_ = kraw[ri]
                    for l in range(16):
                        A("pe", lambda e, i=i, hh=hh, l=l, col=col, src_=src_: e.matmul(
                            out=p5[:, col:col + 32], lhsT=w1[i][:, l * 256 + hh * 128:l * 256 + (hh + 1) * 128],
                            rhs=src_[:, 2 * l:2 * l + 16 * 31 + 1:16], start=(l == 0), stop=(l == 15), skip_group_check=True),
                            reads=[("w1", i), ("kraw", ri)], writes=[("ps", 5)])
        for r_ in range(8):
            A("pool", lambda e, r_=r_: e.tensor_copy(out=kraw[r_][:, 0:16], in_=kraw[r_][:, T:T + 16]),
              reads=[("kraw", r_)], writes=[("kraw", r_)])
        S.stop_at("P5")
        for q_ in range(4):
            A("act", lambda e, q_=q_: e.activation(out=hx[:, q_ * 128:(q_ + 1) * 128], in_=p5[:, q_ * 128:(q_ + 1) * 128],
                                                   func=AF.Identity, bias=hb[:, q_:q_ + 1]),
              reads=[("ps", 5), "hb"], writes=["hx"])
        A("dve", lambda e: e.tensor_tensor(out=hy, in0=hx, in1=hx, op=ALU.mult), reads=["hx"], writes=["hy"])
        A("dve", lambda e: e.tensor_scalar(out=hy, in0=hy, scalar1=0.044715, scalar2=1.0, op0=ALU.mult, op1=ALU.add),
          reads=["hy"], writes=["hy"])
        A("dve", lambda e: e.tensor_tensor(out=hy, in0=hy, in1=hx, op=ALU.mult), reads=["hy", "hx"], writes=["hy"])
        A("act", lambda e: e.activation(out=hy, in_=hy, func=AF.Tanh, scale=0.7978845608028654), reads=["hy"], writes=["hy"])
        A("dve", lambda e: e.tensor_scalar(out=hy, in0=hy, scalar1=0.5, scalar2=0.5, op0=ALU.mult, op1=ALU.add),
          reads=["hy"], writes=["hy"])
        A("dve", lambda e: e.tensor_tensor(out=hidT, in0=hy, in1=hx, op=ALU.mult), reads=["hy", "hx"], writes=["hidT"])
        p6 = self.ps[:, 6 * 512:6 * 512 + 128]
        for g in range(4):
            for hh in range(2):
                col = ((0 * 2 + hh) * 4 + g) * 32
                A("pe", lambda e, g=g, hh=hh, col=col: e.matmul(out=p6[:, g * 32:(g + 1) * 32], lhsT=w2k[:, hh * 128:(hh + 1) * 128],
                                                                rhs=hidT[:, col:col + 32], start=(hh == 0), stop=(hh == 1),
                                                                skip_group_check=True),
                  reads=["hidT", "w2k"], writes=[("ps", 6)])
        A("act", lambda e: e.activation(out=kcw, in_=p6, func=AF.Identity, bias=ncs["b2k"]), reads=[("ps", 6), "ncs"], writes=["kcw"])
        A("act", lambda e: e.activation(out=hwk[0][:, 0:128], in_=kcw, func=AF.Square), reads=["kcw"], writes=["kcsq"])
        A("pe", lambda e: e.matmul(out=self.bank(2, 128), lhsT=ncs["ones_bd"], rhs=hwk[0][:, 0:128], start=True, stop=True),
          reads=["kcsq", "ncs"], writes=[("ps", 2)])
        A("act", lambda e: e.activation(out=hwk[1][:, 0:128], in_=self.bank(2, 128), func=AF.Sqrt, bias=self.epsc, scale=1.0 / 64),
          reads=[("ps", 2)], writes=["kcrs"])
        A("dve", lambda e: e.reciprocal(out=hwk[1][:, 0:128], in_=hwk[1][:, 0:128]), reads=["kcrs"], writes=["kcrs"])
        kc3 = kcT.rearrange("p (g n) -> p g n", g=4)[:, :, t * 32:(t + 1) * 32]
        A("dve", lambda e, kc3=kc3: e.scalar_tensor_tensor(out=kc3, in0=v3(kcw, 4), scalar=ncs["kw3"][:, 0:1],
                                                            in1=v3(hwk[1][:, 0:128], 4), op0=ALU.mult, op1=ALU.mult),
          reads=["kcw", "kcrs", "ncs"], writes=[("kcT", t)])
        p6v = self.ps[0:32, 6 * 512 + 128:6 * 512 + 128 + 256]
        for g in range(4):
            for hh in range(2):
                col = ((1 * 2 + hh) * 4 + g) * 32
                A("pe", lambda e, g=g, hh=hh, col=col: e.matmul(out=p6v[:, g * 64:(g + 1) * 64], lhsT=hidT[:, col:col + 32],
                                                                rhs=w2v[:, hh * 64:(hh + 1) * 64], start=(hh == 0), stop=False,
                                                                skip_group_check=True),
                  reads=["hidT", "w2v"], writes=[("ps", 6)])
            A("pe", lambda e, g=g: e.matmul(out=p6v[:, g * 64:(g + 1) * 64], lhsT=one1[0:1, :], rhs=b2v[0:1, :], start=False, stop=True,
                                            skip_group_check=True), reads=["one1", "b2v"], writes=[("ps", 6)])
        for g in range(4):
            off = t * VW + VOFF[g] + (64 if g % 2 else 0)
            A("act", lambda e, g=g, off=off: e.copy(out=vcS[0:32, off:off + 64], in_=p6v[:, g * 64:(g + 1) * 64]),
              reads=[("ps", 6)], writes=[("vcS", t)])

    for t in range(NT):
        tileA(t)
        S.stop_at("P6a")
    S.stop_at("P6")
    self.nsa_st = dict(ncs=ncs, ksT=ksT, kwT=kwT, vsS=vsS, vwS=vwS, kcT=kcT, vcS=vcS, gam=gam)
    self.dbg = dict(ksT=ksT, kwT=kwT, vsS=vsS, vwS=vwS, kcT=kcT, vcS=vcS)
    sb.release(mB)
    S.barrier()
    nsa_queries(self, src, dst, W, pos_d, T, NT, SQ)
    sb.release(m)


K.nsa_phase = nsa_phase


def nsa_queries(self, src, dst, W, pos_d, T, NT, SQ):
    S, sb = self.S, self.sb
    A = S.add
    st = self.nsa_st
    ncs, ksT, kwT, vsS, vwS, kcT, vcS, gam = (st[k_] for k_ in ("ncs", "ksT", "kwT", "vsS", "vwS", "kcT", "vcS", "gam"))

    def v3(ap, a):
        return ap.rearrange("p (a b) -> p a b", a=a)

    nbc = sb.bf16(NBC_W)
    self.wload(nbc, W["nbc"], NBC_W, "nbc")
    nb = {n: nbc[:, o:o + w] for n, (o, w) in NBC.items()}
    x32 = sb.f32(8 * T)
    xn = sb.bf16(8 * T)
    sq = [sb.f32(512) for _ in range(2)]
    rstd = sb.f32(512)
    cosb, sinb = sb.f32(T), sb.f32(T)
    hwk = [sb.f32(T) for _ in range(4)]
    wv = sb.bf16(8 * 512)
    self.wload(wv, W["wv"], 4096, "wv")
    qnT = [[sb.bf16(T) for _ in range(4)] for _ in range(2)]
    qrT = [[sb.bf16(T) for _ in range(4)] for _ in range(2)]
    for hf in range(2):
        for i in range(4):
            A("pool", lambda e, hf=hf, i=i: e.memset(qnT[hf][i], 0.0), writes=[("qn", i, hf)])
            A("pool", lambda e, hf=hf, i=i: e.memset(qrT[hf][i], 0.0), writes=[("qr", i, hf)])
    gt = [sb.bf16(T) for _ in range(12)]
    ogacc = [sb.f32(T) for _ in range(4)]
    ogb = sb.bf16(8 * T)
    impT = sb.f32(T)
    selc = sb.f32(T)
    sc = sb.f32(256)
    sc2 = sb.f32(64)
    m8 = sb.f32(16)
    bm = sb.bf16(4 * 128)
    selbT = sb.bf16(T)
    pT = [sb.bf16(T) for _ in range(4)]
    rz = sb.f32(T)
    rzb = sb.f32(T)
    otmp = sb.f32(T)
    imptmp = sb.f32(T)
    rwk = [rzb, otmp, imptmp]
    rwkeys = ["rzb", "otmp", "imptmp"]
    NW = 3
    wbs = [sb.bf16(1024) for _ in range(NW)]
    wob = [sb.bf16(1024) for _ in range(2)]
    A("pool", lambda e: e.memset(bm, 0.0), writes=["bm"])
    A("pool", lambda e: e.memset(rz, 0.0), writes=["rz"])
    wc = [0, 0]
    pcnt = [0, 0]

    hbc = [0]

    def head_branch(kind, i, g, t, first, ch=0):
        pp, hf = g // 2, g % 2
        q_ap = (qnT if kind == "cmp" else qrT)[hf][i]
        qkey = ("qn" if kind == "cmp" else "qr", i, hf)
        even = (g % 2 == 0)
        M = 65 if even else 128
        voff = VOFF[g]
        ob = 4 + ch
        scb = (2, 3) if ch == 0 else (0, 1)
        okey = ("ps", ob)
        oacc = self.ps[0:M, ob * 512:(ob + 1) * 512]
        if kind == "cmp":
            tiles = list(range(t + 1))
            KP = 32
        elif kind == "sel":
            tiles = list(range(4 * t + 4))
            KP = 128
        else:
            tiles = list(range(max(0, 4 * t - 4), 4 * t + 4))
            KP = 128
        nt_ = len(tiles)
        store, snm = {"cmp": (vcS, "vcS"), "sel": (vsS, "vsS"), "win": (vwS, "vwS")}[kind]

        def emit_pv(n_, kt, pbuf, pkey):
            vi = kt if kind != "win" else ((kt // 4) % 2) * 4 + kt % 4
            vl = store[0:KP, vi * VW + voff:vi * VW + voff + M]
            A("pe", lambda e, vl=vl, pbuf=pbuf, n_=n_: e.matmul(
                out=oacc, lhsT=vl, rhs=pbuf[0:KP, :], start=(n_ == 0), stop=(n_ == nt_ - 1)),
                reads=[pkey, (snm, vi)], writes=[okey])
            if kind == "cmp":
                A("pe", lambda e, kt=kt, pbuf=pbuf, n_=n_: e.matmul(
                    out=self.ps[0:64, 6 * 512:7 * 512], lhsT=nb["ovl"][0:32, kt * 64:(kt + 1) * 64], rhs=pbuf[0:32, :],
                    start=(n_ == 0), stop=(n_ == nt_ - 1)), reads=[pkey, "nbc"], writes=[("ps", 6)])

        pend = None
        for n_, kt in enumerate(tiles):
            sbk = scb[pcnt[ch] % 2]
            pbuf = pT[ch * 2 + pcnt[ch] % 2]
            pkey = ("pT", ch * 2 + pcnt[ch] % 2)
            pcnt[ch] += 1
            ps_s = self.ps[0:KP, sbk * 512:(sbk + 1) * 512]
            mm = []
            if kind == "cmp":
                mm.append((kcT[:, g * 32 * NT + kt * 32:g * 32 * NT + (kt + 1) * 32], q_ap, [("kcT", kt), qkey]))
                if kt == t:
                    mm.append((nb["id128"][:, 0:32], (nb["cmpb0"] if t == 0 else nb["cmpb"]), ["nbc"]))
                elif kt == 0:
                    mm.append((nb["id128"][:, 0:32], nb["cmpr0"], ["nbc"]))
            elif kind == "sel":
                mm.append((ksT[:, pp * SQ + kt * 128:pp * SQ + (kt + 1) * 128], q_ap, [("ksT", pp, kt // 4), qkey]))
                mm.append((nb["efull"][:, kt * 128:(kt + 1) * 128], selbT, ["nbc", "selbT"]))
                if kt >= 4 * t:
                    dd = kt - 4 * t
                    mm.append((nb["id128"], nb["causb"][:, dd * 512:(dd + 1) * 512], ["nbc"]))
            else:
                slot = (kt // 4) % 2
                ko = pp * 1024 + slot * 512 + (kt % 4) * 128
                mm.append((kwT[:, ko:ko + 128], q_ap, [("kwT", pp, slot), qkey]))
                dd = kt - 4 * t
                mask = nb["causb"][:, dd * 512:(dd + 1) * 512] if dd >= 0 else nb["bandb"][:, (dd + 4) * 512:(dd + 5) * 512]
                mm.append((nb["id128"], mask, ["nbc"]))
            for j_, (l_, r_, rd) in enumerate(mm):
                A("pe", lambda e, l_=l_, r_=r_, j_=j_, ps_s=ps_s, last=(j_ == len(mm) - 1): e.matmul(
                    out=ps_s, lhsT=l_, rhs=r_, start=(j_ == 0), stop=last), reads=rd, writes=[("ps", sbk)])
            A("act", lambda e, pbuf=pbuf, ps_s=ps_s: e.activation(out=pbuf[0:KP, :], in_=ps_s, func=AF.Exp),
              reads=[("ps", sbk)], writes=[pkey])
            if pend is not None:
                emit_pv(*pend)
            pend = (n_, kt, pbuf, pkey)
            yield
        emit_pv(*pend)
        yield
        zr = 64 if even else 0
        A("act", lambda e, zr=zr: e.activation(out=rz[zr:zr + 1, :], in_=self.ps[zr:zr + 1, ob * 512:(ob + 1) * 512], func=AF.Ln, bias=1e-18),
          reads=[okey], writes=["rz"])
        A("pe", lambda e, zr=zr: e.matmul(out=self.bank(7), lhsT=ncs["sel64" if zr == 64 else "sel0"], rhs=rz, start=True, stop=True),
          reads=["rz", "ncs"], writes=[("ps", 7)])
        A("act", lambda e: e.activation(out=rzb, in_=self.bank(7), func=AF.Exp, scale=-1.0), reads=[("ps", 7)], writes=["rzb"])
        r_ = {"cmp": 0, "sel": 1, "win": 2}[kind]
        gtile = gt[r_ * 4 + i]
        orow = slice(0, 64) if even else slice(64, 128)
        A("dve", lambda e, orow=orow: e.tensor_tensor(out=otmp[orow, :], in0=self.ps[orow, ob * 512:(ob + 1) * 512], in1=rzb[orow, :], op=ALU.mult),
          reads=[okey, "rzb"], writes=["otmp"])
        if first:
            A("dve", lambda e, orow=orow, gtile=gtile, i=i: e.tensor_tensor(out=ogacc[i][orow, :], in0=otmp[orow, :], in1=gtile[orow, :], op=ALU.mult),
              reads=["otmp", ("gt", r_ * 4 + i)], writes=[("og", i, g % 2)])
        else:
            A("dve", lambda e, orow=orow, gtile=gtile: e.tensor_tensor(out=otmp[orow, :], in0=otmp[orow, :], in1=gtile[orow, :], op=ALU.mult),
              reads=["otmp", ("gt", r_ * 4 + i)], writes=["otmp"])
            A("pool", lambda e, orow=orow, i=i: e.tensor_tensor(out=ogacc[i][orow, :], in0=ogacc[i][orow, :], in1=otmp[orow, :], op=ALU.add),
              reads=["otmp", ("og", i, g % 2)], writes=[("og", i, g % 2)])
        if kind == "cmp":
            if i == 0:
                A("dve", lambda e: e.tensor_tensor(out=impT[0:64, :], in0=self.ps[0:64, 6 * 512:7 * 512], in1=rzb[0:64, :], op=ALU.mult),
                  reads=[("ps", 6), "rzb"], writes=["impT"])
            else:
                A("dve", lambda e: e.tensor_tensor(out=imptmp[0:64, :], in0=self.ps[0:64, 6 * 512:7 * 512],
                                                   in1=rzb[0:64, :], op=ALU.mult), reads=[("ps", 6), "rzb"], writes=["imptmp"])
                A("pool", lambda e: e.tensor_tensor(out=impT[0:64, :], in0=impT[0:64, :], in1=imptmp[0:64, :], op=ALU.add),
                  reads=["imptmp", "impT"], writes=["impT"])

    def sel_mask(g, t):
        pm = self.ps[:, 6 * 512:6 * 512 + 256]
        for blk in range(4):
            A("pe", lambda e, blk=blk: e.matmul(out=pm[:, blk * 64:(blk + 1) * 64], lhsT=impT[0:64, blk * 128:(blk + 1) * 128],
                                                rhs=ncs["id128"][0:64, 0:64], start=True, stop=True, skip_group_check=True),
              reads=["impT", "ncs"], writes=[("ps", 6)])
        valid = selc[:, 0:256]
        addm = selc[:, 256:512]
        A("dve", lambda e: e.tensor_tensor(out=sc, in0=pm, in1=valid, op=ALU.mult), reads=[("ps", 6), "selc"], writes=["sc"])
        A("dve", lambda e: e.tensor_tensor(out=sc, in0=sc, in1=addm, op=ALU.add), reads=["sc", "selc"], writes=["sc"])
        for blk in range(4):
            sblk = sc[:, blk * 64:(blk + 1) * 64]
            A("dve", lambda e, sblk=sblk: e.max(out=m8[:, 0:8], in_=sblk), reads=["sc"], writes=["m8"])
            A("dve", lambda e, sblk=sblk: e.match_replace(out=sc2, in_to_replace=m8[:, 0:8], in_values=sblk, imm_value=-1e30),
              reads=["sc", "m8"], writes=["sc2"])
            A("dve", lambda e: e.max(out=m8[:, 8:16], in_=sc2), reads=["sc2"], writes=["m8"])
            A("dve", lambda e, sblk=sblk: e.tensor_scalar(out=sc2, in0=sblk, scalar1=m8[:, 15:16], scalar2=None, op0=ALU.is_ge),
              reads=["sc", "m8"], writes=["sc2"])
            A("dve", lambda e, blk=blk: e.tensor_tensor(out=sc2, in0=sc2, in1=valid[:, blk * 64:(blk + 1) * 64], op=ALU.mult),
              reads=["sc2", "selc"], writes=["sc2"])
            A("dve", lambda e, blk=blk: e.tensor_scalar(out=bm[:, blk * 128:blk * 128 + 64], in0=sc2, scalar1=-NEGB, scalar2=NEGB,
                                                        op0=ALU.mult, op1=ALU.add), reads=["sc2"], writes=["bm"])
        pm2 = self.ps[:, 6 * 512:7 * 512]
        for blk in range(4):
            A("pe", lambda e, blk=blk: e.matmul(out=pm2[:, blk * 128:(blk + 1) * 128], lhsT=bm[:, blk * 128:(blk + 1) * 128],
                                                rhs=nb["id128"], start=True, stop=True, skip_group_check=True),
              reads=["bm", "nbc"], writes=[("ps", 6)])
        A("act", lambda e: e.copy(out=selbT, in_=pm2), reads=[("ps", 6)], writes=["selbT"])

    def tileB(t):
        self.load_x_tile(src, x32, "x32", t, T)
        self.rmsnorm_tile(x32, "x32", xn, "xn", gam, "gam", T, sq, rstd, 7, "n")
        self.rope_tables(pos_d, t, T, cosb, sinb, rwk, ncs["inv"], "rope", rwkeys)
        A("sp", lambda e: e.dma_start(out=selc[:, 0:512], in_=W["selc"][t]), writes=["selc"], dkey="selc")
        slot = t % 2
        for pp in range(2):
            wi = wc[0] % NW
            wc[0] += 1
            self.wload(wbs[wi], W["wka"][10 + pp], 1024, ("win", wi))
            pbk = pp % 2
            pt = self.bank(pbk)
            for k in range(8):
                A("pe", lambda e, k=k, pt=pt, wi=wi: e.matmul(out=pt, lhsT=wbs[wi][:, k * 128:(k + 1) * 128],
                                                            rhs=xn[:, k * T:(k + 1) * T], start=(k == 0), stop=(k == 7)),
                  reads=[("win", wi), ("xn", k, 0)], writes=[("ps", pbk)])
            o_ap = kwT[:, pp * 1024 + slot * 512:pp * 1024 + (slot + 1) * 512]
            self.headnorm_rope(pt, ("ps", pbk), ncs["kw3"][:, 2:3], cosb, sinb, "hn",
                               [(o_ap, True, ("kwT", pp, slot))], 1.0, hwk, ncs, T)
        for blk in range(4):
            vi = slot * 4 + blk
            pv = self.bank(4, 256)
            for k in range(8):
                A("pe", lambda e, k=k, blk=blk, pv=pv: e.matmul(
                    out=pv, lhsT=xn[:, k * T + blk * 128:k * T + (blk + 1) * 128], rhs=wv[:, k * 512 + 256:(k + 1) * 512],
                    start=(k == 0), stop=(k == 7)), reads=[("xn", k, 0), "wv"], writes=[("ps", 4)])
            for g in range(4):
                off = vi * VW + VOFF[g] + (64 if g % 2 else 0)
                A("act", lambda e, off=off, g=g, pv=pv: e.copy(out=vwS[:, off:off + 64], in_=pv[:, g * 64:(g + 1) * 64]),
                  reads=[("ps", 4)], writes=[("vwS", vi)])
        for pp in range(2):
            for i in range(4):
                wi = wc[0] % NW
                wc[0] += 1
                self.wload(wbs[wi], W["wq"][pp * 4 + i], 1024, ("win", wi))
                pbk = i % 2
                pt = self.bank(pbk)
                for k in range(8):
                    A("pe", lambda e, k=k, pt=pt, wi=wi: e.matmul(out=pt, lhsT=wbs[wi][:, k * 128:(k + 1) * 128],
                                                                rhs=xn[:, k * T:(k + 1) * T], start=(k == 0), stop=(k == 7)),
                      reads=[("win", wi), ("xn", k, 0)], writes=[("ps", pbk)])
                self.headnorm_rope(pt, ("ps", pbk), ncs["qw"], cosb, sinb, "hn",
                                   [(qnT[0][i], False, ("qn", i, 0), slice(0, 64)), (qnT[1][i], False, ("qn", i, 1), slice(64, 128)),
                                    (qrT[0][i], True, ("qr", i, 0), slice(0, 64)), (qrT[1][i], True, ("qr", i, 1), slice(64, 128))],
                                   0.125, hwk, ncs, T)
            for r_ in range(3):
                for i in range(4):
                    wi = wc[0] % NW
                    wc[0] += 1
                    self.wload(wbs[wi], W["wg"][pp * 12 + r_ * 4 + i], 1024, ("win", wi))
                    pbk = i % 2
                    pt = self.bank(pbk)
                    for k in range(8):
                        A("pe", lambda e, k=k, pt=pt, wi=wi: e.matmul(out=pt, lhsT=wbs[wi][:, k * 128:(k + 1) * 128],
                                                                    rhs=xn[:, k * T:(k + 1) * T], start=(k == 0), stop=(k == 7)),
                          reads=[("win", wi), ("xn", k, 0)], writes=[("ps", pbk)])
                    A("act", lambda e, r_=r_, i=i, pt=pt: e.activation(out=gt[r_ * 4 + i], in_=pt, func=AF.Sigmoid),
                      reads=[("ps", pbk)], writes=[("gt", r_ * 4 + i)])
            S.stop_at("P7")
            def run_gens(gens):
                gens = list(gens)
                while gens:
                    for gn in list(gens):
                        try:
                            next(gn)
                        except StopIteration:
                            gens.remove(gn)

            for g in (2 * pp, 2 * pp + 1):
                for i in range(4):
                    run_gens([head_branch("cmp", i, g, t, True, i % 2)])
                sel_mask(g, t)
                for i in (0, 2):
                    run_gens([head_branch("sel", i, g, t, False, 0), head_branch("sel", i + 1, g, t, False, 1)])
                for i in (0, 2):
                    run_gens([head_branch("win", i, g, t, False, 0), head_branch("win", i + 1, g, t, False, 1)])
            for i in range(4):
                A("act", lambda e, pp=pp, i=i: e.copy(out=ogb[:, (pp * 4 + i) * T:(pp * 4 + i + 1) * T], in_=ogacc[i]),
                  reads=[("og", i, 0), ("og", i, 1)], writes=[("ogb", pp * 4 + i)])
        for o in range(8):
            wi = wc[1] % 2
            wc[1] += 1
            self.wload(wob[wi], W["wo"][o], 1024, ("wout", wi))
            pbk = o % 2
            pt = self.bank(pbk)
            for k in range(8):
                A("pe", lambda e, k=k, pt=pt, wi=wi: e.matmul(out=pt, lhsT=wob[wi][:, k * 128:(k + 1) * 128],
                                                            rhs=ogb[:, k * T:(k + 1) * T], start=(k == 0), stop=(k == 7)),
                  reads=[("wout", wi), ("ogb", k)], writes=[("ps", pbk)])
            xs = x32[:, o * T:(o + 1) * T]
            A("dve", lambda e, xs=xs, pt=pt: e.tensor_tensor(out=xs, in0=pt, in1=xs, op=ALU.add),
              reads=[("ps", pbk), ("x32", o)], writes=[("x32", o)])
            A("act", lambda e, o=o, t=t: e.dma_start(out=dst[o * 128:(o + 1) * 128, t * T:(t + 1) * T], in_=x32[:, o * T:(o + 1) * T]),
              reads=[("x32", o)], dkey=("st", o))

    for t in range(NT):
        tileB(t)
```
